# Optimizing a Trainium2 kernel written in Bass

```python
import math
import jax
import jax.numpy as jnp
from jax import lax
import numpy as np

D_MODEL = 1024
BATCH = 8
SEQ = 2048
DEPTH = 4
DEC_BATCH = 128
DEC_SEQ = 4
PAST_LEN = 16384
PAGE_SIZE = 128

N_MIXERS = 3
N_A = (DEPTH + 2) // 3
N_B = (DEPTH + 1) // 3
N_C = DEPTH // 3

D_FF = -(-8 * D_MODEL // (3 * 256)) * 256
NORM_EPS = 1e-6

HG_EXPAND = 128
HG_HEADS = D_MODEL // HG_EXPAND
HG_DK = HG_EXPAND
HG_DV = D_MODEL // HG_HEADS
HG_CHUNK = 64
HG_MIN_F = 1e-30

RW_HEAD = 64
RW_HEADS = D_MODEL // RW_HEAD
RW_DECAY_LORA = 64
RW_A_LORA = 64
RW_GATE_LORA = 128
RW_GN_EPS = 64e-5

MB_EXPAND = 2
MB_DI = MB_EXPAND * D_MODEL
MB_HEADDIM = 64
MB_HEADS = MB_DI // MB_HEADDIM
MB_GROUPS = 4
MB_DSTATE = 128
MB_CONV = 4
MB_CONV_DIM = MB_DI + 2 * MB_GROUPS * MB_DSTATE
MB_IN = 2 * MB_DI + 2 * MB_GROUPS * MB_DSTATE + MB_HEADS
MB_CHUNK = 128

kernel_name = 'hybrid_hgrn2_rwkv7_mamba2_decode_step'

F32 = jnp.float32


def _rmsnorm(x, g, eps=NORM_EPS):
    xf = x.astype(F32)
    y = xf * lax.rsqrt(jnp.mean(xf * xf, axis=-1, keepdims=True) + eps)
    return (y * g.astype(F32)).astype(x.dtype)


def _swiglu(h, w_gate, w_up, w_down):
    return (jax.nn.silu(h @ w_gate) * (h @ w_up)) @ w_down


def _to_chunks(a, c):
    b, t = a.shape[0], a.shape[1]
    return jnp.moveaxis(a.reshape((b, t // c, c) + a.shape[2:]), 1, 0)


def _from_chunks(a):
    a = jnp.moveaxis(a, 0, 1)
    return a.reshape((a.shape[0], a.shape[1] * a.shape[2]) + a.shape[3:])


def _masked_exp(tri, diff):
    return jnp.where(tri, jnp.exp(jnp.where(tri, diff, 0.0)), 0.0)


def _gla_chunk_scan(q, k, v, log_f, s0, c):
    tri = jnp.tril(jnp.ones((c, c), dtype=bool))[None, :, :, None, None]

    def step(s, inp):
        qc, kc, vc, lfc = inp
        g = jnp.cumsum(lfc, axis=1)
        o_inter = jnp.einsum('bihk,bhkv->bihv', qc * jnp.exp(g), s)
        decay = _masked_exp(tri, g[:, :, None] - g[:, None, :])
        att = jnp.einsum('bihk,bjhk,bijhk->bhij', qc, kc, decay)
        o_intra = jnp.einsum('bhij,bjhv->bihv', att, vc)
        g_last = g[:, -1]
        k_dec = kc * jnp.exp(g_last[:, None] - g)
        s_new = jnp.exp(g_last)[..., None] * s + jnp.einsum('bjhk,bjhv->bhkv', k_dec, vc)
        return s_new, o_inter + o_intra

    s, o = lax.scan(step, s0, (_to_chunks(q, c), _to_chunks(k, c), _to_chunks(v, c), _to_chunks(log_f, c)))
    return _from_chunks(o), s


def _hgrn2(h, s0, w_in, lb_logits, j, norm_g, w_out):
    b, t, _ = h.shape
    proj = (h @ w_in).astype(F32)
    q_r, f_r, i_r, g_r = jnp.split(proj, 4, axis=-1)
    sm = jax.nn.softmax(lb_logits.astype(F32), axis=0)
    lb = (jnp.cumsum(sm, axis=0) - sm[0])[j]
    f = lb + (1.0 - lb) * jax.nn.sigmoid(f_r)
    log_f = jnp.log(jnp.maximum(f, HG_MIN_F))
    k = (1.0 - lb) * jax.nn.sigmoid(-f_r)
    q = jax.nn.silu(q_r)
    hd = lambda a: a.reshape(b, t, HG_HEADS, -1)
    c = math.gcd(t, HG_CHUNK)
    o, s = _gla_chunk_scan(hd(q), hd(k), hd(i_r), hd(log_f), s0.astype(F32), c)
    o = _rmsnorm(o, norm_g) * jax.nn.silu(hd(g_r))
    y = o.reshape(b, t, D_MODEL).astype(h.dtype) @ w_out
    return y, s.astype(s0.dtype)


def _rwkv7(h, s0, shift0, mu, w_rkv, w0, w1, w2, a0, a1, a2, g1, g2, k_k, k_a, r_k,
           lnx_w, lnx_b, w_out):
    b, t, d = h.shape
    hf = h.astype(F32)
    prev = jnp.concatenate([shift0[:, None].astype(F32), hf[:, :-1]], axis=1)
    xx = prev - hf
    xs = hf[None] + xx[None] * mu[:, None, None, :].astype(F32)
    rkv = jnp.einsum('nbtd,nde->nbte', xs[:3], w_rkv)
    r, k, v = rkv[0], rkv[1], rkv[2]
    w_log = -jax.nn.softplus(-(w0 + jnp.tanh(xs[3] @ w1) @ w2)) - 0.5
    decay = jnp.exp(-jnp.exp(w_log))
    a = jax.nn.sigmoid(a0 + (xs[4] @ a1) @ a2)
    gate = jax.nn.sigmoid(xs[5] @ g1) @ g2
    hd = lambda z: z.reshape(b, t, RW_HEADS, RW_HEAD)
    kk = hd(k * k_k)
    kk = kk * lax.rsqrt(jnp.maximum(jnp.sum(kk * kk, axis=-1, keepdims=True), 1e-24))
    k = k * (1.0 + (a - 1.0) * k_a)
    r, k, v, decay, a = hd(r), hd(k), hd(v), hd(decay), hd(a)

    def step(s, inp):
        r_t, w_t, k_t, v_t, kk_t, a_t = inp
        sa = jnp.einsum('bhvk,bhk->bhv', s, -kk_t)
        s = (s * w_t[:, :, None, :] + sa[..., None] * (kk_t * a_t)[:, :, None, :]
             + v_t[..., None] * k_t[:, :, None, :])
        return s, jnp.einsum('bhvk,bhk->bhv', s, r_t)

    tm = lambda z: jnp.moveaxis(z, 1, 0)
    s, o = lax.scan(step, s0.astype(F32), (tm(r), tm(decay), tm(k), tm(v), tm(kk), tm(a)))
    o = jnp.moveaxis(o, 0, 1)
    mean = jnp.mean(o, axis=-1, keepdims=True)
    var = jnp.mean(jnp.square(o - mean), axis=-1, keepdims=True)
    o = ((o - mean) * lax.rsqrt(var + RW_GN_EPS)).reshape(b, t, d) * lnx_w + lnx_b
    bonus = jnp.sum(r * k * r_k, axis=-1, keepdims=True) * v
    o = (o + bonus.reshape(b, t, d)) * gate
    y = o.astype(h.dtype) @ w_out
    return y, s.astype(s0.dtype), hf[:, -1].astype(shift0.dtype)


def _ssd_chunk_scan(x, bm, cm, dt, da, s0, c):
    tri = jnp.tril(jnp.ones((c, c), dtype=bool))[None, :, :, None, None]

    def step(s, inp):
        xc, bc, cc, dtc, dac = inp
        cum = jnp.cumsum(dac, axis=1)
        seg = _masked_exp(tri, cum[:, :, None] - cum[:, None, :])
        cb = jnp.einsum('bign,bjgn->bijg', cc, bc)
        m = cb[..., None] * seg * dtc[:, None]
        y_intra = jnp.einsum('bijgh,bjghp->bighp', m, xc)
        y_inter = jnp.einsum('bign,bghpn->bighp', cc, s) * jnp.exp(cum)[..., None]
        last = cum[:, -1]
        wts = dtc * jnp.exp(last[:, None] - cum)
        s_new = jnp.exp(last)[..., None, None] * s + jnp.einsum('bjgn,bjgh,bjghp->bghpn', bc, wts, xc)
        return s_new, y_intra + y_inter

    s, y = lax.scan(step, s0, (_to_chunks(x, c), _to_chunks(bm, c), _to_chunks(cm, c),
                               _to_chunks(dt, c), _to_chunks(da, c)))
    return _from_chunks(y), s


def _mamba2(h, s0, conv0, w_in, conv_w, conv_b, dt_bias, a_log, d_skip, norm_g, w_out):
    b, t, _ = h.shape
    ng, hg, pd, ns = MB_GROUPS, MB_HEADS // MB_GROUPS, MB_HEADDIM, MB_DSTATE
    zxbcdt = (h @ w_in).astype(F32)
    z = zxbcdt[..., :MB_DI]
    xbc = zxbcdt[..., MB_DI:MB_DI + MB_CONV_DIM]
    dt_r = zxbcdt[..., MB_DI + MB_CONV_DIM:]
    xpad = jnp.concatenate([conv0.astype(F32), xbc], axis=1)
    conv = conv_b.astype(F32)
    for w in range(MB_CONV):
        conv = conv + xpad[:, w:w + t] * conv_w[w]
    xbc = jax.nn.silu(conv)
    xs = xbc[..., :MB_DI].reshape(b, t, ng, hg, pd)
    bm = xbc[..., MB_DI:MB_DI + ng * ns].reshape(b, t, ng, ns)
    cm = xbc[..., MB_DI + ng * ns:].reshape(b, t, ng, ns)
    dt = jax.nn.softplus(dt_r + dt_bias).reshape(b, t, ng, hg)
    da = dt * (-jnp.exp(a_log.astype(F32))).reshape(ng, hg)
    c = math.gcd(t, MB_CHUNK)
    y, s = _ssd_chunk_scan(xs, bm, cm, dt, da, s0.astype(F32).reshape(b, ng, hg, pd, ns), c)
    y = y + d_skip.reshape(ng, hg)[:, :, None] * xs
    y = y.reshape(b, t, MB_DI) * jax.nn.silu(z)
    yg = y.reshape(b, t, ng, -1)
    yg = yg * lax.rsqrt(jnp.mean(yg * yg, axis=-1, keepdims=True) + NORM_EPS)
    y = yg.reshape(b, t, MB_DI) * norm_g
    out = y.astype(h.dtype) @ w_out
    return (out, s.reshape(b, MB_HEADS, pd, ns).astype(s0.dtype),
            xpad[:, -(MB_CONV - 1):].astype(conv0.dtype))


def _trunk(x, st_hg, st_wkv, st_shift, st_ssm, st_conv, p):
    new_hg, new_wkv, new_shift, new_ssm, new_conv = [], [], [], [], []
    for i in range(DEPTH):
        j = i // N_MIXERS
        h = _rmsnorm(x, p['norm_mix'][i])
        if i % N_MIXERS == 0:
            out, s = _hgrn2(h, st_hg[j], p['hg_w_in'][j], p['hg_lb_logits'], j,
                            p['hg_norm'][j], p['hg_w_out'][j])
            new_hg.append(s)
        elif i % N_MIXERS == 1:
            out, s, sh = _rwkv7(h, st_wkv[j], st_shift[j], p['rw_mu'][j], p['rw_w_rkv'][j],
                                p['rw_w0'][j], p['rw_w1'][j], p['rw_w2'][j],
                                p['rw_a0'][j], p['rw_a1'][j], p['rw_a2'][j],
                                p['rw_g1'][j], p['rw_g2'][j], p['rw_k_k'][j], p['rw_k_a'][j],
                                p['rw_r_k'][j], p['rw_lnx_w'][j], p['rw_lnx_b'][j], p['rw_w_out'][j])
            new_wkv.append(s)
            new_shift.append(sh)
        else:
            out, s, cv = _mamba2(h, st_ssm[j], st_conv[j], p['mb_w_in'][j], p['mb_conv_w'][j],
                                 p['mb_conv_b'][j], p['mb_dt_bias'][j], p['mb_A_log'][j],
                                 p['mb_D'][j], p['mb_norm'][j], p['mb_w_out'][j])
            new_ssm.append(s)
            new_conv.append(cv)
        x = x + out
        x = x + _swiglu(_rmsnorm(x, p['norm_ffn'][i]), p['ffn_w_gate'][i], p['ffn_w_up'][i],
                        p['ffn_w_down'][i])
    y = _rmsnorm(x, p['norm_final'])
    return (y, jnp.stack(new_hg), jnp.stack(new_wkv), jnp.stack(new_shift),
            jnp.stack(new_ssm), jnp.stack(new_conv))


def setup_inputs(seed: int = 0) -> dict:
    key = jax.random.key(seed)
    ks = iter(jax.random.split(key, 64))
    nrm = lambda shape, scale: scale * jax.random.normal(next(ks), shape, F32)
    D = D_MODEL
    inp = {}
    inp['x_prompt'] = nrm((BATCH, SEQ, D), 1.0)
    inp['x_sample'] = nrm((DEC_BATCH, DEC_SEQ, D), 1.0)
    inp['state_hgrn'] = nrm((N_A, DEC_BATCH, HG_HEADS, HG_DK, HG_DV), 0.5)
    inp['state_wkv'] = nrm((N_B, DEC_BATCH, RW_HEADS, RW_HEAD, RW_HEAD), 0.3)
    inp['state_shift'] = nrm((N_B, DEC_BATCH, D), 1.0)
    inp['state_ssm'] = nrm((N_C, DEC_BATCH, MB_HEADS, MB_HEADDIM, MB_DSTATE), 0.3)
    inp['state_conv'] = nrm((N_C, DEC_BATCH, MB_CONV - 1, MB_CONV_DIM), 1.0)
    inp['norm_mix'] = 1.0 + nrm((DEPTH, D), 0.02)
    inp['norm_ffn'] = 1.0 + nrm((DEPTH, D), 0.02)
    inp['norm_final'] = 1.0 + nrm((D,), 0.02)
    inp['ffn_w_gate'] = nrm((DEPTH, D, D_FF), D ** -0.5)
    inp['ffn_w_up'] = nrm((DEPTH, D, D_FF), D ** -0.5)
    inp['ffn_w_down'] = nrm((DEPTH, D_FF, D), D_FF ** -0.5)
    inp['hg_w_in'] = nrm((N_A, D, 4 * D), D ** -0.5)
    inp['hg_lb_logits'] = nrm((N_A, D), 0.5)
    inp['hg_norm'] = 1.0 + nrm((N_A, HG_DV), 0.02)
    inp['hg_w_out'] = nrm((N_A, D, D), D ** -0.5)
    inp['rw_mu'] = jax.random.uniform(next(ks), (N_B, 6, D), F32)
    inp['rw_w_rkv'] = nrm((N_B, 3, D, D), D ** -0.5)
    inp['rw_w0'] = jnp.linspace(-6.0, -1.0, D, dtype=F32)[None] + nrm((N_B, D), 0.1)
    inp['rw_w1'] = nrm((N_B, D, RW_DECAY_LORA), D ** -0.5)
    inp['rw_w2'] = nrm((N_B, RW_DECAY_LORA, D), 0.1 * RW_DECAY_LORA ** -0.5)
    inp['rw_a0'] = nrm((N_B, D), 0.1)
    inp['rw_a1'] = nrm((N_B, D, RW_A_LORA), D ** -0.5)
    inp['rw_a2'] = nrm((N_B, RW_A_LORA, D), 0.1 * RW_A_LORA ** -0.5)
    inp['rw_g1'] = nrm((N_B, D, RW_GATE_LORA), D ** -0.5)
    inp['rw_g2'] = nrm((N_B, RW_GATE_LORA, D), RW_GATE_LORA ** -0.5)
    inp['rw_k_k'] = 0.85 + nrm((N_B, D), 0.02)
    inp['rw_k_a'] = 1.0 + nrm((N_B, D), 0.02)
    inp['rw_r_k'] = nrm((N_B, RW_HEADS, RW_HEAD), 0.1)
    inp['rw_lnx_w'] = 1.0 + nrm((N_B, D), 0.02)
    inp['rw_lnx_b'] = nrm((N_B, D), 0.01)
    inp['rw_w_out'] = nrm((N_B, D, D), D ** -0.5)
    inp['mb_w_in'] = nrm((N_C, D, MB_IN), D ** -0.5)
    inp['mb_conv_w'] = nrm((N_C, MB_CONV, MB_CONV_DIM), 0.5)
    inp['mb_conv_b'] = nrm((N_C, MB_CONV_DIM), 0.01)
    dt0 = jnp.exp(jax.random.uniform(next(ks), (N_C, MB_HEADS), F32,
                                     minval=math.log(1e-3), maxval=math.log(1e-1)))
    inp['mb_dt_bias'] = dt0 + jnp.log(-jnp.expm1(-dt0))
    inp['mb_A_log'] = jnp.log(jax.random.uniform(next(ks), (N_C, MB_HEADS), F32, minval=1.0, maxval=16.0))
    inp['mb_D'] = 1.0 + nrm((N_C, MB_HEADS), 0.1)
    inp['mb_norm'] = 1.0 + nrm((N_C, MB_DI), 0.02)
    inp['mb_w_out'] = nrm((N_C, MB_DI, D), MB_DI ** -0.5)
    return inp


def reference(x_prompt, x_sample, state_hgrn, state_wkv, state_shift, state_ssm, state_conv,
              norm_mix, norm_ffn, norm_final, ffn_w_gate, ffn_w_up, ffn_w_down,
              hg_w_in, hg_lb_logits, hg_norm, hg_w_out,
              rw_mu, rw_w_rkv, rw_w0, rw_w1, rw_w2, rw_a0, rw_a1, rw_a2, rw_g1, rw_g2,
              rw_k_k, rw_k_a, rw_r_k, rw_lnx_w, rw_lnx_b, rw_w_out,
              mb_w_in, mb_conv_w, mb_conv_b, mb_dt_bias, mb_A_log, mb_D, mb_norm, mb_w_out):
    p = dict(norm_mix=norm_mix, norm_ffn=norm_ffn, norm_final=norm_final,
             ffn_w_gate=ffn_w_gate, ffn_w_up=ffn_w_up, ffn_w_down=ffn_w_down,
             hg_w_in=hg_w_in, hg_lb_logits=hg_lb_logits, hg_norm=hg_norm, hg_w_out=hg_w_out,
             rw_mu=rw_mu, rw_w_rkv=rw_w_rkv, rw_w0=rw_w0, rw_w1=rw_w1, rw_w2=rw_w2,
             rw_a0=rw_a0, rw_a1=rw_a1, rw_a2=rw_a2, rw_g1=rw_g1, rw_g2=rw_g2,
             rw_k_k=rw_k_k, rw_k_a=rw_k_a, rw_r_k=rw_r_k, rw_lnx_w=rw_lnx_w,
             rw_lnx_b=rw_lnx_b, rw_w_out=rw_w_out,
             mb_w_in=mb_w_in, mb_conv_w=mb_conv_w, mb_conv_b=mb_conv_b,
             mb_dt_bias=mb_dt_bias, mb_A_log=mb_A_log, mb_D=mb_D, mb_norm=mb_norm,
             mb_w_out=mb_w_out)
    nb = x_prompt.shape[0]
    empty = lambda s: jnp.zeros((s.shape[0], nb) + s.shape[2:], s.dtype)
    y_prompt, hg_p, wkv_p, sh_p, ssm_p, cv_p = _trunk(
        x_prompt, empty(state_hgrn), empty(state_wkv), empty(state_shift),
        empty(state_ssm), empty(state_conv), p)
    y_sample, hg_s, wkv_s, sh_s, ssm_s, cv_s = _trunk(
        x_sample, state_hgrn, state_wkv, state_shift, state_ssm, state_conv, p)
    return (y_prompt, y_sample, hg_p, hg_s, wkv_p, wkv_s, sh_p, sh_s, ssm_p, ssm_s, cv_p, cv_s)
```

```python
import numpy as np
from contextlib import ExitStack, contextmanager
import concourse.bass as bass
import concourse.mybir as mybir
from concourse.bass_utils import run_bass_kernel_spmd

F32 = mybir.dt.float32
BF16 = mybir.dt.bfloat16
AF = mybir.ActivationFunctionType
ALU = mybir.AluOpType
AX = mybir.AxisListType

D = 1024
KC = 8
DFF = 2816
FC = 22
NCORES = 8
NB = 16
TS = 4
NS = NB * TS
EPS = 1e-6
ROT = 30000
import os
MBSTOP = int(os.environ.get('MBSTOP', '0'))
RWSTOP = int(os.environ.get('RWSTOP', '0'))


class _Rec:
    def __init__(self):
        self.call = None

    def __getattr__(self, name):
        def f(*a, **k):
            self.call = (name, a, k)
            return self
        return f


class Sched:
    ENGS = ("pe", "act", "dve", "pool", "sp")

    def __init__(self, nc):
        self.nc = nc
        self.streams = {e: [] for e in self.ENGS}
        self.count = {e: 0 for e in self.ENGS}
        self.observed = {e: {} for e in self.ENGS}
        self.last_write = {}
        self.readers = {}
        self.dmacount = {}
        self.semkeys = []
        self.lastmark = {}

    def _semkey(self, sk):
        if sk not in self.lastmark:
            self.semkeys.append(sk)
        return sk

    def op(self, eng, fn, reads=(), writes=(), dma=None):
        rec = _Rec()
        fn(rec)
        fn = rec.call
        need = {}

        def add(m):
            if m is None:
                return
            sk, v = m
            if need.get(sk, 0) < v:
                need[sk] = v

        for k in reads:
            add(self.last_write.get(k))
            if isinstance(k, tuple) and k[0] == "ps":
                for m in self.readers.get(k, ()):
                    add(m)
        for k in writes:
            add(self.last_write.get(k))
            for m in self.readers.get(k, ()):
                add(m)
        st = self.streams[eng]
        obs = self.observed[eng]
        for sk, v in need.items():
            if eng == "pe" and sk[0] == "pe":
                continue
            if obs.get(sk, 0) >= v:
                continue
            st.append(("wait", sk, v))
            obs[sk] = v
        if dma is not None:
            sk = self._semkey(("dma", dma))
            self.dmacount[dma] = self.dmacount.get(dma, 0) + 16
            marker = (sk, self.dmacount[dma])
            amt = 16
        else:
            n = self.count[eng]
            sk = self._semkey((eng, n // ROT))
            marker = (sk, n % ROT + 1)
            self.count[eng] = n + 1
            amt = 1
        self.lastmark[sk] = marker[1]
        st.append(("op", fn, sk, amt))
        for k in writes:
            self.last_write[k] = marker
            self.readers[k] = []
        for k in reads:
            if k not in writes:
                self.readers.setdefault(k, []).append(marker)
        return marker

    def barrier(self):
        for e in self.ENGS:
            st = self.streams[e]
            obs = self.observed[e]
            for sk, v in self.lastmark.items():
                if obs.get(sk, 0) >= v:
                    continue
                st.append(("wait", sk, v))
                obs[sk] = v
        self.last_write = {}
        self.readers = {}

    def emit(self):
        nc = self.nc
        sems = {}
        for sk in self.semkeys:
            sems[sk] = nc.alloc_semaphore(name="s_" + "_".join(str(x) for x in sk))
        streams = self.streams
        engmap = {"pe": "tensor", "act": "scalar", "dve": "vector", "pool": "gpsimd", "sp": "sync"}

        def run(e, engine):
            for ent in streams[e]:
                if ent[0] == "wait":
                    engine.wait_ge(sems[ent[1]], ent[2])
                else:
                    name, a, k = ent[1]
                    ins = getattr(engine, name)(*a, **k)
                    ins.then_inc(sems[ent[2]], ent[3])

        with nc.Block() as block:
            for e in self.ENGS:
                if not streams[e]:
                    continue

                def mk(e=e):
                    def f(engine):
                        run(e, engine)
                    return f

                getattr(block, engmap[e])(mk())


def _fm(v):
    v = np.asarray(v, np.float32)
    lead = v.shape[:-1]
    v = v.reshape(lead + (KC, 128))
    v = np.moveaxis(v, -1, 0)
    return np.ascontiguousarray(v)


class Builder:
    def __init__(self, TP, layers=(0, 1, 2, 0), with_ffn=True):
        self.TP = TP
        self.N = TP + NS
        self.layers = layers
        self.with_ffn = with_ffn
        self.nc = bass.Bass("TRN2", target_bir_lowering=False)
        self.S = Sched(self.nc)
        self.dram = {}
        self.in_names = []
        self.NSTG = 3
        self.stg_i = 0
        self.ps_i = 0
        self.rr = 0
        self.tbs = [(i * 512, 512, "p") for i in range(TP // 512)] + [(TP, NS, "s")]

    def din(self, name, shape):
        t = self.nc.dram_tensor(name, list(shape), F32, kind="ExternalInput").ap()
        self.dram[name] = t
        self.in_names.append(name)
        return t

    def dout(self, name, shape):
        t = self.nc.dram_tensor(name, list(shape), F32, kind="ExternalOutput").ap()
        self.dram[name] = t
        return t

    @contextmanager
    def T(self, name, shape, dt=F32):
        self.uid = getattr(self, "uid", 0) + 1
        with self.nc.sbuf_tensor("%s_%d" % (name, self.uid), list(shape), dt) as t:
            yield t.ap()

    def sb(self, name, shape, dt=F32):
        return self.nc.alloc_sbuf_tensor(name, list(shape), dt).ap()

    def nextps(self):
        i = self.ps_i % 6
        self.ps_i += 1
        return i

    def nextlong(self):
        self.pl_i = getattr(self, "pl_i", 0) + 1
        return 6 + self.pl_i % 2

    def ew(self):
        self.rr += 1
        return "act" if self.rr % 2 else "dve"

    def copy(self, eng, out, in_, reads, writes):
        if eng == "act":
            self.S.op("act", lambda e: e.activation(out=out, in_=in_, func=AF.Copy), reads=reads, writes=writes)
        else:
            self.S.op(eng, lambda e: e.tensor_copy(out=out, in_=in_), reads=reads, writes=writes)

    def load_w(self, dst, src, shape, wkey, scale=None):
        S = self.S
        slot = self.stg_i % self.NSTG
        self.stg_i += 1
        rows = shape[0]
        free = int(np.prod(shape[1:]))
        assert free <= 1024
        st = self.stage[slot][0:rows, 0:free]
        if len(shape) == 3:
            st = st.rearrange("p (a b) -> p a b", a=shape[1])
        S.op("sp", lambda e: e.dma_start(out=st, in_=src), writes=[("stg", slot)], dma="stg%d" % slot)
        if scale is None:
            S.op("pool", lambda e: e.tensor_copy(out=dst, in_=st), reads=[("stg", slot)], writes=[wkey])
        else:
            S.op("pool", lambda e: e.tensor_scalar(out=dst, in0=st, scalar1=scale, scalar2=None, op0=ALU.mult),
                 reads=[("stg", slot), "consts"], writes=[wkey])

    def build(self):
        nc, S, TP, N = self.nc, self.S, self.TP, self.N
        nl = len(self.layers)
        x_p = self.din("x_p", [TP, D])
        x_s = self.din("x_s", [NS, D])
        ident_d = self.din("ident", [128, 128])
        nmix_d = self.din("norm_mix", [128, 4, KC])
        nffn_d = self.din("norm_ffn", [128, 4, KC])
        nfin_d = self.din("norm_final", [128, KC])
        wg_d = self.din("ffn_w_gate", [4, D, DFF])
        wu_d = self.din("ffn_w_up", [4, D, DFF])
        wd_d = self.din("ffn_w_down", [4, DFF, D])
        y_p = self.dout("y_p", [TP, D])
        y_s = self.dout("y_s", [NS, D])

        self.xres = self.sb("xres", [128, KC, N])
        self.hT = self.sb("hT", [128, KC, N + 2], BF16)
        self.stage = [self.sb("stage%d" % i, [128, 1024]) for i in range(self.NSTG)]
        self.ident = self.sb("ident_sb", [128, 128])
        self.identb = self.sb("identb_sb", [128, 128], BF16)
        self.onesb = self.sb("onesb", [128, 128], BF16)
        self.nmix = self.sb("nmix", [128, 4, KC])
        self.nffn = self.sb("nffn", [128, 4, KC])
        self.nfin = self.sb("nfin", [128, KC])
        self.ps = [nc.alloc_psum_tensor("ps%d" % i, [128, 512], F32).ap() for i in range(8)]
        xres, hT = self.xres, self.hT

        S.op("sp", lambda e: e.dma_start(out=self.ident, in_=ident_d), writes=["consts"], dma="c0")
        S.op("sp", lambda e: e.dma_start(out=self.nmix, in_=nmix_d), writes=["consts"], dma="c0")
        S.op("sp", lambda e: e.dma_start(out=self.nffn, in_=nffn_d), writes=["consts"], dma="c0")
        S.op("sp", lambda e: e.dma_start(out=self.nfin, in_=nfin_d), writes=["consts"], dma="c0")
        S.op("pool", lambda e: e.tensor_copy(out=self.identb, in_=self.ident), reads=["consts"], writes=["consts2"])
        S.op("pool", lambda e: e.memset(self.onesb, 1.0), writes=["consts2"])
        self.onesf = self.sb("onesf", [128, 128])
        S.op("pool", lambda e: e.memset(self.onesf, 1.0), writes=["consts2"])
        S.op("pool", lambda e: e.memset(hT[:, :, 0:2], 0.0), writes=["hT0"])
        self.extra_consts()
        S.barrier()

        with self.T("xin0", [128, D], F32) as xin0, self.T("xin1", [128, D], F32) as xin1:
            xins = [xin0, xin1]
            ntile = TP // 128 + 1
            for j in range(ntile):
                xin = xins[j % 2]
                rows = 128 if j < TP // 128 else NS
                src = x_p[j * 128:(j + 1) * 128, :] if j < TP // 128 else x_s
                S.op("sp", lambda e, xin=xin, rows=rows, src=src: e.dma_start(out=xin[0:rows, :], in_=src),
                     writes=[("xin", j % 2)], dma="xin%d" % (j % 2))
                for half in range(2):
                    pi = self.nextps()
                    for q in range(4):
                        kc = half * 4 + q
                        S.op("pe", lambda e, pi=pi, q=q, kc=kc, xin=xin, rows=rows: e.transpose(
                            self.ps[pi][:, q * 128:q * 128 + rows], xin[0:rows, kc * 128:(kc + 1) * 128],
                            self.ident[0:rows, 0:rows]),
                            reads=[("xin", j % 2), "consts"], writes=[("ps", pi)])
                    dst = xres[:, half * 4:(half + 1) * 4, j * 128:j * 128 + rows]
                    srcp = self.ps[pi].rearrange("p (q t) -> p q t", q=4)[:, :, 0:rows]
                    self.copy(self.ew(), dst, srcp, [("ps", pi)], [("xres", j)])
        S.barrier()

        for li, kind in enumerate(self.layers):
            self.cur_li = li
            self.rmsnorm(self.nmix[:, li, :], to_h=True)
            S.barrier()
            if kind == 0:
                self.hgrn2(li // 3)
            elif kind == 1:
                self.rwkv7(li // 3)
            elif kind == 2:
                self.mamba2(li // 3)
            S.barrier()
            if self.with_ffn:
                self.rmsnorm(self.nffn[:, li, :], to_h=True)
                S.barrier()
                self.ffn(li, wg_d, wu_d, wd_d)
                S.barrier()
        self.final_norm(y_p, y_s)
        S.barrier()
        S.emit()
        return nc

    def extra_consts(self):
        nc, S = self.nc, self.S
        def cload(name, shape):
            d = self.din(name, shape)
            t = self.sb("c_" + name, shape)
            S.op("sp", lambda e: e.dma_start(out=t, in_=d), writes=["consts"], dma="c0")
            return t
        self.maskU2 = cload("maskU2", [128, 128])
        self.maskS = cload("maskS", [64, 64])
        self.maskB = cload("maskB", [64, 16])
        lg = cload("hg_lb_logits", [128, 2, KC])
        self.hgn = cload("hg_norm", [128, 2])
        self.hg_lb = self.sb("hg_lb", [128, 2, KC])
        self.hg_oml = self.sb("hg_oml", [128, 2, KC])
        self.hg_noml = self.sb("hg_noml", [128, 2, KC])
        lb, oml, noml = self.hg_lb, self.hg_oml, self.hg_noml
        S.op("dve", lambda e: e.memset(lb[:, 0, :], 0.0), writes=["hgc0"])
        S.op("dve", lambda e: e.tensor_tensor(out=lb[:, 1, :], in0=lg[:, 1, :], in1=lg[:, 0, :], op=ALU.subtract), reads=["consts"], writes=["hgc1"])
        S.op("act", lambda e: e.activation(out=lb[:, 1, :], in_=lb[:, 1, :], func=AF.Sigmoid), reads=["hgc1"], writes=["hgc1"])
        S.op("dve", lambda e: e.tensor_scalar(out=oml, in0=lb, scalar1=-1.0, scalar2=1.0, op0=ALU.mult, op1=ALU.add), reads=["hgc0", "hgc1"], writes=["hgc2"])
        S.op("dve", lambda e: e.tensor_scalar(out=noml, in0=lb, scalar1=1.0, scalar2=-1.0, op0=ALU.mult, op1=ALU.add), reads=["hgc0", "hgc1"], writes=["hgc3"])
        self.maskU = cload("maskU", [128, 128])
        self.sL = cload("sL", [128, 128])
        self.sLS = cload("sLS", [64, 64])
        self.maskC = cload("maskC", [128, NB, NS])
        self.mb_cw = cload("mb_conv_w", [128, 24, 4])
        self.mb_cb = cload("mb_conv_b", [128, 24])
        self.mb_dtb = cload("mb_dt_bias", [128, 32])
        alog = cload("mb_A_log", [128, 32])
        self.mb_D = cload("mb_D", [128, 32])
        self.din("mb_norm", [128, 2048])
        self.mb_negA = self.sb("mb_negA", [128, 32])
        S.op("act", lambda e: e.activation(out=self.mb_negA, in_=alog, func=AF.Exp), reads=["consts"], writes=["mbc0"])
        S.op("dve", lambda e: e.tensor_scalar(out=self.mb_negA, in0=self.mb_negA, scalar1=-1.0, scalar2=None, op0=ALU.mult), reads=["mbc0"], writes=["mbc0"])
        self.din("mb_w_in", [1, D, 5152])
        self.din("mb_w_out", [1, 2048, D])
        self.din("st_ssm", [NB, 32, 64, 128])
        self.din("st_conv", [NB, 3, 3072])
        self.dout("ssm_p", [32, 64, 128])
        self.dout("ssm_s", [NB, 32, 64, 128])
        self.dout("cv_p", [3, 3072])
        self.dout("cv_s", [NB, 3, 3072])
        self.sU = cload("sU", [128, 128])
        self.sUS = cload("sUS", [64, 64])
        self.blk1 = cload("blk1", [128, 128])
        self.rw_mu = cload("rw_mu", [128, 6, KC])
        self.rw_omu = self.sb("rw_omu", [128, 6, KC])
        S.op("dve", lambda e: e.tensor_scalar(out=self.rw_omu, in0=self.rw_mu, scalar1=-1.0, scalar2=1.0, op0=ALU.mult, op1=ALU.add), reads=["consts"], writes=["rwc0"])
        self.rw_vec = {}
        for nm in ("rw_w0", "rw_a0", "rw_k_k", "rw_k_a", "rw_r_k", "rw_lnx_w", "rw_lnx_b"):
            self.rw_vec[nm] = cload(nm, [128, KC])
        self.rw_omka = self.sb("rw_omka", [128, KC])
        S.op("dve", lambda e: e.tensor_scalar(out=self.rw_omka, in0=self.rw_vec["rw_k_a"], scalar1=-1.0, scalar2=1.0, op0=ALU.mult, op1=ALU.add), reads=["consts"], writes=["rwc1"])
        for nm, shp in (("rw_w_rkv", [3, D, D]), ("rw_w1", [D, 64]), ("rw_w2", [64, D]), ("rw_a1", [D, 64]), ("rw_a2", [64, D]),
                        ("rw_g1", [D, 128]), ("rw_g2", [128, D]), ("rw_w_out", [D, D]), ("st_wkv", [NB, 16, 64, 64]), ("st_shift", [NB, D])):
            self.din(nm, shp)
        self.dout("wkv_p", [16, 64, 64])
        self.dout("wkv_s", [NB, 16, 64, 64])
        self.dout("sh_p", [1, D])
        self.dout("sh_s", [NB, D])
        self.din("hg_w_in", [2, D, 4 * D])
        self.din("hg_w_out", [2, D, D])
        self.din("st_hg", [2, NB, 8, 128, 128])
        self.dout("hg_p", [2, 8, 128, 128])
        self.dout("hg_s", [2, NB, 8, 128, 128])

    def rmsnorm(self, gain, to_h=True, out_f32=None):
        nc, S = self.nc, self.S
        xres, hT = self.xres, self.hT
        with self.T("n_sq", [128, 2, 512], BF16) as sq, self.T("n_r", [128, 2, 512], F32) as rr:
            for bi, (c0, w, kind) in enumerate(self.tbs):
                pi = self.nextps()
                for kc in range(KC):
                    s = kc % 2
                    S.op("act", lambda e, s=s, kc=kc, c0=c0, w=w: e.activation(out=sq[:, s, 0:w], in_=xres[:, kc, c0:c0 + w], func=AF.Square),
                         reads=[("xres", "all")], writes=[("n_sq", s)])
                    S.op("pe", lambda e, pi=pi, s=s, kc=kc, w=w: e.matmul(self.ps[pi][:, 0:w], lhsT=self.onesb, rhs=sq[:, s, 0:w], start=(kc == 0), stop=(kc == KC - 1)),
                         reads=[("n_sq", s), "consts2"], writes=[("ps", pi)])
                r = rr[:, bi % 2, 0:w]
                S.op("act", lambda e, pi=pi, r=r, w=w: e.activation(out=r, in_=self.ps[pi][:, 0:w], func=AF.Ln, scale=1.0 / D, bias=EPS),
                     reads=[("ps", pi)], writes=[("n_r", bi % 2)])
                S.op("act", lambda e, r=r: e.activation(out=r, in_=r, func=AF.Exp, scale=-0.5),
                     reads=[("n_r", bi % 2)], writes=[("n_r", bi % 2)])
                for kc in range(KC):
                    if out_f32 is None:
                        dst = hT[:, kc, 2 + c0:2 + c0 + w]
                    else:
                        dst = out_f32(kc, c0, w)
                    S.op("dve", lambda e, dst=dst, kc=kc, c0=c0, w=w, r=r: e.scalar_tensor_tensor(
                        out=dst, in0=xres[:, kc, c0:c0 + w], scalar=gain[:, kc:kc + 1], in1=r, op0=ALU.mult, op1=ALU.mult),
                        reads=[("xres", "all"), ("n_r", bi % 2), "consts"], writes=[("hT", kc, bi)])

    def ffn(self, li, wg_d, wu_d, wd_d):
        nc, S = self.nc, self.S
        xres, hT = self.xres, self.hT
        nprompt = len(self.tbs) - 1
        half = max(1, nprompt // 2)
        sbs = [self.tbs[:half], self.tbs[half:]] if nprompt >= 2 else [self.tbs]
        maxw = max(sum(w for (_, w, _) in sb_) for sb_ in sbs)
        with self.T("f_act", [128, FC, maxw], BF16) as act, \
                self.T("f_wgu", [128, 2, KC, 2, 256], BF16) as wgu, \
                self.T("f_wd", [128, 2, FC, 128], BF16) as wd, \
                self.T("f_sg", [128, 2, 512], F32) as sg:
            for sbi, sb_ in enumerate(sbs):
                base = sb_[0][0]
                for fp in range(FC // 2):
                    slot = fp % 2
                    for gi, wsrc in enumerate((wg_d, wu_d)):
                        for kk in range(0, KC, 4):
                            src = wsrc[li, kk * 128:(kk + 4) * 128, fp * 256:(fp + 1) * 256].rearrange("(k p) c -> p k c", p=128)
                            self.load_w(wgu[:, slot, kk:kk + 4, gi, :], src, [128, 4, 256], ("f_wgu", slot, gi, kk))
                    for fi in range(2):
                        f = fp * 2 + fi
                        for (c0, w, kind) in sb_:
                            pg, pu = self.nextps(), self.nextps()
                            for gi, pi in ((0, pg), (1, pu)):
                                for kc in range(KC):
                                    S.op("pe", lambda e, pi=pi, slot=slot, kc=kc, gi=gi, fi=fi, c0=c0, w=w: e.matmul(
                                        self.ps[pi][:, 0:w], lhsT=wgu[:, slot, kc, gi, fi * 128:(fi + 1) * 128],
                                        rhs=hT[:, kc, 2 + c0:2 + c0 + w], start=(kc == 0), stop=(kc == KC - 1)),
                                        reads=[("f_wgu", slot, gi, 0), ("f_wgu", slot, gi, 4), ("hT", "all")], writes=[("ps", pi)])
                            ss = self.rr % 2
                            self.rr += 1
                            S.op("act", lambda e, pg=pg, ss=ss, w=w: e.activation(out=sg[:, ss, 0:w], in_=self.ps[pg][:, 0:w], func=AF.Silu),
                                 reads=[("ps", pg)], writes=[("f_sg", ss)])
                            S.op("dve", lambda e, pu=pu, ss=ss, f=f, c0=c0, w=w: e.tensor_tensor(
                                out=act[:, f, c0 - base:c0 - base + w], in0=sg[:, ss, 0:w], in1=self.ps[pu][:, 0:w], op=ALU.mult),
                                reads=[("ps", pu), ("f_sg", ss)], writes=[("f_act", f)])
                for dc in range(KC):
                    slot = dc % 2
                    for f0 in range(0, FC, 8):
                        nf = min(8, FC - f0)
                        src = wd_d[li, f0 * 128:(f0 + nf) * 128, dc * 128:(dc + 1) * 128].rearrange("(k p) c -> p k c", p=128)
                        self.load_w(wd[:, slot, f0:f0 + nf, :], src, [128, nf, 128], ("f_wd", slot, f0))
                    for (c0, w, kind) in sb_:
                        pi = self.nextps()
                        for f in range(FC):
                            S.op("pe", lambda e, pi=pi, slot=slot, f=f, c0=c0, w=w: e.matmul(
                                self.ps[pi][:, 0:w], lhsT=wd[:, slot, f, :],
                                rhs=act[:, f, c0 - base:c0 - base + w], start=(f == 0), stop=(f == FC - 1)),
                                reads=[("f_wd", slot, (f // 8) * 8), ("f_act", f)], writes=[("ps", pi)])
                        S.op("dve", lambda e, pi=pi, dc=dc, c0=c0, w=w: e.tensor_tensor(
                            out=xres[:, dc, c0:c0 + w], in0=xres[:, dc, c0:c0 + w], in1=self.ps[pi][:, 0:w], op=ALU.add),
                            reads=[("ps", pi), ("xres", dc, c0)], writes=[("xres", dc, c0)])

    def final_norm(self, y_p, y_s):
        nc, S, TP = self.nc, self.S, self.TP
        xres = self.xres
        gain = self.nfin
        with self.T("fn_y", [128, KC, 512], F32) as yT, self.T("fn_o", [128, 2, D], F32) as yo, \
                self.T("fn_sq", [128, 2, 512], BF16) as sq, self.T("fn_r", [128, 512], F32) as rr:
            cnt = 0
            for bi, (c0, w, kind) in enumerate(self.tbs):
                pi = self.nextps()
                for kc in range(KC):
                    s = kc % 2
                    S.op("act", lambda e, s=s, kc=kc, c0=c0, w=w: e.activation(out=sq[:, s, 0:w], in_=xres[:, kc, c0:c0 + w], func=AF.Square),
                         reads=[("xres", "all")], writes=[("fn_sq", s)])
                    S.op("pe", lambda e, s=s, kc=kc, pi=pi, w=w: e.matmul(self.ps[pi][:, 0:w], lhsT=self.onesb, rhs=sq[:, s, 0:w], start=(kc == 0), stop=(kc == KC - 1)),
                         reads=[("fn_sq", s), "consts2"], writes=[("ps", pi)])
                r = rr[:, 0:w]
                S.op("act", lambda e, pi=pi, r=r, w=w: e.activation(out=r, in_=self.ps[pi][:, 0:w], func=AF.Ln, scale=1.0 / D, bias=EPS),
                     reads=[("ps", pi)], writes=["fn_r"])
                S.op("act", lambda e, r=r: e.activation(out=r, in_=r, func=AF.Exp, scale=-0.5), reads=["fn_r"], writes=["fn_r"])
                for kc in range(KC):
                    S.op("dve", lambda e, kc=kc, c0=c0, w=w, r=r: e.scalar_tensor_tensor(
                        out=yT[:, kc, 0:w], in0=xres[:, kc, c0:c0 + w], scalar=gain[:, kc:kc + 1], in1=r, op0=ALU.mult, op1=ALU.mult),
                        reads=[("xres", "all"), "fn_r", "consts"], writes=[("fn_y", kc)])
                for j in range((w + 127) // 128):
                    rows = min(128, w - j * 128)
                    os_ = cnt % 2
                    cnt += 1
                    for half in range(2):
                        pi = self.nextps()
                        for q in range(4):
                            kc = half * 4 + q
                            S.op("pe", lambda e, pi=pi, q=q, kc=kc, j=j, rows=rows: e.transpose(
                                self.ps[pi][0:rows, q * 128:(q + 1) * 128], yT[:, kc, j * 128:j * 128 + rows], self.ident),
                                reads=[("fn_y", kc), "consts"], writes=[("ps", pi)])
                        self.copy(self.ew(), yo[0:rows, os_, half * 512:(half + 1) * 512], self.ps[pi][0:rows, :], [("ps", pi)], [("fn_o", os_, half)])
                    if kind == "p":
                        dst = y_p[c0 + j * 128:c0 + j * 128 + rows, :]
                    else:
                        dst = y_s
                    S.op("sp", lambda e, dst=dst, os_=os_, rows=rows: e.dma_start(out=dst, in_=yo[0:rows, os_, :]),
                         reads=[("fn_o", os_, 0), ("fn_o", os_, 1)], writes=[], dma="yout%d" % os_)

    def hgrn2(self, j):
        nc, S, TP = self.nc, self.S, self.TP
        xres, hT = self.xres, self.hT
        w_in, w_out = self.dram["hg_w_in"], self.dram["hg_w_out"]
        st_hg, hg_p, hg_s = self.dram["st_hg"], self.dram["hg_p"], self.dram["hg_s"]
        ps = self.ps
        W = 512
        with ExitStack() as es:
            win = es.enter_context(self.T("h_win", [128, 2, KC, 4, 128], BF16))
            wout = es.enter_context(self.T("h_wout", [128, 2, D], BF16))
            sig = es.enter_context(self.T("h_sig", [128, W]))
            lf = es.enter_context(self.T("h_lf", [128, W]))
            kk = es.enter_context(self.T("h_kk", [128, W]))
            q = es.enter_context(self.T("h_q", [128, W]))
            gate = es.enter_context(self.T("h_gate", [128, W]))
            g = es.enter_context(self.T("h_g", [128, W]))
            tmp = es.enter_context(self.T("h_tmp", [128, W]))
            tmp2 = es.enter_context(self.T("h_tmp2", [128, W]))
            ee = es.enter_context(self.T("h_e", [128, 4, W]))
            qg = es.enter_context(self.T("h_qg", [128, W], BF16))
            kg = es.enter_context(self.T("h_kg", [128, W], BF16))
            qG = es.enter_context(self.T("h_qG", [128, W], BF16))
            kdec = es.enter_context(self.T("h_kdec", [128, W], BF16))
            vtok = es.enter_context(self.T("h_vtok", [128, 4, 128], BF16))
            vT = es.enter_context(self.T("h_vT", [128, W], BF16))
            kdtok = es.enter_context(self.T("h_kdtok", [128, 4, 128], BF16))
            osb = es.enter_context(self.T("h_osb", [128, W]))
            osq = es.enter_context(self.T("h_osq", [128, W], BF16))
            rstd = es.enter_context(self.T("h_rstd", [128, W]))
            og = es.enter_context(self.T("h_og", [128, W], BF16))
            Sr = es.enter_context(self.T("h_Sr", [128, 9, 128]))
            qGf = es.enter_context(self.T("h_qGf", [128, W]))
            attm = es.enter_context(self.T("h_attm", [128, 4, 128], BF16))
            egl = es.enter_context(self.T("h_egl", [128, 16]))
            S0 = es.enter_context(self.T("h_S0", [128, NB, 128]))
            S0bf = es.enter_context(self.T("h_S0bf", [128, NB, 128], BF16))
            Vblk = es.enter_context(self.T("h_Vblk", [64, NB, 128], BF16))
            for h in range(8):
                slot = h % 2
                for p in range(4):
                    for kk0 in (0, 4):
                        src = w_in[j, kk0 * 128:(kk0 + 4) * 128, p * D + h * 128:p * D + (h + 1) * 128].rearrange("(k p) c -> p k c", p=128)
                        self.load_w(win[:, slot, kk0:kk0 + 4, p, :], src, [128, 4, 128], ("h_win", slot, p, kk0))
                self.load_w(wout[:, slot, :], w_out[j, h * 128:(h + 1) * 128, :], [128, D], ("h_wout", slot))
                wkeys = lambda p: [("h_win", slot, p, 0), ("h_win", slot, p, 4)]
                S.op("pool", lambda e: e.memset(Sr[:, 0, :], 0.0), writes=[("h_Sr", 0)])
                lbh, omlh, nomlh = self.hg_lb[:, j, h:h + 1], self.hg_oml[:, j, h:h + 1], self.hg_noml[:, j, h:h + 1]
                for (c0, w, kind) in self.tbs:
                    smp = kind == "s"
                    hc = 2 + c0
                    pq, pf, pg, pv = self.nextps(), self.nextps(), self.nextps(), self.nextps()
                    for p, pi in ((0, pq), (1, pf), (3, pg)):
                        for kc in range(KC):
                            S.op("pe", lambda e, pi=pi, p=p, kc=kc, hc=hc, w=w: e.matmul(ps[pi][:, 0:w], lhsT=win[:, slot, kc, p, :], rhs=hT[:, kc, hc:hc + w],
                                 start=(kc == 0), stop=(kc == KC - 1)), reads=wkeys(p) + [("hT", "all")], writes=[("ps", pi)])
                    ntile = (w + 127) // 128
                    rows = min(128, w)
                    for kc in range(KC):
                        S.op("pe", lambda e, kc=kc, hc=hc, w=w: e.matmul(ps[pv][:, 0:w], lhsT=win[:, slot, kc, 2, :], rhs=hT[:, kc, hc:hc + w],
                             start=(kc == 0), stop=(kc == KC - 1)), reads=wkeys(2) + [("hT", "all")], writes=[("ps", pv)])
                    self.copy("act", vT[:, 0:w], ps[pv][:, 0:w], [("ps", pv)], ["h_vT"])
                    pvt = self.nextps()
                    pvtb = ps[pvt].bitcast(BF16)
                    for jt in range(ntile):
                        S.op("pe", lambda e, jt=jt, rows=rows: e.transpose(pvtb[0:rows, jt * 128:(jt + 1) * 128], vT[:, jt * 128:jt * 128 + rows], self.identb),
                             reads=["h_vT", "consts2"], writes=[("ps", pvt)])
                    self.copy("dve", vtok[0:rows, 0:ntile, :], pvtb[:, 0:512].rearrange("p (a b) -> p a b", a=4)[0:rows, 0:ntile, :], [("ps", pvt)], ["h_vtok"])
                    S.op("act", lambda e, w=w: e.activation(out=sig[:, 0:w], in_=ps[pf][:, 0:w], func=AF.Sigmoid), reads=[("ps", pf)], writes=["h_sig"])
                    S.op("act", lambda e, w=w: e.activation(out=q[:, 0:w], in_=ps[pq][:, 0:w], func=AF.Silu), reads=[("ps", pq)], writes=["h_q"])
                    S.op("act", lambda e, w=w: e.activation(out=gate[:, 0:w], in_=ps[pg][:, 0:w], func=AF.Silu), reads=[("ps", pg)], writes=["h_gate"])
                    S.op("dve", lambda e, w=w: e.tensor_scalar(out=lf[:, 0:w], in0=sig[:, 0:w], scalar1=omlh, scalar2=lbh, op0=ALU.mult, op1=ALU.add), reads=["h_sig"], writes=["h_lf"])
                    S.op("act", lambda e, w=w: e.activation(out=lf[:, 0:w], in_=lf[:, 0:w], func=AF.Ln), reads=["h_lf"], writes=["h_lf"])
                    S.op("dve", lambda e, w=w: e.tensor_scalar(out=kk[:, 0:w], in0=sig[:, 0:w], scalar1=nomlh, scalar2=omlh, op0=ALU.mult, op1=ALU.add), reads=["h_sig"], writes=["h_kk"])
                    if not smp:
                        nch = w // 64
                        for c in range(nch):
                            S.op("dve", lambda e, c=c: e.tensor_tensor_scan(out=g[:, c * 64:(c + 1) * 64], data0=self.onesf[:, 0:64], data1=lf[:, c * 64:(c + 1) * 64],
                                 initial=0.0, op0=ALU.mult, op1=ALU.add), reads=["h_lf", "consts2"], writes=["h_g"])
                        g3 = g.rearrange("p (c t) -> p c t", t=64)
                        bc = lambda col: g3[:, :, col:col + 1].broadcast_to([128, nch, 64])
                        v3 = lambda t_: t_.rearrange("p (c t) -> p c t", t=64)
                        S.op("dve", lambda e: e.tensor_tensor(out=v3(tmp), in0=g3, in1=bc(31), op=ALU.subtract), reads=["h_g"], writes=["h_tmp"])
                        S.op("dve", lambda e: e.tensor_tensor(out=v3(tmp2), in0=bc(63), in1=g3, op=ALU.subtract), reads=["h_g"], writes=["h_tmp2"])
                        S.op("act", lambda e: e.activation(out=ee[:, 0, :], in_=tmp, func=AF.Exp), reads=["h_tmp"], writes=[("h_e", 0)])
                        S.op("act", lambda e: e.activation(out=ee[:, 1, :], in_=tmp, func=AF.Exp, scale=-1.0), reads=["h_tmp"], writes=[("h_e", 1)])
                        S.op("act", lambda e: e.activation(out=ee[:, 2, :], in_=g, func=AF.Exp), reads=["h_g"], writes=[("h_e", 2)])
                        S.op("act", lambda e: e.activation(out=ee[:, 3, :], in_=tmp2, func=AF.Exp), reads=["h_tmp2"], writes=[("h_e", 3)])
                        S.op("act", lambda e: e.activation(out=egl[:, 0:nch], in_=g3[:, :, 63], func=AF.Exp), reads=["h_g"], writes=["h_egl"])
                        S.op("pool", lambda e: e.tensor_tensor(out=qg, in0=q, in1=ee[:, 0, :], op=ALU.mult), reads=["h_q", ("h_e", 0)], writes=["h_qg"])
                        S.op("pool", lambda e: e.tensor_tensor(out=kg, in0=kk, in1=ee[:, 1, :], op=ALU.mult), reads=["h_kk", ("h_e", 1)], writes=["h_kg"])
                        S.op("dve", lambda e: e.tensor_tensor(out=qGf, in0=q, in1=ee[:, 2, :], op=ALU.mult), reads=["h_q", ("h_e", 2)], writes=["h_qGf"])
                        S.op("pool", lambda e: e.tensor_tensor(out=kdec, in0=kk, in1=ee[:, 3, :], op=ALU.mult), reads=["h_kk", ("h_e", 3)], writes=["h_kdec"])
                    else:
                        g3 = g[:, 0:NS].rearrange("p (b t) -> p b t", t=TS)
                        lf3 = lf[:, 0:NS].rearrange("p (b t) -> p b t", t=TS)
                        S.op("dve", lambda e: e.tensor_copy(out=g3[:, :, 0:1], in_=lf3[:, :, 0:1]), reads=["h_lf"], writes=["h_g"])
                        for t_ in range(1, TS):
                            S.op("dve", lambda e, t_=t_: e.tensor_tensor(out=g3[:, :, t_:t_ + 1], in0=g3[:, :, t_ - 1:t_], in1=lf3[:, :, t_:t_ + 1], op=ALU.add), reads=["h_lf", "h_g"], writes=["h_g"])
                        t23 = tmp2[:, 0:NS].rearrange("p (b t) -> p b t", t=TS)
                        S.op("dve", lambda e: e.tensor_tensor(out=t23, in0=g3[:, :, 3:4].broadcast_to([128, NB, TS]), in1=g3, op=ALU.subtract), reads=["h_g"], writes=["h_tmp2"])
                        S.op("act", lambda e: e.activation(out=ee[:, 1, 0:NS], in_=g[:, 0:NS], func=AF.Exp, scale=-1.0), reads=["h_g"], writes=[("h_e", 1)])
                        S.op("act", lambda e: e.activation(out=ee[:, 2, 0:NS], in_=g[:, 0:NS], func=AF.Exp), reads=["h_g"], writes=[("h_e", 2)])
                        S.op("act", lambda e: e.activation(out=ee[:, 3, 0:NS], in_=tmp2[:, 0:NS], func=AF.Exp), reads=["h_tmp2"], writes=[("h_e", 3)])
                        S.op("act", lambda e: e.activation(out=egl[:, 0:NB], in_=g3[:, :, 3], func=AF.Exp), reads=["h_g"], writes=["h_egl"])
                        S.op("pool", lambda e: e.tensor_tensor(out=kg[:, 0:NS], in0=kk[:, 0:NS], in1=ee[:, 1, 0:NS], op=ALU.mult), reads=["h_kk", ("h_e", 1)], writes=["h_kg"])
                        S.op("dve", lambda e: e.tensor_tensor(out=qG[:, 0:NS], in0=q[:, 0:NS], in1=ee[:, 2, 0:NS], op=ALU.mult), reads=["h_q", ("h_e", 2)], writes=["h_qG"])
                        S.op("pool", lambda e: e.tensor_tensor(out=kdec[:, 0:NS], in0=kk[:, 0:NS], in1=ee[:, 3, 0:NS], op=ALU.mult), reads=["h_kk", ("h_e", 3)], writes=["h_kdec"])
                    pt = self.nextps()
                    ptb = ps[pt].bitcast(BF16)
                    for jt in range(ntile):
                        S.op("pe", lambda e, jt=jt, rows=rows: e.transpose(ptb[0:rows, jt * 128:(jt + 1) * 128], kdec[:, jt * 128:jt * 128 + rows], self.identb),
                             reads=["h_kdec", "consts2"], writes=[("ps", pt)])
                    self.copy("dve", kdtok[0:rows, 0:ntile, :], ptb[:, 0:512].rearrange("p (a b) -> p a b", a=4)[0:rows, 0:ntile, :], [("ps", pt)], ["h_kdtok"])
                    po = self.nextlong()
                    if not smp:
                        nchk = w // 64
                        if c0 > 0:
                            S.op("dve", lambda e: e.tensor_copy(out=Sr[:, 0, :], in_=Sr[:, 8, :]), reads=[("h_Sr", c_) for c_ in range(9)], writes=[("h_Sr", 0)])
                        pa = self.nextps()
                        for jt in range(ntile):
                            S.op("pe", lambda e, jt=jt: e.matmul(ps[pa][:, jt * 128:(jt + 1) * 128], lhsT=kg[:, jt * 128:(jt + 1) * 128], rhs=qg[:, jt * 128:(jt + 1) * 128], start=True, stop=True),
                                 reads=["h_kg", "h_qg"], writes=[("ps", pa)])
                        S.op("dve", lambda e: e.tensor_tensor(out=attm[:, 0:ntile, :], in0=ps[pa][:, 0:ntile * 128].rearrange("p (a b) -> p a b", a=ntile),
                             in1=self.maskU2.unsqueeze(1).broadcast_to([128, ntile, 128]), op=ALU.mult), reads=[("ps", pa), "consts"], writes=["h_attm"])
                        pus = [self.nextps(), self.nextps()]
                        for c in range(nchk):
                            jt, cc = c // 2, c % 2
                            S.op("pe", lambda e, c=c, cc=cc, jt=jt: e.matmul(ps[pus[c % 2]][:, (c // 2) * 128:(c // 2 + 1) * 128], lhsT=kdtok[cc * 64:(cc + 1) * 64, jt, :], rhs=vtok[cc * 64:(cc + 1) * 64, jt, :], start=True, stop=True),
                                 reads=["h_kdtok", "h_vtok"], writes=[("ps", pus[c % 2])])
                        for c in range(nchk):
                            S.op("dve", lambda e, c=c: e.scalar_tensor_tensor(out=Sr[:, c + 1, :], in0=Sr[:, c, :], scalar=egl[:, c:c + 1], in1=ps[pus[c % 2]][:, (c // 2) * 128:(c // 2 + 1) * 128], op0=ALU.mult, op1=ALU.add),
                                 reads=[("ps", pus[c % 2]), ("h_Sr", c), "h_egl"], writes=[("h_Sr", c + 1)])
                        for jt in range(ntile):
                            S.op("pe", lambda e, jt=jt: e.matmul(ps[po][:, jt * 128:(jt + 1) * 128], lhsT=vtok[:, jt, :], rhs=attm[:, jt, :], start=True, stop=False),
                                 reads=["h_vtok", "h_attm"], writes=[("ps", po)])
                            for cc in range(2):
                                c = jt * 2 + cc
                                S.op("pe", lambda e, c=c, cc=cc: e.matmul(ps[po][:, c * 64:(c + 1) * 64], lhsT=Sr[:, c, :], rhs=qGf[:, c * 64:(c + 1) * 64], start=False, stop=(cc == 1)),
                                     reads=[("h_Sr", c), "h_qGf"], writes=[("ps", po)])
                        if c0 + w == TP:
                            S.op("sp", lambda e: e.dma_start(out=hg_p[j, h], in_=Sr[:, 8, :]), reads=[("h_Sr", 8)], dma="hg_p")
                    else:
                        S.op("sp", lambda e: e.dma_start(out=S0, in_=st_hg[j, :, h, :, :].rearrange("b k v -> k b v")), writes=["h_S0"], dma="h_S0")
                        S.op("pool", lambda e: e.tensor_copy(out=S0bf, in_=S0), reads=["h_S0"], writes=["h_S0bf"])
                        pa = self.nextps()
                        S.op("pe", lambda e, pa=pa: e.matmul(ps[pa][0:NS, 0:NS], lhsT=kg[:, 0:NS], rhs=qG[:, 0:NS], start=True, stop=True), reads=["h_kg", "h_qG"], writes=[("ps", pa)])
                        S.op("dve", lambda e, pa=pa: e.tensor_tensor(out=attm[0:NS, 0, 0:NS], in0=ps[pa][0:NS, 0:NS], in1=self.maskS, op=ALU.mult), reads=[("ps", pa), "consts"], writes=[("h_attm", 0)])
                        S.op("pe", lambda e: e.matmul(ps[po][:, 0:NS], lhsT=vtok[0:NS, 0, :], rhs=attm[0:NS, 0, 0:NS], start=True, stop=False), reads=["h_vtok", ("h_attm", 0)], writes=[("ps", po)])
                        for b in range(NB):
                            S.op("pe", lambda e, b=b: e.matmul(ps[po][:, b * TS:(b + 1) * TS], lhsT=S0bf[:, b, :], rhs=qG[:, b * TS:(b + 1) * TS], start=False, stop=(b == NB - 1)),
                                 reads=["h_S0bf", "h_qG"], writes=[("ps", po)])
                        S.op("dve", lambda e: e.tensor_tensor(out=Vblk, in0=vtok[0:NS, 0, :].unsqueeze(1).broadcast_to([NS, NB, 128]),
                             in1=self.maskB.unsqueeze(2).broadcast_to([NS, NB, 128]), op=ALU.mult), reads=["h_vtok", "consts"], writes=["h_Vblk"])
                        S.op("dve", lambda e: e.tensor_tensor(out=S0, in0=S0, in1=egl[:, 0:NB].unsqueeze(2).broadcast_to([128, NB, 128]), op=ALU.mult), reads=["h_S0", "h_egl", "h_S0bf"], writes=["h_S0"])
                        for bq in range(4):
                            pu = self.nextps()
                            S.op("pe", lambda e, pu=pu, bq=bq: e.matmul(ps[pu][:, 0:512], lhsT=kdtok[0:NS, 0, :], rhs=Vblk[:, bq * 4:(bq + 1) * 4, :], start=True, stop=True),
                                 reads=["h_kdtok", "h_Vblk"], writes=[("ps", pu)])
                            S.op("dve", lambda e, pu=pu, bq=bq: e.tensor_tensor(out=S0[:, bq * 4:(bq + 1) * 4, :], in0=S0[:, bq * 4:(bq + 1) * 4, :],
                                 in1=ps[pu].rearrange("p (a b) -> p a b", a=4), op=ALU.add), reads=[("ps", pu), "h_S0"], writes=["h_S0"])
                        S.op("sp", lambda e: e.dma_start(out=hg_s[j, :, h, :, :].rearrange("b k v -> k b v"), in_=S0), reads=["h_S0"], dma="hg_s")
                    S.op("act", lambda e, w=w: e.activation(out=osb[:, 0:w], in_=ps[po][:, 0:w], func=AF.Copy), reads=[("ps", po)], writes=["h_osb"])
                    S.op("act", lambda e, w=w: e.activation(out=osq[:, 0:w], in_=ps[po][:, 0:w], func=AF.Square), reads=[("ps", po)], writes=["h_osq"])
                    pn = self.nextps()
                    S.op("pe", lambda e, pn=pn, w=w: e.matmul(ps[pn][:, 0:w], lhsT=self.onesb, rhs=osq[:, 0:w], start=True, stop=True), reads=["h_osq", "consts2"], writes=[("ps", pn)])
                    S.op("act", lambda e, pn=pn, w=w: e.activation(out=rstd[:, 0:w], in_=ps[pn][:, 0:w], func=AF.Ln, scale=1.0 / 128, bias=EPS), reads=[("ps", pn)], writes=["h_rstd"])
                    S.op("act", lambda e, w=w: e.activation(out=rstd[:, 0:w], in_=rstd[:, 0:w], func=AF.Exp, scale=-0.5), reads=["h_rstd"], writes=["h_rstd"])
                    S.op("dve", lambda e, w=w: e.tensor_tensor(out=osb[:, 0:w], in0=osb[:, 0:w], in1=rstd[:, 0:w], op=ALU.mult), reads=["h_osb", "h_rstd"], writes=["h_osb"])
                    S.op("dve", lambda e, w=w: e.scalar_tensor_tensor(out=og[:, 0:w], in0=osb[:, 0:w], scalar=self.hgn[:, j:j + 1], in1=gate[:, 0:w], op0=ALU.mult, op1=ALU.mult),
                         reads=["h_osb", "h_gate", "consts"], writes=["h_og"])
                    for dc in range(KC):
                        pi = self.nextps()
                        S.op("pe", lambda e, pi=pi, dc=dc, w=w: e.matmul(ps[pi][:, 0:w], lhsT=wout[:, slot, dc * 128:(dc + 1) * 128], rhs=og[:, 0:w], start=True, stop=True),
                             reads=[("h_wout", slot), "h_og"], writes=[("ps", pi)])
                        S.op("dve", lambda e, pi=pi, dc=dc, c0=c0, w=w: e.tensor_tensor(out=xres[:, dc, c0:c0 + w], in0=xres[:, dc, c0:c0 + w], in1=ps[pi][:, 0:w], op=ALU.add),
                             reads=[("ps", pi), ("xres", dc, c0)], writes=[("xres", dc, c0)])

    def rwkv7(self, j):
        nc, S, TP = self.nc, self.S, self.TP
        xres, hT, ps = self.xres, self.hT, self.ps
        dr = self.dram
        V = self.rw_vec
        GN_EPS = 64e-5
        li = self.cur_li
        with ExitStack() as es0:
            T0 = lambda n, sh, dt=F32: es0.enter_context(self.T(n, sh, dt))
            nblk = len(self.tbs)
            edge = T0("r_edge", [128, KC, 2 * nblk + 2], BF16)
            shiftT = T0("r_shiftT", [128, KC, NB], BF16)
            l1T = T0("r_l1T", [128, 3, self.N], BF16)
            prevS = T0("r_prevS", [128, KC, NS], BF16)
            prevB = T0("r_prevB", [128, KC, 512], BF16)

            def fill_prev(c0, w, kind):
                if kind == "p":
                    S.op("dve", lambda e: e.tensor_copy(out=prevB[:, :, 0:w], in_=hT[:, :, 1 + c0:1 + c0 + w]), reads=[("hT", "all"), "hT0"], writes=["r_prevB"])
            with ExitStack() as es:
                T = lambda n, sh, dt=F32: es.enter_context(self.T(n, sh, dt))
                shin, xsh, sq, rr = T("r_shin", [32, D]), T("r_xsh", [128, KC, 32]), T("r_sq", [128, KC, 32]), T("r_rr", [128, 32])
                shrow = T("r_shrow", [32, D])
                S.op("pool", lambda e: e.memset(shin, 0.0), writes=["r_shin"])
                S.op("pool", lambda e: e.memset(xsh, 0.0), writes=["r_xsh"])
                S.op("pool", lambda e: e.memset(edge, 0.0), writes=["r_edge"])
                S.op("sp", lambda e: e.dma_start(out=shin[0:NB, :], in_=dr["st_shift"]), reads=[], writes=["r_shin"], dma="r_shin")
                for half in range(2):
                    pi = self.nextps()
                    for q in range(4):
                        kc = half * 4 + q
                        S.op("pe", lambda e, q=q, kc=kc, pi=pi: e.transpose(ps[pi][:, q * 32:(q + 1) * 32], shin[:, kc * 128:(kc + 1) * 128], self.ident[0:32, 0:32]), reads=["r_shin", "consts"], writes=[("ps", pi)])
                    self.copy("dve", shiftT[:, half * 4:(half + 1) * 4, :], ps[pi][:, 0:128].rearrange("p (q t) -> p q t", q=4)[:, :, 0:NB], [("ps", pi)], ["r_shiftT"])
                S.op("dve", lambda e: e.tensor_copy(out=xsh[:, :, 0:1], in_=xres[:, :, TP - 1:TP]), reads=[("xres", "all"), "r_xsh"], writes=["r_xsh"])
                S.op("dve", lambda e: e.tensor_copy(out=xsh[:, :, 1:1 + NB], in_=xres[:, :, TP:TP + NS].rearrange("p k (b t) -> p k b t", t=TS)[:, :, :, 3]), reads=[("xres", "all"), "r_xsh"], writes=["r_xsh"])
                S.op("act", lambda e: e.activation(out=sq, in_=xsh, func=AF.Square), reads=["r_xsh"], writes=["r_sq"])
                pi = self.nextps()
                for kc in range(KC):
                    S.op("pe", lambda e, kc=kc: e.matmul(ps[pi][:, 0:32], lhsT=self.onesf, rhs=sq[:, kc, :], start=(kc == 0), stop=(kc == KC - 1)), reads=["r_sq", "consts2"], writes=[("ps", pi)])
                S.op("act", lambda e: e.activation(out=rr, in_=ps[pi][:, 0:32], func=AF.Ln, scale=1.0 / D, bias=EPS), reads=[("ps", pi)], writes=["r_rr"])
                S.op("act", lambda e: e.activation(out=rr, in_=rr, func=AF.Exp, scale=-0.5), reads=["r_rr"], writes=["r_rr"])
                for kc in range(KC):
                    S.op("dve", lambda e, kc=kc: e.scalar_tensor_tensor(out=xsh[:, kc, :], in0=xsh[:, kc, :], scalar=self.nmix[:, li, kc:kc + 1], in1=rr, op0=ALU.mult, op1=ALU.mult), reads=["r_xsh", "r_rr", "consts"], writes=["r_xsh"])
                for half in range(2):
                    pi = self.nextps()
                    for q in range(4):
                        kc = half * 4 + q
                        S.op("pe", lambda e, q=q, kc=kc, pi=pi: e.transpose(ps[pi][0:32, q * 128:(q + 1) * 128], xsh[:, kc, :], self.ident), reads=["r_xsh", "consts"], writes=[("ps", pi)])
                    self.copy("act", shrow[:, half * 512:(half + 1) * 512], ps[pi][0:32, :], [("ps", pi)], [("r_shrow", half)])
                S.op("sp", lambda e: e.dma_start(out=dr["sh_p"], in_=shrow[0:1, :]), reads=[("r_shrow", 0), ("r_shrow", 1)], dma="r_sh")
                S.op("sp", lambda e: e.dma_start(out=dr["sh_s"], in_=shrow[1:1 + NB, :]), reads=[("r_shrow", 0), ("r_shrow", 1)], dma="r_sh")
                for bi, (c0, w, kind) in enumerate(self.tbs):
                    if kind == "p" and c0 > 0:
                        S.op("dve", lambda e, bi=bi, c0=c0: e.tensor_copy(out=edge[:, :, 2 * bi:2 * bi + 1], in_=hT[:, :, 1 + c0:2 + c0]), reads=[("hT", "all"), "r_edge"], writes=["r_edge"])
                hs3 = hT[:, :, 2 + TP:2 + TP + NS].rearrange("p k (b t) -> p k b t", t=TS)
                pv3 = prevS.rearrange("p k (b t) -> p k b t", t=TS)
                for kc in range(KC):
                    S.op("dve", lambda e, kc=kc: e.tensor_copy(out=pv3[:, kc, :, 1:TS], in_=hs3[:, kc, :, 0:TS - 1]), reads=[("hT", "all")], writes=["r_prevS"])
                    S.op("dve", lambda e, kc=kc: e.tensor_copy(out=pv3[:, kc, :, 0], in_=shiftT[:, kc, :]), reads=["r_shiftT", "r_prevS"], writes=["r_prevS"])
            S.barrier()

            def shifted_proj(pi, w_a, w_b, c0, w, kind, rd, M=128):
                hc = 2 + c0
                out = ps[pi][0:M, 0:w]
                for kc in range(KC):
                    S.op("pe", lambda e, kc=kc: e.matmul(out, lhsT=w_a(kc), rhs=hT[:, kc, hc:hc + w], start=(kc == 0), stop=False), reads=rd + [("hT", "all")], writes=[("ps", pi)])
                if kind == "p":
                    for kc in range(KC):
                        S.op("pe", lambda e, kc=kc: e.matmul(out, lhsT=w_b(kc), rhs=prevB[:, kc, 0:w], start=False, stop=(kc == KC - 1)), reads=rd + ["r_prevB"], writes=[("ps", pi)])
                else:
                    for kc in range(KC):
                        S.op("pe", lambda e, kc=kc: e.matmul(out, lhsT=w_b(kc), rhs=prevS[:, kc, :], start=False, stop=(kc == KC - 1)), reads=rd + ["r_prevS"], writes=[("ps", pi)])

            nq = TP // 256
            self.r_e2 = T0("r_e2", [128, KC, 2 * nq], BF16)
            e2 = self.r_e2
            S.op("pool", lambda e: e.memset(e2, 0.0), writes=["r_e2"])
            for q in range(nq):
                c0 = q * 256
                if c0 > 0:
                    S.op("dve", lambda e, q=q, c0=c0: e.tensor_copy(out=e2[:, :, 2 * q:2 * q + 1], in_=hT[:, :, 1 + c0:2 + c0]), reads=[("hT", "all"), "r_e2"], writes=["r_e2"])
            S.barrier()
            rblocks = [(q * 256, 256, "p") for q in range(nq)] + [(TP, NS, "s")]

            def load_scaled(dst_a, dst_b, src, shape, n, kk0, nk, key):
                slot = self.stg_i % self.NSTG
                self.stg_i += 1
                cols = shape[2]
                st = self.stage[slot][:, 0:nk * cols].rearrange("p (a b) -> p a b", a=nk)
                S.op("sp", lambda e: e.dma_start(out=st, in_=src), writes=[("stg", slot)], dma="stg%d" % slot)
                for q in range(nk):
                    kc = kk0 + q
                    S.op("dve", lambda e, q=q, kc=kc: e.tensor_scalar(out=dst_a(kc), in0=st[:, q, :], scalar1=self.rw_omu[:, n, kc:kc + 1], scalar2=1.0, op0=ALU.mult, op1=ALU.mult), reads=[("stg", slot), "rwc0"], writes=[key])
                    S.op("dve", lambda e, q=q, kc=kc: e.tensor_scalar(out=dst_b(kc), in0=st[:, q, :], scalar1=self.rw_mu[:, n, kc:kc + 1], scalar2=1.0, op0=ALU.mult, op1=ALU.mult), reads=[("stg", slot), "consts"], writes=[key])

            with ExitStack() as es:
                T = lambda n, sh, dt=F32: es.enter_context(self.T(n, sh, dt))
                wl = T("r_wl", [128, 3, 2, KC, 128], BF16)
                for li_, (nm, n, cols) in enumerate((("rw_w1", 3, 64), ("rw_a1", 4, 64), ("rw_g1", 5, 128))):
                    src = dr[nm].rearrange("(k p) c -> p k c", p=128)
                    load_scaled(lambda kc, li_=li_, cols=cols: wl[:, li_, 0, kc, 0:cols], lambda kc, li_=li_, cols=cols: wl[:, li_, 1, kc, 0:cols], src, [128, KC, cols], n, 0, KC, ("r_wl", li_))
                for (c0, w, kind) in rblocks:
                    fill_prev(c0, w, kind)
                    for li_, (M, fn) in enumerate(((64, AF.Tanh), (64, AF.Copy), (128, AF.Sigmoid))):
                        pi = self.nextps()
                        shifted_proj(pi, lambda kc, li_=li_, M=M: wl[:, li_, 0, kc, 0:M], lambda kc, li_=li_, M=M: wl[:, li_, 1, kc, 0:M], c0, w, kind, [("r_wl", li_)], M=M)
                        S.op("act", lambda e, li_=li_, M=M, fn=fn, pi=pi: e.activation(out=l1T[0:M, li_, c0:c0 + w], in_=ps[pi][0:M, 0:w], func=fn), reads=[("ps", pi)], writes=[("r_l1T", li_)])
            S.barrier()

            for sample_pass in ((True,) if os.environ.get('RWSKIP') else (False, True)):
                blocks = [b_ for b_ in rblocks if (b_[2] == "s") == sample_pass]
                Wd = NS if sample_pass else 256
                R = NS if sample_pass else 128
                nch = 1 if sample_pass else 2
                nlev = 1 if sample_pass else 6
                mStrict = self.sUS if sample_pass else self.sU
                mIncl = self.maskS if sample_pass else self.maskU
                mLow = self.sLS if sample_pass else self.sL
                with ExitStack() as es:
                    T = lambda n, sh, dt=F32: es.enter_context(self.T(n, sh, dt))
                    wrkv = T("r_wrkv", [128, 3, 2, KC, 128], BF16)
                    w2nd = T("r_w2nd", [128, 3, 128], BF16)
                    wout2 = [T("r_wout", [128, D], BF16) for _ in range(2)]
                    rfA, kfA, vfA, lwf, af = [T("r_f%d" % i_, [128, Wd]) for i_ in range(5)]
                    rkvA = [[rfA, kfA, vfA, None], [T("r_rf2", [128, Wd]), T("r_kf2", [128, Wd]), T("r_vf2", [128, Wd]), T("r_vb2", [128, Wd], BF16)]]
                    gf2 = [T("r_gf", [128, Wd]) for _ in range(2)]
                    bon2 = [T("r_bon", [128, Wd]) for _ in range(2)]
                    G, t1, e0, e1 = T("r_G", [128, Wd]), T("r_t1", [128, Wd]), T("r_e0", [128, Wd]), T("r_e1", [128, Wd])
                    kap, k2, beta, osb = T("r_kap", [128, Wd]), T("r_k2", [128, Wd]), T("r_beta", [128, Wd]), T("r_osb", [128, Wd])
                    KR2 = [T("r_KR", [128, nch, 2, 128], BF16) for _ in range(2)]
                    BK2 = [T("r_BK", [128, nch, 2, 128], BF16) for _ in range(2)]
                    btT, ktT, vbA, og = T("r_btT", [128, Wd], BF16), T("r_ktT", [128, Wd], BF16), T("r_vb", [128, Wd], BF16), T("r_og", [128, Wd], BF16)
                    rkvA[0][3] = vbA
                    bttok2 = [T("r_bttok", [128, nch, 128], BF16) for _ in range(2)]
                    kttok2 = [T("r_kttok", [128, nch, 128], BF16) for _ in range(2)]
                    vtok2 = [T("r_vtok", [128, nch, 128], BF16) for _ in range(2)]
                    Nk, Ak = T("r_Nk", [128, 2, 2 * nch, 128], BF16), T("r_Ak", [128, 2, 2 * nch, 128], BF16)
                    Wm = T("r_Wm", [128, 2 * nch, 128], BF16)
                    MbT, BTm, MkT = T("r_MbT", [128, 2 * nch, 128], BF16), T("r_BTm", [128, 2 * nch, 128], BF16), T("r_MkT", [128, 2 * nch, 128], BF16)
                    EC2 = [T("r_EC", [128, NB]) for _ in range(2)]
                    t2 = T("r_t2", [128, Wd])
                    Xsb, Ssb = T("r_Xsb", [128, 2, 64], BF16), T("r_Ssb", [128, 2, 64], BF16)
                    Sst, Sbf = T("r_Sst", [128, 64]), T("r_Sbf", [128, 64], BF16)
                    sT = T("r_sT", [64, 128])
                    if sample_pass:
                        s0nat = T("r_s0nat", [64, NB, 128])
                        S0, S0bf = T("r_S0", [128, NB, 64]), T("r_S0bf", [128, NB, 64], BF16)
                        kblk = T("r_kblk", [128, NB, NS], BF16)
                        rblk = T("r_rblk", [128, NB, NS], BF16)
                        Sblk, Vblk = T("r_Sblk", [128, 2, NB, 64], BF16), T("r_Vblk", [128, 2, NB, 64], BF16)
                        for t_, k_ in ((Ssb, ("r_Ssb", 0)), (MbT, "r_zMbT"), (MkT, "r_zMkT"), (BTm, "r_zBTm"), (vtok2[0], ("r_vtok", 0)), (vtok2[1], ("r_vtok", 1)), (bttok2[0], ("r_bttok", 0)), (bttok2[1], ("r_bttok", 1)), (kttok2[0], ("r_kttok", 0)), (kttok2[1], ("r_kttok", 1)), (Sblk, ("r_Sblk", 0)), (Vblk, ("r_Vblk", 0))):
                            S.op("pool", lambda e, t_=t_: e.memset(t_, 0.0), writes=[k_])
                        S.barrier()
                    def blk(pc, bi_, c0, w, kind, par):
                        cs = slice(pc * 128, (pc + 1) * 128)
                        KR, BK, bt_tok, kt_tok, v_tok, EC, gf, bon = KR2[par], BK2[par], bttok2[par], kttok2[par], vtok2[par], EC2[par], gf2[par], bon2[par]
                        wout = wout2[pc % 2]
                        if bi_ == 0:
                            for n in range(3):
                                for kk0 in (0, 4):
                                    src = dr["rw_w_rkv"][n, kk0 * 128:(kk0 + 4) * 128, cs].rearrange("(k p) c -> p k c", p=128)
                                    load_scaled(lambda kc, n=n: wrkv[:, n, 0, kc, :], lambda kc, n=n: wrkv[:, n, 1, kc, :], src, [128, 4, 128], n, kk0, 4, ("r_wrkv", n, kk0))
                            self.load_w(w2nd[0:64, 0, :], dr["rw_w2"][:, cs], [64, 128], ("r_w2nd", 0))
                            self.load_w(w2nd[0:64, 1, :], dr["rw_a2"][:, cs], [64, 128], ("r_w2nd", 1))
                            self.load_w(w2nd[:, 2, :], dr["rw_g2"][:, cs], [128, 128], ("r_w2nd", 2))
                            self.load_w(wout, dr["rw_w_out"][cs, :], [128, D], ("r_wout", pc % 2))
                        wk = lambda n: [("r_wrkv", n, 0), ("r_wrkv", n, 4)]
                        vec = lambda nm: V[nm][:, pc:pc + 1]
                        for _once in (0,):
                            half = bi_ % 2 if kind == "p" else 0
                            rf, kf, vf, vb = rkvA[half]
                            if kind == "s" or half == 0:
                                wp = w if kind == "s" else 2 * w
                                fill_prev(c0, wp, kind)
                                pr, pk, pv = self.nextps(), self.nextps(), self.nextps()
                                for n, pi in ((0, pr), (1, pk), (2, pv)):
                                    shifted_proj(pi, lambda kc, n=n: wrkv[:, n, 0, kc, :], lambda kc, n=n: wrkv[:, n, 1, kc, :], c0, wp, kind, wk(n))
                                for hh_ in range(wp // w):
                                    rf_, kf_, vf_, vb_ = rkvA[hh_]
                                    cs_ = slice(hh_ * w, (hh_ + 1) * w)
                                    self.copy("act", rf_[:, 0:w], ps[pr][:, cs_], [("ps", pr)], [("r_rf", hh_)])
                                    self.copy("act", kf_[:, 0:w], ps[pk][:, cs_], [("ps", pk)], [("r_kf", hh_)])
                                    self.copy("act", vf_[:, 0:w], ps[pv][:, cs_], [("ps", pv)], [("r_vf", hh_)])
                                    S.op("dve", lambda e, vb_=vb_, cs_=cs_: e.tensor_copy(out=vb_[:, 0:w], in_=ps[pv][:, cs_]), reads=[("ps", pv)], writes=[("r_vb", hh_)])
                            yield 0
                            pw, pa, pg = self.nextps(), self.nextps(), self.nextps()
                            S.op("pe", lambda e: e.matmul(ps[pw][:, 0:w], lhsT=w2nd[0:64, 0, :], rhs=l1T[0:64, 0, c0:c0 + w], start=True, stop=True), reads=[("r_w2nd", 0), ("r_l1T", 0)], writes=[("ps", pw)])
                            S.op("pe", lambda e: e.matmul(ps[pa][:, 0:w], lhsT=w2nd[0:64, 1, :], rhs=l1T[0:64, 1, c0:c0 + w], start=True, stop=True), reads=[("r_w2nd", 1), ("r_l1T", 1)], writes=[("ps", pa)])
                            S.op("pe", lambda e: e.matmul(ps[pg][:, 0:w], lhsT=w2nd[:, 2, :], rhs=l1T[:, 2, c0:c0 + w], start=True, stop=True), reads=[("r_w2nd", 2), ("r_l1T", 2)], writes=[("ps", pg)])
                            S.op("act", lambda e: e.activation(out=lwf[:, 0:w], in_=ps[pw][:, 0:w], func=AF.Sigmoid, bias=vec("rw_w0")), reads=[("ps", pw), "consts"], writes=["r_lwf"])
                            S.op("dve", lambda e: e.tensor_scalar(out=lwf[:, 0:w], in0=lwf[:, 0:w], scalar1=-0.6065306597126334, scalar2=1.0, op0=ALU.mult, op1=ALU.mult), reads=["r_lwf"], writes=["r_lwf"])
                            S.op("act", lambda e: e.activation(out=af[:, 0:w], in_=ps[pa][:, 0:w], func=AF.Sigmoid, bias=vec("rw_a0")), reads=[("ps", pa), "consts"], writes=["r_af"])
                            self.copy("act", gf[:, 0:w], ps[pg][:, 0:w], [("ps", pg)], [("r_gf", par)])
                            yield 0
                            if not sample_pass:
                                for c in range(nch):
                                    S.op("dve", lambda e, c=c: e.tensor_tensor_scan(out=G[:, c * 128:(c + 1) * 128], data0=self.onesf[:, 0:128], data1=lwf[:, c * 128:(c + 1) * 128], initial=0.0, op0=ALU.mult, op1=ALU.add), reads=["r_lwf", "consts2"], writes=["r_G"])
                                v3 = lambda t_: t_[:, 0:w].rearrange("p (c t) -> p c t", t=128)
                                glast = v3(G)[:, :, 127:128].broadcast_to([128, nch, 128])
                                ngrp = nch
                                S.op("act", lambda e: e.activation(out=EC[:, 0:nch], in_=v3(G)[:, :, 127], func=AF.Exp), reads=["r_G"], writes=[("r_EC", par)])
                            else:
                                G3 = G[:, 0:NS].rearrange("p (b t) -> p b t", t=TS)
                                l3 = lwf[:, 0:NS].rearrange("p (b t) -> p b t", t=TS)
                                S.op("dve", lambda e: e.tensor_copy(out=G3[:, :, 0:1], in_=l3[:, :, 0:1]), reads=["r_lwf"], writes=["r_G"])
                                for t_ in range(1, TS):
                                    S.op("dve", lambda e, t_=t_: e.tensor_tensor(out=G3[:, :, t_:t_ + 1], in0=G3[:, :, t_ - 1:t_], in1=l3[:, :, t_:t_ + 1], op=ALU.add), reads=["r_lwf", "r_G"], writes=["r_G"])
                                v3 = lambda t_: t_[:, 0:NS].rearrange("p (b t) -> p b t", t=TS)
                                glast = G3[:, :, TS - 1:TS].broadcast_to([128, NB, TS])
                                S.op("act", lambda e: e.activation(out=EC[:, 0:NB], in_=G3[:, :, TS - 1], func=AF.Exp), reads=["r_G"], writes=[("r_EC", par)])
                            yield 0
                            S.op("dve", lambda e: e.tensor_scalar(out=kap[:, 0:w], in0=kf[:, 0:w], scalar1=vec("rw_k_k"), scalar2=1.0, op0=ALU.mult, op1=ALU.mult), reads=[("r_kf", half), "consts"], writes=["r_kap"])
                            S.op("act", lambda e: e.activation(out=t1[:, 0:w], in_=kap[:, 0:w], func=AF.Square), reads=["r_kap"], writes=["r_t1"])
                            pn = self.nextps()
                            S.op("pe", lambda e: e.matmul(ps[pn][:, 0:w], lhsT=self.blk1, rhs=t1[:, 0:w], start=True, stop=True), reads=["r_t1", "consts"], writes=[("ps", pn)])
                            S.op("dve", lambda e: e.tensor_scalar(out=t1[:, 0:w], in0=ps[pn][:, 0:w], scalar1=1e-24, scalar2=None, op0=ALU.max), reads=[("ps", pn), "r_t1"], writes=["r_t1"])
                            S.op("act", lambda e: e.activation(out=t1[:, 0:w], in_=t1[:, 0:w], func=AF.Ln), reads=["r_t1"], writes=["r_t1"])
                            S.op("act", lambda e: e.activation(out=t1[:, 0:w], in_=t1[:, 0:w], func=AF.Exp, scale=-0.5), reads=["r_t1"], writes=["r_t1"])
                            S.op("dve", lambda e: e.tensor_tensor(out=kap[:, 0:w], in0=kap[:, 0:w], in1=t1[:, 0:w], op=ALU.mult), reads=["r_kap", "r_t1"], writes=["r_kap"])
                            yield 0
                            S.op("dve", lambda e: e.tensor_scalar(out=k2[:, 0:w], in0=af[:, 0:w], scalar1=vec("rw_k_a"), scalar2=self.rw_omka[:, pc:pc + 1], op0=ALU.mult, op1=ALU.add), reads=["r_af", "consts", "rwc1"], writes=["r_k2"])
                            S.op("pool", lambda e: e.tensor_tensor(out=k2[:, 0:w], in0=k2[:, 0:w], in1=kf[:, 0:w], op=ALU.mult), reads=["r_k2", ("r_kf", half)], writes=["r_k2"])
                            S.op("pool", lambda e: e.tensor_tensor(out=beta[:, 0:w], in0=af[:, 0:w], in1=kap[:, 0:w], op=ALU.mult), reads=["r_af", "r_kap"], writes=["r_beta"])
                            yield 0
                            S.op("dve", lambda e: e.scalar_tensor_tensor(out=bon[:, 0:w], in0=rf[:, 0:w], scalar=vec("rw_r_k"), in1=k2[:, 0:w], op0=ALU.mult, op1=ALU.mult), reads=[("r_rf", half), "r_k2", "consts"], writes=[("r_bon", par)])
                            pb = self.nextps()
                            S.op("pe", lambda e: e.matmul(ps[pb][:, 0:w], lhsT=self.blk1, rhs=bon[:, 0:w], start=True, stop=True), reads=[("r_bon", par), "consts"], writes=[("ps", pb)])
                            S.op("dve", lambda e: e.tensor_tensor(out=bon[:, 0:w], in0=ps[pb][:, 0:w], in1=vf[:, 0:w], op=ALU.mult), reads=[("ps", pb), ("r_vf", half), ("r_bon", par)], writes=[("r_bon", par)])
                            yield 0
                            if not sample_pass:
                                kr = lambda i_: KR[:, :, i_, :]
                                bk = lambda i_: BK[:, :, i_, :]
                            else:
                                kr = lambda i_: KR[:, 0, i_, 0:NS].rearrange("p (b t) -> p b t", t=TS)
                                bk = lambda i_: BK[:, 0, i_, 0:NS].rearrange("p (b t) -> p b t", t=TS)
                            S.op("act", lambda e: e.activation(out=e0[:, 0:w], in_=G[:, 0:w], func=AF.Exp), reads=["r_G"], writes=["r_e0"])
                            S.op("pool", lambda e: e.tensor_tensor(out=kr(1), in0=v3(rf), in1=v3(e0), op=ALU.mult), reads=[("r_rf", half), "r_e0"], writes=[("r_KR1", par)])
                            S.op("dve", lambda e: e.tensor_tensor(out=t1[:, 0:w], in0=G[:, 0:w], in1=lwf[:, 0:w], op=ALU.subtract), reads=["r_G", "r_lwf", "r_t1"], writes=["r_t1"])
                            S.op("act", lambda e: e.activation(out=e1[:, 0:w], in_=t1[:, 0:w], func=AF.Exp), reads=["r_t1"], writes=["r_e1"])
                            S.op("pool", lambda e: e.tensor_tensor(out=kr(0), in0=v3(kap), in1=v3(e1), op=ALU.mult), reads=["r_kap", "r_e1"], writes=[("r_KR0", par)])
                            S.op("act", lambda e: e.activation(out=e0[:, 0:w], in_=G[:, 0:w], func=AF.Exp, scale=-1.0), reads=["r_G", ("r_KR1", par)], writes=["r_e0"])
                            S.op("pool", lambda e: e.tensor_tensor(out=bk(0), in0=v3(beta), in1=v3(e0), op=ALU.mult), reads=["r_beta", "r_e0"], writes=[("r_BK0", par)])
                            S.op("dve", lambda e: e.tensor_tensor(out=bk(1), in0=v3(k2), in1=v3(e0), op=ALU.mult), reads=["r_k2", "r_e0"], writes=[("r_BK1", par)])
                            S.op("dve", lambda e: e.tensor_tensor(out=v3(t1), in0=glast, in1=v3(G), op=ALU.subtract), reads=["r_G", "r_t1", "r_e1"], writes=["r_t1"])
                            S.op("act", lambda e: e.activation(out=e1[:, 0:w], in_=t1[:, 0:w], func=AF.Exp), reads=["r_t1", ("r_KR0", par)], writes=["r_e1"])
                            S.op("pool", lambda e: e.tensor_tensor(out=btT[:, 0:w], in0=beta[:, 0:w], in1=e1[:, 0:w], op=ALU.mult), reads=["r_beta", "r_e1"], writes=["r_btT"])
                            S.op("dve", lambda e: e.tensor_tensor(out=ktT[:, 0:w], in0=k2[:, 0:w], in1=e1[:, 0:w], op=ALU.mult), reads=["r_k2", "r_e1"], writes=["r_ktT"])
                            yield 0
                            pt = self.nextps()
                            ptb = ps[pt].bitcast(BF16)
                            for i_, (src_, nm_) in enumerate(((btT, "r_btT"), (ktT, "r_ktT"), (vb, ("r_vb", half)))):
                                for c in range(nch):
                                    S.op("pe", lambda e, i_=i_, c=c, src_=src_: e.transpose(ptb[0:R, (i_ * nch + c) * 128:(i_ * nch + c + 1) * 128], src_[:, c * 128:c * 128 + R], self.identb), reads=[nm_, "consts2"], writes=[("ps", pt)])
                            for i_, (dst_, nm_) in enumerate(((bt_tok, ("r_bttok", par)), (kt_tok, ("r_kttok", par)), (v_tok, ("r_vtok", par)))):
                                self.copy("dve", dst_[0:R, :, :], ptb[0:R, i_ * nch * 128:(i_ + 1) * nch * 128].rearrange("p (c k) -> p c k", c=nch), [("ps", pt)], [nm_])
                            yield "MID"
                            if bi_ == 0:
                                S.op("pool", lambda e: e.memset(Sst, 0.0), writes=[("r_Sst", 0), ("r_Sst", 1)])
                                S.op("pool", lambda e: e.memset(Sbf, 0.0), writes=[("r_Sbf", 0), ("r_Sbf", 1)])
                            for hd in range(2):
                                P0 = hd * 64
                                for c in range(nch):
                                    m_ = hd * nch + c
                                    kr_c = KR[P0:P0 + 64, c, :, 0:R]
                                    p1, p2, p3 = self.nextps(), self.nextps(), self.nextps()
                                    o1 = ps[p1][0:R, 0:2 * R].rearrange("p (i t) -> p i t", i=2)
                                    o2 = ps[p2][0:R, 0:2 * R].rearrange("p (i t) -> p i t", i=2)
                                    S.op("pe", lambda e, c=c, P0=P0, kr_c=kr_c, o1=o1: e.matmul(o1, lhsT=BK[P0:P0 + 64, c, 0, 0:R], rhs=kr_c, start=True, stop=True), reads=[("r_BK0", par), ("r_KR0", par), ("r_KR1", par)], writes=[("ps", p1)])
                                    S.op("pe", lambda e, c=c, P0=P0, kr_c=kr_c, o2=o2: e.matmul(o2, lhsT=BK[P0:P0 + 64, c, 1, 0:R], rhs=kr_c, start=True, stop=True), reads=[("r_BK1", par), ("r_KR0", par), ("r_KR1", par)], writes=[("ps", p2)])
                                    S.op("pe", lambda e, c=c, P0=P0, p3=p3: e.matmul(ps[p3][0:R, 0:R], lhsT=KR[P0:P0 + 64, c, 0, 0:R], rhs=BK[P0:P0 + 64, c, 0, 0:R], start=True, stop=True), reads=[("r_BK0", par), ("r_KR0", par)], writes=[("ps", p3)])
                                    S.op("dve", lambda e, m_=m_, p1=p1: e.tensor_tensor(out=Nk[0:R, 0, m_, 0:R], in0=ps[p1][0:R, 0:R], in1=mStrict, op=ALU.mult), reads=[("ps", p1), "consts"], writes=[("r_Nk", 0, m_)])
                                    S.op("dve", lambda e, m_=m_, p1=p1: e.tensor_tensor(out=MbT[0:R, m_, 0:R], in0=ps[p1][0:R, R:2 * R], in1=mIncl, op=ALU.mult), reads=[("ps", p1), "consts"], writes=[("r_MbT", m_)])
                                    S.op("dve", lambda e, m_=m_, p2=p2: e.tensor_tensor(out=BTm[0:R, m_, 0:R], in0=ps[p2][0:R, 0:R], in1=mStrict, op=ALU.mult), reads=[("ps", p2), "consts"], writes=[("r_BTm", m_)])
                                    S.op("dve", lambda e, m_=m_, p2=p2: e.tensor_tensor(out=MkT[0:R, m_, 0:R], in0=ps[p2][0:R, R:2 * R], in1=mIncl, op=ALU.mult), reads=[("ps", p2), "consts"], writes=[("r_MkT", m_)])
                                    S.op("dve", lambda e, m_=m_, p3=p3: e.tensor_tensor(out=Ak[0:R, 0, m_, 0:R], in0=ps[p3][0:R, 0:R], in1=mLow, op=ALU.mult), reads=[("ps", p3), "consts"], writes=[("r_Ak", 0, m_)])
                                    S.op("pool", lambda e, m_=m_: e.tensor_tensor(out=Wm[0:R, m_, 0:R], in0=self.ident[0:R, 0:R], in1=Nk[0:R, 0, m_, 0:R], op=ALU.subtract), reads=[("r_Nk", 0, m_), "consts"], writes=[("r_Wm", m_)])
                                    yield 0
                            yield 0
                            nm_ = 2 * nch
                            for lev in range(1, nlev + 1):
                                cur, nxt = (lev - 1) % 2, lev % 2
                                for m_ in range(nm_):
                                    p1 = self.nextps()
                                    S.op("pe", lambda e, m_=m_, p1=p1, cur=cur: e.matmul(ps[p1][0:R, 0:R], lhsT=Nk[0:R, cur, m_, 0:R], rhs=Ak[0:R, cur, m_, 0:R], start=True, stop=True), reads=[("r_Nk", cur, m_), ("r_Ak", cur, m_)], writes=[("ps", p1)])
                                    if lev < nlev:
                                        S.op("pe", lambda e, m_=m_, p1=p1, cur=cur: e.matmul(ps[p1][0:R, 128:128 + R], lhsT=Ak[0:R, cur, m_, 0:R], rhs=Nk[0:R, cur, m_, 0:R], start=True, stop=True), reads=[("r_Nk", cur, m_), ("r_Ak", cur, m_)], writes=[("ps", p1)])
                                    self.copy("act", Ak[0:R, nxt, m_, 0:R], ps[p1][0:R, 0:R], [("ps", p1)], [("r_Ak", nxt, m_)])
                                    if lev < nlev:
                                        self.copy("dve", Nk[0:R, nxt, m_, 0:R], ps[p1][0:R, 128:128 + R], [("ps", p1)], [("r_Nk", nxt, m_)])
                                    yield 0
                                for m_ in range(nm_):
                                    p2 = self.nextps()
                                    S.op("pe", lambda e, m_=m_, p2=p2, nxt=nxt: e.matmul(ps[p2][0:R, 0:R], lhsT=Ak[0:R, nxt, m_, 0:R], rhs=Wm[0:R, m_, 0:R], start=True, stop=True), reads=[("r_Ak", nxt, m_), ("r_Wm", m_)], writes=[("ps", p2)])
                                    S.op("dve", lambda e, m_=m_, p2=p2: e.tensor_tensor(out=Wm[0:R, m_, 0:R], in0=Wm[0:R, m_, 0:R], in1=ps[p2][0:R, 0:R], op=ALU.add), reads=[("ps", p2), ("r_Wm", m_)], writes=[("r_Wm", m_)])
                                    yield 0
                            yield 0
                            pO = self.nextlong()
                            if sample_pass:
                                for hd in range(2):
                                    S.op("sp", lambda e, hd=hd: e.dma_start(out=s0nat[:, :, hd * 64:(hd + 1) * 64], in_=dr["st_wkv"][:, 2 * pc + hd].rearrange("b v k -> v b k")), writes=[("r_s0nat", hd)], dma="r_s0nat")
                                for q4 in range(4):
                                    p1 = self.nextps()
                                    for bb in range(4):
                                        b = q4 * 4 + bb
                                        S.op("pe", lambda e, b=b, bb=bb, p1=p1: e.transpose(ps[p1][:, bb * 64:(bb + 1) * 64], s0nat[:, b, :], self.ident[0:64, 0:64]), reads=[("r_s0nat", 0), ("r_s0nat", 1), "consts"], writes=[("ps", p1)])
                                    self.copy("act", S0[:, q4 * 4:(q4 + 1) * 4, :], ps[p1][:, 0:256].rearrange("p (b v) -> p b v", b=4), [("ps", p1)], [("r_S0", q4)])
                                    S.op("dve", lambda e, q4=q4, p1=p1: e.tensor_copy(out=S0bf[:, q4 * 4:(q4 + 1) * 4, :], in_=ps[p1][:, 0:256].rearrange("p (b v) -> p b v", b=4)), reads=[("ps", p1)], writes=[("r_S0bf", q4)])
                                s0k = [("r_S0", q4) for q4 in range(4)]
                                s0bk = [("r_S0bf", q4) for q4 in range(4)]
                                S.op("dve", lambda e: e.tensor_tensor(out=kblk, in0=KR[:, 0, 0, 0:NS].unsqueeze(1).broadcast_to([128, NB, NS]), in1=self.maskC, op=ALU.mult), reads=[("r_KR0", par), "consts"], writes=["r_kblk"])
                                S.op("dve", lambda e: e.tensor_tensor(out=rblk, in0=KR[:, 0, 1, 0:NS].unsqueeze(1).broadcast_to([128, NB, NS]), in1=self.maskC, op=ALU.mult), reads=[("r_KR1", par), "consts"], writes=["r_rblk"])
                            for c in range(nch):
                                for hd in range(2):
                                    P0 = hd * 64
                                    m_ = hd * nch + c
                                    hs = slice(P0, P0 + 64)
                                    yield 0
                                    pX = self.nextps()
                                    if not sample_pass:
                                        S.op("pe", lambda e, c=c, hs=hs, pX=pX: e.matmul(ps[pX][0:R, 0:64], lhsT=KR[hs, c, 0, 0:R], rhs=Sbf[hs, :], start=True, stop=False), reads=[("r_KR0", par), ("r_Sbf", hd)], writes=[("ps", pX)])
                                    else:
                                        for b in range(NB):
                                            S.op("pe", lambda e, b=b, hs=hs, pX=pX: e.matmul(ps[pX][0:R, 0:64], lhsT=kblk[hs, b, :], rhs=S0bf[hs, b, :], start=(b == 0), stop=False), reads=["r_kblk"] + s0bk, writes=[("ps", pX)])
                                    S.op("pe", lambda e, c=c, hs=hs, pX=pX, m_=m_: e.matmul(ps[pX][0:R, 0:64], lhsT=BTm[:, m_, 0:R], rhs=v_tok[:, c, hs], start=False, stop=True), reads=[("r_BTm", m_), ("r_vtok", par)], writes=[("ps", pX)])
                                    S.op("act", lambda e, hd=hd, pX=pX: e.activation(out=Xsb[0:R, hd, :], in_=ps[pX][0:R, 0:64], func=AF.Copy, scale=-1.0), reads=[("ps", pX)], writes=[("r_Xsb", hd)])
                                    pS = self.nextps()
                                    S.op("pe", lambda e, hd=hd, pS=pS, m_=m_: e.matmul(ps[pS][0:R, 0:64], lhsT=Wm[0:R, m_, 0:R], rhs=Xsb[0:R, hd, :], start=True, stop=True), reads=[("r_Wm", m_), ("r_Xsb", hd)], writes=[("ps", pS)])
                                    S.op("dve", lambda e, hd=hd, pS=pS: e.tensor_copy(out=Ssb[0:R, hd, :], in_=ps[pS][0:R, 0:64]), reads=[("ps", pS)], writes=[("r_Ssb", hd)])
                                    oo = ps[pO][hs, c * 128:c * 128 + R]
                                    if not sample_pass:
                                        S.op("pe", lambda e, c=c, hs=hs, oo=oo: e.matmul(oo, lhsT=Sbf[hs, :], rhs=KR[hs, c, 1, 0:R], start=True, stop=False), reads=[("r_KR1", par), ("r_Sbf", hd)], writes=[("ps", pO)])
                                        S.op("pe", lambda e, hd=hd, m_=m_, oo=oo: e.matmul(oo, lhsT=Ssb[0:R, hd, :], rhs=MbT[0:R, m_, 0:R], start=False, stop=False), reads=[("r_Ssb", hd), ("r_MbT", m_)], writes=[("ps", pO)])
                                        S.op("pe", lambda e, c=c, hs=hs, m_=m_, oo=oo: e.matmul(oo, lhsT=v_tok[0:R, c, hs], rhs=MkT[0:R, m_, 0:R], start=False, stop=True), reads=[("r_vtok", par), ("r_MkT", m_)], writes=[("ps", pO)])
                                        pU = self.nextps()
                                        S.op("pe", lambda e, c=c, hs=hs, hd=hd, pU=pU: e.matmul(ps[pU][hs, 0:64], lhsT=bt_tok[0:R, c, hs], rhs=Ssb[0:R, hd, :], start=True, stop=False), reads=[("r_bttok", par), ("r_Ssb", hd)], writes=[("ps", pU)])
                                        S.op("pe", lambda e, c=c, hs=hs, pU=pU: e.matmul(ps[pU][hs, 0:64], lhsT=kt_tok[0:R, c, hs], rhs=v_tok[0:R, c, hs], start=False, stop=True), reads=[("r_kttok", par), ("r_vtok", par)], writes=[("ps", pU)])
                                        S.op("dve", lambda e, c=c, hs=hs, pU=pU: e.scalar_tensor_tensor(out=Sst[hs, :], in0=Sst[hs, :], scalar=EC[hs, c:c + 1], in1=ps[pU][hs, 0:64], op0=ALU.mult, op1=ALU.add), reads=[("ps", pU), ("r_Sst", hd), ("r_EC", par)], writes=[("r_Sst", hd)])
                                        S.op("pool", lambda e, hs=hs: e.tensor_copy(out=Sbf[hs, :], in_=Sst[hs, :]), reads=[("r_Sst", hd)], writes=[("r_Sbf", hd)])
                                    else:
                                        S.op("pe", lambda e, hd=hd, m_=m_, oo=oo: e.matmul(oo, lhsT=Ssb[:, hd, :], rhs=MbT[:, m_, 0:R], start=True, stop=False), reads=[("r_Ssb", hd), ("r_MbT", m_)], writes=[("ps", pO)])
                                        S.op("pe", lambda e, hs=hs, m_=m_, oo=oo: e.matmul(oo, lhsT=v_tok[:, 0, hs], rhs=MkT[:, m_, 0:R], start=False, stop=False), reads=[("r_vtok", par), ("r_MkT", m_)], writes=[("ps", pO)])
                                        for b in range(NB):
                                            S.op("pe", lambda e, b=b, hs=hs, oo=oo: e.matmul(oo, lhsT=S0bf[hs, b, :], rhs=rblk[hs, b, :], start=False, stop=(b == NB - 1)), reads=["r_rblk"] + s0bk, writes=[("ps", pO)])
                                        S.op("dve", lambda e, hd=hd: e.tensor_tensor(out=Sblk[0:NS, hd, :, :], in0=Ssb[0:NS, hd, :].unsqueeze(1).broadcast_to([NS, NB, 64]), in1=self.maskB.unsqueeze(2).broadcast_to([NS, NB, 64]), op=ALU.mult), reads=[("r_Ssb", hd), "consts"], writes=[("r_Sblk", hd)])
                                        S.op("dve", lambda e, hd=hd, hs=hs: e.tensor_tensor(out=Vblk[0:NS, hd, :, :], in0=v_tok[0:NS, 0, hs].unsqueeze(1).broadcast_to([NS, NB, 64]), in1=self.maskB.unsqueeze(2).broadcast_to([NS, NB, 64]), op=ALU.mult), reads=[("r_vtok", par), "consts"], writes=[("r_Vblk", hd)])
                                        for half in range(2):
                                            pU = self.nextps()
                                            S.op("pe", lambda e, hs=hs, hd=hd, half=half, pU=pU: e.matmul(ps[pU][hs, 0:512], lhsT=bt_tok[:, 0, hs], rhs=Sblk[:, hd, half * 8:(half + 1) * 8, :], start=True, stop=False), reads=[("r_bttok", par), ("r_Sblk", hd)], writes=[("ps", pU)])
                                            S.op("pe", lambda e, hs=hs, hd=hd, half=half, pU=pU: e.matmul(ps[pU][hs, 0:512], lhsT=kt_tok[:, 0, hs], rhs=Vblk[:, hd, half * 8:(half + 1) * 8, :], start=False, stop=True), reads=[("r_kttok", par), ("r_Vblk", hd)], writes=[("ps", pU)])
                                            bs = slice(half * 8, (half + 1) * 8)
                                            S.op("dve", lambda e, hs=hs, bs=bs: e.tensor_tensor(out=S0[hs, bs, :], in0=S0[hs, bs, :], in1=EC[hs, bs].unsqueeze(2).broadcast_to([64, 8, 64]), op=ALU.mult), reads=s0k + [("r_EC", par)], writes=s0k)
                                            S.op("dve", lambda e, hs=hs, bs=bs, pU=pU: e.tensor_tensor(out=S0[hs, bs, :], in0=S0[hs, bs, :], in1=ps[pU][hs, 0:512].rearrange("p (b v) -> p b v", b=8), op=ALU.add), reads=s0k + [("ps", pU)], writes=s0k)
                            if sample_pass:
                                for q4 in range(4):
                                    p1 = self.nextps()
                                    for bb in range(4):
                                        b = q4 * 4 + bb
                                        S.op("pe", lambda e, b=b, bb=bb, p1=p1: e.transpose(ps[p1][0:64, bb * 128:(bb + 1) * 128], S0[:, b, :], self.ident), reads=s0k + ["consts"], writes=[("ps", p1)])
                                    self.copy("act", s0nat[:, q4 * 4:(q4 + 1) * 4, :], ps[p1][0:64, :].rearrange("p (b k) -> p b k", b=4), [("ps", p1)], [("r_s0nat", 0), ("r_s0nat", 1)])
                                for hd in range(2):
                                    S.op("sp", lambda e, hd=hd: e.dma_start(out=dr["wkv_s"][:, 2 * pc + hd].rearrange("b v k -> v b k"), in_=s0nat[:, :, hd * 64:(hd + 1) * 64]), reads=[("r_s0nat", 0), ("r_s0nat", 1)], dma="r_wkv_s")
                            elif c0 + w == TP:
                                p1 = self.nextps()
                                S.op("pe", lambda e, p1=p1: e.transpose(ps[p1][0:64, 0:128], Sst, self.ident), reads=[("r_Sst", 0), ("r_Sst", 1), "consts"], writes=[("ps", p1)])
                                self.copy("act", sT, ps[p1][0:64, 0:128], [("ps", p1)], ["r_sT"])
                                S.op("sp", lambda e: e.dma_start(out=dr["wkv_p"][2 * pc:2 * pc + 2].rearrange("h v k -> v h k"), in_=sT.rearrange("v (h k) -> v h k", h=2)), reads=["r_sT"], dma="r_wkv_p")
                            yield 0
                            self.copy("act", osb[:, 0:w], ps[pO][:, 0:w], [("ps", pO)], ["r_osb"])
                            pm = self.nextps()
                            S.op("pe", lambda e: e.matmul(ps[pm][:, 0:w], lhsT=self.blk1, rhs=osb[:, 0:w], start=True, stop=True), reads=["r_osb", "consts"], writes=[("ps", pm)])
                            S.op("dve", lambda e: e.scalar_tensor_tensor(out=osb[:, 0:w], in0=ps[pm][:, 0:w], scalar=-1.0 / 64, in1=osb[:, 0:w], op0=ALU.mult, op1=ALU.add), reads=[("ps", pm), "r_osb"], writes=["r_osb"])
                            S.op("act", lambda e: e.activation(out=t2[:, 0:w], in_=osb[:, 0:w], func=AF.Square), reads=["r_osb", "r_t2"], writes=["r_t2"])
                            pv2 = self.nextps()
                            S.op("pe", lambda e: e.matmul(ps[pv2][:, 0:w], lhsT=self.blk1, rhs=t2[:, 0:w], start=True, stop=True), reads=["r_t2", "consts"], writes=[("ps", pv2)])
                            S.op("act", lambda e: e.activation(out=t2[:, 0:w], in_=ps[pv2][:, 0:w], func=AF.Ln, scale=1.0 / 64, bias=GN_EPS), reads=[("ps", pv2), "r_t2"], writes=["r_t2"])
                            S.op("act", lambda e: e.activation(out=t2[:, 0:w], in_=t2[:, 0:w], func=AF.Exp, scale=-0.5), reads=["r_t2"], writes=["r_t2"])
                            S.op("dve", lambda e: e.tensor_tensor(out=osb[:, 0:w], in0=osb[:, 0:w], in1=t2[:, 0:w], op=ALU.mult), reads=["r_osb", "r_t2"], writes=["r_osb"])
                            S.op("dve", lambda e: e.tensor_scalar(out=osb[:, 0:w], in0=osb[:, 0:w], scalar1=vec("rw_lnx_w"), scalar2=vec("rw_lnx_b"), op0=ALU.mult, op1=ALU.add), reads=["r_osb", "consts"], writes=["r_osb"])
                            S.op("pool", lambda e: e.tensor_tensor(out=osb[:, 0:w], in0=osb[:, 0:w], in1=bon[:, 0:w], op=ALU.add), reads=["r_osb", ("r_bon", par)], writes=["r_osb"])
                            S.op("dve", lambda e: e.tensor_tensor(out=og[:, 0:w], in0=osb[:, 0:w], in1=gf[:, 0:w], op=ALU.mult), reads=["r_osb", ("r_gf", par)], writes=["r_og"])
                            for dc in range(KC):
                                pi = self.nextps()
                                S.op("pe", lambda e, pi=pi, dc=dc: e.matmul(ps[pi][:, 0:w], lhsT=wout[:, dc * 128:(dc + 1) * 128], rhs=og[:, 0:w], start=True, stop=True), reads=[("r_wout", pc % 2), "r_og"], writes=[("ps", pi)])
                                S.op("dve", lambda e, pi=pi, dc=dc: e.tensor_tensor(out=xres[:, dc, c0:c0 + w], in0=xres[:, dc, c0:c0 + w], in1=ps[pi][:, 0:w], op=ALU.add), reads=[("ps", pi), ("xres", dc, c0)], writes=[("xres", dc, c0)])
                                yield 0


                    gens = []
                    gi = 0
                    for pc in range(KC):
                        for bi_, (c0, w, kind) in enumerate(blocks):
                            gens.append(blk(pc, bi_, c0, w, kind, gi % 2))
                            gi += 1
                    back = None
                    for g in gens:
                        fd, bd = False, back is None
                        while not (fd and bd):
                            if not fd:
                                fd = next(g) == "MID"
                            if not bd:
                                bd = next(back, "END") == "END"
                        back = g
                    for _ in back:
                        pass
                S.barrier()

    def mamba2(self, j):
        nc, S, TP = self.nc, self.S, self.TP
        xres, hT, ps = self.xres, self.hT, self.ps
        w_in, w_out = self.dram["mb_w_in"], self.dram["mb_w_out"]
        st_ssm, st_conv = self.dram["st_ssm"], self.dram["st_conv"]
        ssm_p, ssm_s, cv_p, cv_s = self.dram["ssm_p"], self.dram["ssm_s"], self.dram["cv_p"], self.dram["cv_s"]
        W = 512
        with ExitStack() as es:
            T = lambda n, sh, dt=F32: es.enter_context(self.T(n, sh, dt))
            wz, wx = T("m_wz", [128, KC, 512], BF16), T("m_wx", [128, KC, 512], BF16)
            wB, wC, wdt = T("m_wB", [128, KC, 128], BF16), T("m_wC", [128, KC, 128], BF16), T("m_wdt", [128, KC, 8], BF16)
            wout = T("m_wout", [128, 4, D], BF16)
            ngt = T("m_ng", [128, 512])
            pre = T("m_pre", [128, 6, 3 + W])
            acc = T("m_acc", [128, 1, W])
            xsT, BT, CT = T("m_xsT", [128, 4, W], BF16), T("m_BT", [128, W], BF16), T("m_CT", [128, W], BF16)
            cvT, cvrow = T("m_cvT", [128, 6, 48]), T("m_cvrow", [48, 768])
            zs, dtv, da = T("m_zs", [128, 512]), T("m_dt", [128, 8]), T("m_da", [128, 8])
            xtok, Btok = T("m_xtok", [128, 512], BF16), T("m_Btok", [128, 128], BF16)
            cbm, ex, wts = T("m_cbm", [128, 128]), T("m_ex", [128, 24]), T("m_wts", [128, 8])
            daM, seg, mT = T("m_daM", [128, 2, 128]), T("m_seg", [128, 2, 128]), T("m_mT", [128, 2, 128], BF16)
            y1, xd = T("m_y1", [128, 512]), T("m_xd", [128, 512])
            ssq, yn = T("m_ssq", [128, 2]), T("m_yn", [128, 512], BF16)
            yTb = T("m_yTb", [128, 4, W], BF16)
            xw = T("m_xw", [128, 512], BF16)
            ST, STbf = T("m_ST", [128, 512]), T("m_STbf", [128, 512], BF16)
            stT = T("m_stT", [128, 4, 128])
            cv0 = cvrow
            Cblk = T("m_Cblk", [128, NB, NS], BF16)
            ST0bf = T("m_ST0bf", [128, 2, 512], BF16)
            snat = T("m_snat", [128, 2, 4, 128])
            dablk, etots = T("m_dablk", [64, NB, 8]), T("m_etots", [128, NB, 8])
            for g in range(4):
                for kk0 in range(0, KC, 2):
                    for (wt, col) in ((wz, g * 512), (wx, 2048 + g * 512)):
                        src = w_in[j, kk0 * 128:(kk0 + 2) * 128, col:col + 512].rearrange("(k p) c -> p k c", p=128)
                        self.load_w(wt[:, kk0:kk0 + 2, :], src, [128, 2, 512], ("m_w", id(wt), kk0))
                for (wt, col) in ((wB, 4096 + g * 128), (wC, 4608 + g * 128)):
                    self.load_w(wt, w_in[j, :, col:col + 128].rearrange("(k p) c -> p k c", p=128), [128, KC, 128], ("m_w", id(wt)))
                self.load_w(wdt, w_in[j, :, 5120 + 8 * g:5128 + 8 * g].rearrange("(k p) c -> p k c", p=128), [128, KC, 8], ("m_w", id(wdt)))
                for fc in range(4):
                    self.load_w(wout[:, fc, :], w_out[j, g * 512 + fc * 128:g * 512 + (fc + 1) * 128, :], [128, D], ("m_wout", fc))
                S.op("sp", lambda e: e.dma_start(out=ngt, in_=self.dram["mb_norm"][:, g * 512:(g + 1) * 512]), writes=["m_ng"], dma="m_ng")
                wk2 = lambda wt: [("m_w", id(wt), k_) for k_ in range(0, KC, 2)]
                wk1 = lambda wt: [("m_w", id(wt))]
                fcol = [g * 512 + i * 128 for i in range(4)] + [2048 + g * 128, 2560 + g * 128]
                fch24 = [c_ // 128 for c_ in fcol]
                S.op("pool", lambda e: e.memset(ST, 0.0), writes=["m_ST"])
                S.op("pool", lambda e: e.memset(STbf, 0.0), writes=["m_STbf"])
                S.op("pool", lambda e: e.memset(pre[:, :, 0:3], 0.0), writes=["m_pre"])
                S.op("pool", lambda e: e.memset(cvT, 0.0), writes=["m_cvT"])
                for (c0, w, kind) in self.tbs:
                    smp = kind == "s"
                    hc = 2 + c0
                    R = 64 if smp else 128
                    if smp:
                        pre4 = pre[:, :, 0:NB * 7].rearrange("p f (b t) -> p f b t", t=7)
                        for i_, (c_, n_) in enumerate(((fcol[0], 512), (fcol[4], 128), (fcol[5], 128))):
                            o_ = (0, 512, 640)[i_]
                            S.op("sp", lambda e, c_=c_, n_=n_, o_=o_: e.dma_start(out=cv0[:, o_:o_ + n_], in_=st_conv[:, :, c_:c_ + n_].rearrange("b w f -> (b w) f")), writes=["m_cvrow"], dma="m_cv0")
                        pt = self.nextps()
                        for f in range(6):
                            S.op("pe", lambda e, f=f: e.transpose(ps[pt][:, f * 48:(f + 1) * 48], cv0[:, f * 128:(f + 1) * 128], self.ident[0:48, 0:48]), reads=["m_cvrow", "consts"], writes=[("ps", pt)])
                        self.copy("dve", pre4[:, :, :, 0:3], ps[pt][:, 0:288].rearrange("p (f b t) -> p f b t", f=6, t=3), [("ps", pt)], ["m_pre"])
                    elif c0 > 0:
                        self.copy("dve", pre[:, :, 0:3], pre[:, :, W:W + 3], ["m_pre"], ["m_pre"])
                    for f in range(6):
                        pi = self.nextps()
                        wt, cs = (wx, f * 128) if f < 4 else ((wB, 0) if f == 4 else (wC, 0))
                        for kc in range(KC):
                            S.op("pe", lambda e, pi=pi, kc=kc, wt=wt, cs=cs: e.matmul(ps[pi][:, 0:w], lhsT=wt[:, kc, cs:cs + 128], rhs=hT[:, kc, hc:hc + w], start=(kc == 0), stop=(kc == KC - 1)),
                                 reads=(wk2(wt) if f < 4 else wk1(wt)) + [("hT", "all")], writes=[("ps", pi)])
                        if smp:
                            self.copy(self.ew(), pre4[:, f, :, 3:7], ps[pi][:, 0:NS].rearrange("p (b t) -> p b t", t=TS), [("ps", pi)], ["m_pre"])
                        else:
                            self.copy(self.ew(), pre[:, f, 3:3 + W], ps[pi][:, 0:W], [("ps", pi)], ["m_pre"])
                    if MBSTOP == 2:
                        return
                    for f in range(6):
                        f24 = fch24[f]
                        if smp:
                            a_ = acc[:, 0, 0:NS].rearrange("p (b t) -> p b t", t=TS)
                            src_k = lambda k_: pre4[:, f, :, k_:k_ + TS]
                        else:
                            a_ = acc[:, 0, :]
                            src_k = lambda k_: pre[:, f, k_:k_ + W]
                        S.op("act", lambda e, a_=a_, f24=f24, s0=src_k(0): e.activation(out=a_, in_=s0, func=AF.Identity, scale=self.mb_cw[:, f24, 0:1], bias=self.mb_cb[:, f24:f24 + 1]),
                             reads=["m_pre", "consts"], writes=[("m_acc", 0)])
                        for k_ in range(1, 4):
                            S.op("dve", lambda e, a_=a_, f24=f24, k_=k_, sk=src_k(k_): e.scalar_tensor_tensor(out=a_, in0=sk, scalar=self.mb_cw[:, f24, k_:k_ + 1], in1=a_, op0=ALU.mult, op1=ALU.add),
                                 reads=["m_pre", "consts", ("m_acc", 0)], writes=[("m_acc", 0)])
                        dst = xsT[:, f, 0:w] if f < 4 else (BT[:, 0:w] if f == 4 else CT[:, 0:w])
                        S.op("act", lambda e, dst=dst, f=f: e.activation(out=dst, in_=acc[:, 0, 0:w], func=AF.Silu), reads=[("m_acc", 0)], writes=[("m_xbc", f)])
                    if MBSTOP == 3:
                        return
                    if smp or c0 + w == TP:
                        nr = 48 if smp else 3
                        if smp:
                            self.copy("dve", cvT.rearrange("p f (b t) -> p f b t", t=3), pre4[:, :, :, 4:7], ["m_pre"], ["m_cvT"])
                        else:
                            self.copy("dve", cvT[:, :, 0:3], pre[:, :, W:W + 3], ["m_pre"], ["m_cvT"])
                        pt, ptx = self.nextps(), self.nextps()
                        for f in range(6):
                            pp = pt if f < 4 else ptx
                            nrt = max(nr, 32)
                            S.op("pe", lambda e, f=f, nrt=nrt, pp=pp: e.transpose(ps[pp][0:nrt, (f % 4) * 128:(f % 4 + 1) * 128], cvT[:, f, 0:nrt], self.ident), reads=["m_cvT", "consts"], writes=[("ps", pp)])
                        self.copy("act", cvrow[0:nr, 0:512], ps[pt][0:nr, 0:512], [("ps", pt)], ["m_cvrow"])
                        self.copy("act", cvrow[0:nr, 512:768], ps[ptx][0:nr, 0:256], [("ps", ptx)], ["m_cvrow"])
                        for i_, (c_, n_) in enumerate(((fcol[0], 512), (fcol[4], 128), (fcol[5], 128))):
                            o_ = (0, 512, 640)[i_]
                            dstd = cv_s[:, :, c_:c_ + n_].rearrange("b w f -> (b w) f") if smp else cv_p[:, c_:c_ + n_]
                            S.op("sp", lambda e, dstd=dstd, o_=o_, n_=n_, nr=nr: e.dma_start(out=dstd, in_=cvrow[0:nr, o_:o_ + n_]), reads=["m_cvrow"], dma="m_cvout")
                    if MBSTOP == 4:
                        return
                    xbk = [("m_xbc", f) for f in range(6)]
                    mU = self.maskS if smp else self.maskU
                    mL = self.sLS if smp else self.sL
                    for ct in range(1 if smp else w // 128):
                        t0 = ct * 128
                        pz, pd = self.nextps(), self.nextps()
                        for kc in range(KC):
                            S.op("pe", lambda e, kc=kc: e.matmul(ps[pz][0:R, 0:512], lhsT=hT[:, kc, hc + t0:hc + t0 + R], rhs=wz[:, kc, :], start=(kc == 0), stop=(kc == KC - 1)),
                                 reads=wk2(wz) + [("hT", "all")], writes=[("ps", pz)])
                        for kc in range(KC):
                            S.op("pe", lambda e, kc=kc: e.matmul(ps[pd][0:R, 0:8], lhsT=hT[:, kc, hc + t0:hc + t0 + R], rhs=wdt[:, kc, :], start=(kc == 0), stop=(kc == KC - 1)),
                                 reads=wk1(wdt) + [("hT", "all")], writes=[("ps", pd)])
                        S.op("act", lambda e: e.activation(out=zs[0:R, :], in_=ps[pz][0:R, 0:512], func=AF.Silu), reads=[("ps", pz)], writes=["m_zs"])
                        S.op("dve", lambda e: e.tensor_tensor(out=dtv[0:R, :], in0=ps[pd][0:R, 0:8], in1=self.mb_dtb[0:R, 8 * g:8 * g + 8], op=ALU.add), reads=[("ps", pd), "consts"], writes=["m_dt"])
                        S.op("act", lambda e: e.activation(out=dtv[0:R, :], in_=dtv[0:R, :], func=AF.Exp), reads=["m_dt"], writes=["m_dt"])
                        S.op("act", lambda e: e.activation(out=dtv[0:R, :], in_=dtv[0:R, :], func=AF.Ln, bias=1.0), reads=["m_dt"], writes=["m_dt"])
                        S.op("dve", lambda e: e.tensor_tensor(out=da[0:R, :], in0=dtv[0:R, :], in1=self.mb_negA[0:R, 8 * g:8 * g + 8], op=ALU.mult), reads=["m_dt", "mbc0"], writes=["m_da"])
                        if MBSTOP == 5:
                            return
                        pt = self.nextps()
                        ptb = ps[pt].bitcast(BF16)
                        for fc in range(4):
                            S.op("pe", lambda e, fc=fc: e.transpose(ptb[0:R, fc * 128:(fc + 1) * 128], xsT[:, fc, t0:t0 + R], self.identb), reads=xbk[0:4] + ["consts2"], writes=[("ps", pt)])
                        S.op("pe", lambda e: e.transpose(ptb[0:R, 512:640], BT[:, t0:t0 + R], self.identb), reads=[xbk[4], "consts2"], writes=[("ps", pt)])
                        self.copy("dve", xtok[0:R, :], ptb[0:R, 0:512], [("ps", pt)], ["m_xtok"])
                        self.copy("dve", Btok[0:R, :], ptb[0:R, 512:640], [("ps", pt)], ["m_Btok"])
                        pc = self.nextps()
                        S.op("pe", lambda e: e.matmul(ps[pc][0:R, 0:R], lhsT=BT[:, t0:t0 + R], rhs=CT[:, t0:t0 + R], start=True, stop=True), reads=[xbk[4], xbk[5]], writes=[("ps", pc)])
                        S.op("dve", lambda e: e.tensor_tensor(out=cbm[0:R, 0:R], in0=ps[pc][0:R, 0:R], in1=mU, op=ALU.mult), reads=[("ps", pc), "consts"], writes=["m_cbm"])
                        if MBSTOP == 6:
                            return
                        pm = self.nextps()
                        S.op("pe", lambda e: e.matmul(ps[pm][0:R, 0:8], lhsT=mU, rhs=da[0:R, :], start=True, stop=True), reads=["m_da", "consts"], writes=[("ps", pm)])
                        S.op("pe", lambda e: e.matmul(ps[pm][0:R, 8:16], lhsT=mL, rhs=da[0:R, :], start=True, stop=True), reads=["m_da", "consts"], writes=[("ps", pm)])
                        if not smp:
                            S.op("pe", lambda e: e.matmul(ps[pm][:, 16:24], lhsT=self.onesf, rhs=da, start=True, stop=True), reads=["m_da", "consts2"], writes=[("ps", pm)])
                            S.op("act", lambda e: e.activation(out=ex, in_=ps[pm][:, 0:24], func=AF.Exp), reads=[("ps", pm)], writes=["m_ex"])
                        else:
                            S.op("act", lambda e: e.activation(out=ex[0:R, 0:16], in_=ps[pm][0:R, 0:16], func=AF.Exp), reads=[("ps", pm)], writes=["m_ex"])
                            S.op("dve", lambda e: e.tensor_tensor(out=dablk, in0=da[0:NS, :].unsqueeze(1).broadcast_to([NS, NB, 8]), in1=self.maskB.unsqueeze(2).broadcast_to([NS, NB, 8]), op=ALU.mult),
                                 reads=["m_da", "consts"], writes=["m_dablk"])
                            pm2 = self.nextps()
                            S.op("pe", lambda e: e.matmul(ps[pm2][:, 0:NB * 8], lhsT=self.onesf[0:NS, :], rhs=dablk.rearrange("p b h -> p (b h)"), start=True, stop=True), reads=["m_dablk", "consts2"], writes=[("ps", pm2)])
                            S.op("act", lambda e: e.activation(out=etots.rearrange("p b h -> p (b h)"), in_=ps[pm2][:, 0:NB * 8], func=AF.Exp), reads=[("ps", pm2)], writes=["m_etots"])
                        S.op("dve", lambda e: e.tensor_tensor(out=wts[0:R, :], in0=dtv[0:R, :], in1=ex[0:R, 8:16], op=ALU.mult), reads=["m_dt", "m_ex"], writes=["m_wts"])
                        if MBSTOP == 7:
                            return
                        py = self.nextlong()
                        for hh in range(8):
                            sl = hh % 2
                            S.op("dve", lambda e, hh=hh, sl=sl: e.tensor_scalar(out=daM[0:R, sl, 0:R], in0=mL, scalar1=da[0:R, hh:hh + 1], scalar2=None, op0=ALU.mult), reads=["m_da", "consts"], writes=[("m_daM", sl)])
                            pD = self.nextps()
                            S.op("pe", lambda e, sl=sl, pD=pD: e.matmul(ps[pD][0:R, 0:R], lhsT=daM[0:R, sl, 0:R], rhs=mU, start=True, stop=True), reads=[("m_daM", sl), "consts"], writes=[("ps", pD)])
                            S.op("act", lambda e, sl=sl, pD=pD: e.activation(out=seg[0:R, sl, 0:R], in_=ps[pD][0:R, 0:R], func=AF.Exp), reads=[("ps", pD)], writes=[("m_seg", sl)])
                            S.op("dve", lambda e, sl=sl, hh=hh: e.scalar_tensor_tensor(out=mT[0:R, sl, 0:R], in0=seg[0:R, sl, 0:R], scalar=dtv[0:R, hh:hh + 1], in1=cbm[0:R, 0:R], op0=ALU.mult, op1=ALU.mult),
                                 reads=[("m_seg", sl), "m_dt", "m_cbm"], writes=[("m_mT", sl)])
                            S.op("pe", lambda e, sl=sl, hh=hh: e.matmul(ps[py][0:R, hh * 64:(hh + 1) * 64], lhsT=mT[0:R, sl, 0:R], rhs=xtok[0:R, hh * 64:(hh + 1) * 64], start=True, stop=True),
                                 reads=[("m_mT", sl), "m_xtok"], writes=[("ps", py)])
                        if MBSTOP == 8:
                            return
                        pyi = self.nextlong()
                        if not smp:
                            S.op("pe", lambda e: e.matmul(ps[pyi][:, 0:512], lhsT=CT[:, t0:t0 + 128], rhs=STbf, start=True, stop=True), reads=[xbk[5], "m_STbf"], writes=[("ps", pyi)])
                        else:
                            S.op("dve", lambda e: e.tensor_tensor(out=Cblk, in0=CT[:, 0:NS].unsqueeze(1).broadcast_to([128, NB, NS]), in1=self.maskC, op=ALU.mult), reads=[xbk[5], "consts"], writes=["m_Cblk"])
                            for b in range(NB):
                                sl = b % 2
                                S.op("sp", lambda e, b=b, sl=sl: e.dma_start(out=snat[:, sl, :, :], in_=st_ssm[b, 8 * g:8 * g + 8].rearrange("h p n -> (h p) n").rearrange("(q r) n -> r q n", r=128)),
                                     writes=[("m_snat", sl)], dma="m_snat%d" % sl)
                                pq_ = self.nextps()
                                for q_ in range(4):
                                    S.op("pe", lambda e, q_=q_, sl=sl, pq_=pq_: e.transpose(ps[pq_][:, q_ * 128:(q_ + 1) * 128], snat[:, sl, q_, :], self.ident), reads=[("m_snat", sl), "consts"], writes=[("ps", pq_)])
                                self.copy("act", ST0bf[:, sl, :], ps[pq_][:, 0:512], [("ps", pq_)], [("m_ST0bf", sl)])
                                S.op("pe", lambda e, b=b, sl=sl: e.matmul(ps[pyi][0:NS, 0:512], lhsT=Cblk[:, b, :], rhs=ST0bf[:, sl, :], start=(b == 0), stop=(b == NB - 1)),
                                     reads=["m_Cblk", ("m_ST0bf", sl)], writes=[("ps", pyi)])
                        ecb = ex[0:R, 0:8].unsqueeze(2).broadcast_to([R, 8, 64])
                        v3 = lambda t_: t_.rearrange("p (h q) -> p h q", q=64)
                        S.op("dve", lambda e: e.tensor_tensor(out=v3(y1[0:R, :]), in0=v3(ps[pyi][0:R, 0:512]), in1=ecb, op=ALU.mult), reads=[("ps", pyi), "m_ex"], writes=["m_y1"])
                        S.op("dve", lambda e: e.tensor_tensor(out=y1[0:R, :], in0=y1[0:R, :], in1=ps[py][0:R, 0:512], op=ALU.add), reads=[("ps", py), "m_y1"], writes=["m_y1"])
                        S.op("dve", lambda e: e.tensor_tensor(out=v3(xd[0:R, :]), in0=v3(xtok[0:R, :]), in1=self.mb_D[0:R, 8 * g:8 * g + 8].unsqueeze(2).broadcast_to([R, 8, 64]), op=ALU.mult), reads=["m_xtok", "consts"], writes=["m_xd"])
                        S.op("dve", lambda e: e.tensor_tensor(out=y1[0:R, :], in0=y1[0:R, :], in1=xd[0:R, :], op=ALU.add), reads=["m_xd", "m_y1"], writes=["m_y1"])
                        S.op("dve", lambda e: e.tensor_tensor(out=y1[0:R, :], in0=y1[0:R, :], in1=zs[0:R, :], op=ALU.mult), reads=["m_zs", "m_y1"], writes=["m_y1"])
                        S.op("act", lambda e: e.activation(out=xd[0:R, :], in_=y1[0:R, :], func=AF.Square, accum_out=ssq[0:R, 0:1]), reads=["m_y1", "m_xd"], writes=["m_xd", "m_ssq"])
                        S.op("act", lambda e: e.activation(out=ssq[0:R, 1:2], in_=ssq[0:R, 0:1], func=AF.Ln, scale=1.0 / 512, bias=EPS), reads=["m_ssq"], writes=["m_ssq"])
                        S.op("act", lambda e: e.activation(out=ssq[0:R, 1:2], in_=ssq[0:R, 1:2], func=AF.Exp, scale=-0.5), reads=["m_ssq"], writes=["m_ssq"])
                        S.op("dve", lambda e: e.scalar_tensor_tensor(out=yn[0:R, :], in0=y1[0:R, :], scalar=ssq[0:R, 1:2], in1=ngt[0:R, :], op0=ALU.mult, op1=ALU.mult),
                             reads=["m_y1", "m_ssq", "m_ng"], writes=["m_yn"])
                        pt2 = self.nextps()
                        pt2b = ps[pt2].bitcast(BF16)
                        for fc in range(4):
                            S.op("pe", lambda e, fc=fc: e.transpose(pt2b[:, fc * 128:fc * 128 + R], yn[0:R, fc * 128:(fc + 1) * 128], self.identb[0:R, 0:R]), reads=["m_yn", "consts2"], writes=[("ps", pt2)])
                        self.copy("dve", yTb[:, :, t0:t0 + R], pt2b[:, 0:512].rearrange("p (f t) -> p f t", f=4)[:, :, 0:R], [("ps", pt2)], [("m_yTb", ct)])
                        if MBSTOP == 9:
                            return
                        S.op("dve", lambda e: e.tensor_tensor(out=v3(xw[0:R, :]), in0=v3(xtok[0:R, :]), in1=wts[0:R, :].unsqueeze(2).broadcast_to([R, 8, 64]), op=ALU.mult), reads=["m_xtok", "m_wts"], writes=["m_xw"])
                        if not smp:
                            pS = self.nextps()
                            S.op("pe", lambda e: e.matmul(ps[pS][:, 0:512], lhsT=Btok, rhs=xw, start=True, stop=True), reads=["m_Btok", "m_xw"], writes=[("ps", pS)])
                            S.op("dve", lambda e: e.tensor_tensor(out=v3(ST), in0=v3(ST), in1=ex[:, 16:24].unsqueeze(2).broadcast_to([128, 8, 64]), op=ALU.mult), reads=["m_ST", "m_ex"], writes=["m_ST"])
                            S.op("dve", lambda e: e.tensor_tensor(out=ST, in0=ST, in1=ps[pS][:, 0:512], op=ALU.add), reads=["m_ST", ("ps", pS)], writes=["m_ST"])
                            S.op("pool", lambda e: e.tensor_copy(out=STbf, in_=ST), reads=["m_ST"], writes=["m_STbf"])
                        else:
                            for b in range(NB):
                                sl = b % 2
                                S.op("sp", lambda e, b=b, sl=sl: e.dma_start(out=snat[:, sl, :, :], in_=st_ssm[b, 8 * g:8 * g + 8].rearrange("h p n -> (h p) n").rearrange("(q r) n -> r q n", r=128)),
                                     writes=[("m_snat", sl)], dma="m_snat%d" % sl)
                                pq_ = self.nextps()
                                for q_ in range(4):
                                    S.op("pe", lambda e, q_=q_, sl=sl, pq_=pq_: e.transpose(ps[pq_][:, q_ * 128:(q_ + 1) * 128], snat[:, sl, q_, :], self.ident), reads=[("m_snat", sl), "consts"], writes=[("ps", pq_)])
                                S.op("dve", lambda e, b=b, pq_=pq_: e.tensor_tensor(out=v3(ST), in0=v3(ps[pq_][:, 0:512]), in1=etots[:, b, :].unsqueeze(2).broadcast_to([128, 8, 64]), op=ALU.mult), reads=[("ps", pq_), "m_etots"], writes=["m_ST"])
                                S.op("dve", lambda e, b=b: e.tensor_scalar(out=y1[0:NS, :], in0=xw[0:NS, :], scalar1=self.maskB[:, b:b + 1], scalar2=None, op0=ALU.mult), reads=["m_xw", "consts", "m_y1"], writes=["m_y1"])
                                S.op("pool", lambda e: e.tensor_copy(out=yn[0:NS, :], in_=y1[0:NS, :]), reads=["m_y1", "m_yn"], writes=["m_yn"])
                                pS = self.nextps()
                                S.op("pe", lambda e, pS=pS: e.matmul(ps[pS][:, 0:512], lhsT=Btok[0:NS, :], rhs=yn[0:NS, :], start=True, stop=True), reads=["m_Btok", "m_yn"], writes=[("ps", pS)])
                                S.op("dve", lambda e, pS=pS: e.tensor_tensor(out=ST, in0=ST, in1=ps[pS][:, 0:512], op=ALU.add), reads=["m_ST", ("ps", pS)], writes=["m_ST"])
                                pq2 = self.nextps()
                                for q_ in range(4):
                                    S.op("pe", lambda e, q_=q_, pq2=pq2: e.transpose(ps[pq2][:, q_ * 128:(q_ + 1) * 128], ST[:, q_ * 128:(q_ + 1) * 128], self.ident), reads=["m_ST", "consts"], writes=[("ps", pq2)])
                                self.copy("act", stT.rearrange("p q n -> p (q n)"), ps[pq2][:, 0:512], [("ps", pq2)], ["m_stT"])
                                S.op("sp", lambda e, b=b: e.dma_start(out=ssm_s[b, 8 * g:8 * g + 8].rearrange("h p n -> (h p) n").rearrange("(q r) n -> r q n", r=128), in_=stT), reads=["m_stT"], dma="m_ssm_s")
                    if MBSTOP == 10:
                        return
                    for dc in range(KC):
                        pi = self.nextps()
                        for fc in range(4):
                            S.op("pe", lambda e, pi=pi, dc=dc, fc=fc: e.matmul(ps[pi][:, 0:w], lhsT=wout[:, fc, dc * 128:(dc + 1) * 128], rhs=yTb[:, fc, 0:w], start=(fc == 0), stop=(fc == 3)),
                                 reads=[("m_wout", fc)] + [("m_yTb", c_) for c_ in range(4)], writes=[("ps", pi)])
                        S.op("dve", lambda e, pi=pi, dc=dc: e.tensor_tensor(out=xres[:, dc, c0:c0 + w], in0=xres[:, dc, c0:c0 + w], in1=ps[pi][:, 0:w], op=ALU.add),
                             reads=[("ps", pi), ("xres", dc, c0)], writes=[("xres", dc, c0)])
                    if (not smp) and c0 + w == TP:
                        pq2 = self.nextps()
                        for q_ in range(4):
                            S.op("pe", lambda e, q_=q_: e.transpose(ps[pq2][:, q_ * 128:(q_ + 1) * 128], ST[:, q_ * 128:(q_ + 1) * 128], self.ident), reads=["m_ST", "consts"], writes=[("ps", pq2)])
                        self.copy("act", stT.rearrange("p q n -> p (q n)"), ps[pq2][:, 0:512], [("ps", pq2)], ["m_stT"])
                        S.op("sp", lambda e: e.dma_start(out=ssm_p[8 * g:8 * g + 8].rearrange("h p n -> (h p) n").rearrange("(q r) n -> r q n", r=128), in_=stT), reads=["m_stT"], dma="m_ssm_p")


def make_in_map(inp, core, TP, names):
    b0 = core * NB
    m = {}
    m["x_p"] = np.ascontiguousarray(inp["x_prompt"][core, :TP])
    m["x_s"] = np.ascontiguousarray(inp["x_sample"][b0:b0 + NB].reshape(NS, D))
    m["ident"] = np.eye(128, dtype=np.float32)
    m["norm_mix"] = np.ascontiguousarray(_fm(inp["norm_mix"]))
    m["norm_ffn"] = np.ascontiguousarray(_fm(inp["norm_ffn"]))
    m["norm_final"] = np.ascontiguousarray(_fm(inp["norm_final"]))
    for k in ("ffn_w_gate", "ffn_w_up", "ffn_w_down"):
        m[k] = inp[k]
    extra_in_map(m, inp, core, TP)
    return {k: np.ascontiguousarray(m[k], dtype=np.float32) for k in names}


def _masks():
    r = np.arange(128)
    maskU2 = ((r[:, None] // 64 == r[None, :] // 64) & (r[None, :] % 64 >= r[:, None] % 64)).astype(np.float32)
    r = np.arange(64)
    maskS = ((r[:, None] // TS == r[None, :] // TS) & (r[None, :] >= r[:, None])).astype(np.float32)
    maskB = (r[:, None] // TS == np.arange(NB)[None, :]).astype(np.float32)
    return maskU2, maskS, maskB


def extra_in_map(m, inp, core, TP):
    b0 = core * NB
    m["maskU2"], m["maskS"], m["maskB"] = _masks()
    m["hg_lb_logits"] = _fm(inp["hg_lb_logits"])
    m["hg_norm"] = np.ascontiguousarray(inp["hg_norm"].T)
    m["hg_w_in"] = inp["hg_w_in"]
    m["hg_w_out"] = inp["hg_w_out"]
    m["st_hg"] = inp["state_hgrn"][:, b0:b0 + NB]
    r = np.arange(128)
    m["maskU"] = (r[None, :] >= r[:, None]).astype(np.float32)
    m["sL"] = (r[:, None] > r[None, :]).astype(np.float32)
    r = np.arange(64)
    m["sLS"] = ((r[:, None] // TS == r[None, :] // TS) & (r[:, None] > r[None, :])).astype(np.float32)
    m["maskC"] = np.broadcast_to((r[None, :] // TS == np.arange(NB)[:, None]).astype(np.float32)[None], (128, NB, NS))
    r = np.arange(128)
    m["sU"] = (r[:, None] < r[None, :]).astype(np.float32)
    m["blk1"] = (r[:, None] // 64 == r[None, :] // 64).astype(np.float32)
    r = np.arange(64)
    m["sUS"] = ((r[:, None] // TS == r[None, :] // TS) & (r[:, None] < r[None, :])).astype(np.float32)
    m["rw_mu"] = _fm(inp["rw_mu"][0])
    for k in ("rw_w0", "rw_a0", "rw_k_k", "rw_k_a", "rw_lnx_w", "rw_lnx_b"):
        m[k] = _fm(inp[k][0])
    m["rw_r_k"] = _fm(inp["rw_r_k"][0].reshape(D))
    m["rw_w_rkv"] = inp["rw_w_rkv"][0]
    for k in ("rw_w1", "rw_w2", "rw_a1", "rw_a2", "rw_g1", "rw_g2", "rw_w_out"):
        m[k] = inp[k][0]
    m["st_wkv"] = inp["state_wkv"][0, b0:b0 + NB]
    m["st_shift"] = inp["state_shift"][0, b0:b0 + NB]
    m["mb_conv_w"] = np.ascontiguousarray(inp["mb_conv_w"][0].reshape(4, 24, 128).transpose(2, 1, 0))
    m["mb_conv_b"] = np.ascontiguousarray(inp["mb_conv_b"][0].reshape(24, 128).T)
    for k in ("mb_dt_bias", "mb_A_log", "mb_D"):
        m[k] = np.broadcast_to(inp[k][0][None, :], (128, 32))
    m["mb_norm"] = np.broadcast_to(inp["mb_norm"][0][None, :], (128, 2048))
    m["mb_w_in"] = inp["mb_w_in"]
    m["mb_w_out"] = inp["mb_w_out"]
    m["st_ssm"] = inp["state_ssm"][0, b0:b0 + NB]
    m["st_conv"] = inp["state_conv"][0, b0:b0 + NB]


_CACHE = {}


def run_cores(inp, TP, cores, layers=(0, 1, 2, 0), with_ffn=True):
    key = (TP, tuple(layers), with_ffn)
    bld = Builder(TP, layers, with_ffn)
    nc = bld.build()
    names = [k for k, v in bld.dram.items() if k in bld.in_names]
    in_maps = [make_in_map(inp, c, TP, names) for c in cores]
    res = run_bass_kernel_spmd(nc, in_maps, core_ids=list(range(len(cores))))
    return res.results


def kernel(**inputs):
    inp = {k: np.asarray(v) for k, v in inputs.items()}
    TP = inp["x_prompt"].shape[1]
    res = run_cores(inp, TP, list(range(NCORES)))
    st = lambda k: np.stack([r[k] for r in res], axis=0)
    cat = lambda k, ax=0: np.concatenate([r[k] for r in res], axis=ax)
    f = lambda a: np.ascontiguousarray(a, dtype=np.float32)
    y_p = st("y_p")
    y_s = cat("y_s").reshape(NCORES * NB, TS, D)
    hg_p = np.stack([r["hg_p"] for r in res], axis=1)
    hg_s = cat("hg_s", 1)
    return (f(y_p), f(y_s), f(hg_p), f(hg_s), f(st("wkv_p")[None]), f(cat("wkv_s")[None]),
            f(cat("sh_p")[None]), f(cat("sh_s")[None]), f(st("ssm_p")[None]), f(cat("ssm_s")[None]),
            f(st("cv_p")[None]), f(cat("cv_s")[None]))
```

```python
import numpy as np
from contextlib import ExitStack, contextmanager
import concourse.bass as bass
import concourse.mybir as mybir
from concourse.bass_utils import run_bass_kernel_spmd

F32 = mybir.dt.float32
BF16 = mybir.dt.bfloat16
AF = mybir.ActivationFunctionType
ALU = mybir.AluOpType
AX = mybir.AxisListType

D = 1024
KC = 8
DFF = 2816
FC = 22
NCORES = 8
NB = 16
TS = 4
NS = NB * TS
EPS = 1e-6
ROT = 30000
import os
MBSTOP = int(os.environ.get('MBSTOP', '0'))
RWSTOP = int(os.environ.get('RWSTOP', '0'))


class _Rec:
    def __init__(self):
        self.call = None

    def __getattr__(self, name):
        def f(*a, **k):
            self.call = (name, a, k)
            return self
        return f


class Sched:
    ENGS = ("pe", "act", "dve", "pool", "sp")

    def __init__(self, nc):
        self.nc = nc
        self.streams = {e: [] for e in self.ENGS}
        self.count = {e: 0 for e in self.ENGS}
        self.observed = {e: {} for e in self.ENGS}
        self.last_write = {}
        self.readers = {}
        self.dmacount = {}
        self.semkeys = []
        self.lastmark = {}

    def _semkey(self, sk):
        if sk not in self.lastmark:
            self.semkeys.append(sk)
        return sk

    def op(self, eng, fn, reads=(), writes=(), dma=None):
        rec = _Rec()
        fn(rec)
        fn = rec.call
        need = {}

        def add(m):
            if m is None:
                return
            sk, v = m
            if need.get(sk, 0) < v:
                need[sk] = v

        for k in reads:
            add(self.last_write.get(k))
            if isinstance(k, tuple) and k[0] == "ps":
                for m in self.readers.get(k, ()):
                    add(m)
        for k in writes:
            add(self.last_write.get(k))
            for m in self.readers.get(k, ()):
                add(m)
        st = self.streams[eng]
        obs = self.observed[eng]
        for sk, v in need.items():
            if eng == "pe" and sk[0] == "pe":
                continue
            if obs.get(sk, 0) >= v:
                continue
            st.append(("wait", sk, v))
            obs[sk] = v
        if dma is not None:
            sk = self._semkey(("dma", dma))
            self.dmacount[dma] = self.dmacount.get(dma, 0) + 16
            marker = (sk, self.dmacount[dma])
            amt = 16
        else:
            n = self.count[eng]
            sk = self._semkey((eng, n // ROT))
            marker = (sk, n % ROT + 1)
            self.count[eng] = n + 1
            amt = 1
        self.lastmark[sk] = marker[1]
        st.append(("op", fn, sk, amt))
        for k in writes:
            self.last_write[k] = marker
            self.readers[k] = []
        for k in reads:
            if k not in writes:
                self.readers.setdefault(k, []).append(marker)
        return marker

    def barrier(self):
        for e in self.ENGS:
            st = self.streams[e]
            obs = self.observed[e]
            for sk, v in self.lastmark.items():
                if obs.get(sk, 0) >= v:
                    continue
                st.append(("wait", sk, v))
                obs[sk] = v
        self.last_write = {}
        self.readers = {}

    def emit(self):
        nc = self.nc
        sems = {}
        for sk in self.semkeys:
            sems[sk] = nc.alloc_semaphore(name="s_" + "_".join(str(x) for x in sk))
        streams = self.streams
        engmap = {"pe": "tensor", "act": "scalar", "dve": "vector", "pool": "gpsimd", "sp": "sync"}

        def run(e, engine):
            for ent in streams[e]:
                if ent[0] == "wait":
                    engine.wait_ge(sems[ent[1]], ent[2])
                else:
                    name, a, k = ent[1]
                    ins = getattr(engine, name)(*a, **k)
                    ins.then_inc(sems[ent[2]], ent[3])

        with nc.Block() as block:
            for e in self.ENGS:
                if not streams[e]:
                    continue

                def mk(e=e):
                    def f(engine):
                        run(e, engine)
                    return f

                getattr(block, engmap[e])(mk())


def _fm(v):
    v = np.asarray(v, np.float32)
    lead = v.shape[:-1]
    v = v.reshape(lead + (KC, 128))
    v = np.moveaxis(v, -1, 0)
    return np.ascontiguousarray(v)


class Builder:
    def __init__(self, TP, layers=(0, 1, 2, 0), with_ffn=True):
        self.TP = TP
        self.N = TP + NS
        self.layers = layers
        self.with_ffn = with_ffn
        self.nc = bass.Bass("TRN2", target_bir_lowering=False)
        self.S = Sched(self.nc)
        self.dram = {}
        self.in_names = []
        self.NSTG = 3
        self.stg_i = 0
        self.ps_i = 0
        self.rr = 0
        self.tbs = [(i * 512, 512, "p") for i in range(TP // 512)] + [(TP, NS, "s")]

    def din(self, name, shape):
        t = self.nc.dram_tensor(name, list(shape), F32, kind="ExternalInput").ap()
        self.dram[name] = t
        self.in_names.append(name)
        return t

    def dout(self, name, shape):
        t = self.nc.dram_tensor(name, list(shape), F32, kind="ExternalOutput").ap()
        self.dram[name] = t
        return t

    @contextmanager
    def T(self, name, shape, dt=F32):
        self.uid = getattr(self, "uid", 0) + 1
        with self.nc.sbuf_tensor("%s_%d" % (name, self.uid), list(shape), dt) as t:
            yield t.ap()

    def sb(self, name, shape, dt=F32):
        return self.nc.alloc_sbuf_tensor(name, list(shape), dt).ap()

    def nextps(self):
        i = self.ps_i % 6
        self.ps_i += 1
        return i

    def nextlong(self):
        self.pl_i = getattr(self, "pl_i", 0) + 1
        return 6 + self.pl_i % 2

    def ew(self):
        self.rr += 1
        return "act" if self.rr % 2 else "dve"

    def copy(self, eng, out, in_, reads, writes):
        if eng == "act":
            self.S.op("act", lambda e: e.activation(out=out, in_=in_, func=AF.Copy), reads=reads, writes=writes)
        else:
            self.S.op(eng, lambda e: e.tensor_copy(out=out, in_=in_), reads=reads, writes=writes)

    def load_w(self, dst, src, shape, wkey, scale=None):
        S = self.S
        slot = self.stg_i % self.NSTG
        self.stg_i += 1
        rows = shape[0]
        free = int(np.prod(shape[1:]))
        assert free <= 1024
        st = self.stage[slot][0:rows, 0:free]
        if len(shape) == 3:
            st = st.rearrange("p (a b) -> p a b", a=shape[1])
        S.op("sp", lambda e: e.dma_start(out=st, in_=src), writes=[("stg", slot)], dma="stg%d" % slot)
        if scale is None:
            S.op("pool", lambda e: e.tensor_copy(out=dst, in_=st), reads=[("stg", slot)], writes=[wkey])
        else:
            S.op("pool", lambda e: e.tensor_scalar(out=dst, in0=st, scalar1=scale, scalar2=None, op0=ALU.mult),
                 reads=[("stg", slot), "consts"], writes=[wkey])

    def build(self):
        nc, S, TP, N = self.nc, self.S, self.TP, self.N
        nl = len(self.layers)
        x_p = self.din("x_p", [TP, D])
        x_s = self.din("x_s", [NS, D])
        ident_d = self.din("ident", [128, 128])
        nmix_d = self.din("norm_mix", [128, 4, KC])
        nffn_d = self.din("norm_ffn", [128, 4, KC])
        nfin_d = self.din("norm_final", [128, KC])
        wg_d = self.din("ffn_w_gate", [4, D, DFF])
        wu_d = self.din("ffn_w_up", [4, D, DFF])
        wd_d = self.din("ffn_w_down", [4, DFF, D])
        y_p = self.dout("y_p", [TP, D])
        y_s = self.dout("y_s", [NS, D])

        self.xres = self.sb("xres", [128, KC, N])
        self.hT = self.sb("hT", [128, KC, N + 2], BF16)
        self.stage = [self.sb("stage%d" % i, [128, 1024]) for i in range(self.NSTG)]
        self.ident = self.sb("ident_sb", [128, 128])
        self.identb = self.sb("identb_sb", [128, 128], BF16)
        self.onesb = self.sb("onesb", [128, 128], BF16)
        self.nmix = self.sb("nmix", [128, 4, KC])
        self.nffn = self.sb("nffn", [128, 4, KC])
        self.nfin = self.sb("nfin", [128, KC])
        self.ps = [nc.alloc_psum_tensor("ps%d" % i, [128, 512], F32).ap() for i in range(8)]
        xres, hT = self.xres, self.hT

        S.op("sp", lambda e: e.dma_start(out=self.ident, in_=ident_d), writes=["consts"], dma="c0")
        S.op("sp", lambda e: e.dma_start(out=self.nmix, in_=nmix_d), writes=["consts"], dma="c0")
        S.op("sp", lambda e: e.dma_start(out=self.nffn, in_=nffn_d), writes=["consts"], dma="c0")
        S.op("sp", lambda e: e.dma_start(out=self.nfin, in_=nfin_d), writes=["consts"], dma="c0")
        S.op("pool", lambda e: e.tensor_copy(out=self.identb, in_=self.ident), reads=["consts"], writes=["consts2"])
        S.op("pool", lambda e: e.memset(self.onesb, 1.0), writes=["consts2"])
        self.onesf = self.sb("onesf", [128, 128])
        S.op("pool", lambda e: e.memset(self.onesf, 1.0), writes=["consts2"])
        S.op("pool", lambda e: e.memset(hT[:, :, 0:2], 0.0), writes=["hT0"])
        self.extra_consts()
        S.barrier()

        with self.T("xin0", [128, D], F32) as xin0, self.T("xin1", [128, D], F32) as xin1:
            xins = [xin0, xin1]
            ntile = TP // 128 + 1
            for j in range(ntile):
                xin = xins[j % 2]
                rows = 128 if j < TP // 128 else NS
                src = x_p[j * 128:(j + 1) * 128, :] if j < TP // 128 else x_s
                S.op("sp", lambda e, xin=xin, rows=rows, src=src: e.dma_start(out=xin[0:rows, :], in_=src),
                     writes=[("xin", j % 2)], dma="xin%d" % (j % 2))
                for half in range(2):
                    pi = self.nextps()
                    for q in range(4):
                        kc = half * 4 + q
                        S.op("pe", lambda e, pi=pi, q=q, kc=kc, xin=xin, rows=rows: e.transpose(
                            self.ps[pi][:, q * 128:q * 128 + rows], xin[0:rows, kc * 128:(kc + 1) * 128],
                            self.ident[0:rows, 0:rows]),
                            reads=[("xin", j % 2), "consts"], writes=[("ps", pi)])
                    dst = xres[:, half * 4:(half + 1) * 4, j * 128:j * 128 + rows]
                    srcp = self.ps[pi].rearrange("p (q t) -> p q t", q=4)[:, :, 0:rows]
                    self.copy(self.ew(), dst, srcp, [("ps", pi)], [("xres", j)])
        S.barrier()

        for li, kind in enumerate(self.layers):
            self.cur_li = li
            self.rmsnorm(self.nmix[:, li, :], to_h=True)
            S.barrier()
            if kind == 0:
                self.hgrn2(li // 3)
            elif kind == 1:
                self.rwkv7(li // 3)
            elif kind == 2:
                self.mamba2(li // 3)
            S.barrier()
            if self.with_ffn:
                self.rmsnorm(self.nffn[:, li, :], to_h=True)
                S.barrier()
                self.ffn(li, wg_d, wu_d, wd_d)
                S.barrier()
        self.final_norm(y_p, y_s)
        S.barrier()
        S.emit()
        return nc

    def extra_consts(self):
        nc, S = self.nc, self.S
        def cload(name, shape):
            d = self.din(name, shape)
            t = self.sb("c_" + name, shape)
            S.op("sp", lambda e: e.dma_start(out=t, in_=d), writes=["consts"], dma="c0")
            return t
        self.maskU2 = cload("maskU2", [128, 128])
        self.maskS = cload("maskS", [64, 64])
        self.maskB = cload("maskB", [64, 16])
        lg = cload("hg_lb_logits", [128, 2, KC])
        self.hgn = cload("hg_norm", [128, 2])
        self.hg_lb = self.sb("hg_lb", [128, 2, KC])
        self.hg_oml = self.sb("hg_oml", [128, 2, KC])
        self.hg_noml = self.sb("hg_noml", [128, 2, KC])
        lb, oml, noml = self.hg_lb, self.hg_oml, self.hg_noml
        S.op("dve", lambda e: e.memset(lb[:, 0, :], 0.0), writes=["hgc0"])
        S.op("dve", lambda e: e.tensor_tensor(out=lb[:, 1, :], in0=lg[:, 1, :], in1=lg[:, 0, :], op=ALU.subtract), reads=["consts"], writes=["hgc1"])
        S.op("act", lambda e: e.activation(out=lb[:, 1, :], in_=lb[:, 1, :], func=AF.Sigmoid), reads=["hgc1"], writes=["hgc1"])
        S.op("dve", lambda e: e.tensor_scalar(out=oml, in0=lb, scalar1=-1.0, scalar2=1.0, op0=ALU.mult, op1=ALU.add), reads=["hgc0", "hgc1"], writes=["hgc2"])
        S.op("dve", lambda e: e.tensor_scalar(out=noml, in0=lb, scalar1=1.0, scalar2=-1.0, op0=ALU.mult, op1=ALU.add), reads=["hgc0", "hgc1"], writes=["hgc3"])
        self.maskU = cload("maskU", [128, 128])
        self.sL = cload("sL", [128, 128])
        self.sLS = cload("sLS", [64, 64])
        self.maskC = cload("maskC", [128, NB, NS])
        self.mb_cw = cload("mb_conv_w", [128, 24, 4])
        self.mb_cb = cload("mb_conv_b", [128, 24])
        self.mb_dtb = cload("mb_dt_bias", [128, 32])
        alog = cload("mb_A_log", [128, 32])
        self.mb_D = cload("mb_D", [128, 32])
        self.din("mb_norm", [128, 2048])
        self.mb_negA = self.sb("mb_negA", [128, 32])
        S.op("act", lambda e: e.activation(out=self.mb_negA, in_=alog, func=AF.Exp), reads=["consts"], writes=["mbc0"])
        S.op("dve", lambda e: e.tensor_scalar(out=self.mb_negA, in0=self.mb_negA, scalar1=-1.0, scalar2=None, op0=ALU.mult), reads=["mbc0"], writes=["mbc0"])
        self.din("mb_w_in", [1, D, 5152])
        self.din("mb_w_out", [1, 2048, D])
        self.din("st_ssm", [NB, 32, 64, 128])
        self.din("st_conv", [NB, 3, 3072])
        self.dout("ssm_p", [32, 64, 128])
        self.dout("ssm_s", [NB, 32, 64, 128])
        self.dout("cv_p", [3, 3072])
        self.dout("cv_s", [NB, 3, 3072])
        self.sU = cload("sU", [128, 128])
        self.sUS = cload("sUS", [64, 64])
        self.blk1 = cload("blk1", [128, 128])
        self.rw_mu = cload("rw_mu", [128, 6, KC])
        self.rw_omu = self.sb("rw_omu", [128, 6, KC])
        S.op("dve", lambda e: e.tensor_scalar(out=self.rw_omu, in0=self.rw_mu, scalar1=-1.0, scalar2=1.0, op0=ALU.mult, op1=ALU.add), reads=["consts"], writes=["rwc0"])
        self.rw_vec = {}
        for nm in ("rw_w0", "rw_a0", "rw_k_k", "rw_k_a", "rw_r_k", "rw_lnx_w", "rw_lnx_b"):
            self.rw_vec[nm] = cload(nm, [128, KC])
        self.rw_omka = self.sb("rw_omka", [128, KC])
        S.op("dve", lambda e: e.tensor_scalar(out=self.rw_omka, in0=self.rw_vec["rw_k_a"], scalar1=-1.0, scalar2=1.0, op0=ALU.mult, op1=ALU.add), reads=["consts"], writes=["rwc1"])
        for nm, shp in (("rw_w_rkv", [3, D, D]), ("rw_w1", [D, 64]), ("rw_w2", [64, D]), ("rw_a1", [D, 64]), ("rw_a2", [64, D]),
                        ("rw_g1", [D, 128]), ("rw_g2", [128, D]), ("rw_w_out", [D, D]), ("st_wkv", [NB, 16, 64, 64]), ("st_shift", [NB, D])):
            self.din(nm, shp)
        self.dout("wkv_p", [16, 64, 64])
        self.dout("wkv_s", [NB, 16, 64, 64])
        self.dout("sh_p", [1, D])
        self.dout("sh_s", [NB, D])
        self.din("hg_w_in", [2, D, 4 * D])
        self.din("hg_w_out", [2, D, D])
        self.din("st_hg", [2, NB, 8, 128, 128])
        self.dout("hg_p", [2, 8, 128, 128])
        self.dout("hg_s", [2, NB, 8, 128, 128])

    def rmsnorm(self, gain, to_h=True, out_f32=None):
        nc, S = self.nc, self.S
        xres, hT = self.xres, self.hT
        with self.T("n_sq", [128, 2, 512], BF16) as sq, self.T("n_r", [128, 2, 512], F32) as rr:
            for bi, (c0, w, kind) in enumerate(self.tbs):
                pi = self.nextps()
                for kc in range(KC):
                    s = kc % 2
                    S.op("act", lambda e, s=s, kc=kc, c0=c0, w=w: e.activation(out=sq[:, s, 0:w], in_=xres[:, kc, c0:c0 + w], func=AF.Square),
                         reads=[("xres", "all")], writes=[("n_sq", s)])
                    S.op("pe", lambda e, pi=pi, s=s, kc=kc, w=w: e.matmul(self.ps[pi][:, 0:w], lhsT=self.onesb, rhs=sq[:, s, 0:w], start=(kc == 0), stop=(kc == KC - 1)),
                         reads=[("n_sq", s), "consts2"], writes=[("ps", pi)])
                r = rr[:, bi % 2, 0:w]
                S.op("act", lambda e, pi=pi, r=r, w=w: e.activation(out=r, in_=self.ps[pi][:, 0:w], func=AF.Ln, scale=1.0 / D, bias=EPS),
                     reads=[("ps", pi)], writes=[("n_r", bi % 2)])
                S.op("act", lambda e, r=r: e.activation(out=r, in_=r, func=AF.Exp, scale=-0.5),
                     reads=[("n_r", bi % 2)], writes=[("n_r", bi % 2)])
                for kc in range(KC):
                    if out_f32 is None:
                        dst = hT[:, kc, 2 + c0:2 + c0 + w]
                    else:
                        dst = out_f32(kc, c0, w)
                    S.op("dve", lambda e, dst=dst, kc=kc, c0=c0, w=w, r=r: e.scalar_tensor_tensor(
                        out=dst, in0=xres[:, kc, c0:c0 + w], scalar=gain[:, kc:kc + 1], in1=r, op0=ALU.mult, op1=ALU.mult),
                        reads=[("xres", "all"), ("n_r", bi % 2), "consts"], writes=[("hT", kc, bi)])

    def ffn(self, li, wg_d, wu_d, wd_d):
        nc, S = self.nc, self.S
        xres, hT = self.xres, self.hT
        nprompt = len(self.tbs) - 1
        half = max(1, nprompt // 2)
        sbs = [self.tbs[:half], self.tbs[half:]] if nprompt >= 2 else [self.tbs]
        maxw = max(sum(w for (_, w, _) in sb_) for sb_ in sbs)
        with self.T("f_act", [128, FC, maxw], BF16) as act, \
                self.T("f_wgu", [128, 2, KC, 2, 256], BF16) as wgu, \
                self.T("f_wd", [128, 2, FC, 128], BF16) as wd, \
                self.T("f_sg", [128, 2, 512], F32) as sg:
            for sbi, sb_ in enumerate(sbs):
                base = sb_[0][0]
                for fp in range(FC // 2):
                    slot = fp % 2
                    for gi, wsrc in enumerate((wg_d, wu_d)):
                        for kk in range(0, KC, 4):
                            src = wsrc[li, kk * 128:(kk + 4) * 128, fp * 256:(fp + 1) * 256].rearrange("(k p) c -> p k c", p=128)
                            self.load_w(wgu[:, slot, kk:kk + 4, gi, :], src, [128, 4, 256], ("f_wgu", slot, gi, kk))
                    for fi in range(2):
                        f = fp * 2 + fi
                        for (c0, w, kind) in sb_:
                            pg, pu = self.nextps(), self.nextps()
                            for gi, pi in ((0, pg), (1, pu)):
                                for kc in range(KC):
                                    S.op("pe", lambda e, pi=pi, slot=slot, kc=kc, gi=gi, fi=fi, c0=c0, w=w: e.matmul(
                                        self.ps[pi][:, 0:w], lhsT=wgu[:, slot, kc, gi, fi * 128:(fi + 1) * 128],
                                        rhs=hT[:, kc, 2 + c0:2 + c0 + w], start=(kc == 0), stop=(kc == KC - 1)),
                                        reads=[("f_wgu", slot, gi, 0), ("f_wgu", slot, gi, 4), ("hT", "all")], writes=[("ps", pi)])
                            ss = self.rr % 2
                            self.rr += 1
                            S.op("act", lambda e, pg=pg, ss=ss, w=w: e.activation(out=sg[:, ss, 0:w], in_=self.ps[pg][:, 0:w], func=AF.Silu),
                                 reads=[("ps", pg)], writes=[("f_sg", ss)])
                            S.op("dve", lambda e, pu=pu, ss=ss, f=f, c0=c0, w=w: e.tensor_tensor(
                                out=act[:, f, c0 - base:c0 - base + w], in0=sg[:, ss, 0:w], in1=self.ps[pu][:, 0:w], op=ALU.mult),
                                reads=[("ps", pu), ("f_sg", ss)], writes=[("f_act", f)])
                for dc in range(KC):
                    slot = dc % 2
                    for f0 in range(0, FC, 8):
                        nf = min(8, FC - f0)
                        src = wd_d[li, f0 * 128:(f0 + nf) * 128, dc * 128:(dc + 1) * 128].rearrange("(k p) c -> p k c", p=128)
                        self.load_w(wd[:, slot, f0:f0 + nf, :], src, [128, nf, 128], ("f_wd", slot, f0))
                    for (c0, w, kind) in sb_:
                        pi = self.nextps()
                        for f in range(FC):
                            S.op("pe", lambda e, pi=pi, slot=slot, f=f, c0=c0, w=w: e.matmul(
                                self.ps[pi][:, 0:w], lhsT=wd[:, slot, f, :],
                                rhs=act[:, f, c0 - base:c0 - base + w], start=(f == 0), stop=(f == FC - 1)),
                                reads=[("f_wd", slot, (f // 8) * 8), ("f_act", f)], writes=[("ps", pi)])
                        S.op("dve", lambda e, pi=pi, dc=dc, c0=c0, w=w: e.tensor_tensor(
                            out=xres[:, dc, c0:c0 + w], in0=xres[:, dc, c0:c0 + w], in1=self.ps[pi][:, 0:w], op=ALU.add),
                            reads=[("ps", pi), ("xres", dc, c0)], writes=[("xres", dc, c0)])

    def final_norm(self, y_p, y_s):
        nc, S, TP = self.nc, self.S, self.TP
        xres = self.xres
        gain = self.nfin
        with self.T("fn_y", [128, KC, 512], F32) as yT, self.T("fn_o", [128, 2, D], F32) as yo, \
                self.T("fn_sq", [128, 2, 512], BF16) as sq, self.T("fn_r", [128, 512], F32) as rr:
            cnt = 0
            for bi, (c0, w, kind) in enumerate(self.tbs):
                pi = self.nextps()
                for kc in range(KC):
                    s = kc % 2
                    S.op("act", lambda e, s=s, kc=kc, c0=c0, w=w: e.activation(out=sq[:, s, 0:w], in_=xres[:, kc, c0:c0 + w], func=AF.Square),
                         reads=[("xres", "all")], writes=[("fn_sq", s)])
                    S.op("pe", lambda e, s=s, kc=kc, pi=pi, w=w: e.matmul(self.ps[pi][:, 0:w], lhsT=self.onesb, rhs=sq[:, s, 0:w], start=(kc == 0), stop=(kc == KC - 1)),
                         reads=[("fn_sq", s), "consts2"], writes=[("ps", pi)])
                r = rr[:, 0:w]
                S.op("act", lambda e, pi=pi, r=r, w=w: e.activation(out=r, in_=self.ps[pi][:, 0:w], func=AF.Ln, scale=1.0 / D, bias=EPS),
                     reads=[("ps", pi)], writes=["fn_r"])
                S.op("act", lambda e, r=r: e.activation(out=r, in_=r, func=AF.Exp, scale=-0.5), reads=["fn_r"], writes=["fn_r"])
                for kc in range(KC):
                    S.op("dve", lambda e, kc=kc, c0=c0, w=w, r=r: e.scalar_tensor_tensor(
                        out=yT[:, kc, 0:w], in0=xres[:, kc, c0:c0 + w], scalar=gain[:, kc:kc + 1], in1=r, op0=ALU.mult, op1=ALU.mult),
                        reads=[("xres", "all"), "fn_r", "consts"], writes=[("fn_y", kc)])
                for j in range((w + 127) // 128):
                    rows = min(128, w - j * 128)
                    os_ = cnt % 2
                    cnt += 1
                    for half in range(2):
                        pi = self.nextps()
                        for q in range(4):
                            kc = half * 4 + q
                            S.op("pe", lambda e, pi=pi, q=q, kc=kc, j=j, rows=rows: e.transpose(
                                self.ps[pi][0:rows, q * 128:(q + 1) * 128], yT[:, kc, j * 128:j * 128 + rows], self.ident),
                                reads=[("fn_y", kc), "consts"], writes=[("ps", pi)])
                        self.copy(self.ew(), yo[0:rows, os_, half * 512:(half + 1) * 512], self.ps[pi][0:rows, :], [("ps", pi)], [("fn_o", os_, half)])
                    if kind == "p":
                        dst = y_p[c0 + j * 128:c0 + j * 128 + rows, :]
                    else:
                        dst = y_s
                    S.op("sp", lambda e, dst=dst, os_=os_, rows=rows: e.dma_start(out=dst, in_=yo[0:rows, os_, :]),
                         reads=[("fn_o", os_, 0), ("fn_o", os_, 1)], writes=[], dma="yout%d" % os_)

    def hgrn2(self, j):
        nc, S, TP = self.nc, self.S, self.TP
        xres, hT = self.xres, self.hT
        w_in, w_out = self.dram["hg_w_in"], self.dram["hg_w_out"]
        st_hg, hg_p, hg_s = self.dram["st_hg"], self.dram["hg_p"], self.dram["hg_s"]
        ps = self.ps
        W = 512
        with ExitStack() as es:
            win = es.enter_context(self.T("h_win", [128, 2, KC, 4, 128], BF16))
            wout = es.enter_context(self.T("h_wout", [128, 2, D], BF16))
            sig = es.enter_context(self.T("h_sig", [128, W]))
            lf = es.enter_context(self.T("h_lf", [128, W]))
            kk = es.enter_context(self.T("h_kk", [128, W]))
            q = es.enter_context(self.T("h_q", [128, W]))
            gate = es.enter_context(self.T("h_gate", [128, W]))
            g = es.enter_context(self.T("h_g", [128, W]))
            tmp = es.enter_context(self.T("h_tmp", [128, W]))
            tmp2 = es.enter_context(self.T("h_tmp2", [128, W]))
            ee = es.enter_context(self.T("h_e", [128, 4, W]))
            qg = es.enter_context(self.T("h_qg", [128, W], BF16))
            kg = es.enter_context(self.T("h_kg", [128, W], BF16))
            qG = es.enter_context(self.T("h_qG", [128, W], BF16))
            kdec = es.enter_context(self.T("h_kdec", [128, W], BF16))
            vtok = es.enter_context(self.T("h_vtok", [128, 4, 128], BF16))
            vT = es.enter_context(self.T("h_vT", [128, W], BF16))
            kdtok = es.enter_context(self.T("h_kdtok", [128, 4, 128], BF16))
            osb = es.enter_context(self.T("h_osb", [128, W]))
            osq = es.enter_context(self.T("h_osq", [128, W], BF16))
            rstd = es.enter_context(self.T("h_rstd", [128, W]))
            og = es.enter_context(self.T("h_og", [128, W], BF16))
            Sr = es.enter_context(self.T("h_Sr", [128, 9, 128]))
            qGf = es.enter_context(self.T("h_qGf", [128, W]))
            attm = es.enter_context(self.T("h_attm", [128, 4, 128], BF16))
            egl = es.enter_context(self.T("h_egl", [128, 16]))
            S0 = es.enter_context(self.T("h_S0", [128, NB, 128]))
            S0bf = es.enter_context(self.T("h_S0bf", [128, NB, 128], BF16))
            Vblk = es.enter_context(self.T("h_Vblk", [64, NB, 128], BF16))
            for h in range(8):
                slot = h % 2
                for p in range(4):
                    for kk0 in (0, 4):
                        src = w_in[j, kk0 * 128:(kk0 + 4) * 128, p * D + h * 128:p * D + (h + 1) * 128].rearrange("(k p) c -> p k c", p=128)
                        self.load_w(win[:, slot, kk0:kk0 + 4, p, :], src, [128, 4, 128], ("h_win", slot, p, kk0))
                self.load_w(wout[:, slot, :], w_out[j, h * 128:(h + 1) * 128, :], [128, D], ("h_wout", slot))
                wkeys = lambda p: [("h_win", slot, p, 0), ("h_win", slot, p, 4)]
                S.op("pool", lambda e: e.memset(Sr[:, 0, :], 0.0), writes=[("h_Sr", 0)])
                lbh, omlh, nomlh = self.hg_lb[:, j, h:h + 1], self.hg_oml[:, j, h:h + 1], self.hg_noml[:, j, h:h + 1]
                for (c0, w, kind) in self.tbs:
                    smp = kind == "s"
                    hc = 2 + c0
                    pq, pf, pg, pv = self.nextps(), self.nextps(), self.nextps(), self.nextps()
                    for p, pi in ((0, pq), (1, pf), (3, pg)):
                        for kc in range(KC):
                            S.op("pe", lambda e, pi=pi, p=p, kc=kc, hc=hc, w=w: e.matmul(ps[pi][:, 0:w], lhsT=win[:, slot, kc, p, :], rhs=hT[:, kc, hc:hc + w],
                                 start=(kc == 0), stop=(kc == KC - 1)), reads=wkeys(p) + [("hT", "all")], writes=[("ps", pi)])
                    ntile = (w + 127) // 128
                    rows = min(128, w)
                    for kc in range(KC):
                        S.op("pe", lambda e, kc=kc, hc=hc, w=w: e.matmul(ps[pv][:, 0:w], lhsT=win[:, slot, kc, 2, :], rhs=hT[:, kc, hc:hc + w],
                             start=(kc == 0), stop=(kc == KC - 1)), reads=wkeys(2) + [("hT", "all")], writes=[("ps", pv)])
                    self.copy("act", vT[:, 0:w], ps[pv][:, 0:w], [("ps", pv)], ["h_vT"])
                    pvt = self.nextps()
                    pvtb = ps[pvt].bitcast(BF16)
                    for jt in range(ntile):
                        S.op("pe", lambda e, jt=jt, rows=rows: e.transpose(pvtb[0:rows, jt * 128:(jt + 1) * 128], vT[:, jt * 128:jt * 128 + rows], self.identb),
                             reads=["h_vT", "consts2"], writes=[("ps", pvt)])
                    self.copy("dve", vtok[0:rows, 0:ntile, :], pvtb[:, 0:512].rearrange("p (a b) -> p a b", a=4)[0:rows, 0:ntile, :], [("ps", pvt)], ["h_vtok"])
                    S.op("act", lambda e, w=w: e.activation(out=sig[:, 0:w], in_=ps[pf][:, 0:w], func=AF.Sigmoid), reads=[("ps", pf)], writes=["h_sig"])
                    S.op("act", lambda e, w=w: e.activation(out=q[:, 0:w], in_=ps[pq][:, 0:w], func=AF.Silu), reads=[("ps", pq)], writes=["h_q"])
                    S.op("act", lambda e, w=w: e.activation(out=gate[:, 0:w], in_=ps[pg][:, 0:w], func=AF.Silu), reads=[("ps", pg)], writes=["h_gate"])
                    S.op("dve", lambda e, w=w: e.tensor_scalar(out=lf[:, 0:w], in0=sig[:, 0:w], scalar1=omlh, scalar2=lbh, op0=ALU.mult, op1=ALU.add), reads=["h_sig"], writes=["h_lf"])
                    S.op("act", lambda e, w=w: e.activation(out=lf[:, 0:w], in_=lf[:, 0:w], func=AF.Ln), reads=["h_lf"], writes=["h_lf"])
                    S.op("dve", lambda e, w=w: e.tensor_scalar(out=kk[:, 0:w], in0=sig[:, 0:w], scalar1=nomlh, scalar2=omlh, op0=ALU.mult, op1=ALU.add), reads=["h_sig"], writes=["h_kk"])
                    if not smp:
                        nch = w // 64
                        for c in range(nch):
                            S.op("dve", lambda e, c=c: e.tensor_tensor_scan(out=g[:, c * 64:(c + 1) * 64], data0=self.onesf[:, 0:64], data1=lf[:, c * 64:(c + 1) * 64],
                                 initial=0.0, op0=ALU.mult, op1=ALU.add), reads=["h_lf", "consts2"], writes=["h_g"])
                        g3 = g.rearrange("p (c t) -> p c t", t=64)
                        bc = lambda col: g3[:, :, col:col + 1].broadcast_to([128, nch, 64])
                        v3 = lambda t_: t_.rearrange("p (c t) -> p c t", t=64)
                        S.op("dve", lambda e: e.tensor_tensor(out=v3(tmp), in0=g3, in1=bc(31), op=ALU.subtract), reads=["h_g"], writes=["h_tmp"])
                        S.op("dve", lambda e: e.tensor_tensor(out=v3(tmp2), in0=bc(63), in1=g3, op=ALU.subtract), reads=["h_g"], writes=["h_tmp2"])
                        S.op("act", lambda e: e.activation(out=ee[:, 0, :], in_=tmp, func=AF.Exp), reads=["h_tmp"], writes=[("h_e", 0)])
                        S.op("act", lambda e: e.activation(out=ee[:, 1, :], in_=tmp, func=AF.Exp, scale=-1.0), reads=["h_tmp"], writes=[("h_e", 1)])
                        S.op("act", lambda e: e.activation(out=ee[:, 2, :], in_=g, func=AF.Exp), reads=["h_g"], writes=[("h_e", 2)])
                        S.op("act", lambda e: e.activation(out=ee[:, 3, :], in_=tmp2, func=AF.Exp), reads=["h_tmp2"], writes=[("h_e", 3)])
                        S.op("act", lambda e: e.activation(out=egl[:, 0:nch], in_=g3[:, :, 63], func=AF.Exp), reads=["h_g"], writes=["h_egl"])
                        S.op("pool", lambda e: e.tensor_tensor(out=qg, in0=q, in1=ee[:, 0, :], op=ALU.mult), reads=["h_q", ("h_e", 0)], writes=["h_qg"])
                        S.op("pool", lambda e: e.tensor_tensor(out=kg, in0=kk, in1=ee[:, 1, :], op=ALU.mult), reads=["h_kk", ("h_e", 1)], writes=["h_kg"])
                        S.op("dve", lambda e: e.tensor_tensor(out=qGf, in0=q, in1=ee[:, 2, :], op=ALU.mult), reads=["h_q", ("h_e", 2)], writes=["h_qGf"])
                        S.op("pool", lambda e: e.tensor_tensor(out=kdec, in0=kk, in1=ee[:, 3, :], op=ALU.mult), reads=["h_kk", ("h_e", 3)], writes=["h_kdec"])
                    else:
                        g3 = g[:, 0:NS].rearrange("p (b t) -> p b t", t=TS)
                        lf3 = lf[:, 0:NS].rearrange("p (b t) -> p b t", t=TS)
                        S.op("dve", lambda e: e.tensor_copy(out=g3[:, :, 0:1], in_=lf3[:, :, 0:1]), reads=["h_lf"], writes=["h_g"])
                        for t_ in range(1, TS):
                            S.op("dve", lambda e, t_=t_: e.tensor_tensor(out=g3[:, :, t_:t_ + 1], in0=g3[:, :, t_ - 1:t_], in1=lf3[:, :, t_:t_ + 1], op=ALU.add), reads=["h_lf", "h_g"], writes=["h_g"])
                        t23 = tmp2[:, 0:NS].rearrange("p (b t) -> p b t", t=TS)
                        S.op("dve", lambda e: e.tensor_tensor(out=t23, in0=g3[:, :, 3:4].broadcast_to([128, NB, TS]), in1=g3, op=ALU.subtract), reads=["h_g"], writes=["h_tmp2"])
                        S.op("act", lambda e: e.activation(out=ee[:, 1, 0:NS], in_=g[:, 0:NS], func=AF.Exp, scale=-1.0), reads=["h_g"], writes=[("h_e", 1)])
                        S.op("act", lambda e: e.activation(out=ee[:, 2, 0:NS], in_=g[:, 0:NS], func=AF.Exp), reads=["h_g"], writes=[("h_e", 2)])
                        S.op("act", lambda e: e.activation(out=ee[:, 3, 0:NS], in_=tmp2[:, 0:NS], func=AF.Exp), reads=["h_tmp2"], writes=[("h_e", 3)])
                        S.op("act", lambda e: e.activation(out=egl[:, 0:NB], in_=g3[:, :, 3], func=AF.Exp), reads=["h_g"], writes=["h_egl"])
                        S.op("pool", lambda e: e.tensor_tensor(out=kg[:, 0:NS], in0=kk[:, 0:NS], in1=ee[:, 1, 0:NS], op=ALU.mult), reads=["h_kk", ("h_e", 1)], writes=["h_kg"])
                        S.op("dve", lambda e: e.tensor_tensor(out=qG[:, 0:NS], in0=q[:, 0:NS], in1=ee[:, 2, 0:NS], op=ALU.mult), reads=["h_q", ("h_e", 2)], writes=["h_qG"])
                        S.op("pool", lambda e: e.tensor_tensor(out=kdec[:, 0:NS], in0=kk[:, 0:NS], in1=ee[:, 3, 0:NS], op=ALU.mult), reads=["h_kk", ("h_e", 3)], writes=["h_kdec"])
                    pt = self.nextps()
                    ptb = ps[pt].bitcast(BF16)
                    for jt in range(ntile):
                        S.op("pe", lambda e, jt=jt, rows=rows: e.transpose(ptb[0:rows, jt * 128:(jt + 1) * 128], kdec[:, jt * 128:jt * 128 + rows], self.identb),
                             reads=["h_kdec", "consts2"], writes=[("ps", pt)])
                    self.copy("dve", kdtok[0:rows, 0:ntile, :], ptb[:, 0:512].rearrange("p (a b) -> p a b", a=4)[0:rows, 0:ntile, :], [("ps", pt)], ["h_kdtok"])
                    po = self.nextlong()
                    if not smp:
                        nchk = w // 64
                        if c0 > 0:
                            S.op("dve", lambda e: e.tensor_copy(out=Sr[:, 0, :], in_=Sr[:, 8, :]), reads=[("h_Sr", c_) for c_ in range(9)], writes=[("h_Sr", 0)])
                        pa = self.nextps()
                        for jt in range(ntile):
                            S.op("pe", lambda e, jt=jt: e.matmul(ps[pa][:, jt * 128:(jt + 1) * 128], lhsT=kg[:, jt * 128:(jt + 1) * 128], rhs=qg[:, jt * 128:(jt + 1) * 128], start=True, stop=True),
                                 reads=["h_kg", "h_qg"], writes=[("ps", pa)])
                        S.op("dve", lambda e: e.tensor_tensor(out=attm[:, 0:ntile, :], in0=ps[pa][:, 0:ntile * 128].rearrange("p (a b) -> p a b", a=ntile),
                             in1=self.maskU2.unsqueeze(1).broadcast_to([128, ntile, 128]), op=ALU.mult), reads=[("ps", pa), "consts"], writes=["h_attm"])
                        pus = [self.nextps(), self.nextps()]
                        for c in range(nchk):
                            jt, cc = c // 2, c % 2
                            S.op("pe", lambda e, c=c, cc=cc, jt=jt: e.matmul(ps[pus[c % 2]][:, (c // 2) * 128:(c // 2 + 1) * 128], lhsT=kdtok[cc * 64:(cc + 1) * 64, jt, :], rhs=vtok[cc * 64:(cc + 1) * 64, jt, :], start=True, stop=True),
                                 reads=["h_kdtok", "h_vtok"], writes=[("ps", pus[c % 2])])
                        for c in range(nchk):
                            S.op("dve", lambda e, c=c: e.scalar_tensor_tensor(out=Sr[:, c + 1, :], in0=Sr[:, c, :], scalar=egl[:, c:c + 1], in1=ps[pus[c % 2]][:, (c // 2) * 128:(c // 2 + 1) * 128], op0=ALU.mult, op1=ALU.add),
                                 reads=[("ps", pus[c % 2]), ("h_Sr", c), "h_egl"], writes=[("h_Sr", c + 1)])
                        for jt in range(ntile):
                            S.op("pe", lambda e, jt=jt: e.matmul(ps[po][:, jt * 128:(jt + 1) * 128], lhsT=vtok[:, jt, :], rhs=attm[:, jt, :], start=True, stop=False),
                                 reads=["h_vtok", "h_attm"], writes=[("ps", po)])
                            for cc in range(2):
                                c = jt * 2 + cc
                                S.op("pe", lambda e, c=c, cc=cc: e.matmul(ps[po][:, c * 64:(c + 1) * 64], lhsT=Sr[:, c, :], rhs=qGf[:, c * 64:(c + 1) * 64], start=False, stop=(cc == 1)),
                                     reads=[("h_Sr", c), "h_qGf"], writes=[("ps", po)])
                        if c0 + w == TP:
                            S.op("sp", lambda e: e.dma_start(out=hg_p[j, h], in_=Sr[:, 8, :]), reads=[("h_Sr", 8)], dma="hg_p")
                    else:
                        S.op("sp", lambda e: e.dma_start(out=S0, in_=st_hg[j, :, h, :, :].rearrange("b k v -> k b v")), writes=["h_S0"], dma="h_S0")
                        S.op("pool", lambda e: e.tensor_copy(out=S0bf, in_=S0), reads=["h_S0"], writes=["h_S0bf"])
                        pa = self.nextps()
                        S.op("pe", lambda e, pa=pa: e.matmul(ps[pa][0:NS, 0:NS], lhsT=kg[:, 0:NS], rhs=qG[:, 0:NS], start=True, stop=True), reads=["h_kg", "h_qG"], writes=[("ps", pa)])
                        S.op("dve", lambda e, pa=pa: e.tensor_tensor(out=attm[0:NS, 0, 0:NS], in0=ps[pa][0:NS, 0:NS], in1=self.maskS, op=ALU.mult), reads=[("ps", pa), "consts"], writes=[("h_attm", 0)])
                        S.op("pe", lambda e: e.matmul(ps[po][:, 0:NS], lhsT=vtok[0:NS, 0, :], rhs=attm[0:NS, 0, 0:NS], start=True, stop=False), reads=["h_vtok", ("h_attm", 0)], writes=[("ps", po)])
                        for b in range(NB):
                            S.op("pe", lambda e, b=b: e.matmul(ps[po][:, b * TS:(b + 1) * TS], lhsT=S0bf[:, b, :], rhs=qG[:, b * TS:(b + 1) * TS], start=False, stop=(b == NB - 1)),
                                 reads=["h_S0bf", "h_qG"], writes=[("ps", po)])
                        S.op("dve", lambda e: e.tensor_tensor(out=Vblk, in0=vtok[0:NS, 0, :].unsqueeze(1).broadcast_to([NS, NB, 128]),
                             in1=self.maskB.unsqueeze(2).broadcast_to([NS, NB, 128]), op=ALU.mult), reads=["h_vtok", "consts"], writes=["h_Vblk"])
                        S.op("dve", lambda e: e.tensor_tensor(out=S0, in0=S0, in1=egl[:, 0:NB].unsqueeze(2).broadcast_to([128, NB, 128]), op=ALU.mult), reads=["h_S0", "h_egl", "h_S0bf"], writes=["h_S0"])
                        for bq in range(4):
                            pu = self.nextps()
                            S.op("pe", lambda e, pu=pu, bq=bq: e.matmul(ps[pu][:, 0:512], lhsT=kdtok[0:NS, 0, :], rhs=Vblk[:, bq * 4:(bq + 1) * 4, :], start=True, stop=True),
                                 reads=["h_kdtok", "h_Vblk"], writes=[("ps", pu)])
                            S.op("dve", lambda e, pu=pu, bq=bq: e.tensor_tensor(out=S0[:, bq * 4:(bq + 1) * 4, :], in0=S0[:, bq * 4:(bq + 1) * 4, :],
                                 in1=ps[pu].rearrange("p (a b) -> p a b", a=4), op=ALU.add), reads=[("ps", pu), "h_S0"], writes=["h_S0"])
                        S.op("sp", lambda e: e.dma_start(out=hg_s[j, :, h, :, :].rearrange("b k v -> k b v"), in_=S0), reads=["h_S0"], dma="hg_s")
                    S.op("act", lambda e, w=w: e.activation(out=osb[:, 0:w], in_=ps[po][:, 0:w], func=AF.Copy), reads=[("ps", po)], writes=["h_osb"])
                    S.op("act", lambda e, w=w: e.activation(out=osq[:, 0:w], in_=ps[po][:, 0:w], func=AF.Square), reads=[("ps", po)], writes=["h_osq"])
                    pn = self.nextps()
                    S.op("pe", lambda e, pn=pn, w=w: e.matmul(ps[pn][:, 0:w], lhsT=self.onesb, rhs=osq[:, 0:w], start=True, stop=True), reads=["h_osq", "consts2"], writes=[("ps", pn)])
                    S.op("act", lambda e, pn=pn, w=w: e.activation(out=rstd[:, 0:w], in_=ps[pn][:, 0:w], func=AF.Ln, scale=1.0 / 128, bias=EPS), reads=[("ps", pn)], writes=["h_rstd"])
                    S.op("act", lambda e, w=w: e.activation(out=rstd[:, 0:w], in_=rstd[:, 0:w], func=AF.Exp, scale=-0.5), reads=["h_rstd"], writes=["h_rstd"])
                    S.op("dve", lambda e, w=w: e.tensor_tensor(out=osb[:, 0:w], in0=osb[:, 0:w], in1=rstd[:, 0:w], op=ALU.mult), reads=["h_osb", "h_rstd"], writes=["h_osb"])
                    S.op("dve", lambda e, w=w: e.scalar_tensor_tensor(out=og[:, 0:w], in0=osb[:, 0:w], scalar=self.hgn[:, j:j + 1], in1=gate[:, 0:w], op0=ALU.mult, op1=ALU.mult),
                         reads=["h_osb", "h_gate", "consts"], writes=["h_og"])
                    for dc in range(KC):
                        pi = self.nextps()
                        S.op("pe", lambda e, pi=pi, dc=dc, w=w: e.matmul(ps[pi][:, 0:w], lhsT=wout[:, slot, dc * 128:(dc + 1) * 128], rhs=og[:, 0:w], start=True, stop=True),
                             reads=[("h_wout", slot), "h_og"], writes=[("ps", pi)])
                        S.op("dve", lambda e, pi=pi, dc=dc, c0=c0, w=w: e.tensor_tensor(out=xres[:, dc, c0:c0 + w], in0=xres[:, dc, c0:c0 + w], in1=ps[pi][:, 0:w], op=ALU.add),
                             reads=[("ps", pi), ("xres", dc, c0)], writes=[("xres", dc, c0)])

    def rwkv7(self, j):
        nc, S, TP = self.nc, self.S, self.TP
        xres, hT, ps = self.xres, self.hT, self.ps
        dr = self.dram
        V = self.rw_vec
        GN_EPS = 64e-5
        li = self.cur_li
        with ExitStack() as es0:
            T0 = lambda n, sh, dt=F32: es0.enter_context(self.T(n, sh, dt))
            nblk = len(self.tbs)
            edge = T0("r_edge", [128, KC, 2 * nblk + 2], BF16)
            shiftT = T0("r_shiftT", [128, KC, NB], BF16)
            l1T = T0("r_l1T", [128, 3, self.N], BF16)
            prevS = T0("r_prevS", [128, KC, NS], BF16)
            prevB = T0("r_prevB", [128, KC, 512], BF16)

            def fill_prev(c0, w, kind):
                if kind == "p":
                    S.op("dve", lambda e: e.tensor_copy(out=prevB[:, :, 0:w], in_=hT[:, :, 1 + c0:1 + c0 + w]), reads=[("hT", "all"), "hT0"], writes=["r_prevB"])
            with ExitStack() as es:
                T = lambda n, sh, dt=F32: es.enter_context(self.T(n, sh, dt))
                shin, xsh, sq, rr = T("r_shin", [32, D]), T("r_xsh", [128, KC, 32]), T("r_sq", [128, KC, 32]), T("r_rr", [128, 32])
                shrow = T("r_shrow", [32, D])
                S.op("pool", lambda e: e.memset(shin, 0.0), writes=["r_shin"])
                S.op("pool", lambda e: e.memset(xsh, 0.0), writes=["r_xsh"])
                S.op("pool", lambda e: e.memset(edge, 0.0), writes=["r_edge"])
                S.op("sp", lambda e: e.dma_start(out=shin[0:NB, :], in_=dr["st_shift"]), reads=[], writes=["r_shin"], dma="r_shin")
                for half in range(2):
                    pi = self.nextps()
                    for q in range(4):
                        kc = half * 4 + q
                        S.op("pe", lambda e, q=q, kc=kc, pi=pi: e.transpose(ps[pi][:, q * 32:(q + 1) * 32], shin[:, kc * 128:(kc + 1) * 128], self.ident[0:32, 0:32]), reads=["r_shin", "consts"], writes=[("ps", pi)])
                    self.copy("dve", shiftT[:, half * 4:(half + 1) * 4, :], ps[pi][:, 0:128].rearrange("p (q t) -> p q t", q=4)[:, :, 0:NB], [("ps", pi)], ["r_shiftT"])
                S.op("dve", lambda e: e.tensor_copy(out=xsh[:, :, 0:1], in_=xres[:, :, TP - 1:TP]), reads=[("xres", "all"), "r_xsh"], writes=["r_xsh"])
                S.op("dve", lambda e: e.tensor_copy(out=xsh[:, :, 1:1 + NB], in_=xres[:, :, TP:TP + NS].rearrange("p k (b t) -> p k b t", t=TS)[:, :, :, 3]), reads=[("xres", "all"), "r_xsh"], writes=["r_xsh"])
                S.op("act", lambda e: e.activation(out=sq, in_=xsh, func=AF.Square), reads=["r_xsh"], writes=["r_sq"])
                pi = self.nextps()
                for kc in range(KC):
                    S.op("pe", lambda e, kc=kc: e.matmul(ps[pi][:, 0:32], lhsT=self.onesf, rhs=sq[:, kc, :], start=(kc == 0), stop=(kc == KC - 1)), reads=["r_sq", "consts2"], writes=[("ps", pi)])
                S.op("act", lambda e: e.activation(out=rr, in_=ps[pi][:, 0:32], func=AF.Ln, scale=1.0 / D, bias=EPS), reads=[("ps", pi)], writes=["r_rr"])
                S.op("act", lambda e: e.activation(out=rr, in_=rr, func=AF.Exp, scale=-0.5), reads=["r_rr"], writes=["r_rr"])
                for kc in range(KC):
                    S.op("dve", lambda e, kc=kc: e.scalar_tensor_tensor(out=xsh[:, kc, :], in0=xsh[:, kc, :], scalar=self.nmix[:, li, kc:kc + 1], in1=rr, op0=ALU.mult, op1=ALU.mult), reads=["r_xsh", "r_rr", "consts"], writes=["r_xsh"])
                for half in range(2):
                    pi = self.nextps()
                    for q in range(4):
                        kc = half * 4 + q
                        S.op("pe", lambda e, q=q, kc=kc, pi=pi: e.transpose(ps[pi][0:32, q * 128:(q + 1) * 128], xsh[:, kc, :], self.ident), reads=["r_xsh", "consts"], writes=[("ps", pi)])
                    self.copy("act", shrow[:, half * 512:(half + 1) * 512], ps[pi][0:32, :], [("ps", pi)], [("r_shrow", half)])
                S.op("sp", lambda e: e.dma_start(out=dr["sh_p"], in_=shrow[0:1, :]), reads=[("r_shrow", 0), ("r_shrow", 1)], dma="r_sh")
                S.op("sp", lambda e: e.dma_start(out=dr["sh_s"], in_=shrow[1:1 + NB, :]), reads=[("r_shrow", 0), ("r_shrow", 1)], dma="r_sh")
                for bi, (c0, w, kind) in enumerate(self.tbs):
                    if kind == "p" and c0 > 0:
                        S.op("dve", lambda e, bi=bi, c0=c0: e.tensor_copy(out=edge[:, :, 2 * bi:2 * bi + 1], in_=hT[:, :, 1 + c0:2 + c0]), reads=[("hT", "all"), "r_edge"], writes=["r_edge"])
                hs3 = hT[:, :, 2 + TP:2 + TP + NS].rearrange("p k (b t) -> p k b t", t=TS)
                pv3 = prevS.rearrange("p k (b t) -> p k b t", t=TS)
                for kc in range(KC):
                    S.op("dve", lambda e, kc=kc: e.tensor_copy(out=pv3[:, kc, :, 1:TS], in_=hs3[:, kc, :, 0:TS - 1]), reads=[("hT", "all")], writes=["r_prevS"])
                    S.op("dve", lambda e, kc=kc: e.tensor_copy(out=pv3[:, kc, :, 0], in_=shiftT[:, kc, :]), reads=["r_shiftT", "r_prevS"], writes=["r_prevS"])
            S.barrier()

            def shifted_proj(pi, w_a, w_b, c0, w, kind, rd, M=128):
                hc = 2 + c0
                out = ps[pi][0:M, 0:w]
                for kc in range(KC):
                    S.op("pe", lambda e, kc=kc: e.matmul(out, lhsT=w_a(kc), rhs=hT[:, kc, hc:hc + w], start=(kc == 0), stop=False), reads=rd + [("hT", "all")], writes=[("ps", pi)])
                if kind == "p":
                    for kc in range(KC):
                        S.op("pe", lambda e, kc=kc: e.matmul(out, lhsT=w_b(kc), rhs=prevB[:, kc, 0:w], start=False, stop=(kc == KC - 1)), reads=rd + ["r_prevB"], writes=[("ps", pi)])
                else:
                    for kc in range(KC):
                        S.op("pe", lambda e, kc=kc: e.matmul(out, lhsT=w_b(kc), rhs=prevS[:, kc, :], start=False, stop=(kc == KC - 1)), reads=rd + ["r_prevS"], writes=[("ps", pi)])

            nq = TP // 256
            self.r_e2 = T0("r_e2", [128, KC, 2 * nq], BF16)
            e2 = self.r_e2
            S.op("pool", lambda e: e.memset(e2, 0.0), writes=["r_e2"])
            for q in range(nq):
                c0 = q * 256
                if c0 > 0:
                    S.op("dve", lambda e, q=q, c0=c0: e.tensor_copy(out=e2[:, :, 2 * q:2 * q + 1], in_=hT[:, :, 1 + c0:2 + c0]), reads=[("hT", "all"), "r_e2"], writes=["r_e2"])
            S.barrier()
            rblocks = [(q * 256, 256, "p") for q in range(nq)] + [(TP, NS, "s")]

            def load_scaled(dst_a, dst_b, src, shape, n, kk0, nk, key):
                slot = self.stg_i % self.NSTG
                self.stg_i += 1
                cols = shape[2]
                st = self.stage[slot][:, 0:nk * cols].rearrange("p (a b) -> p a b", a=nk)
                S.op("sp", lambda e: e.dma_start(out=st, in_=src), writes=[("stg", slot)], dma="stg%d" % slot)
                for q in range(nk):
                    kc = kk0 + q
                    S.op("dve", lambda e, q=q, kc=kc: e.tensor_scalar(out=dst_a(kc), in0=st[:, q, :], scalar1=self.rw_omu[:, n, kc:kc + 1], scalar2=1.0, op0=ALU.mult, op1=ALU.mult), reads=[("stg", slot), "rwc0"], writes=[key])
                    S.op("dve", lambda e, q=q, kc=kc: e.tensor_scalar(out=dst_b(kc), in0=st[:, q, :], scalar1=self.rw_mu[:, n, kc:kc + 1], scalar2=1.0, op0=ALU.mult, op1=ALU.mult), reads=[("stg", slot), "consts"], writes=[key])

            with ExitStack() as es:
                T = lambda n, sh, dt=F32: es.enter_context(self.T(n, sh, dt))
                wl = T("r_wl", [128, 3, 2, KC, 128], BF16)
                for li_, (nm, n, cols) in enumerate((("rw_w1", 3, 64), ("rw_a1", 4, 64), ("rw_g1", 5, 128))):
                    src = dr[nm].rearrange("(k p) c -> p k c", p=128)
                    load_scaled(lambda kc, li_=li_, cols=cols: wl[:, li_, 0, kc, 0:cols], lambda kc, li_=li_, cols=cols: wl[:, li_, 1, kc, 0:cols], src, [128, KC, cols], n, 0, KC, ("r_wl", li_))
                for (c0, w, kind) in rblocks:
                    fill_prev(c0, w, kind)
                    for li_, (M, fn) in enumerate(((64, AF.Tanh), (64, AF.Copy), (128, AF.Sigmoid))):
                        pi = self.nextps()
                        shifted_proj(pi, lambda kc, li_=li_, M=M: wl[:, li_, 0, kc, 0:M], lambda kc, li_=li_, M=M: wl[:, li_, 1, kc, 0:M], c0, w, kind, [("r_wl", li_)], M=M)
                        S.op("act", lambda e, li_=li_, M=M, fn=fn, pi=pi: e.activation(out=l1T[0:M, li_, c0:c0 + w], in_=ps[pi][0:M, 0:w], func=fn), reads=[("ps", pi)], writes=[("r_l1T", li_)])
            S.barrier()

            for sample_pass in ((True,) if os.environ.get('RWSKIP') else (False, True)):
                blocks = [b_ for b_ in rblocks if (b_[2] == "s") == sample_pass]
                Wd = NS if sample_pass else 256
                R = NS if sample_pass else 128
                nch = 1 if sample_pass else 2
                nlev = 1 if sample_pass else 6
                mStrict = self.sUS if sample_pass else self.sU
                mIncl = self.maskS if sample_pass else self.maskU
                mLow = self.sLS if sample_pass else self.sL
                with ExitStack() as es:
                    T = lambda n, sh, dt=F32: es.enter_context(self.T(n, sh, dt))
                    wrkv = T("r_wrkv", [128, 3, 2, KC, 128], BF16)
                    w2nd = T("r_w2nd", [128, 3, 128], BF16)
                    wout2 = [T("r_wout", [128, D], BF16) for _ in range(2)]
                    rfA, kfA, vfA, lwf, af = [T("r_f%d" % i_, [128, Wd]) for i_ in range(5)]
                    rkvA = [[rfA, kfA, vfA, None], [T("r_rf2", [128, Wd]), T("r_kf2", [128, Wd]), T("r_vf2", [128, Wd]), T("r_vb2", [128, Wd], BF16)]]
                    gf2 = [T("r_gf", [128, Wd]) for _ in range(2)]
                    bon2 = [T("r_bon", [128, Wd]) for _ in range(2)]
                    G, t1, e0, e1 = T("r_G", [128, Wd]), T("r_t1", [128, Wd]), T("r_e0", [128, Wd]), T("r_e1", [128, Wd])
                    kap, k2, beta, osb = T("r_kap", [128, Wd]), T("r_k2", [128, Wd]), T("r_beta", [128, Wd]), T("r_osb", [128, Wd])
                    KR2 = [T("r_KR", [128, nch, 2, 128], BF16) for _ in range(2)]
                    BK2 = [T("r_BK", [128, nch, 2, 128], BF16) for _ in range(2)]
                    btT, ktT, vbA, og = T("r_btT", [128, Wd], BF16), T("r_ktT", [128, Wd], BF16), T("r_vb", [128, Wd], BF16), T("r_og", [128, Wd], BF16)
                    rkvA[0][3] = vbA
                    bttok2 = [T("r_bttok", [128, nch, 128], BF16) for _ in range(2)]
                    kttok2 = [T("r_kttok", [128, nch, 128], BF16) for _ in range(2)]
                    vtok2 = [T("r_vtok", [128, nch, 128], BF16) for _ in range(2)]
                    Nk, Ak = T("r_Nk", [128, 2, 2 * nch, 128], BF16), T("r_Ak", [128, 2, 2 * nch, 128], BF16)
                    Wm = T("r_Wm", [128, 2 * nch, 128], BF16)
                    MbT, BTm, MkT = T("r_MbT", [128, 2 * nch, 128], BF16), T("r_BTm", [128, 2 * nch, 128], BF16), T("r_MkT", [128, 2 * nch, 128], BF16)
                    EC2 = [T("r_EC", [128, NB]) for _ in range(2)]
                    t2 = T("r_t2", [128, Wd])
                    Xsb, Ssb = T("r_Xsb", [128, 2, 64], BF16), T("r_Ssb", [128, 2, 64], BF16)
                    Sst, Sbf = T("r_Sst", [128, 64]), T("r_Sbf", [128, 64], BF16)
                    sT = T("r_sT", [64, 128])
                    if sample_pass:
                        s0nat = T("r_s0nat", [64, NB, 128])
                        S0, S0bf = T("r_S0", [128, NB, 64]), T("r_S0bf", [128, NB, 64], BF16)
                        kblk = T("r_kblk", [128, NB, NS], BF16)
                        rblk = T("r_rblk", [128, NB, NS], BF16)
                        Sblk, Vblk = T("r_Sblk", [128, 2, NB, 64], BF16), T("r_Vblk", [128, 2, NB, 64], BF16)
                        for t_, k_ in ((Ssb, ("r_Ssb", 0)), (MbT, "r_zMbT"), (MkT, "r_zMkT"), (BTm, "r_zBTm"), (vtok2[0], ("r_vtok", 0)), (vtok2[1], ("r_vtok", 1)), (bttok2[0], ("r_bttok", 0)), (bttok2[1], ("r_bttok", 1)), (kttok2[0], ("r_kttok", 0)), (kttok2[1], ("r_kttok", 1)), (Sblk, ("r_Sblk", 0)), (Vblk, ("r_Vblk", 0))):
                            S.op("pool", lambda e, t_=t_: e.memset(t_, 0.0), writes=[k_])
                        S.barrier()
                    def blk(pc, bi_, c0, w, kind, par):
                        cs = slice(pc * 128, (pc + 1) * 128)
                        KR, BK, bt_tok, kt_tok, v_tok, EC, gf, bon = KR2[par], BK2[par], bttok2[par], kttok2[par], vtok2[par], EC2[par], gf2[par], bon2[par]
                        wout = wout2[pc % 2]
                        if bi_ == 0:
                            for n in range(3):
                                for kk0 in (0, 4):
                                    src = dr["rw_w_rkv"][n, kk0 * 128:(kk0 + 4) * 128, cs].rearrange("(k p) c -> p k c", p=128)
                                    load_scaled(lambda kc, n=n: wrkv[:, n, 0, kc, :], lambda kc, n=n: wrkv[:, n, 1, kc, :], src, [128, 4, 128], n, kk0, 4, ("r_wrkv", n, kk0))
                            self.load_w(w2nd[0:64, 0, :], dr["rw_w2"][:, cs], [64, 128], ("r_w2nd", 0))
                            self.load_w(w2nd[0:64, 1, :], dr["rw_a2"][:, cs], [64, 128], ("r_w2nd", 1))
                            self.load_w(w2nd[:, 2, :], dr["rw_g2"][:, cs], [128, 128], ("r_w2nd", 2))
                            self.load_w(wout, dr["rw_w_out"][cs, :], [128, D], ("r_wout", pc % 2))
                        wk = lambda n: [("r_wrkv", n, 0), ("r_wrkv", n, 4)]
                        vec = lambda nm: V[nm][:, pc:pc + 1]
                        for _once in (0,):
                            half = bi_ % 2 if kind == "p" else 0
                            rf, kf, vf, vb = rkvA[half]
                            if kind == "s" or half == 0:
                                wp = w if kind == "s" else 2 * w
                                fill_prev(c0, wp, kind)
                                pr, pk, pv = self.nextps(), self.nextps(), self.nextps()
                                for n, pi in ((0, pr), (1, pk), (2, pv)):
                                    shifted_proj(pi, lambda kc, n=n: wrkv[:, n, 0, kc, :], lambda kc, n=n: wrkv[:, n, 1, kc, :], c0, wp, kind, wk(n))
                                for hh_ in range(wp // w):
                                    rf_, kf_, vf_, vb_ = rkvA[hh_]
                                    cs_ = slice(hh_ * w, (hh_ + 1) * w)
                                    self.copy("act", rf_[:, 0:w], ps[pr][:, cs_], [("ps", pr)], [("r_rf", hh_)])
                                    self.copy("act", kf_[:, 0:w], ps[pk][:, cs_], [("ps", pk)], [("r_kf", hh_)])
                                    self.copy("act", vf_[:, 0:w], ps[pv][:, cs_], [("ps", pv)], [("r_vf", hh_)])
                                    S.op("dve", lambda e, vb_=vb_, cs_=cs_: e.tensor_copy(out=vb_[:, 0:w], in_=ps[pv][:, cs_]), reads=[("ps", pv)], writes=[("r_vb", hh_)])
                            yield 0
                            pw, pa, pg = self.nextps(), self.nextps(), self.nextps()
                            S.op("pe", lambda e: e.matmul(ps[pw][:, 0:w], lhsT=w2nd[0:64, 0, :], rhs=l1T[0:64, 0, c0:c0 + w], start=True, stop=True), reads=[("r_w2nd", 0), ("r_l1T", 0)], writes=[("ps", pw)])
                            S.op("pe", lambda e: e.matmul(ps[pa][:, 0:w], lhsT=w2nd[0:64, 1, :], rhs=l1T[0:64, 1, c0:c0 + w], start=True, stop=True), reads=[("r_w2nd", 1), ("r_l1T", 1)], writes=[("ps", pa)])
                            S.op("pe", lambda e: e.matmul(ps[pg][:, 0:w], lhsT=w2nd[:, 2, :], rhs=l1T[:, 2, c0:c0 + w], start=True, stop=True), reads=[("r_w2nd", 2), ("r_l1T", 2)], writes=[("ps", pg)])
                            S.op("act", lambda e: e.activation(out=lwf[:, 0:w], in_=ps[pw][:, 0:w], func=AF.Sigmoid, bias=vec("rw_w0")), reads=[("ps", pw), "consts"], writes=["r_lwf"])
                            S.op("dve", lambda e: e.tensor_scalar(out=lwf[:, 0:w], in0=lwf[:, 0:w], scalar1=-0.6065306597126334, scalar2=1.0, op0=ALU.mult, op1=ALU.mult), reads=["r_lwf"], writes=["r_lwf"])
                            S.op("act", lambda e: e.activation(out=af[:, 0:w], in_=ps[pa][:, 0:w], func=AF.Sigmoid, bias=vec("rw_a0")), reads=[("ps", pa), "consts"], writes=["r_af"])
                            self.copy("act", gf[:, 0:w], ps[pg][:, 0:w], [("ps", pg)], [("r_gf", par)])
                            yield 0
                            if not sample_pass:
                                for c in range(nch):
                                    S.op("dve", lambda e, c=c: e.tensor_tensor_scan(out=G[:, c * 128:(c + 1) * 128], data0=self.onesf[:, 0:128], data1=lwf[:, c * 128:(c + 1) * 128], initial=0.0, op0=ALU.mult, op1=ALU.add), reads=["r_lwf", "consts2"], writes=["r_G"])
                                v3 = lambda t_: t_[:, 0:w].rearrange("p (c t) -> p c t", t=128)
                                glast = v3(G)[:, :, 127:128].broadcast_to([128, nch, 128])
                                ngrp = nch
                                S.op("act", lambda e: e.activation(out=EC[:, 0:nch], in_=v3(G)[:, :, 127], func=AF.Exp), reads=["r_G"], writes=[("r_EC", par)])
                            else:
                                G3 = G[:, 0:NS].rearrange("p (b t) -> p b t", t=TS)
                                l3 = lwf[:, 0:NS].rearrange("p (b t) -> p b t", t=TS)
                                S.op("dve", lambda e: e.tensor_copy(out=G3[:, :, 0:1], in_=l3[:, :, 0:1]), reads=["r_lwf"], writes=["r_G"])
                                for t_ in range(1, TS):
                                    S.op("dve", lambda e, t_=t_: e.tensor_tensor(out=G3[:, :, t_:t_ + 1], in0=G3[:, :, t_ - 1:t_], in1=l3[:, :, t_:t_ + 1], op=ALU.add), reads=["r_lwf", "r_G"], writes=["r_G"])
                                v3 = lambda t_: t_[:, 0:NS].rearrange("p (b t) -> p b t", t=TS)
                                glast = G3[:, :, TS - 1:TS].broadcast_to([128, NB, TS])
                                S.op("act", lambda e: e.activation(out=EC[:, 0:NB], in_=G3[:, :, TS - 1], func=AF.Exp), reads=["r_G"], writes=[("r_EC", par)])
                            yield 0
                            S.op("dve", lambda e: e.tensor_scalar(out=kap[:, 0:w], in0=kf[:, 0:w], scalar1=vec("rw_k_k"), scalar2=1.0, op0=ALU.mult, op1=ALU.mult), reads=[("r_kf", half), "consts"], writes=["r_kap"])
                            S.op("act", lambda e: e.activation(out=t1[:, 0:w], in_=kap[:, 0:w], func=AF.Square), reads=["r_kap"], writes=["r_t1"])
                            pn = self.nextps()
                            S.op("pe", lambda e: e.matmul(ps[pn][:, 0:w], lhsT=self.blk1, rhs=t1[:, 0:w], start=True, stop=True), reads=["r_t1", "consts"], writes=[("ps", pn)])
                            S.op("dve", lambda e: e.tensor_scalar(out=t1[:, 0:w], in0=ps[pn][:, 0:w], scalar1=1e-24, scalar2=None, op0=ALU.max), reads=[("ps", pn), "r_t1"], writes=["r_t1"])
                            S.op("act", lambda e: e.activation(out=t1[:, 0:w], in_=t1[:, 0:w], func=AF.Ln), reads=["r_t1"], writes=["r_t1"])
                            S.op("act", lambda e: e.activation(out=t1[:, 0:w], in_=t1[:, 0:w], func=AF.Exp, scale=-0.5), reads=["r_t1"], writes=["r_t1"])
                            S.op("dve", lambda e: e.tensor_tensor(out=kap[:, 0:w], in0=kap[:, 0:w], in1=t1[:, 0:w], op=ALU.mult), reads=["r_kap", "r_t1"], writes=["r_kap"])
                            yield 0
                            S.op("dve", lambda e: e.tensor_scalar(out=k2[:, 0:w], in0=af[:, 0:w], scalar1=vec("rw_k_a"), scalar2=self.rw_omka[:, pc:pc + 1], op0=ALU.mult, op1=ALU.add), reads=["r_af", "consts", "rwc1"], writes=["r_k2"])
                            S.op("pool", lambda e: e.tensor_tensor(out=k2[:, 0:w], in0=k2[:, 0:w], in1=kf[:, 0:w], op=ALU.mult), reads=["r_k2", ("r_kf", half)], writes=["r_k2"])
                            S.op("pool", lambda e: e.tensor_tensor(out=beta[:, 0:w], in0=af[:, 0:w], in1=kap[:, 0:w], op=ALU.mult), reads=["r_af", "r_kap"], writes=["r_beta"])
                            yield 0
                            S.op("dve", lambda e: e.scalar_tensor_tensor(out=bon[:, 0:w], in0=rf[:, 0:w], scalar=vec("rw_r_k"), in1=k2[:, 0:w], op0=ALU.mult, op1=ALU.mult), reads=[("r_rf", half), "r_k2", "consts"], writes=[("r_bon", par)])
                            pb = self.nextps()
                            S.op("pe", lambda e: e.matmul(ps[pb][:, 0:w], lhsT=self.blk1, rhs=bon[:, 0:w], start=True, stop=True), reads=[("r_bon", par), "consts"], writes=[("ps", pb)])
                            S.op("dve", lambda e: e.tensor_tensor(out=bon[:, 0:w], in0=ps[pb][:, 0:w], in1=vf[:, 0:w], op=ALU.mult), reads=[("ps", pb), ("r_vf", half), ("r_bon", par)], writes=[("r_bon", par)])
                            yield 0
                            if not sample_pass:
                                kr = lambda i_: KR[:, :, i_, :]
                                bk = lambda i_: BK[:, :, i_, :]
                            else:
                                kr = lambda i_: KR[:, 0, i_, 0:NS].rearrange("p (b t) -> p b t", t=TS)
                                bk = lambda i_: BK[:, 0, i_, 0:NS].rearrange("p (b t) -> p b t", t=TS)
                            S.op("act", lambda e: e.activation(out=e0[:, 0:w], in_=G[:, 0:w], func=AF.Exp), reads=["r_G"], writes=["r_e0"])
                            S.op("pool", lambda e: e.tensor_tensor(out=kr(1), in0=v3(rf), in1=v3(e0), op=ALU.mult), reads=[("r_rf", half), "r_e0"], writes=[("r_KR1", par)])
                            S.op("dve", lambda e: e.tensor_tensor(out=t1[:, 0:w], in0=G[:, 0:w], in1=lwf[:, 0:w], op=ALU.subtract), reads=["r_G", "r_lwf", "r_t1"], writes=["r_t1"])
                            S.op("act", lambda e: e.activation(out=e1[:, 0:w], in_=t1[:, 0:w], func=AF.Exp), reads=["r_t1"], writes=["r_e1"])
                            S.op("pool", lambda e: e.tensor_tensor(out=kr(0), in0=v3(kap), in1=v3(e1), op=ALU.mult), reads=["r_kap", "r_e1"], writes=[("r_KR0", par)])
                            S.op("act", lambda e: e.activation(out=e0[:, 0:w], in_=G[:, 0:w], func=AF.Exp, scale=-1.0), reads=["r_G", ("r_KR1", par)], writes=["r_e0"])
                            S.op("pool", lambda e: e.tensor_tensor(out=bk(0), in0=v3(beta), in1=v3(e0), op=ALU.mult), reads=["r_beta", "r_e0"], writes=[("r_BK0", par)])
                            S.op("dve", lambda e: e.tensor_tensor(out=bk(1), in0=v3(k2), in1=v3(e0), op=ALU.mult), reads=["r_k2", "r_e0"], writes=[("r_BK1", par)])
                            S.op("dve", lambda e: e.tensor_tensor(out=v3(t1), in0=glast, in1=v3(G), op=ALU.subtract), reads=["r_G", "r_t1", "r_e1"], writes=["r_t1"])
                            S.op("act", lambda e: e.activation(out=e1[:, 0:w], in_=t1[:, 0:w], func=AF.Exp), reads=["r_t1", ("r_KR0", par)], writes=["r_e1"])
                            S.op("pool", lambda e: e.tensor_tensor(out=btT[:, 0:w], in0=beta[:, 0:w], in1=e1[:, 0:w], op=ALU.mult), reads=["r_beta", "r_e1"], writes=["r_btT"])
                            S.op("dve", lambda e: e.tensor_tensor(out=ktT[:, 0:w], in0=k2[:, 0:w], in1=e1[:, 0:w], op=ALU.mult), reads=["r_k2", "r_e1"], writes=["r_ktT"])
                            yield 0
                            pt = self.nextps()
                            ptb = ps[pt].bitcast(BF16)
                            for i_, (src_, nm_) in enumerate(((btT, "r_btT"), (ktT, "r_ktT"), (vb, ("r_vb", half)))):
                                for c in range(nch):
                                    S.op("pe", lambda e, i_=i_, c=c, src_=src_: e.transpose(ptb[0:R, (i_ * nch + c) * 128:(i_ * nch + c + 1) * 128], src_[:, c * 128:c * 128 + R], self.identb), reads=[nm_, "consts2"], writes=[("ps", pt)])
                            for i_, (dst_, nm_) in enumerate(((bt_tok, ("r_bttok", par)), (kt_tok, ("r_kttok", par)), (v_tok, ("r_vtok", par)))):
                                self.copy("dve", dst_[0:R, :, :], ptb[0:R, i_ * nch * 128:(i_ + 1) * nch * 128].rearrange("p (c k) -> p c k", c=nch), [("ps", pt)], [nm_])
                            yield "MID"
                            if bi_ == 0:
                                S.op("pool", lambda e: e.memset(Sst, 0.0), writes=[("r_Sst", 0), ("r_Sst", 1)])
                                S.op("pool", lambda e: e.memset(Sbf, 0.0), writes=[("r_Sbf", 0), ("r_Sbf", 1)])
                            for hd in range(2):
                                P0 = hd * 64
                                for c in range(nch):
                                    m_ = hd * nch + c
                                    kr_c = KR[P0:P0 + 64, c, :, 0:R]
                                    p1, p2, p3 = self.nextps(), self.nextps(), self.nextps()
                                    o1 = ps[p1][0:R, 0:2 * R].rearrange("p (i t) -> p i t", i=2)
                                    o2 = ps[p2][0:R, 0:2 * R].rearrange("p (i t) -> p i t", i=2)
                                    S.op("pe", lambda e, c=c, P0=P0, kr_c=kr_c, o1=o1: e.matmul(o1, lhsT=BK[P0:P0 + 64, c, 0, 0:R], rhs=kr_c, start=True, stop=True), reads=[("r_BK0", par), ("r_KR0", par), ("r_KR1", par)], writes=[("ps", p1)])
                                    S.op("pe", lambda e, c=c, P0=P0, kr_c=kr_c, o2=o2: e.matmul(o2, lhsT=BK[P0:P0 + 64, c, 1, 0:R], rhs=kr_c, start=True, stop=True), reads=[("r_BK1", par), ("r_KR0", par), ("r_KR1", par)], writes=[("ps", p2)])
                                    S.op("pe", lambda e, c=c, P0=P0, p3=p3: e.matmul(ps[p3][0:R, 0:R], lhsT=KR[P0:P0 + 64, c, 0, 0:R], rhs=BK[P0:P0 + 64, c, 0, 0:R], start=True, stop=True), reads=[("r_BK0", par), ("r_KR0", par)], writes=[("ps", p3)])
                                    S.op("dve", lambda e, m_=m_, p1=p1: e.tensor_tensor(out=Nk[0:R, 0, m_, 0:R], in0=ps[p1][0:R, 0:R], in1=mStrict, op=ALU.mult), reads=[("ps", p1), "consts"], writes=[("r_Nk", 0, m_)])
                                    S.op("dve", lambda e, m_=m_, p1=p1: e.tensor_tensor(out=MbT[0:R, m_, 0:R], in0=ps[p1][0:R, R:2 * R], in1=mIncl, op=ALU.mult), reads=[("ps", p1), "consts"], writes=[("r_MbT", m_)])
                                    S.op("dve", lambda e, m_=m_, p2=p2: e.tensor_tensor(out=BTm[0:R, m_, 0:R], in0=ps[p2][0:R, 0:R], in1=mStrict, op=ALU.mult), reads=[("ps", p2), "consts"], writes=[("r_BTm", m_)])
                                    S.op("dve", lambda e, m_=m_, p2=p2: e.tensor_tensor(out=MkT[0:R, m_, 0:R], in0=ps[p2][0:R, R:2 * R], in1=mIncl, op=ALU.mult), reads=[("ps", p2), "consts"], writes=[("r_MkT", m_)])
                                    S.op("dve", lambda e, m_=m_, p3=p3: e.tensor_tensor(out=Ak[0:R, 0, m_, 0:R], in0=ps[p3][0:R, 0:R], in1=mLow, op=ALU.mult), reads=[("ps", p3), "consts"], writes=[("r_Ak", 0, m_)])
                                    S.op("pool", lambda e, m_=m_: e.tensor_tensor(out=Wm[0:R, m_, 0:R], in0=self.ident[0:R, 0:R], in1=Nk[0:R, 0, m_, 0:R], op=ALU.subtract), reads=[("r_Nk", 0, m_), "consts"], writes=[("r_Wm", m_)])
                                    yield 0
                            yield 0
                            nm_ = 2 * nch
                            for lev in range(1, nlev + 1):
                                cur, nxt = (lev - 1) % 2, lev % 2
                                for m_ in range(nm_):
                                    p1 = self.nextps()
                                    S.op("pe", lambda e, m_=m_, p1=p1, cur=cur: e.matmul(ps[p1][0:R, 0:R], lhsT=Nk[0:R, cur, m_, 0:R], rhs=Ak[0:R, cur, m_, 0:R], start=True, stop=True), reads=[("r_Nk", cur, m_), ("r_Ak", cur, m_)], writes=[("ps", p1)])
                                    if lev < nlev:
                                        S.op("pe", lambda e, m_=m_, p1=p1, cur=cur: e.matmul(ps[p1][0:R, 128:128 + R], lhsT=Ak[0:R, cur, m_, 0:R], rhs=Nk[0:R, cur, m_, 0:R], start=True, stop=True), reads=[("r_Nk", cur, m_), ("r_Ak", cur, m_)], writes=[("ps", p1)])
                                    self.copy("act", Ak[0:R, nxt, m_, 0:R], ps[p1][0:R, 0:R], [("ps", p1)], [("r_Ak", nxt, m_)])
                                    if lev < nlev:
                                        self.copy("dve", Nk[0:R, nxt, m_, 0:R], ps[p1][0:R, 128:128 + R], [("ps", p1)], [("r_Nk", nxt, m_)])
                                    yield 0
                                for m_ in range(nm_):
                                    p2 = self.nextps()
                                    S.op("pe", lambda e, m_=m_, p2=p2, nxt=nxt: e.matmul(ps[p2][0:R, 0:R], lhsT=Ak[0:R, nxt, m_, 0:R], rhs=Wm[0:R, m_, 0:R], start=True, stop=True), reads=[("r_Ak", nxt, m_), ("r_Wm", m_)], writes=[("ps", p2)])
                                    S.op("dve", lambda e, m_=m_, p2=p2: e.tensor_tensor(out=Wm[0:R, m_, 0:R], in0=Wm[0:R, m_, 0:R], in1=ps[p2][0:R, 0:R], op=ALU.add), reads=[("ps", p2), ("r_Wm", m_)], writes=[("r_Wm", m_)])
                                    yield 0
                            yield 0
                            pO = self.nextlong()
                            if sample_pass:
                                for hd in range(2):
                                    S.op("sp", lambda e, hd=hd: e.dma_start(out=s0nat[:, :, hd * 64:(hd + 1) * 64], in_=dr["st_wkv"][:, 2 * pc + hd].rearrange("b v k -> v b k")), writes=[("r_s0nat", hd)], dma="r_s0nat")
                                for q4 in range(4):
                                    p1 = self.nextps()
                                    for bb in range(4):
                                        b = q4 * 4 + bb
                                        S.op("pe", lambda e, b=b, bb=bb, p1=p1: e.transpose(ps[p1][:, bb * 64:(bb + 1) * 64], s0nat[:, b, :], self.ident[0:64, 0:64]), reads=[("r_s0nat", 0), ("r_s0nat", 1), "consts"], writes=[("ps", p1)])
                                    self.copy("act", S0[:, q4 * 4:(q4 + 1) * 4, :], ps[p1][:, 0:256].rearrange("p (b v) -> p b v", b=4), [("ps", p1)], [("r_S0", q4)])
                                    S.op("dve", lambda e, q4=q4, p1=p1: e.tensor_copy(out=S0bf[:, q4 * 4:(q4 + 1) * 4, :], in_=ps[p1][:, 0:256].rearrange("p (b v) -> p b v", b=4)), reads=[("ps", p1)], writes=[("r_S0bf", q4)])
                                s0k = [("r_S0", q4) for q4 in range(4)]
                                s0bk = [("r_S0bf", q4) for q4 in range(4)]
                                S.op("dve", lambda e: e.tensor_tensor(out=kblk, in0=KR[:, 0, 0, 0:NS].unsqueeze(1).broadcast_to([128, NB, NS]), in1=self.maskC, op=ALU.mult), reads=[("r_KR0", par), "consts"], writes=["r_kblk"])
                                S.op("dve", lambda e: e.tensor_tensor(out=rblk, in0=KR[:, 0, 1, 0:NS].unsqueeze(1).broadcast_to([128, NB, NS]), in1=self.maskC, op=ALU.mult), reads=[("r_KR1", par), "consts"], writes=["r_rblk"])
                            for c in range(nch):
                                for hd in range(2):
                                    P0 = hd * 64
                                    m_ = hd * nch + c
                                    hs = slice(P0, P0 + 64)
                                    yield 0
                                    pX = self.nextps()
                                    if not sample_pass:
                                        S.op("pe", lambda e, c=c, hs=hs, pX=pX: e.matmul(ps[pX][0:R, 0:64], lhsT=KR[hs, c, 0, 0:R], rhs=Sbf[hs, :], start=True, stop=False), reads=[("r_KR0", par), ("r_Sbf", hd)], writes=[("ps", pX)])
                                    else:
                                        for b in range(NB):
                                            S.op("pe", lambda e, b=b, hs=hs, pX=pX: e.matmul(ps[pX][0:R, 0:64], lhsT=kblk[hs, b, :], rhs=S0bf[hs, b, :], start=(b == 0), stop=False), reads=["r_kblk"] + s0bk, writes=[("ps", pX)])
                                    S.op("pe", lambda e, c=c, hs=hs, pX=pX, m_=m_: e.matmul(ps[pX][0:R, 0:64], lhsT=BTm[:, m_, 0:R], rhs=v_tok[:, c, hs], start=False, stop=True), reads=[("r_BTm", m_), ("r_vtok", par)], writes=[("ps", pX)])
                                    S.op("act", lambda e, hd=hd, pX=pX: e.activation(out=Xsb[0:R, hd, :], in_=ps[pX][0:R, 0:64], func=AF.Copy, scale=-1.0), reads=[("ps", pX)], writes=[("r_Xsb", hd)])
                                    pS = self.nextps()
                                    S.op("pe", lambda e, hd=hd, pS=pS, m_=m_: e.matmul(ps[pS][0:R, 0:64], lhsT=Wm[0:R, m_, 0:R], rhs=Xsb[0:R, hd, :], start=True, stop=True), reads=[("r_Wm", m_), ("r_Xsb", hd)], writes=[("ps", pS)])
                                    S.op("dve", lambda e, hd=hd, pS=pS: e.tensor_copy(out=Ssb[0:R, hd, :], in_=ps[pS][0:R, 0:64]), reads=[("ps", pS)], writes=[("r_Ssb", hd)])
                                    oo = ps[pO][hs, c * 128:c * 128 + R]
                                    if not sample_pass:
                                        S.op("pe", lambda e, c=c, hs=hs, oo=oo: e.matmul(oo, lhsT=Sbf[hs, :], rhs=KR[hs, c, 1, 0:R], start=True, stop=False), reads=[("r_KR1", par), ("r_Sbf", hd)], writes=[("ps", pO)])
                                        S.op("pe", lambda e, hd=hd, m_=m_, oo=oo: e.matmul(oo, lhsT=Ssb[0:R, hd, :], rhs=MbT[0:R, m_, 0:R], start=False, stop=False), reads=[("r_Ssb", hd), ("r_MbT", m_)], writes=[("ps", pO)])
                                        S.op("pe", lambda e, c=c, hs=hs, m_=m_, oo=oo: e.matmul(oo, lhsT=v_tok[0:R, c, hs], rhs=MkT[0:R, m_, 0:R], start=False, stop=True), reads=[("r_vtok", par), ("r_MkT", m_)], writes=[("ps", pO)])
                                        pU = self.nextps()
                                        S.op("pe", lambda e, c=c, hs=hs, hd=hd, pU=pU: e.matmul(ps[pU][hs, 0:64], lhsT=bt_tok[0:R, c, hs], rhs=Ssb[0:R, hd, :], start=True, stop=False), reads=[("r_bttok", par), ("r_Ssb", hd)], writes=[("ps", pU)])
                                        S.op("pe", lambda e, c=c, hs=hs, pU=pU: e.matmul(ps[pU][hs, 0:64], lhsT=kt_tok[0:R, c, hs], rhs=v_tok[0:R, c, hs], start=False, stop=True), reads=[("r_kttok", par), ("r_vtok", par)], writes=[("ps", pU)])
                                        S.op("dve", lambda e, c=c, hs=hs, pU=pU: e.scalar_tensor_tensor(out=Sst[hs, :], in0=Sst[hs, :], scalar=EC[hs, c:c + 1], in1=ps[pU][hs, 0:64], op0=ALU.mult, op1=ALU.add), reads=[("ps", pU), ("r_Sst", hd), ("r_EC", par)], writes=[("r_Sst", hd)])
                                        S.op("pool", lambda e, hs=hs: e.tensor_copy(out=Sbf[hs, :], in_=Sst[hs, :]), reads=[("r_Sst", hd)], writes=[("r_Sbf", hd)])
                                    else:
                                        S.op("pe", lambda e, hd=hd, m_=m_, oo=oo: e.matmul(oo, lhsT=Ssb[:, hd, :], rhs=MbT[:, m_, 0:R], start=True, stop=False), reads=[("r_Ssb", hd), ("r_MbT", m_)], writes=[("ps", pO)])
                                        S.op("pe", lambda e, hs=hs, m_=m_, oo=oo: e.matmul(oo, lhsT=v_tok[:, 0, hs], rhs=MkT[:, m_, 0:R], start=False, stop=False), reads=[("r_vtok", par), ("r_MkT", m_)], writes=[("ps", pO)])
                                        for b in range(NB):
                                            S.op("pe", lambda e, b=b, hs=hs, oo=oo: e.matmul(oo, lhsT=S0bf[hs, b, :], rhs=rblk[hs, b, :], start=False, stop=(b == NB - 1)), reads=["r_rblk"] + s0bk, writes=[("ps", pO)])
                                        S.op("dve", lambda e, hd=hd: e.tensor_tensor(out=Sblk[0:NS, hd, :, :], in0=Ssb[0:NS, hd, :].unsqueeze(1).broadcast_to([NS, NB, 64]), in1=self.maskB.unsqueeze(2).broadcast_to([NS, NB, 64]), op=ALU.mult), reads=[("r_Ssb", hd), "consts"], writes=[("r_Sblk", hd)])
                                        S.op("dve", lambda e, hd=hd, hs=hs: e.tensor_tensor(out=Vblk[0:NS, hd, :, :], in0=v_tok[0:NS, 0, hs].unsqueeze(1).broadcast_to([NS, NB, 64]), in1=self.maskB.unsqueeze(2).broadcast_to([NS, NB, 64]), op=ALU.mult), reads=[("r_vtok", par), "consts"], writes=[("r_Vblk", hd)])
                                        for half in range(2):
                                            pU = self.nextps()
                                            S.op("pe", lambda e, hs=hs, hd=hd, half=half, pU=pU: e.matmul(ps[pU][hs, 0:512], lhsT=bt_tok[:, 0, hs], rhs=Sblk[:, hd, half * 8:(half + 1) * 8, :], start=True, stop=False), reads=[("r_bttok", par), ("r_Sblk", hd)], writes=[("ps", pU)])
                                            S.op("pe", lambda e, hs=hs, hd=hd, half=half, pU=pU: e.matmul(ps[pU][hs, 0:512], lhsT=kt_tok[:, 0, hs], rhs=Vblk[:, hd, half * 8:(half + 1) * 8, :], start=False, stop=True), reads=[("r_kttok", par), ("r_Vblk", hd)], writes=[("ps", pU)])
                                            bs = slice(half * 8, (half + 1) * 8)
                                            S.op("dve", lambda e, hs=hs, bs=bs: e.tensor_tensor(out=S0[hs, bs, :], in0=S0[hs, bs, :], in1=EC[hs, bs].unsqueeze(2).broadcast_to([64, 8, 64]), op=ALU.mult), reads=s0k + [("r_EC", par)], writes=s0k)
                                            S.op("dve", lambda e, hs=hs, bs=bs, pU=pU: e.tensor_tensor(out=S0[hs, bs, :], in0=S0[hs, bs, :], in1=ps[pU][hs, 0:512].rearrange("p (b v) -> p b v", b=8), op=ALU.add), reads=s0k + [("ps", pU)], writes=s0k)
                            if sample_pass:
                                for q4 in range(4):
                                    p1 = self.nextps()
                                    for bb in range(4):
                                        b = q4 * 4 + bb
                                        S.op("pe", lambda e, b=b, bb=bb, p1=p1: e.transpose(ps[p1][0:64, bb * 128:(bb + 1) * 128], S0[:, b, :], self.ident), reads=s0k + ["consts"], writes=[("ps", p1)])
                                    self.copy("act", s0nat[:, q4 * 4:(q4 + 1) * 4, :], ps[p1][0:64, :].rearrange("p (b k) -> p b k", b=4), [("ps", p1)], [("r_s0nat", 0), ("r_s0nat", 1)])
                                for hd in range(2):
                                    S.op("sp", lambda e, hd=hd: e.dma_start(out=dr["wkv_s"][:, 2 * pc + hd].rearrange("b v k -> v b k"), in_=s0nat[:, :, hd * 64:(hd + 1) * 64]), reads=[("r_s0nat", 0), ("r_s0nat", 1)], dma="r_wkv_s")
                            elif c0 + w == TP:
                                p1 = self.nextps()
                                S.op("pe", lambda e, p1=p1: e.transpose(ps[p1][0:64, 0:128], Sst, self.ident), reads=[("r_Sst", 0), ("r_Sst", 1), "consts"], writes=[("ps", p1)])
                                self.copy("act", sT, ps[p1][0:64, 0:128], [("ps", p1)], ["r_sT"])
                                S.op("sp", lambda e: e.dma_start(out=dr["wkv_p"][2 * pc:2 * pc + 2].rearrange("h v k -> v h k"), in_=sT.rearrange("v (h k) -> v h k", h=2)), reads=["r_sT"], dma="r_wkv_p")
                            yield 0
                            self.copy("act", osb[:, 0:w], ps[pO][:, 0:w], [("ps", pO)], ["r_osb"])
                            pm = self.nextps()
                            S.op("pe", lambda e: e.matmul(ps[pm][:, 0:w], lhsT=self.blk1, rhs=osb[:, 0:w], start=True, stop=True), reads=["r_osb", "consts"], writes=[("ps", pm)])
                            S.op("dve", lambda e: e.scalar_tensor_tensor(out=osb[:, 0:w], in0=ps[pm][:, 0:w], scalar=-1.0 / 64, in1=osb[:, 0:w], op0=ALU.mult, op1=ALU.add), reads=[("ps", pm), "r_osb"], writes=["r_osb"])
                            S.op("act", lambda e: e.activation(out=t2[:, 0:w], in_=osb[:, 0:w], func=AF.Square), reads=["r_osb", "r_t2"], writes=["r_t2"])
                            pv2 = self.nextps()
                            S.op("pe", lambda e: e.matmul(ps[pv2][:, 0:w], lhsT=self.blk1, rhs=t2[:, 0:w], start=True, stop=True), reads=["r_t2", "consts"], writes=[("ps", pv2)])
                            S.op("act", lambda e: e.activation(out=t2[:, 0:w], in_=ps[pv2][:, 0:w], func=AF.Ln, scale=1.0 / 64, bias=GN_EPS), reads=[("ps", pv2), "r_t2"], writes=["r_t2"])
                            S.op("act", lambda e: e.activation(out=t2[:, 0:w], in_=t2[:, 0:w], func=AF.Exp, scale=-0.5), reads=["r_t2"], writes=["r_t2"])
                            S.op("dve", lambda e: e.tensor_tensor(out=osb[:, 0:w], in0=osb[:, 0:w], in1=t2[:, 0:w], op=ALU.mult), reads=["r_osb", "r_t2"], writes=["r_osb"])
                            S.op("dve", lambda e: e.tensor_scalar(out=osb[:, 0:w], in0=osb[:, 0:w], scalar1=vec("rw_lnx_w"), scalar2=vec("rw_lnx_b"), op0=ALU.mult, op1=ALU.add), reads=["r_osb", "consts"], writes=["r_osb"])
                            S.op("pool", lambda e: e.tensor_tensor(out=osb[:, 0:w], in0=osb[:, 0:w], in1=bon[:, 0:w], op=ALU.add), reads=["r_osb", ("r_bon", par)], writes=["r_osb"])
                            S.op("dve", lambda e: e.tensor_tensor(out=og[:, 0:w], in0=osb[:, 0:w], in1=gf[:, 0:w], op=ALU.mult), reads=["r_osb", ("r_gf", par)], writes=["r_og"])
                            for dc in range(KC):
                                pi = self.nextps()
                                S.op("pe", lambda e, pi=pi, dc=dc: e.matmul(ps[pi][:, 0:w], lhsT=wout[:, dc * 128:(dc + 1) * 128], rhs=og[:, 0:w], start=True, stop=True), reads=[("r_wout", pc % 2), "r_og"], writes=[("ps", pi)])
                                S.op("dve", lambda e, pi=pi, dc=dc: e.tensor_tensor(out=xres[:, dc, c0:c0 + w], in0=xres[:, dc, c0:c0 + w], in1=ps[pi][:, 0:w], op=ALU.add), reads=[("ps", pi), ("xres", dc, c0)], writes=[("xres", dc, c0)])
                                yield 0


                    gens = []
                    gi = 0
                    for pc in range(KC):
                        for bi_, (c0, w, kind) in enumerate(blocks):
                            gens.append(blk(pc, bi_, c0, w, kind, gi % 2))
                            gi += 1
                    back = None
                    for g in gens:
                        fd, bd = False, back is None
                        while not (fd and bd):
                            if not fd:
                                fd = next(g) == "MID"
                            if not bd:
                                bd = next(back, "END") == "END"
                        back = g
                    for _ in back:
                        pass
                S.barrier()

    def mamba2(self, j):
        nc, S, TP = self.nc, self.S, self.TP
        xres, hT, ps = self.xres, self.hT, self.ps
        w_in, w_out = self.dram["mb_w_in"], self.dram["mb_w_out"]
        st_ssm, st_conv = self.dram["st_ssm"], self.dram["st_conv"]
        ssm_p, ssm_s, cv_p, cv_s = self.dram["ssm_p"], self.dram["ssm_s"], self.dram["cv_p"], self.dram["cv_s"]
        W = 512
        with ExitStack() as es:
            T = lambda n, sh, dt=F32: es.enter_context(self.T(n, sh, dt))
            wz, wx = T("m_wz", [128, KC, 512], BF16), T("m_wx", [128, KC, 512], BF16)
            wB, wC, wdt = T("m_wB", [128, KC, 128], BF16), T("m_wC", [128, KC, 128], BF16), T("m_wdt", [128, KC, 8], BF16)
            wout = T("m_wout", [128, 4, D], BF16)
            ngt = T("m_ng", [128, 512])
            pre = T("m_pre", [128, 6, 3 + W])
            acc = T("m_acc", [128, 1, W])
            xsT, BT, CT = T("m_xsT", [128, 4, W], BF16), T("m_BT", [128, W], BF16), T("m_CT", [128, W], BF16)
            cvT, cvrow = T("m_cvT", [128, 6, 48]), T("m_cvrow", [48, 768])
            zs, dtv, da = T("m_zs", [128, 512]), T("m_dt", [128, 8]), T("m_da", [128, 8])
            xtok, Btok = T("m_xtok", [128, 512], BF16), T("m_Btok", [128, 128], BF16)
            cbm, ex, wts = T("m_cbm", [128, 128]), T("m_ex", [128, 24]), T("m_wts", [128, 8])
            daM, seg, mT = T("m_daM", [128, 2, 128]), T("m_seg", [128, 2, 128]), T("m_mT", [128, 2, 128], BF16)
            y1, xd = T("m_y1", [128, 512]), T("m_xd", [128, 512])
            ssq, yn = T("m_ssq", [128, 2]), T("m_yn", [128, 512], BF16)
            yTb = T("m_yTb", [128, 4, W], BF16)
            xw = T("m_xw", [128, 512], BF16)
            ST, STbf = T("m_ST", [128, 512]), T("m_STbf", [128, 512], BF16)
            stT = T("m_stT", [128, 4, 128])
            cv0 = cvrow
            Cblk = T("m_Cblk", [128, NB, NS], BF16)
            ST0bf = T("m_ST0bf", [128, 2, 512], BF16)
            snat = T("m_snat", [128, 2, 4, 128])
            dablk, etots = T("m_dablk", [64, NB, 8]), T("m_etots", [128, NB, 8])
            fz = T("m_fz", [128, 2])
            preflat = pre.rearrange("p f t -> p (f t)")
            snat_slots = [snat[:, 0, :, :], snat[:, 1, :, :]] + [preflat[:, k_ * 512:(k_ + 1) * 512].rearrange("p (q n) -> p q n", q=4) for k_ in range(5)]
            stT_slots = [stT, preflat[:, 2560:3072].rearrange("p (q n) -> p q n", q=4)]
            for g in range(4):
                for kk0 in range(0, KC, 2):
                    for (wt, col) in ((wz, g * 512), (wx, 2048 + g * 512)):
                        src = w_in[j, kk0 * 128:(kk0 + 2) * 128, col:col + 512].rearrange("(k p) c -> p k c", p=128)
                        self.load_w(wt[:, kk0:kk0 + 2, :], src, [128, 2, 512], ("m_w", id(wt), kk0))
                for (wt, col) in ((wB, 4096 + g * 128), (wC, 4608 + g * 128)):
                    self.load_w(wt, w_in[j, :, col:col + 128].rearrange("(k p) c -> p k c", p=128), [128, KC, 128], ("m_w", id(wt)))
                self.load_w(wdt, w_in[j, :, 5120 + 8 * g:5128 + 8 * g].rearrange("(k p) c -> p k c", p=128), [128, KC, 8], ("m_w", id(wdt)))
                for fc in range(4):
                    self.load_w(wout[:, fc, :], w_out[j, g * 512 + fc * 128:g * 512 + (fc + 1) * 128, :], [128, D], ("m_wout", fc))
                S.op("sp", lambda e: e.dma_start(out=ngt, in_=self.dram["mb_norm"][:, g * 512:(g + 1) * 512]), writes=["m_ng"], dma="m_ng")
                wk2 = lambda wt: [("m_w", id(wt), k_) for k_ in range(0, KC, 2)]
                wk1 = lambda wt: [("m_w", id(wt))]
                fcol = [g * 512 + i * 128 for i in range(4)] + [2048 + g * 128, 2560 + g * 128]
                fch24 = [c_ // 128 for c_ in fcol]
                S.op("pool", lambda e: e.memset(ST, 0.0), writes=["m_ST"])
                S.op("pool", lambda e: e.memset(STbf, 0.0), writes=["m_STbf"])
                S.op("pool", lambda e: e.memset(pre[:, :, 0:3], 0.0), writes=["m_pre"] + [("m_snat", k_) for k_ in range(2, 7)] + [("m_stT", 1)])
                S.op("pool", lambda e: e.memset(cvT, 0.0), writes=["m_cvT"])
                for (c0, w, kind) in self.tbs:
                    smp = kind == "s"
                    hc = 2 + c0
                    R = 64 if smp else 128
                    if smp:
                        pre4 = pre[:, :, 0:NB * 7].rearrange("p f (b t) -> p f b t", t=7)
                        for i_, (c_, n_) in enumerate(((fcol[0], 512), (fcol[4], 128), (fcol[5], 128))):
                            o_ = (0, 512, 640)[i_]
                            S.op("sp", lambda e, c_=c_, n_=n_, o_=o_: e.dma_start(out=cv0[:, o_:o_ + n_], in_=st_conv[:, :, c_:c_ + n_].rearrange("b w f -> (b w) f")), writes=["m_cvrow"], dma="m_cv0")
                        pt = self.nextps()
                        for f in range(6):
                            S.op("pe", lambda e, f=f: e.transpose(ps[pt][:, f * 48:(f + 1) * 48], cv0[:, f * 128:(f + 1) * 128], self.ident[0:48, 0:48]), reads=["m_cvrow", "consts"], writes=[("ps", pt)])
                        self.copy("dve", pre4[:, :, :, 0:3], ps[pt][:, 0:288].rearrange("p (f b t) -> p f b t", f=6, t=3), [("ps", pt)], ["m_pre"])
                    elif c0 > 0:
                        self.copy("dve", pre[:, :, 0:3], pre[:, :, W:W + 3], ["m_pre"], ["m_pre"])
                    for f in range(6):
                        pi = self.nextps()
                        wt, cs = (wx, f * 128) if f < 4 else ((wB, 0) if f == 4 else (wC, 0))
                        for kc in range(KC):
                            S.op("pe", lambda e, pi=pi, kc=kc, wt=wt, cs=cs: e.matmul(ps[pi][:, 0:w], lhsT=wt[:, kc, cs:cs + 128], rhs=hT[:, kc, hc:hc + w], start=(kc == 0), stop=(kc == KC - 1)),
                                 reads=(wk2(wt) if f < 4 else wk1(wt)) + [("hT", "all")], writes=[("ps", pi)])
                        if smp:
                            self.copy(self.ew(), pre4[:, f, :, 3:7], ps[pi][:, 0:NS].rearrange("p (b t) -> p b t", t=TS), [("ps", pi)], ["m_pre"])
                        else:
                            self.copy(self.ew(), pre[:, f, 3:3 + W], ps[pi][:, 0:W], [("ps", pi)], ["m_pre"])
                    if MBSTOP == 2:
                        return
                    for f in range(6):
                        f24 = fch24[f]
                        if smp:
                            a_ = acc[:, 0, 0:NS].rearrange("p (b t) -> p b t", t=TS)
                            src_k = lambda k_: pre4[:, f, :, k_:k_ + TS]
                        else:
                            a_ = acc[:, 0, :]
                            src_k = lambda k_: pre[:, f, k_:k_ + W]
                        S.op("act", lambda e, a_=a_, f24=f24, s0=src_k(0): e.activation(out=a_, in_=s0, func=AF.Identity, scale=self.mb_cw[:, f24, 0:1], bias=self.mb_cb[:, f24:f24 + 1]),
                             reads=["m_pre", "consts"], writes=[("m_acc", 0)])
                        for k_ in range(1, 4):
                            S.op("dve", lambda e, a_=a_, f24=f24, k_=k_, sk=src_k(k_): e.scalar_tensor_tensor(out=a_, in0=sk, scalar=self.mb_cw[:, f24, k_:k_ + 1], in1=a_, op0=ALU.mult, op1=ALU.add),
                                 reads=["m_pre", "consts", ("m_acc", 0)], writes=[("m_acc", 0)])
                        dst = xsT[:, f, 0:w] if f < 4 else (BT[:, 0:w] if f == 4 else CT[:, 0:w])
                        S.op("act", lambda e, dst=dst, f=f: e.activation(out=dst, in_=acc[:, 0, 0:w], func=AF.Silu), reads=[("m_acc", 0)], writes=[("m_xbc", f)])
                    if MBSTOP == 3:
                        return
                    if smp or c0 + w == TP:
                        nr = 48 if smp else 3
                        if smp:
                            self.copy("dve", cvT.rearrange("p f (b t) -> p f b t", t=3), pre4[:, :, :, 4:7], ["m_pre"], ["m_cvT"])
                        else:
                            self.copy("dve", cvT[:, :, 0:3], pre[:, :, W:W + 3], ["m_pre"], ["m_cvT"])
                        pt, ptx = self.nextps(), self.nextps()
                        for f in range(6):
                            pp = pt if f < 4 else ptx
                            nrt = max(nr, 32)
                            S.op("pe", lambda e, f=f, nrt=nrt, pp=pp: e.transpose(ps[pp][0:nrt, (f % 4) * 128:(f % 4 + 1) * 128], cvT[:, f, 0:nrt], self.ident), reads=["m_cvT", "consts"], writes=[("ps", pp)])
                        self.copy("act", cvrow[0:nr, 0:512], ps[pt][0:nr, 0:512], [("ps", pt)], ["m_cvrow"])
                        self.copy("act", cvrow[0:nr, 512:768], ps[ptx][0:nr, 0:256], [("ps", ptx)], ["m_cvrow"])
                        for i_, (c_, n_) in enumerate(((fcol[0], 512), (fcol[4], 128), (fcol[5], 128))):
                            o_ = (0, 512, 640)[i_]
                            dstd = cv_s[:, :, c_:c_ + n_].rearrange("b w f -> (b w) f") if smp else cv_p[:, c_:c_ + n_]
                            S.op("sp", lambda e, dstd=dstd, o_=o_, n_=n_, nr=nr: e.dma_start(out=dstd, in_=cvrow[0:nr, o_:o_ + n_]), reads=["m_cvrow"], dma="m_cvout")
                    if MBSTOP == 4:
                        return
                    xbk = [("m_xbc", f) for f in range(6)]
                    mU = self.maskS if smp else self.maskU
                    mL = self.sLS if smp else self.sL
                    for ct in range(1 if smp else w // 128):
                        t0 = ct * 128
                        pz, pd = self.nextps(), self.nextps()
                        for kc in range(KC):
                            S.op("pe", lambda e, kc=kc: e.matmul(ps[pz][0:R, 0:512], lhsT=hT[:, kc, hc + t0:hc + t0 + R], rhs=wz[:, kc, :], start=(kc == 0), stop=(kc == KC - 1)),
                                 reads=wk2(wz) + [("hT", "all")], writes=[("ps", pz)])
                        for kc in range(KC):
                            S.op("pe", lambda e, kc=kc: e.matmul(ps[pd][0:R, 0:8], lhsT=hT[:, kc, hc + t0:hc + t0 + R], rhs=wdt[:, kc, :], start=(kc == 0), stop=(kc == KC - 1)),
                                 reads=wk1(wdt) + [("hT", "all")], writes=[("ps", pd)])
                        S.op("act", lambda e: e.activation(out=zs[0:R, :], in_=ps[pz][0:R, 0:512], func=AF.Silu), reads=[("ps", pz)], writes=["m_zs"])
                        S.op("dve", lambda e: e.tensor_tensor(out=dtv[0:R, :], in0=ps[pd][0:R, 0:8], in1=self.mb_dtb[0:R, 8 * g:8 * g + 8], op=ALU.add), reads=[("ps", pd), "consts"], writes=["m_dt"])
                        S.op("act", lambda e: e.activation(out=dtv[0:R, :], in_=dtv[0:R, :], func=AF.Exp), reads=["m_dt"], writes=["m_dt"])
                        S.op("act", lambda e: e.activation(out=dtv[0:R, :], in_=dtv[0:R, :], func=AF.Ln, bias=1.0), reads=["m_dt"], writes=["m_dt"])
                        S.op("dve", lambda e: e.tensor_tensor(out=da[0:R, :], in0=dtv[0:R, :], in1=self.mb_negA[0:R, 8 * g:8 * g + 8], op=ALU.mult), reads=["m_dt", "mbc0"], writes=["m_da"])
                        if MBSTOP == 5:
                            return
                        pt = self.nextps()
                        ptb = ps[pt].bitcast(BF16)
                        for fc in range(4):
                            S.op("pe", lambda e, fc=fc: e.transpose(ptb[0:R, fc * 128:(fc + 1) * 128], xsT[:, fc, t0:t0 + R], self.identb), reads=xbk[0:4] + ["consts2"], writes=[("ps", pt)])
                        S.op("pe", lambda e: e.transpose(ptb[0:R, 512:640], BT[:, t0:t0 + R], self.identb), reads=[xbk[4], "consts2"], writes=[("ps", pt)])
                        self.copy("dve", xtok[0:R, :], ptb[0:R, 0:512], [("ps", pt)], ["m_xtok"])
                        self.copy("dve", Btok[0:R, :], ptb[0:R, 512:640], [("ps", pt)], ["m_Btok"])
                        pc = self.nextps()
                        S.op("pe", lambda e: e.matmul(ps[pc][0:R, 0:R], lhsT=BT[:, t0:t0 + R], rhs=CT[:, t0:t0 + R], start=True, stop=True), reads=[xbk[4], xbk[5]], writes=[("ps", pc)])
                        S.op("dve", lambda e: e.tensor_tensor(out=cbm[0:R, 0:R], in0=ps[pc][0:R, 0:R], in1=mU, op=ALU.mult), reads=[("ps", pc), "consts"], writes=["m_cbm"])
                        if MBSTOP == 6:
                            return
                        pm = self.nextps()
                        S.op("pe", lambda e: e.matmul(ps[pm][0:R, 0:8], lhsT=mU, rhs=da[0:R, :], start=True, stop=True), reads=["m_da", "consts"], writes=[("ps", pm)])
                        S.op("pe", lambda e: e.matmul(ps[pm][0:R, 8:16], lhsT=mL, rhs=da[0:R, :], start=True, stop=True), reads=["m_da", "consts"], writes=[("ps", pm)])
                        if not smp:
                            S.op("pe", lambda e: e.matmul(ps[pm][:, 16:24], lhsT=self.onesf, rhs=da, start=True, stop=True), reads=["m_da", "consts2"], writes=[("ps", pm)])
                            S.op("act", lambda e: e.activation(out=ex, in_=ps[pm][:, 0:24], func=AF.Exp), reads=[("ps", pm)], writes=["m_ex"])
                        else:
                            S.op("act", lambda e: e.activation(out=ex[0:R, 0:16], in_=ps[pm][0:R, 0:16], func=AF.Exp), reads=[("ps", pm)], writes=["m_ex"])
                            S.op("dve", lambda e: e.tensor_tensor(out=dablk, in0=da[0:NS, :].unsqueeze(1).broadcast_to([NS, NB, 8]), in1=self.maskB.unsqueeze(2).broadcast_to([NS, NB, 8]), op=ALU.mult),
                                 reads=["m_da", "consts"], writes=["m_dablk"])
                            pm2 = self.nextps()
                            S.op("pe", lambda e: e.matmul(ps[pm2][:, 0:NB * 8], lhsT=self.onesf[0:NS, :], rhs=dablk.rearrange("p b h -> p (b h)"), start=True, stop=True), reads=["m_dablk", "consts2"], writes=[("ps", pm2)])
                            S.op("act", lambda e: e.activation(out=etots.rearrange("p b h -> p (b h)"), in_=ps[pm2][:, 0:NB * 8], func=AF.Exp), reads=[("ps", pm2)], writes=["m_etots"])
                        S.op("dve", lambda e: e.tensor_tensor(out=wts[0:R, :], in0=dtv[0:R, :], in1=ex[0:R, 8:16], op=ALU.mult), reads=["m_dt", "m_ex"], writes=["m_wts"])
                        if MBSTOP == 7:
                            return
                        py = self.nextlong()
                        for hh in range(8):
                            sl = hh % 2
                            S.op("dve", lambda e, hh=hh, sl=sl: e.tensor_scalar(out=daM[0:R, sl, 0:R], in0=mL, scalar1=da[0:R, hh:hh + 1], scalar2=None, op0=ALU.mult), reads=["m_da", "consts"], writes=[("m_daM", sl)])
                            pD = self.nextps()
                            S.op("pe", lambda e, sl=sl, pD=pD: e.matmul(ps[pD][0:R, 0:R], lhsT=daM[0:R, sl, 0:R], rhs=mU, start=True, stop=True), reads=[("m_daM", sl), "consts"], writes=[("ps", pD)])
                            S.op("act", lambda e, sl=sl, pD=pD: e.activation(out=seg[0:R, sl, 0:R], in_=ps[pD][0:R, 0:R], func=AF.Exp), reads=[("ps", pD)], writes=[("m_seg", sl)])
                            S.op("dve", lambda e, sl=sl, hh=hh: e.scalar_tensor_tensor(out=mT[0:R, sl, 0:R], in0=seg[0:R, sl, 0:R], scalar=dtv[0:R, hh:hh + 1], in1=cbm[0:R, 0:R], op0=ALU.mult, op1=ALU.mult),
                                 reads=[("m_seg", sl), "m_dt", "m_cbm"], writes=[("m_mT", sl)])
                            S.op("pe", lambda e, sl=sl, hh=hh: e.matmul(ps[py][0:R, hh * 64:(hh + 1) * 64], lhsT=mT[0:R, sl, 0:R], rhs=xtok[0:R, hh * 64:(hh + 1) * 64], start=True, stop=True),
                                 reads=[("m_mT", sl), "m_xtok"], writes=[("ps", py)])
                        if MBSTOP == 8:
                            return
                        v3 = lambda t_: t_.rearrange("p (h q) -> p h q", q=64)
                        pyi = self.nextlong()
                        if not smp:
                            S.op("pe", lambda e: e.matmul(ps[pyi][:, 0:512], lhsT=CT[:, t0:t0 + 128], rhs=STbf, start=True, stop=True), reads=[xbk[5], "m_STbf"], writes=[("ps", pyi)])
                        else:
                            S.op("dve", lambda e: e.tensor_tensor(out=Cblk, in0=CT[:, 0:NS].unsqueeze(1).broadcast_to([128, NB, NS]), in1=self.maskC, op=ALU.mult), reads=[xbk[5], "consts"], writes=["m_Cblk"])
                            S.op("dve", lambda e: e.tensor_tensor(out=v3(xw[0:R, :]), in0=v3(xtok[0:R, :]), in1=wts[0:R, :].unsqueeze(2).broadcast_to([R, 8, 64]), op=ALU.mult), reads=["m_xtok", "m_wts"], writes=["m_xw"])
                            S.op("pool", lambda e: e.memset(fz, 0.0), reads=["m_pre"], writes=[("m_snat", k_) for k_ in range(2, 7)] + [("m_stT", 1)])
                            STb = [ST, xd]
                            STk = ["m_ST", "m_xd"]
                            for b in range(NB):
                                sl = b % 7
                                s2 = b % 2
                                S.op("sp", lambda e, b=b, sl=sl: e.dma_start(out=snat_slots[sl], in_=st_ssm[b, 8 * g:8 * g + 8].rearrange("h p n -> (h p) n").rearrange("(q r) n -> r q n", r=128)),
                                     writes=[("m_snat", sl)], dma="m_snat%d" % sl)
                                pq_ = self.nextps()
                                for q_ in range(4):
                                    S.op("pe", lambda e, q_=q_, sl=sl, pq_=pq_: e.transpose(ps[pq_][:, q_ * 128:(q_ + 1) * 128], snat_slots[sl][:, q_, :], self.ident), reads=[("m_snat", sl), "consts"], writes=[("ps", pq_)])
                                self.copy("act", ST0bf[:, s2, :], ps[pq_][:, 0:512], [("ps", pq_)], [("m_ST0bf", s2)])
                                S.op("pe", lambda e, b=b, s2=s2: e.matmul(ps[pyi][0:NS, 0:512], lhsT=Cblk[:, b, :], rhs=ST0bf[:, s2, :], start=(b == 0), stop=(b == NB - 1)),
                                     reads=["m_Cblk", ("m_ST0bf", s2)], writes=[("ps", pyi)])
                                S.op("dve", lambda e, b=b, pq_=pq_, s2=s2: e.tensor_tensor(out=v3(STb[s2]), in0=v3(ps[pq_][:, 0:512]), in1=etots[:, b, :].unsqueeze(2).broadcast_to([128, 8, 64]), op=ALU.mult), reads=[("ps", pq_), "m_etots"], writes=[STk[s2]])
                                S.op("dve", lambda e, b=b: e.tensor_scalar(out=yn[0:NS, :], in0=xw[0:NS, :], scalar1=self.maskB[:, b:b + 1], scalar2=None, op0=ALU.mult), reads=["m_xw", "consts"], writes=["m_yn"])
                                pS = self.nextps()
                                S.op("pe", lambda e, pS=pS: e.matmul(ps[pS][:, 0:512], lhsT=Btok[0:NS, :], rhs=yn[0:NS, :], start=True, stop=True), reads=["m_Btok", "m_yn"], writes=[("ps", pS)])
                                S.op("dve", lambda e, pS=pS, s2=s2: e.tensor_tensor(out=STb[s2], in0=STb[s2], in1=ps[pS][:, 0:512], op=ALU.add), reads=[STk[s2], ("ps", pS)], writes=[STk[s2]])
                                pq2 = self.nextps()
                                for q_ in range(4):
                                    S.op("pe", lambda e, q_=q_, pq2=pq2, s2=s2: e.transpose(ps[pq2][:, q_ * 128:(q_ + 1) * 128], STb[s2][:, q_ * 128:(q_ + 1) * 128], self.ident), reads=[STk[s2], "consts"], writes=[("ps", pq2)])
                                self.copy("act", stT_slots[s2].rearrange("p q n -> p (q n)"), ps[pq2][:, 0:512], [("ps", pq2)], [("m_stT", s2)])
                                S.op("sp", lambda e, b=b, s2=s2: e.dma_start(out=ssm_s[b, 8 * g:8 * g + 8].rearrange("h p n -> (h p) n").rearrange("(q r) n -> r q n", r=128), in_=stT_slots[s2]), reads=[("m_stT", s2)], dma="m_ssm_s%d" % s2)
                        ecb = ex[0:R, 0:8].unsqueeze(2).broadcast_to([R, 8, 64])
                        S.op("dve", lambda e: e.tensor_tensor(out=v3(y1[0:R, :]), in0=v3(ps[pyi][0:R, 0:512]), in1=ecb, op=ALU.mult), reads=[("ps", pyi), "m_ex"], writes=["m_y1"])
                        S.op("dve", lambda e: e.tensor_tensor(out=y1[0:R, :], in0=y1[0:R, :], in1=ps[py][0:R, 0:512], op=ALU.add), reads=[("ps", py), "m_y1"], writes=["m_y1"])
                        S.op("dve", lambda e: e.tensor_tensor(out=v3(xd[0:R, :]), in0=v3(xtok[0:R, :]), in1=self.mb_D[0:R, 8 * g:8 * g + 8].unsqueeze(2).broadcast_to([R, 8, 64]), op=ALU.mult), reads=["m_xtok", "consts"], writes=["m_xd"])
                        S.op("dve", lambda e: e.tensor_tensor(out=y1[0:R, :], in0=y1[0:R, :], in1=xd[0:R, :], op=ALU.add), reads=["m_xd", "m_y1"], writes=["m_y1"])
                        S.op("dve", lambda e: e.tensor_tensor(out=y1[0:R, :], in0=y1[0:R, :], in1=zs[0:R, :], op=ALU.mult), reads=["m_zs", "m_y1"], writes=["m_y1"])
                        S.op("act", lambda e: e.activation(out=xd[0:R, :], in_=y1[0:R, :], func=AF.Square, accum_out=ssq[0:R, 0:1]), reads=["m_y1", "m_xd"], writes=["m_xd", "m_ssq"])
                        S.op("act", lambda e: e.activation(out=ssq[0:R, 1:2], in_=ssq[0:R, 0:1], func=AF.Ln, scale=1.0 / 512, bias=EPS), reads=["m_ssq"], writes=["m_ssq"])
                        S.op("act", lambda e: e.activation(out=ssq[0:R, 1:2], in_=ssq[0:R, 1:2], func=AF.Exp, scale=-0.5), reads=["m_ssq"], writes=["m_ssq"])
                        S.op("dve", lambda e: e.scalar_tensor_tensor(out=yn[0:R, :], in0=y1[0:R, :], scalar=ssq[0:R, 1:2], in1=ngt[0:R, :], op0=ALU.mult, op1=ALU.mult),
                             reads=["m_y1", "m_ssq", "m_ng"], writes=["m_yn"])
                        pt2 = self.nextps()
                        pt2b = ps[pt2].bitcast(BF16)
                        for fc in range(4):
                            S.op("pe", lambda e, fc=fc: e.transpose(pt2b[:, fc * 128:fc * 128 + R], yn[0:R, fc * 128:(fc + 1) * 128], self.identb[0:R, 0:R]), reads=["m_yn", "consts2"], writes=[("ps", pt2)])
                        self.copy("dve", yTb[:, :, t0:t0 + R], pt2b[:, 0:512].rearrange("p (f t) -> p f t", f=4)[:, :, 0:R], [("ps", pt2)], [("m_yTb", ct)])
                        if MBSTOP == 9:
                            return
                        if not smp:
                            S.op("dve", lambda e: e.tensor_tensor(out=v3(xw[0:R, :]), in0=v3(xtok[0:R, :]), in1=wts[0:R, :].unsqueeze(2).broadcast_to([R, 8, 64]), op=ALU.mult), reads=["m_xtok", "m_wts"], writes=["m_xw"])
                            pS = self.nextps()
                            S.op("pe", lambda e: e.matmul(ps[pS][:, 0:512], lhsT=Btok, rhs=xw, start=True, stop=True), reads=["m_Btok", "m_xw"], writes=[("ps", pS)])
                            S.op("dve", lambda e: e.tensor_tensor(out=v3(ST), in0=v3(ST), in1=ex[:, 16:24].unsqueeze(2).broadcast_to([128, 8, 64]), op=ALU.mult), reads=["m_ST", "m_ex"], writes=["m_ST"])
                            S.op("dve", lambda e: e.tensor_tensor(out=ST, in0=ST, in1=ps[pS][:, 0:512], op=ALU.add), reads=["m_ST", ("ps", pS)], writes=["m_ST"])
                            S.op("pool", lambda e: e.tensor_copy(out=STbf, in_=ST), reads=["m_ST"], writes=["m_STbf"])
                    for dc in range(KC):
                        pi = self.nextps()
                        for fc in range(4):
                            S.op("pe", lambda e, pi=pi, dc=dc, fc=fc: e.matmul(ps[pi][:, 0:w], lhsT=wout[:, fc, dc * 128:(dc + 1) * 128], rhs=yTb[:, fc, 0:w], start=(fc == 0), stop=(fc == 3)),
                                 reads=[("m_wout", fc)] + [("m_yTb", c_) for c_ in range(4)], writes=[("ps", pi)])
                        S.op("dve", lambda e, pi=pi, dc=dc: e.tensor_tensor(out=xres[:, dc, c0:c0 + w], in0=xres[:, dc, c0:c0 + w], in1=ps[pi][:, 0:w], op=ALU.add),
                             reads=[("ps", pi), ("xres", dc, c0)], writes=[("xres", dc, c0)])
                    if (not smp) and c0 + w == TP:
                        pq2 = self.nextps()
                        for q_ in range(4):
                            S.op("pe", lambda e, q_=q_: e.transpose(ps[pq2][:, q_ * 128:(q_ + 1) * 128], ST[:, q_ * 128:(q_ + 1) * 128], self.ident), reads=["m_ST", "consts"], writes=[("ps", pq2)])
                        self.copy("act", stT.rearrange("p q n -> p (q n)"), ps[pq2][:, 0:512], [("ps", pq2)], [("m_stT", 0)])
                        S.op("sp", lambda e: e.dma_start(out=ssm_p[8 * g:8 * g + 8].rearrange("h p n -> (h p) n").rearrange("(q r) n -> r q n", r=128), in_=stT), reads=[("m_stT", 0)], dma="m_ssm_p")


def make_in_map(inp, core, TP, names):
    b0 = core * NB
    m = {}
    m["x_p"] = np.ascontiguousarray(inp["x_prompt"][core, :TP])
    m["x_s"] = np.ascontiguousarray(inp["x_sample"][b0:b0 + NB].reshape(NS, D))
    m["ident"] = np.eye(128, dtype=np.float32)
    m["norm_mix"] = np.ascontiguousarray(_fm(inp["norm_mix"]))
    m["norm_ffn"] = np.ascontiguousarray(_fm(inp["norm_ffn"]))
    m["norm_final"] = np.ascontiguousarray(_fm(inp["norm_final"]))
    for k in ("ffn_w_gate", "ffn_w_up", "ffn_w_down"):
        m[k] = inp[k]
    extra_in_map(m, inp, core, TP)
    return {k: np.ascontiguousarray(m[k], dtype=np.float32) for k in names}


def _masks():
    r = np.arange(128)
    maskU2 = ((r[:, None] // 64 == r[None, :] // 64) & (r[None, :] % 64 >= r[:, None] % 64)).astype(np.float32)
    r = np.arange(64)
    maskS = ((r[:, None] // TS == r[None, :] // TS) & (r[None, :] >= r[:, None])).astype(np.float32)
    maskB = (r[:, None] // TS == np.arange(NB)[None, :]).astype(np.float32)
    return maskU2, maskS, maskB


def extra_in_map(m, inp, core, TP):
    b0 = core * NB
    m["maskU2"], m["maskS"], m["maskB"] = _masks()
    m["hg_lb_logits"] = _fm(inp["hg_lb_logits"])
    m["hg_norm"] = np.ascontiguousarray(inp["hg_norm"].T)
    m["hg_w_in"] = inp["hg_w_in"]
    m["hg_w_out"] = inp["hg_w_out"]
    m["st_hg"] = inp["state_hgrn"][:, b0:b0 + NB]
    r = np.arange(128)
    m["maskU"] = (r[None, :] >= r[:, None]).astype(np.float32)
    m["sL"] = (r[:, None] > r[None, :]).astype(np.float32)
    r = np.arange(64)
    m["sLS"] = ((r[:, None] // TS == r[None, :] // TS) & (r[:, None] > r[None, :])).astype(np.float32)
    m["maskC"] = np.broadcast_to((r[None, :] // TS == np.arange(NB)[:, None]).astype(np.float32)[None], (128, NB, NS))
    r = np.arange(128)
    m["sU"] = (r[:, None] < r[None, :]).astype(np.float32)
    m["blk1"] = (r[:, None] // 64 == r[None, :] // 64).astype(np.float32)
    r = np.arange(64)
    m["sUS"] = ((r[:, None] // TS == r[None, :] // TS) & (r[:, None] < r[None, :])).astype(np.float32)
    m["rw_mu"] = _fm(inp["rw_mu"][0])
    for k in ("rw_w0", "rw_a0", "rw_k_k", "rw_k_a", "rw_lnx_w", "rw_lnx_b"):
        m[k] = _fm(inp[k][0])
    m["rw_r_k"] = _fm(inp["rw_r_k"][0].reshape(D))
    m["rw_w_rkv"] = inp["rw_w_rkv"][0]
    for k in ("rw_w1", "rw_w2", "rw_a1", "rw_a2", "rw_g1", "rw_g2", "rw_w_out"):
        m[k] = inp[k][0]
    m["st_wkv"] = inp["state_wkv"][0, b0:b0 + NB]
    m["st_shift"] = inp["state_shift"][0, b0:b0 + NB]
    m["mb_conv_w"] = np.ascontiguousarray(inp["mb_conv_w"][0].reshape(4, 24, 128).transpose(2, 1, 0))
    m["mb_conv_b"] = np.ascontiguousarray(inp["mb_conv_b"][0].reshape(24, 128).T)
    for k in ("mb_dt_bias", "mb_A_log", "mb_D"):
        m[k] = np.broadcast_to(inp[k][0][None, :], (128, 32))
    m["mb_norm"] = np.broadcast_to(inp["mb_norm"][0][None, :], (128, 2048))
    m["mb_w_in"] = inp["mb_w_in"]
    m["mb_w_out"] = inp["mb_w_out"]
    m["st_ssm"] = inp["state_ssm"][0, b0:b0 + NB]
    m["st_conv"] = inp["state_conv"][0, b0:b0 + NB]


_CACHE = {}


def run_cores(inp, TP, cores, layers=(0, 1, 2, 0), with_ffn=True):
    key = (TP, tuple(layers), with_ffn)
    bld = Builder(TP, layers, with_ffn)
    nc = bld.build()
    names = [k for k, v in bld.dram.items() if k in bld.in_names]
    in_maps = [make_in_map(inp, c, TP, names) for c in cores]
    res = run_bass_kernel_spmd(nc, in_maps, core_ids=list(range(len(cores))))
    return res.results


def kernel(**inputs):
    inp = {k: np.asarray(v) for k, v in inputs.items()}
    TP = inp["x_prompt"].shape[1]
    res = run_cores(inp, TP, list(range(NCORES)))
    st = lambda k: np.stack([r[k] for r in res], axis=0)
    cat = lambda k, ax=0: np.concatenate([r[k] for r in res], axis=ax)
    f = lambda a: np.ascontiguousarray(a, dtype=np.float32)
    y_p = st("y_p")
    y_s = cat("y_s").reshape(NCORES * NB, TS, D)
    hg_p = np.stack([r["hg_p"] for r in res], axis=1)
    hg_s = cat("hg_s", 1)
    return (f(y_p), f(y_s), f(hg_p), f(hg_s), f(st("wkv_p")[None]), f(cat("wkv_s")[None]),
            f(cat("sh_p")[None]), f(cat("sh_s")[None]), f(st("ssm_p")[None]), f(cat("ssm_s")[None]),
            f(st("cv_p")[None]), f(cat("cv_s")[None]))
```

```python
import numpy as np
from contextlib import ExitStack, contextmanager
import concourse.bass as bass
import concourse.mybir as mybir
from concourse.bass_utils import run_bass_kernel_spmd

F32 = mybir.dt.float32
BF16 = mybir.dt.bfloat16
AF = mybir.ActivationFunctionType
ALU = mybir.AluOpType
AX = mybir.AxisListType

D = 1024
KC = 8
DFF = 2816
FC = 22
NCORES = 8
NB = 16
TS = 4
NS = NB * TS
EPS = 1e-6
ROT = 30000
import os
MBSTOP = 0
RWSTOP = 0


class _Rec:
    def __init__(self):
        self.call = None

    def __getattr__(self, name):
        def f(*a, **k):
            self.call = (name, a, k)
            return self
        return f


class Sched:
    ENGS = ("pe", "act", "dve", "pool", "sp")

    def __init__(self, nc):
        self.nc = nc
        self.streams = {e: [] for e in self.ENGS}
        self.count = {e: 0 for e in self.ENGS}
        self.observed = {e: {} for e in self.ENGS}
        self.last_write = {}
        self.readers = {}
        self.dmacount = {}
        self.semkeys = []
        self.lastmark = {}

    def _semkey(self, sk):
        if sk not in self.lastmark:
            self.semkeys.append(sk)
        return sk

    def op(self, eng, fn, reads=(), writes=(), dma=None):
        rec = _Rec()
        fn(rec)
        fn = rec.call
        need = {}

        def add(m):
            if m is None:
                return
            sk, v = m
            if need.get(sk, 0) < v:
                need[sk] = v

        for k in reads:
            add(self.last_write.get(k))
            if isinstance(k, tuple) and k[0] == "ps":
                for m in self.readers.get(k, ()):
                    add(m)
        for k in writes:
            add(self.last_write.get(k))
            for m in self.readers.get(k, ()):
                add(m)
        st = self.streams[eng]
        obs = self.observed[eng]
        for sk, v in need.items():
            if eng == "pe" and sk[0] == "pe":
                continue
            if obs.get(sk, 0) >= v:
                continue
            st.append(("wait", sk, v))
            obs[sk] = v
        if dma is not None:
            sk = self._semkey(("dma", dma))
            self.dmacount[dma] = self.dmacount.get(dma, 0) + 16
            marker = (sk, self.dmacount[dma])
            amt = 16
        else:
            n = self.count[eng]
            sk = self._semkey((eng, n // ROT))
            marker = (sk, n % ROT + 1)
            self.count[eng] = n + 1
            amt = 1
        self.lastmark[sk] = marker[1]
        st.append(("op", fn, sk, amt))
        for k in writes:
            self.last_write[k] = marker
            self.readers[k] = []
        for k in reads:
            if k not in writes:
                self.readers.setdefault(k, []).append(marker)
        return marker

    def barrier(self):
        for e in self.ENGS:
            st = self.streams[e]
            obs = self.observed[e]
            for sk, v in self.lastmark.items():
                if obs.get(sk, 0) >= v:
                    continue
                st.append(("wait", sk, v))
                obs[sk] = v
        self.last_write = {}
        self.readers = {}

    def emit(self):
        nc = self.nc
        sems = {}
        for sk in self.semkeys:
            sems[sk] = nc.alloc_semaphore(name="s_" + "_".join(str(x) for x in sk))
        streams = self.streams
        engmap = {"pe": "tensor", "act": "scalar", "dve": "vector", "pool": "gpsimd", "sp": "sync"}

        def run(e, engine):
            for ent in streams[e]:
                if ent[0] == "wait":
                    engine.wait_ge(sems[ent[1]], ent[2])
                else:
                    name, a, k = ent[1]
                    ins = getattr(engine, name)(*a, **k)
                    ins.then_inc(sems[ent[2]], ent[3])

        with nc.Block() as block:
            for e in self.ENGS:
                if not streams[e]:
                    continue

                def mk(e=e):
                    def f(engine):
                        run(e, engine)
                    return f

                getattr(block, engmap[e])(mk())


def _fm(v):
    v = np.asarray(v, np.float32)
    lead = v.shape[:-1]
    v = v.reshape(lead + (KC, 128))
    v = np.moveaxis(v, -1, 0)
    return np.ascontiguousarray(v)


class Builder:
    def __init__(self, TP, layers=(0, 1, 2, 0), with_ffn=True):
        self.TP = TP
        self.N = TP + NS
        self.layers = layers
        self.with_ffn = with_ffn
        self.nc = bass.Bass("TRN2", target_bir_lowering=False)
        self.S = Sched(self.nc)
        self.dram = {}
        self.in_names = []
        self.NSTG = 3
        self.stg_i = 0
        self.ps_i = 0
        self.rr = 0
        self.tbs = [(i * 512, 512, "p") for i in range(TP // 512)] + [(TP, NS, "s")]

    def din(self, name, shape):
        t = self.nc.dram_tensor(name, list(shape), F32, kind="ExternalInput").ap()
        self.dram[name] = t
        self.in_names.append(name)
        return t

    def dout(self, name, shape):
        t = self.nc.dram_tensor(name, list(shape), F32, kind="ExternalOutput").ap()
        self.dram[name] = t
        return t

    @contextmanager
    def T(self, name, shape, dt=F32):
        self.uid = getattr(self, "uid", 0) + 1
        with self.nc.sbuf_tensor("%s_%d" % (name, self.uid), list(shape), dt) as t:
            yield t.ap()

    def sb(self, name, shape, dt=F32):
        return self.nc.alloc_sbuf_tensor(name, list(shape), dt).ap()

    def nextps(self):
        i = self.ps_i % 6
        self.ps_i += 1
        return i

    def nextlong(self):
        self.pl_i = getattr(self, "pl_i", 0) + 1
        return 6 + self.pl_i % 2

    def ew(self):
        self.rr += 1
        return "act" if self.rr % 2 else "dve"

    def copy(self, eng, out, in_, reads, writes):
        if eng == "act":
            self.S.op("act", lambda e: e.activation(out=out, in_=in_, func=AF.Copy), reads=reads, writes=writes)
        else:
            self.S.op(eng, lambda e: e.tensor_copy(out=out, in_=in_), reads=reads, writes=writes)

    def load_w(self, dst, src, shape, wkey, scale=None):
        S = self.S
        slot = self.stg_i % self.NSTG
        self.stg_i += 1
        rows = shape[0]
        free = int(np.prod(shape[1:]))
        assert free <= 1024
        st = self.stage[slot][0:rows, 0:free]
        if len(shape) == 3:
            st = st.rearrange("p (a b) -> p a b", a=shape[1])
        S.op("sp", lambda e: e.dma_start(out=st, in_=src), writes=[("stg", slot)], dma="stg%d" % slot)
        if scale is None:
            S.op("pool", lambda e: e.tensor_copy(out=dst, in_=st), reads=[("stg", slot)], writes=[wkey])
        else:
            S.op("pool", lambda e: e.tensor_scalar(out=dst, in0=st, scalar1=scale, scalar2=None, op0=ALU.mult),
                 reads=[("stg", slot), "consts"], writes=[wkey])

    def build(self):
        nc, S, TP, N = self.nc, self.S, self.TP, self.N
        nl = len(self.layers)
        x_p = self.din("x_p", [TP, D])
        x_s = self.din("x_s", [NS, D])
        ident_d = self.din("ident", [128, 128])
        nmix_d = self.din("norm_mix", [128, 4, KC])
        nffn_d = self.din("norm_ffn", [128, 4, KC])
        nfin_d = self.din("norm_final", [128, KC])
        wg_d = self.din("ffn_w_gate", [4, D, DFF])
        wu_d = self.din("ffn_w_up", [4, D, DFF])
        wd_d = self.din("ffn_w_down", [4, DFF, D])
        y_p = self.dout("y_p", [TP, D])
        y_s = self.dout("y_s", [NS, D])

        self.xres = self.sb("xres", [128, KC, N])
        self.hT = self.sb("hT", [128, KC, N + 2], BF16)
        self.stage = [self.sb("stage%d" % i, [128, 1024]) for i in range(self.NSTG)]
        self.ident = self.sb("ident_sb", [128, 128])
        self.identb = self.sb("identb_sb", [128, 128], BF16)
        self.onesb = self.sb("onesb", [128, 128], BF16)
        self.nmix = self.sb("nmix", [128, 4, KC])
        self.nffn = self.sb("nffn", [128, 4, KC])
        self.nfin = self.sb("nfin", [128, KC])
        self.ps = [nc.alloc_psum_tensor("ps%d" % i, [128, 512], F32).ap() for i in range(8)]
        xres, hT = self.xres, self.hT

        S.op("sp", lambda e: e.dma_start(out=self.ident, in_=ident_d), writes=["consts"], dma="c0")
        S.op("sp", lambda e: e.dma_start(out=self.nmix, in_=nmix_d), writes=["consts"], dma="c0")
        S.op("sp", lambda e: e.dma_start(out=self.nffn, in_=nffn_d), writes=["consts"], dma="c0")
        S.op("sp", lambda e: e.dma_start(out=self.nfin, in_=nfin_d), writes=["consts"], dma="c0")
        S.op("pool", lambda e: e.tensor_copy(out=self.identb, in_=self.ident), reads=["consts"], writes=["consts2"])
        S.op("pool", lambda e: e.memset(self.onesb, 1.0), writes=["consts2"])
        self.onesf = self.sb("onesf", [128, 128])
        S.op("pool", lambda e: e.memset(self.onesf, 1.0), writes=["consts2"])
        S.op("pool", lambda e: e.memset(hT[:, :, 0:2], 0.0), writes=["hT0"])
        self.extra_consts()
        S.barrier()

        with self.T("xin0", [128, D], F32) as xin0, self.T("xin1", [128, D], F32) as xin1:
            xins = [xin0, xin1]
            ntile = TP // 128 + 1
            for j in range(ntile):
                xin = xins[j % 2]
                rows = 128 if j < TP // 128 else NS
                src = x_p[j * 128:(j + 1) * 128, :] if j < TP // 128 else x_s
                S.op("sp", lambda e, xin=xin, rows=rows, src=src: e.dma_start(out=xin[0:rows, :], in_=src),
                     writes=[("xin", j % 2)], dma="xin%d" % (j % 2))
                for half in range(2):
                    pi = self.nextps()
                    for q in range(4):
                        kc = half * 4 + q
                        S.op("pe", lambda e, pi=pi, q=q, kc=kc, xin=xin, rows=rows: e.transpose(
                            self.ps[pi][:, q * 128:q * 128 + rows], xin[0:rows, kc * 128:(kc + 1) * 128],
                            self.ident[0:rows, 0:rows]),
                            reads=[("xin", j % 2), "consts"], writes=[("ps", pi)])
                    dst = xres[:, half * 4:(half + 1) * 4, j * 128:j * 128 + rows]
                    srcp = self.ps[pi].rearrange("p (q t) -> p q t", q=4)[:, :, 0:rows]
                    self.copy(self.ew(), dst, srcp, [("ps", pi)], [("xres", j)])
        S.barrier()

        for li, kind in enumerate(self.layers):
            self.cur_li = li
            self.rmsnorm(self.nmix[:, li, :], to_h=True)
            S.barrier()
            if kind == 0:
                self.hgrn2(li // 3)
            elif kind == 1:
                self.rwkv7(li // 3)
            elif kind == 2:
                self.mamba2(li // 3)
            S.barrier()
            if self.with_ffn:
                self.rmsnorm(self.nffn[:, li, :], to_h=True)
                S.barrier()
                self.ffn(li, wg_d, wu_d, wd_d)
                S.barrier()
        self.final_norm(y_p, y_s)
        S.barrier()
        S.emit()
        return nc

    def extra_consts(self):
        nc, S = self.nc, self.S
        def cload(name, shape):
            d = self.din(name, shape)
            t = self.sb("c_" + name, shape)
            S.op("sp", lambda e: e.dma_start(out=t, in_=d), writes=["consts"], dma="c0")
            return t
        self.maskU2 = cload("maskU2", [128, 128])
        self.maskS = cload("maskS", [64, 64])
        self.maskB = cload("maskB", [64, 16])
        lg = cload("hg_lb_logits", [128, 2, KC])
        self.hgn = cload("hg_norm", [128, 2])
        self.hg_lb = self.sb("hg_lb", [128, 2, KC])
        self.hg_oml = self.sb("hg_oml", [128, 2, KC])
        self.hg_noml = self.sb("hg_noml", [128, 2, KC])
        lb, oml, noml = self.hg_lb, self.hg_oml, self.hg_noml
        S.op("dve", lambda e: e.memset(lb[:, 0, :], 0.0), writes=["hgc0"])
        S.op("dve", lambda e: e.tensor_tensor(out=lb[:, 1, :], in0=lg[:, 1, :], in1=lg[:, 0, :], op=ALU.subtract), reads=["consts"], writes=["hgc1"])
        S.op("act", lambda e: e.activation(out=lb[:, 1, :], in_=lb[:, 1, :], func=AF.Sigmoid), reads=["hgc1"], writes=["hgc1"])
        S.op("dve", lambda e: e.tensor_scalar(out=oml, in0=lb, scalar1=-1.0, scalar2=1.0, op0=ALU.mult, op1=ALU.add), reads=["hgc0", "hgc1"], writes=["hgc2"])
        S.op("dve", lambda e: e.tensor_scalar(out=noml, in0=lb, scalar1=1.0, scalar2=-1.0, op0=ALU.mult, op1=ALU.add), reads=["hgc0", "hgc1"], writes=["hgc3"])
        self.maskU = cload("maskU", [128, 128])
        self.sL = cload("sL", [128, 128])
        self.sLS = cload("sLS", [64, 64])
        self.maskC = cload("maskC", [128, NB, NS])
        self.mb_cw = cload("mb_conv_w", [128, 24, 4])
        self.mb_cb = cload("mb_conv_b", [128, 24])
        self.mb_dtb = cload("mb_dt_bias", [128, 32])
        alog = cload("mb_A_log", [128, 32])
        self.mb_D = cload("mb_D", [128, 32])
        self.din("mb_norm", [128, 2048])
        self.mb_negA = self.sb("mb_negA", [128, 32])
        S.op("act", lambda e: e.activation(out=self.mb_negA, in_=alog, func=AF.Exp), reads=["consts"], writes=["mbc0"])
        S.op("dve", lambda e: e.tensor_scalar(out=self.mb_negA, in0=self.mb_negA, scalar1=-1.0, scalar2=None, op0=ALU.mult), reads=["mbc0"], writes=["mbc0"])
        self.din("mb_w_in", [1, D, 5152])
        self.din("mb_w_out", [1, 2048, D])
        self.din("st_ssm", [NB, 32, 64, 128])
        self.din("st_conv", [NB, 3, 3072])
        self.dout("ssm_p", [32, 64, 128])
        self.dout("ssm_s", [NB, 32, 64, 128])
        self.dout("cv_p", [3, 3072])
        self.dout("cv_s", [NB, 3, 3072])
        self.sU = cload("sU", [128, 128])
        self.sUS = cload("sUS", [64, 64])
        self.blk1 = cload("blk1", [128, 128])
        self.rw_mu = cload("rw_mu", [128, 6, KC])
        self.rw_omu = self.sb("rw_omu", [128, 6, KC])
        S.op("dve", lambda e: e.tensor_scalar(out=self.rw_omu, in0=self.rw_mu, scalar1=-1.0, scalar2=1.0, op0=ALU.mult, op1=ALU.add), reads=["consts"], writes=["rwc0"])
        self.rw_vec = {}
        for nm in ("rw_w0", "rw_a0", "rw_k_k", "rw_k_a", "rw_r_k", "rw_lnx_w", "rw_lnx_b"):
            self.rw_vec[nm] = cload(nm, [128, KC])
        self.rw_omka = self.sb("rw_omka", [128, KC])
        S.op("dve", lambda e: e.tensor_scalar(out=self.rw_omka, in0=self.rw_vec["rw_k_a"], scalar1=-1.0, scalar2=1.0, op0=ALU.mult, op1=ALU.add), reads=["consts"], writes=["rwc1"])
        for nm, shp in (("rw_w_rkv", [3, D, D]), ("rw_w1", [D, 64]), ("rw_w2", [64, D]), ("rw_a1", [D, 64]), ("rw_a2", [64, D]),
                        ("rw_g1", [D, 128]), ("rw_g2", [128, D]), ("rw_w_out", [D, D]), ("st_wkv", [NB, 16, 64, 64]), ("st_shift", [NB, D])):
            self.din(nm, shp)
        self.dout("wkv_p", [16, 64, 64])
        self.dout("wkv_s", [NB, 16, 64, 64])
        self.dout("sh_p", [1, D])
        self.dout("sh_s", [NB, D])
        self.din("hg_w_in", [2, D, 4 * D])
        self.din("hg_w_out", [2, D, D])
        self.din("st_hg", [2, NB, 8, 128, 128])
        self.dout("hg_p", [2, 8, 128, 128])
        self.dout("hg_s", [2, NB, 8, 128, 128])

    def rmsnorm(self, gain, to_h=True, out_f32=None):
        nc, S = self.nc, self.S
        xres, hT = self.xres, self.hT
        with self.T("n_sq", [128, 2, 512], BF16) as sq, self.T("n_r", [128, 2, 512], F32) as rr:
            for bi, (c0, w, kind) in enumerate(self.tbs):
                pi = self.nextps()
                for kc in range(KC):
                    s = kc % 2
                    S.op("act", lambda e, s=s, kc=kc, c0=c0, w=w: e.activation(out=sq[:, s, 0:w], in_=xres[:, kc, c0:c0 + w], func=AF.Square),
                         reads=[("xres", "all")], writes=[("n_sq", s)])
                    S.op("pe", lambda e, pi=pi, s=s, kc=kc, w=w: e.matmul(self.ps[pi][:, 0:w], lhsT=self.onesb, rhs=sq[:, s, 0:w], start=(kc == 0), stop=(kc == KC - 1)),
                         reads=[("n_sq", s), "consts2"], writes=[("ps", pi)])
                r = rr[:, bi % 2, 0:w]
                S.op("act", lambda e, pi=pi, r=r, w=w: e.activation(out=r, in_=self.ps[pi][:, 0:w], func=AF.Ln, scale=1.0 / D, bias=EPS),
                     reads=[("ps", pi)], writes=[("n_r", bi % 2)])
                S.op("act", lambda e, r=r: e.activation(out=r, in_=r, func=AF.Exp, scale=-0.5),
                     reads=[("n_r", bi % 2)], writes=[("n_r", bi % 2)])
                for kc in range(KC):
                    if out_f32 is None:
                        dst = hT[:, kc, 2 + c0:2 + c0 + w]
                    else:
                        dst = out_f32(kc, c0, w)
                    S.op("dve", lambda e, dst=dst, kc=kc, c0=c0, w=w, r=r: e.scalar_tensor_tensor(
                        out=dst, in0=xres[:, kc, c0:c0 + w], scalar=gain[:, kc:kc + 1], in1=r, op0=ALU.mult, op1=ALU.mult),
                        reads=[("xres", "all"), ("n_r", bi % 2), "consts"], writes=[("hT", kc, bi)])

    def ffn(self, li, wg_d, wu_d, wd_d):
        nc, S = self.nc, self.S
        xres, hT = self.xres, self.hT
        nprompt = len(self.tbs) - 1
        half = max(1, nprompt // 2)
        sbs = [self.tbs[:half], self.tbs[half:]] if nprompt >= 2 else [self.tbs]
        maxw = max(sum(w for (_, w, _) in sb_) for sb_ in sbs)
        with self.T("f_act", [128, FC, maxw], BF16) as act, \
                self.T("f_wgu", [128, 2, KC, 2, 256], BF16) as wgu, \
                self.T("f_wd", [128, 2, FC, 128], BF16) as wd, \
                self.T("f_sg", [128, 2, 512], F32) as sg:
            for sbi, sb_ in enumerate(sbs):
                base = sb_[0][0]
                for fp in range(FC // 2):
                    slot = fp % 2
                    for gi, wsrc in enumerate((wg_d, wu_d)):
                        for kk in range(0, KC, 4):
                            src = wsrc[li, kk * 128:(kk + 4) * 128, fp * 256:(fp + 1) * 256].rearrange("(k p) c -> p k c", p=128)
                            self.load_w(wgu[:, slot, kk:kk + 4, gi, :], src, [128, 4, 256], ("f_wgu", slot, gi, kk))
                    for fi in range(2):
                        f = fp * 2 + fi
                        for (c0, w, kind) in sb_:
                            pg, pu = self.nextps(), self.nextps()
                            for gi, pi in ((0, pg), (1, pu)):
                                for kc in range(KC):
                                    S.op("pe", lambda e, pi=pi, slot=slot, kc=kc, gi=gi, fi=fi, c0=c0, w=w: e.matmul(
                                        self.ps[pi][:, 0:w], lhsT=wgu[:, slot, kc, gi, fi * 128:(fi + 1) * 128],
                                        rhs=hT[:, kc, 2 + c0:2 + c0 + w], start=(kc == 0), stop=(kc == KC - 1)),
                                        reads=[("f_wgu", slot, gi, 0), ("f_wgu", slot, gi, 4), ("hT", "all")], writes=[("ps", pi)])
                            ss = self.rr % 2
                            self.rr += 1
                            S.op("act", lambda e, pg=pg, ss=ss, w=w: e.activation(out=sg[:, ss, 0:w], in_=self.ps[pg][:, 0:w], func=AF.Silu),
                                 reads=[("ps", pg)], writes=[("f_sg", ss)])
                            S.op("dve", lambda e, pu=pu, ss=ss, f=f, c0=c0, w=w: e.tensor_tensor(
                                out=act[:, f, c0 - base:c0 - base + w], in0=sg[:, ss, 0:w], in1=self.ps[pu][:, 0:w], op=ALU.mult),
                                reads=[("ps", pu), ("f_sg", ss)], writes=[("f_act", f)])
                for dc in range(KC):
                    slot = dc % 2
                    for f0 in range(0, FC, 8):
                        nf = min(8, FC - f0)
                        src = wd_d[li, f0 * 128:(f0 + nf) * 128, dc * 128:(dc + 1) * 128].rearrange("(k p) c -> p k c", p=128)
                        self.load_w(wd[:, slot, f0:f0 + nf, :], src, [128, nf, 128], ("f_wd", slot, f0))
                    for (c0, w, kind) in sb_:
                        pi = self.nextps()
                        for f in range(FC):
                            S.op("pe", lambda e, pi=pi, slot=slot, f=f, c0=c0, w=w: e.matmul(
                                self.ps[pi][:, 0:w], lhsT=wd[:, slot, f, :],
                                rhs=act[:, f, c0 - base:c0 - base + w], start=(f == 0), stop=(f == FC - 1)),
                                reads=[("f_wd", slot, (f // 8) * 8), ("f_act", f)], writes=[("ps", pi)])
                        S.op("dve", lambda e, pi=pi, dc=dc, c0=c0, w=w: e.tensor_tensor(
                            out=xres[:, dc, c0:c0 + w], in0=xres[:, dc, c0:c0 + w], in1=self.ps[pi][:, 0:w], op=ALU.add),
                            reads=[("ps", pi), ("xres", dc, c0)], writes=[("xres", dc, c0)])

    def final_norm(self, y_p, y_s):
        nc, S, TP = self.nc, self.S, self.TP
        xres = self.xres
        gain = self.nfin
        with self.T("fn_y", [128, KC, 512], F32) as yT, self.T("fn_o", [128, 2, D], F32) as yo, \
                self.T("fn_sq", [128, 2, 512], BF16) as sq, self.T("fn_r", [128, 512], F32) as rr:
            cnt = 0
            for bi, (c0, w, kind) in enumerate(self.tbs):
                pi = self.nextps()
                for kc in range(KC):
                    s = kc % 2
                    S.op("act", lambda e, s=s, kc=kc, c0=c0, w=w: e.activation(out=sq[:, s, 0:w], in_=xres[:, kc, c0:c0 + w], func=AF.Square),
                         reads=[("xres", "all")], writes=[("fn_sq", s)])
                    S.op("pe", lambda e, s=s, kc=kc, pi=pi, w=w: e.matmul(self.ps[pi][:, 0:w], lhsT=self.onesb, rhs=sq[:, s, 0:w], start=(kc == 0), stop=(kc == KC - 1)),
                         reads=[("fn_sq", s), "consts2"], writes=[("ps", pi)])
                r = rr[:, 0:w]
                S.op("act", lambda e, pi=pi, r=r, w=w: e.activation(out=r, in_=self.ps[pi][:, 0:w], func=AF.Ln, scale=1.0 / D, bias=EPS),
                     reads=[("ps", pi)], writes=["fn_r"])
                S.op("act", lambda e, r=r: e.activation(out=r, in_=r, func=AF.Exp, scale=-0.5), reads=["fn_r"], writes=["fn_r"])
                for kc in range(KC):
                    S.op("dve", lambda e, kc=kc, c0=c0, w=w, r=r: e.scalar_tensor_tensor(
                        out=yT[:, kc, 0:w], in0=xres[:, kc, c0:c0 + w], scalar=gain[:, kc:kc + 1], in1=r, op0=ALU.mult, op1=ALU.mult),
                        reads=[("xres", "all"), "fn_r", "consts"], writes=[("fn_y", kc)])
                for j in range((w + 127) // 128):
                    rows = min(128, w - j * 128)
                    os_ = cnt % 2
                    cnt += 1
                    for half in range(2):
                        pi = self.nextps()
                        for q in range(4):
                            kc = half * 4 + q
                            S.op("pe", lambda e, pi=pi, q=q, kc=kc, j=j, rows=rows: e.transpose(
                                self.ps[pi][0:rows, q * 128:(q + 1) * 128], yT[:, kc, j * 128:j * 128 + rows], self.ident),
                                reads=[("fn_y", kc), "consts"], writes=[("ps", pi)])
                        self.copy(self.ew(), yo[0:rows, os_, half * 512:(half + 1) * 512], self.ps[pi][0:rows, :], [("ps", pi)], [("fn_o", os_, half)])
                    if kind == "p":
                        dst = y_p[c0 + j * 128:c0 + j * 128 + rows, :]
                    else:
                        dst = y_s
                    S.op("sp", lambda e, dst=dst, os_=os_, rows=rows: e.dma_start(out=dst, in_=yo[0:rows, os_, :]),
                         reads=[("fn_o", os_, 0), ("fn_o", os_, 1)], writes=[], dma="yout%d" % os_)

    def hgrn2(self, j):
        nc, S, TP = self.nc, self.S, self.TP
        xres, hT = self.xres, self.hT
        w_in, w_out = self.dram["hg_w_in"], self.dram["hg_w_out"]
        st_hg, hg_p, hg_s = self.dram["st_hg"], self.dram["hg_p"], self.dram["hg_s"]
        ps = self.ps
        W = 512
        with ExitStack() as es:
            win = es.enter_context(self.T("h_win", [128, 2, KC, 4, 128], BF16))
            wout = es.enter_context(self.T("h_wout", [128, 2, D], BF16))
            sig = es.enter_context(self.T("h_sig", [128, W]))
            lf = es.enter_context(self.T("h_lf", [128, W]))
            kk = es.enter_context(self.T("h_kk", [128, W]))
            q = es.enter_context(self.T("h_q", [128, W]))
            gate = es.enter_context(self.T("h_gate", [128, W]))
            g = es.enter_context(self.T("h_g", [128, W]))
            tmp = es.enter_context(self.T("h_tmp", [128, W]))
            tmp2 = es.enter_context(self.T("h_tmp2", [128, W]))
            ee = es.enter_context(self.T("h_e", [128, 4, W]))
            qg = es.enter_context(self.T("h_qg", [128, W], BF16))
            kg = es.enter_context(self.T("h_kg", [128, W], BF16))
            qG = es.enter_context(self.T("h_qG", [128, W], BF16))
            kdec = es.enter_context(self.T("h_kdec", [128, W], BF16))
            vtok = es.enter_context(self.T("h_vtok", [128, 4, 128], BF16))
            vT = es.enter_context(self.T("h_vT", [128, W], BF16))
            kdtok = es.enter_context(self.T("h_kdtok", [128, 4, 128], BF16))
            osb = es.enter_context(self.T("h_osb", [128, W]))
            osq = es.enter_context(self.T("h_osq", [128, W], BF16))
            rstd = es.enter_context(self.T("h_rstd", [128, W]))
            og = es.enter_context(self.T("h_og", [128, W], BF16))
            Sr = es.enter_context(self.T("h_Sr", [128, 9, 128]))
            qGf = es.enter_context(self.T("h_qGf", [128, W]))
            attm = es.enter_context(self.T("h_attm", [128, 4, 128], BF16))
            egl = es.enter_context(self.T("h_egl", [128, 16]))
            S0 = es.enter_context(self.T("h_S0", [128, NB, 128]))
            S0bf = es.enter_context(self.T("h_S0bf", [128, NB, 128], BF16))
            Vblk = es.enter_context(self.T("h_Vblk", [64, NB, 128], BF16))
            for h in range(8):
                slot = h % 2
                for p in range(4):
                    for kk0 in (0, 4):
                        src = w_in[j, kk0 * 128:(kk0 + 4) * 128, p * D + h * 128:p * D + (h + 1) * 128].rearrange("(k p) c -> p k c", p=128)
                        self.load_w(win[:, slot, kk0:kk0 + 4, p, :], src, [128, 4, 128], ("h_win", slot, p, kk0))
                self.load_w(wout[:, slot, :], w_out[j, h * 128:(h + 1) * 128, :], [128, D], ("h_wout", slot))
                wkeys = lambda p: [("h_win", slot, p, 0), ("h_win", slot, p, 4)]
                S.op("sp", lambda e: e.dma_start(out=S0, in_=st_hg[j, :, h, :, :].rearrange("b k v -> k b v")), writes=["h_S0"], dma="h_S0")
                S.op("pool", lambda e: e.tensor_copy(out=S0bf, in_=S0), reads=["h_S0"], writes=["h_S0bf"])
                S.op("pool", lambda e: e.memset(Sr[:, 0, :], 0.0), writes=[("h_Sr", 0)])
                lbh, omlh, nomlh = self.hg_lb[:, j, h:h + 1], self.hg_oml[:, j, h:h + 1], self.hg_noml[:, j, h:h + 1]
                for (c0, w, kind) in self.tbs:
                    smp = kind == "s"
                    hc = 2 + c0
                    pq, pf, pg, pv = self.nextps(), self.nextps(), self.nextps(), self.nextps()
                    for p, pi in ((0, pq), (1, pf), (3, pg)):
                        for kc in range(KC):
                            S.op("pe", lambda e, pi=pi, p=p, kc=kc, hc=hc, w=w: e.matmul(ps[pi][:, 0:w], lhsT=win[:, slot, kc, p, :], rhs=hT[:, kc, hc:hc + w],
                                 start=(kc == 0), stop=(kc == KC - 1)), reads=wkeys(p) + [("hT", "all")], writes=[("ps", pi)])
                    ntile = (w + 127) // 128
                    rows = min(128, w)
                    for kc in range(KC):
                        S.op("pe", lambda e, kc=kc, hc=hc, w=w: e.matmul(ps[pv][:, 0:w], lhsT=win[:, slot, kc, 2, :], rhs=hT[:, kc, hc:hc + w],
                             start=(kc == 0), stop=(kc == KC - 1)), reads=wkeys(2) + [("hT", "all")], writes=[("ps", pv)])
                    self.copy("act", vT[:, 0:w], ps[pv][:, 0:w], [("ps", pv)], ["h_vT"])
                    pvt = self.nextps()
                    pvtb = ps[pvt].bitcast(BF16)
                    for jt in range(ntile):
                        S.op("pe", lambda e, jt=jt, rows=rows: e.transpose(pvtb[0:rows, jt * 128:(jt + 1) * 128], vT[:, jt * 128:jt * 128 + rows], self.identb),
                             reads=["h_vT", "consts2"], writes=[("ps", pvt)])
                    self.copy("dve", vtok[0:rows, 0:ntile, :], pvtb[:, 0:512].rearrange("p (a b) -> p a b", a=4)[0:rows, 0:ntile, :], [("ps", pvt)], ["h_vtok"])
                    S.op("act", lambda e, w=w: e.activation(out=sig[:, 0:w], in_=ps[pf][:, 0:w], func=AF.Sigmoid), reads=[("ps", pf)], writes=["h_sig"])
                    S.op("act", lambda e, w=w: e.activation(out=q[:, 0:w], in_=ps[pq][:, 0:w], func=AF.Silu), reads=[("ps", pq)], writes=["h_q"])
                    S.op("act", lambda e, w=w: e.activation(out=gate[:, 0:w], in_=ps[pg][:, 0:w], func=AF.Silu), reads=[("ps", pg)], writes=["h_gate"])
                    S.op("dve", lambda e, w=w: e.tensor_scalar(out=lf[:, 0:w], in0=sig[:, 0:w], scalar1=omlh, scalar2=lbh, op0=ALU.mult, op1=ALU.add), reads=["h_sig"], writes=["h_lf"])
                    S.op("act", lambda e, w=w: e.activation(out=lf[:, 0:w], in_=lf[:, 0:w], func=AF.Ln), reads=["h_lf"], writes=["h_lf"])
                    S.op("dve", lambda e, w=w: e.tensor_scalar(out=kk[:, 0:w], in0=sig[:, 0:w], scalar1=nomlh, scalar2=omlh, op0=ALU.mult, op1=ALU.add), reads=["h_sig"], writes=["h_kk"])
                    if not smp:
                        nch = w // 64
                        for c in range(nch):
                            S.op("dve", lambda e, c=c: e.tensor_tensor_scan(out=g[:, c * 64:(c + 1) * 64], data0=self.onesf[:, 0:64], data1=lf[:, c * 64:(c + 1) * 64],
                                 initial=0.0, op0=ALU.mult, op1=ALU.add), reads=["h_lf", "consts2"], writes=["h_g"])
                        g3 = g.rearrange("p (c t) -> p c t", t=64)
                        bc = lambda col: g3[:, :, col:col + 1].broadcast_to([128, nch, 64])
                        v3 = lambda t_: t_.rearrange("p (c t) -> p c t", t=64)
                        S.op("dve", lambda e: e.tensor_tensor(out=v3(tmp), in0=g3, in1=bc(31), op=ALU.subtract), reads=["h_g"], writes=["h_tmp"])
                        S.op("dve", lambda e: e.tensor_tensor(out=v3(tmp2), in0=bc(63), in1=g3, op=ALU.subtract), reads=["h_g"], writes=["h_tmp2"])
                        S.op("act", lambda e: e.activation(out=ee[:, 0, :], in_=tmp, func=AF.Exp), reads=["h_tmp"], writes=[("h_e", 0)])
                        S.op("act", lambda e: e.activation(out=ee[:, 1, :], in_=tmp, func=AF.Exp, scale=-1.0), reads=["h_tmp"], writes=[("h_e", 1)])
                        S.op("act", lambda e: e.activation(out=ee[:, 2, :], in_=g, func=AF.Exp), reads=["h_g"], writes=[("h_e", 2)])
                        S.op("act", lambda e: e.activation(out=ee[:, 3, :], in_=tmp2, func=AF.Exp), reads=["h_tmp2"], writes=[("h_e", 3)])
                        S.op("act", lambda e: e.activation(out=egl[:, 0:nch], in_=g3[:, :, 63], func=AF.Exp), reads=["h_g"], writes=["h_egl"])
                        S.op("pool", lambda e: e.tensor_tensor(out=qg, in0=q, in1=ee[:, 0, :], op=ALU.mult), reads=["h_q", ("h_e", 0)], writes=["h_qg"])
                        S.op("pool", lambda e: e.tensor_tensor(out=kg, in0=kk, in1=ee[:, 1, :], op=ALU.mult), reads=["h_kk", ("h_e", 1)], writes=["h_kg"])
                        S.op("dve", lambda e: e.tensor_tensor(out=qGf, in0=q, in1=ee[:, 2, :], op=ALU.mult), reads=["h_q", ("h_e", 2)], writes=["h_qGf"])
                        S.op("pool", lambda e: e.tensor_tensor(out=kdec, in0=kk, in1=ee[:, 3, :], op=ALU.mult), reads=["h_kk", ("h_e", 3)], writes=["h_kdec"])
                    else:
                        g3 = g[:, 0:NS].rearrange("p (b t) -> p b t", t=TS)
                        lf3 = lf[:, 0:NS].rearrange("p (b t) -> p b t", t=TS)
                        S.op("dve", lambda e: e.tensor_copy(out=g3[:, :, 0:1], in_=lf3[:, :, 0:1]), reads=["h_lf"], writes=["h_g"])
                        for t_ in range(1, TS):
                            S.op("dve", lambda e, t_=t_: e.tensor_tensor(out=g3[:, :, t_:t_ + 1], in0=g3[:, :, t_ - 1:t_], in1=lf3[:, :, t_:t_ + 1], op=ALU.add), reads=["h_lf", "h_g"], writes=["h_g"])
                        t23 = tmp2[:, 0:NS].rearrange("p (b t) -> p b t", t=TS)
                        S.op("dve", lambda e: e.tensor_tensor(out=t23, in0=g3[:, :, 3:4].broadcast_to([128, NB, TS]), in1=g3, op=ALU.subtract), reads=["h_g"], writes=["h_tmp2"])
                        S.op("act", lambda e: e.activation(out=ee[:, 1, 0:NS], in_=g[:, 0:NS], func=AF.Exp, scale=-1.0), reads=["h_g"], writes=[("h_e", 1)])
                        S.op("act", lambda e: e.activation(out=ee[:, 2, 0:NS], in_=g[:, 0:NS], func=AF.Exp), reads=["h_g"], writes=[("h_e", 2)])
                        S.op("act", lambda e: e.activation(out=ee[:, 3, 0:NS], in_=tmp2[:, 0:NS], func=AF.Exp), reads=["h_tmp2"], writes=[("h_e", 3)])
                        S.op("act", lambda e: e.activation(out=egl[:, 0:NB], in_=g3[:, :, 3], func=AF.Exp), reads=["h_g"], writes=["h_egl"])
                        S.op("pool", lambda e: e.tensor_tensor(out=kg[:, 0:NS], in0=kk[:, 0:NS], in1=ee[:, 1, 0:NS], op=ALU.mult), reads=["h_kk", ("h_e", 1)], writes=["h_kg"])
                        S.op("dve", lambda e: e.tensor_tensor(out=qG[:, 0:NS], in0=q[:, 0:NS], in1=ee[:, 2, 0:NS], op=ALU.mult), reads=["h_q", ("h_e", 2)], writes=["h_qG"])
                        S.op("pool", lambda e: e.tensor_tensor(out=kdec[:, 0:NS], in0=kk[:, 0:NS], in1=ee[:, 3, 0:NS], op=ALU.mult), reads=["h_kk", ("h_e", 3)], writes=["h_kdec"])
                    pt = self.nextps()
                    ptb = ps[pt].bitcast(BF16)
                    for jt in range(ntile):
                        S.op("pe", lambda e, jt=jt, rows=rows: e.transpose(ptb[0:rows, jt * 128:(jt + 1) * 128], kdec[:, jt * 128:jt * 128 + rows], self.identb),
                             reads=["h_kdec", "consts2"], writes=[("ps", pt)])
                    self.copy("dve", kdtok[0:rows, 0:ntile, :], ptb[:, 0:512].rearrange("p (a b) -> p a b", a=4)[0:rows, 0:ntile, :], [("ps", pt)], ["h_kdtok"])
                    po = self.nextlong()
                    if not smp:
                        nchk = w // 64
                        if c0 > 0:
                            S.op("dve", lambda e: e.tensor_copy(out=Sr[:, 0, :], in_=Sr[:, 8, :]), reads=[("h_Sr", c_) for c_ in range(9)], writes=[("h_Sr", 0)])
                        pa = self.nextps()
                        for jt in range(ntile):
                            S.op("pe", lambda e, jt=jt: e.matmul(ps[pa][:, jt * 128:(jt + 1) * 128], lhsT=kg[:, jt * 128:(jt + 1) * 128], rhs=qg[:, jt * 128:(jt + 1) * 128], start=True, stop=True),
                                 reads=["h_kg", "h_qg"], writes=[("ps", pa)])
                        S.op("dve", lambda e: e.tensor_tensor(out=attm[:, 0:ntile, :], in0=ps[pa][:, 0:ntile * 128].rearrange("p (a b) -> p a b", a=ntile),
                             in1=self.maskU2.unsqueeze(1).broadcast_to([128, ntile, 128]), op=ALU.mult), reads=[("ps", pa), "consts"], writes=["h_attm"])
                        pus = [self.nextps(), self.nextps()]
                        for c in range(nchk):
                            jt, cc = c // 2, c % 2
                            S.op("pe", lambda e, c=c, cc=cc, jt=jt: e.matmul(ps[pus[c % 2]][:, (c // 2) * 128:(c // 2 + 1) * 128], lhsT=kdtok[cc * 64:(cc + 1) * 64, jt, :], rhs=vtok[cc * 64:(cc + 1) * 64, jt, :], start=True, stop=True),
                                 reads=["h_kdtok", "h_vtok"], writes=[("ps", pus[c % 2])])
                        for c in range(nchk):
                            S.op("dve", lambda e, c=c: e.scalar_tensor_tensor(out=Sr[:, c + 1, :], in0=Sr[:, c, :], scalar=egl[:, c:c + 1], in1=ps[pus[c % 2]][:, (c // 2) * 128:(c // 2 + 1) * 128], op0=ALU.mult, op1=ALU.add),
                                 reads=[("ps", pus[c % 2]), ("h_Sr", c), "h_egl"], writes=[("h_Sr", c + 1)])
                        for jt in range(ntile):
                            S.op("pe", lambda e, jt=jt: e.matmul(ps[po][:, jt * 128:(jt + 1) * 128], lhsT=vtok[:, jt, :], rhs=attm[:, jt, :], start=True, stop=False),
                                 reads=["h_vtok", "h_attm"], writes=[("ps", po)])
                            for cc in range(2):
                                c = jt * 2 + cc
                                S.op("pe", lambda e, c=c, cc=cc: e.matmul(ps[po][:, c * 64:(c + 1) * 64], lhsT=Sr[:, c, :], rhs=qGf[:, c * 64:(c + 1) * 64], start=False, stop=(cc == 1)),
                                     reads=[("h_Sr", c), "h_qGf"], writes=[("ps", po)])
                        if c0 + w == TP:
                            S.op("sp", lambda e: e.dma_start(out=hg_p[j, h], in_=Sr[:, 8, :]), reads=[("h_Sr", 8)], dma="hg_p")
                    else:
                        pa = self.nextps()
                        S.op("pe", lambda e, pa=pa: e.matmul(ps[pa][0:NS, 0:NS], lhsT=kg[:, 0:NS], rhs=qG[:, 0:NS], start=True, stop=True), reads=["h_kg", "h_qG"], writes=[("ps", pa)])
                        S.op("dve", lambda e, pa=pa: e.tensor_tensor(out=attm[0:NS, 0, 0:NS], in0=ps[pa][0:NS, 0:NS], in1=self.maskS, op=ALU.mult), reads=[("ps", pa), "consts"], writes=[("h_attm", 0)])
                        S.op("pe", lambda e: e.matmul(ps[po][:, 0:NS], lhsT=vtok[0:NS, 0, :], rhs=attm[0:NS, 0, 0:NS], start=True, stop=False), reads=["h_vtok", ("h_attm", 0)], writes=[("ps", po)])
                        for b in range(NB):
                            S.op("pe", lambda e, b=b: e.matmul(ps[po][:, b * TS:(b + 1) * TS], lhsT=S0bf[:, b, :], rhs=qG[:, b * TS:(b + 1) * TS], start=False, stop=(b == NB - 1)),
                                 reads=["h_S0bf", "h_qG"], writes=[("ps", po)])
                        S.op("dve", lambda e: e.tensor_tensor(out=Vblk, in0=vtok[0:NS, 0, :].unsqueeze(1).broadcast_to([NS, NB, 128]),
                             in1=self.maskB.unsqueeze(2).broadcast_to([NS, NB, 128]), op=ALU.mult), reads=["h_vtok", "consts"], writes=["h_Vblk"])
                        S.op("dve", lambda e: e.tensor_tensor(out=S0, in0=S0, in1=egl[:, 0:NB].unsqueeze(2).broadcast_to([128, NB, 128]), op=ALU.mult), reads=["h_S0", "h_egl", "h_S0bf"], writes=["h_S0"])
                        for bq in range(4):
                            pu = self.nextps()
                            S.op("pe", lambda e, pu=pu, bq=bq: e.matmul(ps[pu][:, 0:512], lhsT=kdtok[0:NS, 0, :], rhs=Vblk[:, bq * 4:(bq + 1) * 4, :], start=True, stop=True),
                                 reads=["h_kdtok", "h_Vblk"], writes=[("ps", pu)])
                            S.op("dve", lambda e, pu=pu, bq=bq: e.tensor_tensor(out=S0[:, bq * 4:(bq + 1) * 4, :], in0=S0[:, bq * 4:(bq + 1) * 4, :],
                                 in1=ps[pu].rearrange("p (a b) -> p a b", a=4), op=ALU.add), reads=[("ps", pu), "h_S0"], writes=["h_S0"])
                        S.op("sp", lambda e: e.dma_start(out=hg_s[j, :, h, :, :].rearrange("b k v -> k b v"), in_=S0), reads=["h_S0"], dma="hg_s")
                    S.op("act", lambda e, w=w: e.activation(out=osb[:, 0:w], in_=ps[po][:, 0:w], func=AF.Copy), reads=[("ps", po)], writes=["h_osb"])
                    S.op("act", lambda e, w=w: e.activation(out=osq[:, 0:w], in_=ps[po][:, 0:w], func=AF.Square), reads=[("ps", po)], writes=["h_osq"])
                    pn = self.nextps()
                    S.op("pe", lambda e, pn=pn, w=w: e.matmul(ps[pn][:, 0:w], lhsT=self.onesb, rhs=osq[:, 0:w], start=True, stop=True), reads=["h_osq", "consts2"], writes=[("ps", pn)])
                    S.op("act", lambda e, pn=pn, w=w: e.activation(out=rstd[:, 0:w], in_=ps[pn][:, 0:w], func=AF.Ln, scale=1.0 / 128, bias=EPS), reads=[("ps", pn)], writes=["h_rstd"])
                    S.op("act", lambda e, w=w: e.activation(out=rstd[:, 0:w], in_=rstd[:, 0:w], func=AF.Exp, scale=-0.5), reads=["h_rstd"], writes=["h_rstd"])
                    S.op("dve", lambda e, w=w: e.tensor_tensor(out=osb[:, 0:w], in0=osb[:, 0:w], in1=rstd[:, 0:w], op=ALU.mult), reads=["h_osb", "h_rstd"], writes=["h_osb"])
                    S.op("dve", lambda e, w=w: e.scalar_tensor_tensor(out=og[:, 0:w], in0=osb[:, 0:w], scalar=self.hgn[:, j:j + 1], in1=gate[:, 0:w], op0=ALU.mult, op1=ALU.mult),
                         reads=["h_osb", "h_gate", "consts"], writes=["h_og"])
                    for dc in range(KC):
                        pi = self.nextps()
                        S.op("pe", lambda e, pi=pi, dc=dc, w=w: e.matmul(ps[pi][:, 0:w], lhsT=wout[:, slot, dc * 128:(dc + 1) * 128], rhs=og[:, 0:w], start=True, stop=True),
                             reads=[("h_wout", slot), "h_og"], writes=[("ps", pi)])
                        S.op("dve", lambda e, pi=pi, dc=dc, c0=c0, w=w: e.tensor_tensor(out=xres[:, dc, c0:c0 + w], in0=xres[:, dc, c0:c0 + w], in1=ps[pi][:, 0:w], op=ALU.add),
                             reads=[("ps", pi), ("xres", dc, c0)], writes=[("xres", dc, c0)])

    def rwkv7(self, j):
        nc, S, TP = self.nc, self.S, self.TP
        xres, hT, ps = self.xres, self.hT, self.ps
        dr = self.dram
        V = self.rw_vec
        GN_EPS = 64e-5
        li = self.cur_li
        with ExitStack() as es0:
            T0 = lambda n, sh, dt=F32: es0.enter_context(self.T(n, sh, dt))
            nblk = len(self.tbs)
            edge = T0("r_edge", [128, KC, 2 * nblk + 2], BF16)
            shiftT = T0("r_shiftT", [128, KC, NB], BF16)
            l1T = T0("r_l1T", [128, 3, self.N], BF16)
            prevS = T0("r_prevS", [128, KC, NS], BF16)
            prevB = T0("r_prevB", [128, KC, 512], BF16)

            def fill_prev(c0, w, kind):
                if kind == "p":
                    S.op("dve", lambda e: e.tensor_copy(out=prevB[:, :, 0:w], in_=hT[:, :, 1 + c0:1 + c0 + w]), reads=[("hT", "all"), "hT0"], writes=["r_prevB"])
            with ExitStack() as es:
                T = lambda n, sh, dt=F32: es.enter_context(self.T(n, sh, dt))
                shin, xsh, sq, rr = T("r_shin", [32, D]), T("r_xsh", [128, KC, 32]), T("r_sq", [128, KC, 32]), T("r_rr", [128, 32])
                shrow = T("r_shrow", [32, D])
                S.op("pool", lambda e: e.memset(shin, 0.0), writes=["r_shin"])
                S.op("pool", lambda e: e.memset(xsh, 0.0), writes=["r_xsh"])
                S.op("pool", lambda e: e.memset(edge, 0.0), writes=["r_edge"])
                S.op("sp", lambda e: e.dma_start(out=shin[0:NB, :], in_=dr["st_shift"]), reads=[], writes=["r_shin"], dma="r_shin")
                for half in range(2):
                    pi = self.nextps()
                    for q in range(4):
                        kc = half * 4 + q
                        S.op("pe", lambda e, q=q, kc=kc, pi=pi: e.transpose(ps[pi][:, q * 32:(q + 1) * 32], shin[:, kc * 128:(kc + 1) * 128], self.ident[0:32, 0:32]), reads=["r_shin", "consts"], writes=[("ps", pi)])
                    self.copy("dve", shiftT[:, half * 4:(half + 1) * 4, :], ps[pi][:, 0:128].rearrange("p (q t) -> p q t", q=4)[:, :, 0:NB], [("ps", pi)], ["r_shiftT"])
                S.op("dve", lambda e: e.tensor_copy(out=xsh[:, :, 0:1], in_=xres[:, :, TP - 1:TP]), reads=[("xres", "all"), "r_xsh"], writes=["r_xsh"])
                S.op("dve", lambda e: e.tensor_copy(out=xsh[:, :, 1:1 + NB], in_=xres[:, :, TP:TP + NS].rearrange("p k (b t) -> p k b t", t=TS)[:, :, :, 3]), reads=[("xres", "all"), "r_xsh"], writes=["r_xsh"])
                S.op("act", lambda e: e.activation(out=sq, in_=xsh, func=AF.Square), reads=["r_xsh"], writes=["r_sq"])
                pi = self.nextps()
                for kc in range(KC):
                    S.op("pe", lambda e, kc=kc: e.matmul(ps[pi][:, 0:32], lhsT=self.onesf, rhs=sq[:, kc, :], start=(kc == 0), stop=(kc == KC - 1)), reads=["r_sq", "consts2"], writes=[("ps", pi)])
                S.op("act", lambda e: e.activation(out=rr, in_=ps[pi][:, 0:32], func=AF.Ln, scale=1.0 / D, bias=EPS), reads=[("ps", pi)], writes=["r_rr"])
                S.op("act", lambda e: e.activation(out=rr, in_=rr, func=AF.Exp, scale=-0.5), reads=["r_rr"], writes=["r_rr"])
                for kc in range(KC):
                    S.op("dve", lambda e, kc=kc: e.scalar_tensor_tensor(out=xsh[:, kc, :], in0=xsh[:, kc, :], scalar=self.nmix[:, li, kc:kc + 1], in1=rr, op0=ALU.mult, op1=ALU.mult), reads=["r_xsh", "r_rr", "consts"], writes=["r_xsh"])
                for half in range(2):
                    pi = self.nextps()
                    for q in range(4):
                        kc = half * 4 + q
                        S.op("pe", lambda e, q=q, kc=kc, pi=pi: e.transpose(ps[pi][0:32, q * 128:(q + 1) * 128], xsh[:, kc, :], self.ident), reads=["r_xsh", "consts"], writes=[("ps", pi)])
                    self.copy("act", shrow[:, half * 512:(half + 1) * 512], ps[pi][0:32, :], [("ps", pi)], [("r_shrow", half)])
                S.op("sp", lambda e: e.dma_start(out=dr["sh_p"], in_=shrow[0:1, :]), reads=[("r_shrow", 0), ("r_shrow", 1)], dma="r_sh")
                S.op("sp", lambda e: e.dma_start(out=dr["sh_s"], in_=shrow[1:1 + NB, :]), reads=[("r_shrow", 0), ("r_shrow", 1)], dma="r_sh")
                for bi, (c0, w, kind) in enumerate(self.tbs):
                    if kind == "p" and c0 > 0:
                        S.op("dve", lambda e, bi=bi, c0=c0: e.tensor_copy(out=edge[:, :, 2 * bi:2 * bi + 1], in_=hT[:, :, 1 + c0:2 + c0]), reads=[("hT", "all"), "r_edge"], writes=["r_edge"])
                hs3 = hT[:, :, 2 + TP:2 + TP + NS].rearrange("p k (b t) -> p k b t", t=TS)
                pv3 = prevS.rearrange("p k (b t) -> p k b t", t=TS)
                for kc in range(KC):
                    S.op("dve", lambda e, kc=kc: e.tensor_copy(out=pv3[:, kc, :, 1:TS], in_=hs3[:, kc, :, 0:TS - 1]), reads=[("hT", "all")], writes=["r_prevS"])
                    S.op("dve", lambda e, kc=kc: e.tensor_copy(out=pv3[:, kc, :, 0], in_=shiftT[:, kc, :]), reads=["r_shiftT", "r_prevS"], writes=["r_prevS"])
            S.barrier()

            def shifted_proj(pi, w_a, w_b, c0, w, kind, rd, M=128):
                hc = 2 + c0
                out = ps[pi][0:M, 0:w]
                for kc in range(KC):
                    S.op("pe", lambda e, kc=kc: e.matmul(out, lhsT=w_a(kc), rhs=hT[:, kc, hc:hc + w], start=(kc == 0), stop=False), reads=rd + [("hT", "all")], writes=[("ps", pi)])
                if kind == "p":
                    for kc in range(KC):
                        S.op("pe", lambda e, kc=kc: e.matmul(out, lhsT=w_b(kc), rhs=prevB[:, kc, 0:w], start=False, stop=(kc == KC - 1)), reads=rd + ["r_prevB"], writes=[("ps", pi)])
                else:
                    for kc in range(KC):
                        S.op("pe", lambda e, kc=kc: e.matmul(out, lhsT=w_b(kc), rhs=prevS[:, kc, :], start=False, stop=(kc == KC - 1)), reads=rd + ["r_prevS"], writes=[("ps", pi)])

            nq = TP // 256
            self.r_e2 = T0("r_e2", [128, KC, 2 * nq], BF16)
            e2 = self.r_e2
            S.op("pool", lambda e: e.memset(e2, 0.0), writes=["r_e2"])
            for q in range(nq):
                c0 = q * 256
                if c0 > 0:
                    S.op("dve", lambda e, q=q, c0=c0: e.tensor_copy(out=e2[:, :, 2 * q:2 * q + 1], in_=hT[:, :, 1 + c0:2 + c0]), reads=[("hT", "all"), "r_e2"], writes=["r_e2"])
            S.barrier()
            rblocks = [(q * 256, 256, "p") for q in range(nq)] + [(TP, NS, "s")]

            def load_scaled(dst_a, dst_b, src, shape, n, kk0, nk, key):
                slot = self.stg_i % self.NSTG
                self.stg_i += 1
                cols = shape[2]
                st = self.stage[slot][:, 0:nk * cols].rearrange("p (a b) -> p a b", a=nk)
                S.op("sp", lambda e: e.dma_start(out=st, in_=src), writes=[("stg", slot)], dma="stg%d" % slot)
                for q in range(nk):
                    kc = kk0 + q
                    S.op("dve", lambda e, q=q, kc=kc: e.tensor_scalar(out=dst_a(kc), in0=st[:, q, :], scalar1=self.rw_omu[:, n, kc:kc + 1], scalar2=1.0, op0=ALU.mult, op1=ALU.mult), reads=[("stg", slot), "rwc0"], writes=[key])
                    S.op("dve", lambda e, q=q, kc=kc: e.tensor_scalar(out=dst_b(kc), in0=st[:, q, :], scalar1=self.rw_mu[:, n, kc:kc + 1], scalar2=1.0, op0=ALU.mult, op1=ALU.mult), reads=[("stg", slot), "consts"], writes=[key])

            with ExitStack() as es:
                T = lambda n, sh, dt=F32: es.enter_context(self.T(n, sh, dt))
                wl = T("r_wl", [128, 3, 2, KC, 128], BF16)
                for li_, (nm, n, cols) in enumerate((("rw_w1", 3, 64), ("rw_a1", 4, 64), ("rw_g1", 5, 128))):
                    src = dr[nm].rearrange("(k p) c -> p k c", p=128)
                    load_scaled(lambda kc, li_=li_, cols=cols: wl[:, li_, 0, kc, 0:cols], lambda kc, li_=li_, cols=cols: wl[:, li_, 1, kc, 0:cols], src, [128, KC, cols], n, 0, KC, ("r_wl", li_))
                for (c0, w, kind) in rblocks:
                    fill_prev(c0, w, kind)
                    for li_, (M, fn) in enumerate(((64, AF.Tanh), (64, AF.Copy), (128, AF.Sigmoid))):
                        pi = self.nextps()
                        shifted_proj(pi, lambda kc, li_=li_, M=M: wl[:, li_, 0, kc, 0:M], lambda kc, li_=li_, M=M: wl[:, li_, 1, kc, 0:M], c0, w, kind, [("r_wl", li_)], M=M)
                        S.op("act", lambda e, li_=li_, M=M, fn=fn, pi=pi: e.activation(out=l1T[0:M, li_, c0:c0 + w], in_=ps[pi][0:M, 0:w], func=fn), reads=[("ps", pi)], writes=[("r_l1T", li_)])
            S.barrier()

            for sample_pass in (False, True):
                blocks = [b_ for b_ in rblocks if (b_[2] == "s") == sample_pass]
                Wd = NS if sample_pass else 256
                R = NS if sample_pass else 128
                nch = 1 if sample_pass else 2
                nlev = 1 if sample_pass else 6
                mStrict = self.sUS if sample_pass else self.sU
                mIncl = self.maskS if sample_pass else self.maskU
                mLow = self.sLS if sample_pass else self.sL
                with ExitStack() as es:
                    T = lambda n, sh, dt=F32: es.enter_context(self.T(n, sh, dt))
                    wrkv = T("r_wrkv", [128, 3, 2, KC, 128], BF16)
                    w2nd = T("r_w2nd", [128, 3, 128], BF16)
                    wout2 = [T("r_wout", [128, D], BF16) for _ in range(2)]
                    rfA, kfA, vfA, lwf, af = [T("r_f%d" % i_, [128, Wd]) for i_ in range(5)]
                    rkvA = [[rfA, kfA, vfA, None], [T("r_rf2", [128, Wd]), T("r_kf2", [128, Wd]), T("r_vf2", [128, Wd]), T("r_vb2", [128, Wd], BF16)]]
                    gf2 = [T("r_gf", [128, Wd]) for _ in range(2)]
                    bon2 = [T("r_bon", [128, Wd]) for _ in range(2)]
                    G, t1, e0, e1 = T("r_G", [128, Wd]), T("r_t1", [128, Wd]), T("r_e0", [128, Wd]), T("r_e1", [128, Wd])
                    kap, k2, beta, osb = T("r_kap", [128, Wd]), T("r_k2", [128, Wd]), T("r_beta", [128, Wd]), T("r_osb", [128, Wd])
                    KR2 = [T("r_KR", [128, nch, 2, 128], BF16) for _ in range(2)]
                    BK2 = [T("r_BK", [128, nch, 2, 128], BF16) for _ in range(2)]
                    btT, ktT, vbA, og = T("r_btT", [128, Wd], BF16), T("r_ktT", [128, Wd], BF16), T("r_vb", [128, Wd], BF16), T("r_og", [128, Wd], BF16)
                    rkvA[0][3] = vbA
                    bttok2 = [T("r_bttok", [128, nch, 128], BF16) for _ in range(2)]
                    kttok2 = [T("r_kttok", [128, nch, 128], BF16) for _ in range(2)]
                    vtok2 = [T("r_vtok", [128, nch, 128], BF16) for _ in range(2)]
                    Nk, Ak = T("r_Nk", [128, 2, 2 * nch, 128], BF16), T("r_Ak", [128, 2, 2 * nch, 128], BF16)
                    Wm = T("r_Wm", [128, 2 * nch, 128], BF16)
                    MbT, BTm, MkT = T("r_MbT", [128, 2 * nch, 128], BF16), T("r_BTm", [128, 2 * nch, 128], BF16), T("r_MkT", [128, 2 * nch, 128], BF16)
                    EC2 = [T("r_EC", [128, NB]) for _ in range(2)]
                    t2 = T("r_t2", [128, Wd])
                    Xsb, Ssb = T("r_Xsb", [128, 2, 64], BF16), T("r_Ssb", [128, 2, 64], BF16)
                    Sst, Sbf = T("r_Sst", [128, 64]), T("r_Sbf", [128, 64], BF16)
                    sT = T("r_sT", [64, 128])
                    if sample_pass:
                        s0nat = T("r_s0nat", [64, NB, 128])
                        S0, S0bf = T("r_S0", [128, NB, 64]), T("r_S0bf", [128, NB, 64], BF16)
                        kblk = T("r_kblk", [128, NB, NS], BF16)
                        rblk = T("r_rblk", [128, NB, NS], BF16)
                        Sblk, Vblk = T("r_Sblk", [128, 2, NB, 64], BF16), T("r_Vblk", [128, 2, NB, 64], BF16)
                        for t_, k_ in ((Ssb, ("r_Ssb", 0)), (MbT, "r_zMbT"), (MkT, "r_zMkT"), (BTm, "r_zBTm"), (vtok2[0], ("r_vtok", 0)), (vtok2[1], ("r_vtok", 1)), (bttok2[0], ("r_bttok", 0)), (bttok2[1], ("r_bttok", 1)), (kttok2[0], ("r_kttok", 0)), (kttok2[1], ("r_kttok", 1)), (Sblk, ("r_Sblk", 0)), (Vblk, ("r_Vblk", 0))):
                            S.op("pool", lambda e, t_=t_: e.memset(t_, 0.0), writes=[k_])
                        S.barrier()
                    def blk(pc, bi_, c0, w, kind, par):
                        cs = slice(pc * 128, (pc + 1) * 128)
                        KR, BK, bt_tok, kt_tok, v_tok, EC, gf, bon = KR2[par], BK2[par], bttok2[par], kttok2[par], vtok2[par], EC2[par], gf2[par], bon2[par]
                        wout = wout2[pc % 2]
                        if bi_ == 0:
                            for n in range(3):
                                for kk0 in (0, 4):
                                    src = dr["rw_w_rkv"][n, kk0 * 128:(kk0 + 4) * 128, cs].rearrange("(k p) c -> p k c", p=128)
                                    load_scaled(lambda kc, n=n: wrkv[:, n, 0, kc, :], lambda kc, n=n: wrkv[:, n, 1, kc, :], src, [128, 4, 128], n, kk0, 4, ("r_wrkv", n, kk0))
                            self.load_w(w2nd[0:64, 0, :], dr["rw_w2"][:, cs], [64, 128], ("r_w2nd", 0))
                            self.load_w(w2nd[0:64, 1, :], dr["rw_a2"][:, cs], [64, 128], ("r_w2nd", 1))
                            self.load_w(w2nd[:, 2, :], dr["rw_g2"][:, cs], [128, 128], ("r_w2nd", 2))
                            self.load_w(wout, dr["rw_w_out"][cs, :], [128, D], ("r_wout", pc % 2))
                        wk = lambda n: [("r_wrkv", n, 0), ("r_wrkv", n, 4)]
                        vec = lambda nm: V[nm][:, pc:pc + 1]
                        for _once in (0,):
                            half = bi_ % 2 if kind == "p" else 0
                            rf, kf, vf, vb = rkvA[half]
                            if kind == "s" or half == 0:
                                wp = w if kind == "s" else 2 * w
                                fill_prev(c0, wp, kind)
                                pr, pk, pv = self.nextps(), self.nextps(), self.nextps()
                                for n, pi in ((0, pr), (1, pk), (2, pv)):
                                    shifted_proj(pi, lambda kc, n=n: wrkv[:, n, 0, kc, :], lambda kc, n=n: wrkv[:, n, 1, kc, :], c0, wp, kind, wk(n))
                                for hh_ in range(wp // w):
                                    rf_, kf_, vf_, vb_ = rkvA[hh_]
                                    cs_ = slice(hh_ * w, (hh_ + 1) * w)
                                    self.copy("act", rf_[:, 0:w], ps[pr][:, cs_], [("ps", pr)], [("r_rf", hh_)])
                                    self.copy("act", kf_[:, 0:w], ps[pk][:, cs_], [("ps", pk)], [("r_kf", hh_)])
                                    self.copy("act", vf_[:, 0:w], ps[pv][:, cs_], [("ps", pv)], [("r_vf", hh_)])
                                    S.op("dve", lambda e, vb_=vb_, cs_=cs_: e.tensor_copy(out=vb_[:, 0:w], in_=ps[pv][:, cs_]), reads=[("ps", pv)], writes=[("r_vb", hh_)])
                            yield 0
                            pw, pa, pg = self.nextps(), self.nextps(), self.nextps()
                            S.op("pe", lambda e: e.matmul(ps[pw][:, 0:w], lhsT=w2nd[0:64, 0, :], rhs=l1T[0:64, 0, c0:c0 + w], start=True, stop=True), reads=[("r_w2nd", 0), ("r_l1T", 0)], writes=[("ps", pw)])
                            S.op("pe", lambda e: e.matmul(ps[pa][:, 0:w], lhsT=w2nd[0:64, 1, :], rhs=l1T[0:64, 1, c0:c0 + w], start=True, stop=True), reads=[("r_w2nd", 1), ("r_l1T", 1)], writes=[("ps", pa)])
                            S.op("pe", lambda e: e.matmul(ps[pg][:, 0:w], lhsT=w2nd[:, 2, :], rhs=l1T[:, 2, c0:c0 + w], start=True, stop=True), reads=[("r_w2nd", 2), ("r_l1T", 2)], writes=[("ps", pg)])
                            S.op("act", lambda e: e.activation(out=lwf[:, 0:w], in_=ps[pw][:, 0:w], func=AF.Sigmoid, bias=vec("rw_w0")), reads=[("ps", pw), "consts"], writes=["r_lwf"])
                            S.op("dve", lambda e: e.tensor_scalar(out=lwf[:, 0:w], in0=lwf[:, 0:w], scalar1=-0.6065306597126334, scalar2=1.0, op0=ALU.mult, op1=ALU.mult), reads=["r_lwf"], writes=["r_lwf"])
                            S.op("act", lambda e: e.activation(out=af[:, 0:w], in_=ps[pa][:, 0:w], func=AF.Sigmoid, bias=vec("rw_a0")), reads=[("ps", pa), "consts"], writes=["r_af"])
                            self.copy("act", gf[:, 0:w], ps[pg][:, 0:w], [("ps", pg)], [("r_gf", par)])
                            yield 0
                            if not sample_pass:
                                for c in range(nch):
                                    S.op("dve", lambda e, c=c: e.tensor_tensor_scan(out=G[:, c * 128:(c + 1) * 128], data0=self.onesf[:, 0:128], data1=lwf[:, c * 128:(c + 1) * 128], initial=0.0, op0=ALU.mult, op1=ALU.add), reads=["r_lwf", "consts2"], writes=["r_G"])
                                v3 = lambda t_: t_[:, 0:w].rearrange("p (c t) -> p c t", t=128)
                                glast = v3(G)[:, :, 127:128].broadcast_to([128, nch, 128])
                                ngrp = nch
                                S.op("act", lambda e: e.activation(out=EC[:, 0:nch], in_=v3(G)[:, :, 127], func=AF.Exp), reads=["r_G"], writes=[("r_EC", par)])
                            else:
                                G3 = G[:, 0:NS].rearrange("p (b t) -> p b t", t=TS)
                                l3 = lwf[:, 0:NS].rearrange("p (b t) -> p b t", t=TS)
                                S.op("dve", lambda e: e.tensor_copy(out=G3[:, :, 0:1], in_=l3[:, :, 0:1]), reads=["r_lwf"], writes=["r_G"])
                                for t_ in range(1, TS):
                                    S.op("dve", lambda e, t_=t_: e.tensor_tensor(out=G3[:, :, t_:t_ + 1], in0=G3[:, :, t_ - 1:t_], in1=l3[:, :, t_:t_ + 1], op=ALU.add), reads=["r_lwf", "r_G"], writes=["r_G"])
                                v3 = lambda t_: t_[:, 0:NS].rearrange("p (b t) -> p b t", t=TS)
                                glast = G3[:, :, TS - 1:TS].broadcast_to([128, NB, TS])
                                S.op("act", lambda e: e.activation(out=EC[:, 0:NB], in_=G3[:, :, TS - 1], func=AF.Exp), reads=["r_G"], writes=[("r_EC", par)])
                            yield 0
                            S.op("dve", lambda e: e.tensor_scalar(out=kap[:, 0:w], in0=kf[:, 0:w], scalar1=vec("rw_k_k"), scalar2=1.0, op0=ALU.mult, op1=ALU.mult), reads=[("r_kf", half), "consts"], writes=["r_kap"])
                            S.op("act", lambda e: e.activation(out=t1[:, 0:w], in_=kap[:, 0:w], func=AF.Square), reads=["r_kap"], writes=["r_t1"])
                            pn = self.nextps()
                            S.op("pe", lambda e: e.matmul(ps[pn][:, 0:w], lhsT=self.blk1, rhs=t1[:, 0:w], start=True, stop=True), reads=["r_t1", "consts"], writes=[("ps", pn)])
                            S.op("dve", lambda e: e.tensor_scalar(out=t1[:, 0:w], in0=ps[pn][:, 0:w], scalar1=1e-24, scalar2=None, op0=ALU.max), reads=[("ps", pn), "r_t1"], writes=["r_t1"])
                            S.op("act", lambda e: e.activation(out=t1[:, 0:w], in_=t1[:, 0:w], func=AF.Ln), reads=["r_t1"], writes=["r_t1"])
                            S.op("act", lambda e: e.activation(out=t1[:, 0:w], in_=t1[:, 0:w], func=AF.Exp, scale=-0.5), reads=["r_t1"], writes=["r_t1"])
                            S.op("dve", lambda e: e.tensor_tensor(out=kap[:, 0:w], in0=kap[:, 0:w], in1=t1[:, 0:w], op=ALU.mult), reads=["r_kap", "r_t1"], writes=["r_kap"])
                            yield 0
                            S.op("dve", lambda e: e.tensor_scalar(out=k2[:, 0:w], in0=af[:, 0:w], scalar1=vec("rw_k_a"), scalar2=self.rw_omka[:, pc:pc + 1], op0=ALU.mult, op1=ALU.add), reads=["r_af", "consts", "rwc1"], writes=["r_k2"])
                            S.op("pool", lambda e: e.tensor_tensor(out=k2[:, 0:w], in0=k2[:, 0:w], in1=kf[:, 0:w], op=ALU.mult), reads=["r_k2", ("r_kf", half)], writes=["r_k2"])
                            S.op("pool", lambda e: e.tensor_tensor(out=beta[:, 0:w], in0=af[:, 0:w], in1=kap[:, 0:w], op=ALU.mult), reads=["r_af", "r_kap"], writes=["r_beta"])
                            yield 0
                            S.op("dve", lambda e: e.scalar_tensor_tensor(out=bon[:, 0:w], in0=rf[:, 0:w], scalar=vec("rw_r_k"), in1=k2[:, 0:w], op0=ALU.mult, op1=ALU.mult), reads=[("r_rf", half), "r_k2", "consts"], writes=[("r_bon", par)])
                            pb = self.nextps()
                            S.op("pe", lambda e: e.matmul(ps[pb][:, 0:w], lhsT=self.blk1, rhs=bon[:, 0:w], start=True, stop=True), reads=[("r_bon", par), "consts"], writes=[("ps", pb)])
                            S.op("dve", lambda e: e.tensor_tensor(out=bon[:, 0:w], in0=ps[pb][:, 0:w], in1=vf[:, 0:w], op=ALU.mult), reads=[("ps", pb), ("r_vf", half), ("r_bon", par)], writes=[("r_bon", par)])
                            yield 0
                            if not sample_pass:
                                kr = lambda i_: KR[:, :, i_, :]
                                bk = lambda i_: BK[:, :, i_, :]
                            else:
                                kr = lambda i_: KR[:, 0, i_, 0:NS].rearrange("p (b t) -> p b t", t=TS)
                                bk = lambda i_: BK[:, 0, i_, 0:NS].rearrange("p (b t) -> p b t", t=TS)
                            S.op("act", lambda e: e.activation(out=e0[:, 0:w], in_=G[:, 0:w], func=AF.Exp), reads=["r_G"], writes=["r_e0"])
                            S.op("pool", lambda e: e.tensor_tensor(out=kr(1), in0=v3(rf), in1=v3(e0), op=ALU.mult), reads=[("r_rf", half), "r_e0"], writes=[("r_KR1", par)])
                            S.op("dve", lambda e: e.tensor_tensor(out=t1[:, 0:w], in0=G[:, 0:w], in1=lwf[:, 0:w], op=ALU.subtract), reads=["r_G", "r_lwf", "r_t1"], writes=["r_t1"])
                            S.op("act", lambda e: e.activation(out=e1[:, 0:w], in_=t1[:, 0:w], func=AF.Exp), reads=["r_t1"], writes=["r_e1"])
                            S.op("pool", lambda e: e.tensor_tensor(out=kr(0), in0=v3(kap), in1=v3(e1), op=ALU.mult), reads=["r_kap", "r_e1"], writes=[("r_KR0", par)])
                            S.op("act", lambda e: e.activation(out=e0[:, 0:w], in_=G[:, 0:w], func=AF.Exp, scale=-1.0), reads=["r_G", ("r_KR1", par)], writes=["r_e0"])
                            S.op("pool", lambda e: e.tensor_tensor(out=bk(0), in0=v3(beta), in1=v3(e0), op=ALU.mult), reads=["r_beta", "r_e0"], writes=[("r_BK0", par)])
                            S.op("dve", lambda e: e.tensor_tensor(out=bk(1), in0=v3(k2), in1=v3(e0), op=ALU.mult), reads=["r_k2", "r_e0"], writes=[("r_BK1", par)])
                            S.op("dve", lambda e: e.tensor_tensor(out=v3(t1), in0=glast, in1=v3(G), op=ALU.subtract), reads=["r_G", "r_t1", "r_e1"], writes=["r_t1"])
                            S.op("act", lambda e: e.activation(out=e1[:, 0:w], in_=t1[:, 0:w], func=AF.Exp), reads=["r_t1", ("r_KR0", par)], writes=["r_e1"])
                            S.op("pool", lambda e: e.tensor_tensor(out=btT[:, 0:w], in0=beta[:, 0:w], in1=e1[:, 0:w], op=ALU.mult), reads=["r_beta", "r_e1"], writes=["r_btT"])
                            S.op("dve", lambda e: e.tensor_tensor(out=ktT[:, 0:w], in0=k2[:, 0:w], in1=e1[:, 0:w], op=ALU.mult), reads=["r_k2", "r_e1"], writes=["r_ktT"])
                            yield 0
                            pt = self.nextps()
                            ptb = ps[pt].bitcast(BF16)
                            for i_, (src_, nm_) in enumerate(((btT, "r_btT"), (ktT, "r_ktT"), (vb, ("r_vb", half)))):
                                for c in range(nch):
                                    S.op("pe", lambda e, i_=i_, c=c, src_=src_: e.transpose(ptb[0:R, (i_ * nch + c) * 128:(i_ * nch + c + 1) * 128], src_[:, c * 128:c * 128 + R], self.identb), reads=[nm_, "consts2"], writes=[("ps", pt)])
                            for i_, (dst_, nm_) in enumerate(((bt_tok, ("r_bttok", par)), (kt_tok, ("r_kttok", par)), (v_tok, ("r_vtok", par)))):
                                self.copy("dve", dst_[0:R, :, :], ptb[0:R, i_ * nch * 128:(i_ + 1) * nch * 128].rearrange("p (c k) -> p c k", c=nch), [("ps", pt)], [nm_])
                            yield "MID"
                            if bi_ == 0:
                                S.op("pool", lambda e: e.memset(Sst, 0.0), writes=[("r_Sst", 0), ("r_Sst", 1)])
                                S.op("pool", lambda e: e.memset(Sbf, 0.0), writes=[("r_Sbf", 0), ("r_Sbf", 1)])
                            for hd in range(2):
                                P0 = hd * 64
                                for c in range(nch):
                                    m_ = hd * nch + c
                                    kr_c = KR[P0:P0 + 64, c, :, 0:R]
                                    p1, p2, p3 = self.nextps(), self.nextps(), self.nextps()
                                    o1 = ps[p1][0:R, 0:2 * R].rearrange("p (i t) -> p i t", i=2)
                                    o2 = ps[p2][0:R, 0:2 * R].rearrange("p (i t) -> p i t", i=2)
                                    S.op("pe", lambda e, c=c, P0=P0, kr_c=kr_c, o1=o1: e.matmul(o1, lhsT=BK[P0:P0 + 64, c, 0, 0:R], rhs=kr_c, start=True, stop=True), reads=[("r_BK0", par), ("r_KR0", par), ("r_KR1", par)], writes=[("ps", p1)])
                                    S.op("pe", lambda e, c=c, P0=P0, kr_c=kr_c, o2=o2: e.matmul(o2, lhsT=BK[P0:P0 + 64, c, 1, 0:R], rhs=kr_c, start=True, stop=True), reads=[("r_BK1", par), ("r_KR0", par), ("r_KR1", par)], writes=[("ps", p2)])
                                    S.op("pe", lambda e, c=c, P0=P0, p3=p3: e.matmul(ps[p3][0:R, 0:R], lhsT=KR[P0:P0 + 64, c, 0, 0:R], rhs=BK[P0:P0 + 64, c, 0, 0:R], start=True, stop=True), reads=[("r_BK0", par), ("r_KR0", par)], writes=[("ps", p3)])
                                    S.op("dve", lambda e, m_=m_, p1=p1: e.tensor_tensor(out=Nk[0:R, 0, m_, 0:R], in0=ps[p1][0:R, 0:R], in1=mStrict, op=ALU.mult), reads=[("ps", p1), "consts"], writes=[("r_Nk", 0, m_)])
                                    S.op("dve", lambda e, m_=m_, p1=p1: e.tensor_tensor(out=MbT[0:R, m_, 0:R], in0=ps[p1][0:R, R:2 * R], in1=mIncl, op=ALU.mult), reads=[("ps", p1), "consts"], writes=[("r_MbT", m_)])
                                    S.op("dve", lambda e, m_=m_, p2=p2: e.tensor_tensor(out=BTm[0:R, m_, 0:R], in0=ps[p2][0:R, 0:R], in1=mStrict, op=ALU.mult), reads=[("ps", p2), "consts"], writes=[("r_BTm", m_)])
                                    S.op("dve", lambda e, m_=m_, p2=p2: e.tensor_tensor(out=MkT[0:R, m_, 0:R], in0=ps[p2][0:R, R:2 * R], in1=mIncl, op=ALU.mult), reads=[("ps", p2), "consts"], writes=[("r_MkT", m_)])
                                    S.op("dve", lambda e, m_=m_, p3=p3: e.tensor_tensor(out=Ak[0:R, 0, m_, 0:R], in0=ps[p3][0:R, 0:R], in1=mLow, op=ALU.mult), reads=[("ps", p3), "consts"], writes=[("r_Ak", 0, m_)])
                                    S.op("pool", lambda e, m_=m_: e.tensor_tensor(out=Wm[0:R, m_, 0:R], in0=self.ident[0:R, 0:R], in1=Nk[0:R, 0, m_, 0:R], op=ALU.subtract), reads=[("r_Nk", 0, m_), "consts"], writes=[("r_Wm", m_)])
                                    yield 0
                            yield 0
                            nm_ = 2 * nch
                            for lev in range(1, nlev + 1):
                                cur, nxt = (lev - 1) % 2, lev % 2
                                for m_ in range(nm_):
                                    p1 = self.nextps()
                                    S.op("pe", lambda e, m_=m_, p1=p1, cur=cur: e.matmul(ps[p1][0:R, 0:R], lhsT=Nk[0:R, cur, m_, 0:R], rhs=Ak[0:R, cur, m_, 0:R], start=True, stop=True), reads=[("r_Nk", cur, m_), ("r_Ak", cur, m_)], writes=[("ps", p1)])
                                    if lev < nlev:
                                        S.op("pe", lambda e, m_=m_, p1=p1, cur=cur: e.matmul(ps[p1][0:R, 128:128 + R], lhsT=Ak[0:R, cur, m_, 0:R], rhs=Nk[0:R, cur, m_, 0:R], start=True, stop=True), reads=[("r_Nk", cur, m_), ("r_Ak", cur, m_)], writes=[("ps", p1)])
                                    self.copy("act", Ak[0:R, nxt, m_, 0:R], ps[p1][0:R, 0:R], [("ps", p1)], [("r_Ak", nxt, m_)])
                                    if lev < nlev:
                                        self.copy("dve", Nk[0:R, nxt, m_, 0:R], ps[p1][0:R, 128:128 + R], [("ps", p1)], [("r_Nk", nxt, m_)])
                                    yield 0
                                for m_ in range(nm_):
                                    p2 = self.nextps()
                                    S.op("pe", lambda e, m_=m_, p2=p2, nxt=nxt: e.matmul(ps[p2][0:R, 0:R], lhsT=Ak[0:R, nxt, m_, 0:R], rhs=Wm[0:R, m_, 0:R], start=True, stop=True), reads=[("r_Ak", nxt, m_), ("r_Wm", m_)], writes=[("ps", p2)])
                                    S.op("dve", lambda e, m_=m_, p2=p2: e.tensor_tensor(out=Wm[0:R, m_, 0:R], in0=Wm[0:R, m_, 0:R], in1=ps[p2][0:R, 0:R], op=ALU.add), reads=[("ps", p2), ("r_Wm", m_)], writes=[("r_Wm", m_)])
                                    yield 0
                            yield 0
                            pO = self.nextlong()
                            if sample_pass:
                                for hd in range(2):
                                    S.op("sp", lambda e, hd=hd: e.dma_start(out=s0nat[:, :, hd * 64:(hd + 1) * 64], in_=dr["st_wkv"][:, 2 * pc + hd].rearrange("b v k -> v b k")), writes=[("r_s0nat", hd)], dma="r_s0nat")
                                for q4 in range(4):
                                    p1 = self.nextps()
                                    for bb in range(4):
                                        b = q4 * 4 + bb
                                        S.op("pe", lambda e, b=b, bb=bb, p1=p1: e.transpose(ps[p1][:, bb * 64:(bb + 1) * 64], s0nat[:, b, :], self.ident[0:64, 0:64]), reads=[("r_s0nat", 0), ("r_s0nat", 1), "consts"], writes=[("ps", p1)])
                                    self.copy("act", S0[:, q4 * 4:(q4 + 1) * 4, :], ps[p1][:, 0:256].rearrange("p (b v) -> p b v", b=4), [("ps", p1)], [("r_S0", q4)])
                                    S.op("dve", lambda e, q4=q4, p1=p1: e.tensor_copy(out=S0bf[:, q4 * 4:(q4 + 1) * 4, :], in_=ps[p1][:, 0:256].rearrange("p (b v) -> p b v", b=4)), reads=[("ps", p1)], writes=[("r_S0bf", q4)])
                                s0k = [("r_S0", q4) for q4 in range(4)]
                                s0bk = [("r_S0bf", q4) for q4 in range(4)]
                                S.op("dve", lambda e: e.tensor_tensor(out=kblk, in0=KR[:, 0, 0, 0:NS].unsqueeze(1).broadcast_to([128, NB, NS]), in1=self.maskC, op=ALU.mult), reads=[("r_KR0", par), "consts"], writes=["r_kblk"])
                                S.op("dve", lambda e: e.tensor_tensor(out=rblk, in0=KR[:, 0, 1, 0:NS].unsqueeze(1).broadcast_to([128, NB, NS]), in1=self.maskC, op=ALU.mult), reads=[("r_KR1", par), "consts"], writes=["r_rblk"])
                            for c in range(nch):
                                for hd in range(2):
                                    P0 = hd * 64
                                    m_ = hd * nch + c
                                    hs = slice(P0, P0 + 64)
                                    yield 0
                                    pX = self.nextps()
                                    if not sample_pass:
                                        S.op("pe", lambda e, c=c, hs=hs, pX=pX: e.matmul(ps[pX][0:R, 0:64], lhsT=KR[hs, c, 0, 0:R], rhs=Sbf[hs, :], start=True, stop=False), reads=[("r_KR0", par), ("r_Sbf", hd)], writes=[("ps", pX)])
                                    else:
                                        for b in range(NB):
                                            S.op("pe", lambda e, b=b, hs=hs, pX=pX: e.matmul(ps[pX][0:R, 0:64], lhsT=kblk[hs, b, :], rhs=S0bf[hs, b, :], start=(b == 0), stop=False), reads=["r_kblk"] + s0bk, writes=[("ps", pX)])
                                    S.op("pe", lambda e, c=c, hs=hs, pX=pX, m_=m_: e.matmul(ps[pX][0:R, 0:64], lhsT=BTm[:, m_, 0:R], rhs=v_tok[:, c, hs], start=False, stop=True), reads=[("r_BTm", m_), ("r_vtok", par)], writes=[("ps", pX)])
                                    S.op("act", lambda e, hd=hd, pX=pX: e.activation(out=Xsb[0:R, hd, :], in_=ps[pX][0:R, 0:64], func=AF.Copy, scale=-1.0), reads=[("ps", pX)], writes=[("r_Xsb", hd)])
                                    pS = self.nextps()
                                    S.op("pe", lambda e, hd=hd, pS=pS, m_=m_: e.matmul(ps[pS][0:R, 0:64], lhsT=Wm[0:R, m_, 0:R], rhs=Xsb[0:R, hd, :], start=True, stop=True), reads=[("r_Wm", m_), ("r_Xsb", hd)], writes=[("ps", pS)])
                                    S.op("dve", lambda e, hd=hd, pS=pS: e.tensor_copy(out=Ssb[0:R, hd, :], in_=ps[pS][0:R, 0:64]), reads=[("ps", pS)], writes=[("r_Ssb", hd)])
                                    oo = ps[pO][hs, c * 128:c * 128 + R]
                                    if not sample_pass:
                                        S.op("pe", lambda e, c=c, hs=hs, oo=oo: e.matmul(oo, lhsT=Sbf[hs, :], rhs=KR[hs, c, 1, 0:R], start=True, stop=False), reads=[("r_KR1", par), ("r_Sbf", hd)], writes=[("ps", pO)])
                                        S.op("pe", lambda e, hd=hd, m_=m_, oo=oo: e.matmul(oo, lhsT=Ssb[0:R, hd, :], rhs=MbT[0:R, m_, 0:R], start=False, stop=False), reads=[("r_Ssb", hd), ("r_MbT", m_)], writes=[("ps", pO)])
                                        S.op("pe", lambda e, c=c, hs=hs, m_=m_, oo=oo: e.matmul(oo, lhsT=v_tok[0:R, c, hs], rhs=MkT[0:R, m_, 0:R], start=False, stop=True), reads=[("r_vtok", par), ("r_MkT", m_)], writes=[("ps", pO)])
                                        pU = self.nextps()
                                        S.op("pe", lambda e, c=c, hs=hs, hd=hd, pU=pU: e.matmul(ps[pU][hs, 0:64], lhsT=bt_tok[0:R, c, hs], rhs=Ssb[0:R, hd, :], start=True, stop=False), reads=[("r_bttok", par), ("r_Ssb", hd)], writes=[("ps", pU)])
                                        S.op("pe", lambda e, c=c, hs=hs, pU=pU: e.matmul(ps[pU][hs, 0:64], lhsT=kt_tok[0:R, c, hs], rhs=v_tok[0:R, c, hs], start=False, stop=True), reads=[("r_kttok", par), ("r_vtok", par)], writes=[("ps", pU)])
                                        S.op("dve", lambda e, c=c, hs=hs, pU=pU: e.scalar_tensor_tensor(out=Sst[hs, :], in0=Sst[hs, :], scalar=EC[hs, c:c + 1], in1=ps[pU][hs, 0:64], op0=ALU.mult, op1=ALU.add), reads=[("ps", pU), ("r_Sst", hd), ("r_EC", par)], writes=[("r_Sst", hd)])
                                        S.op("pool", lambda e, hs=hs: e.tensor_copy(out=Sbf[hs, :], in_=Sst[hs, :]), reads=[("r_Sst", hd)], writes=[("r_Sbf", hd)])
                                    else:
                                        S.op("pe", lambda e, hd=hd, m_=m_, oo=oo: e.matmul(oo, lhsT=Ssb[:, hd, :], rhs=MbT[:, m_, 0:R], start=True, stop=False), reads=[("r_Ssb", hd), ("r_MbT", m_)], writes=[("ps", pO)])
                                        S.op("pe", lambda e, hs=hs, m_=m_, oo=oo: e.matmul(oo, lhsT=v_tok[:, 0, hs], rhs=MkT[:, m_, 0:R], start=False, stop=False), reads=[("r_vtok", par), ("r_MkT", m_)], writes=[("ps", pO)])
                                        for b in range(NB):
                                            S.op("pe", lambda e, b=b, hs=hs, oo=oo: e.matmul(oo, lhsT=S0bf[hs, b, :], rhs=rblk[hs, b, :], start=False, stop=(b == NB - 1)), reads=["r_rblk"] + s0bk, writes=[("ps", pO)])
                                        S.op("dve", lambda e, hd=hd: e.tensor_tensor(out=Sblk[0:NS, hd, :, :], in0=Ssb[0:NS, hd, :].unsqueeze(1).broadcast_to([NS, NB, 64]), in1=self.maskB.unsqueeze(2).broadcast_to([NS, NB, 64]), op=ALU.mult), reads=[("r_Ssb", hd), "consts"], writes=[("r_Sblk", hd)])
                                        S.op("dve", lambda e, hd=hd, hs=hs: e.tensor_tensor(out=Vblk[0:NS, hd, :, :], in0=v_tok[0:NS, 0, hs].unsqueeze(1).broadcast_to([NS, NB, 64]), in1=self.maskB.unsqueeze(2).broadcast_to([NS, NB, 64]), op=ALU.mult), reads=[("r_vtok", par), "consts"], writes=[("r_Vblk", hd)])
                                        for half in range(2):
                                            pU = self.nextps()
                                            S.op("pe", lambda e, hs=hs, hd=hd, half=half, pU=pU: e.matmul(ps[pU][hs, 0:512], lhsT=bt_tok[:, 0, hs], rhs=Sblk[:, hd, half * 8:(half + 1) * 8, :], start=True, stop=False), reads=[("r_bttok", par), ("r_Sblk", hd)], writes=[("ps", pU)])
                                            S.op("pe", lambda e, hs=hs, hd=hd, half=half, pU=pU: e.matmul(ps[pU][hs, 0:512], lhsT=kt_tok[:, 0, hs], rhs=Vblk[:, hd, half * 8:(half + 1) * 8, :], start=False, stop=True), reads=[("r_kttok", par), ("r_Vblk", hd)], writes=[("ps", pU)])
                                            bs = slice(half * 8, (half + 1) * 8)
                                            S.op("dve", lambda e, hs=hs, bs=bs: e.tensor_tensor(out=S0[hs, bs, :], in0=S0[hs, bs, :], in1=EC[hs, bs].unsqueeze(2).broadcast_to([64, 8, 64]), op=ALU.mult), reads=s0k + [("r_EC", par)], writes=s0k)
                                            S.op("dve", lambda e, hs=hs, bs=bs, pU=pU: e.tensor_tensor(out=S0[hs, bs, :], in0=S0[hs, bs, :], in1=ps[pU][hs, 0:512].rearrange("p (b v) -> p b v", b=8), op=ALU.add), reads=s0k + [("ps", pU)], writes=s0k)
                            if sample_pass:
                                for q4 in range(4):
                                    p1 = self.nextps()
                                    for bb in range(4):
                                        b = q4 * 4 + bb
                                        S.op("pe", lambda e, b=b, bb=bb, p1=p1: e.transpose(ps[p1][0:64, bb * 128:(bb + 1) * 128], S0[:, b, :], self.ident), reads=s0k + ["consts"], writes=[("ps", p1)])
                                    self.copy("act", s0nat[:, q4 * 4:(q4 + 1) * 4, :], ps[p1][0:64, :].rearrange("p (b k) -> p b k", b=4), [("ps", p1)], [("r_s0nat", 0), ("r_s0nat", 1)])
                                for hd in range(2):
                                    S.op("sp", lambda e, hd=hd: e.dma_start(out=dr["wkv_s"][:, 2 * pc + hd].rearrange("b v k -> v b k"), in_=s0nat[:, :, hd * 64:(hd + 1) * 64]), reads=[("r_s0nat", 0), ("r_s0nat", 1)], dma="r_wkv_s")
                            elif c0 + w == TP:
                                p1 = self.nextps()
                                S.op("pe", lambda e, p1=p1: e.transpose(ps[p1][0:64, 0:128], Sst, self.ident), reads=[("r_Sst", 0), ("r_Sst", 1), "consts"], writes=[("ps", p1)])
                                self.copy("act", sT, ps[p1][0:64, 0:128], [("ps", p1)], ["r_sT"])
                                S.op("sp", lambda e: e.dma_start(out=dr["wkv_p"][2 * pc:2 * pc + 2].rearrange("h v k -> v h k"), in_=sT.rearrange("v (h k) -> v h k", h=2)), reads=["r_sT"], dma="r_wkv_p")
                            yield 0
                            self.copy("act", osb[:, 0:w], ps[pO][:, 0:w], [("ps", pO)], ["r_osb"])
                            pm = self.nextps()
                            S.op("pe", lambda e: e.matmul(ps[pm][:, 0:w], lhsT=self.blk1, rhs=osb[:, 0:w], start=True, stop=True), reads=["r_osb", "consts"], writes=[("ps", pm)])
                            S.op("dve", lambda e: e.scalar_tensor_tensor(out=osb[:, 0:w], in0=ps[pm][:, 0:w], scalar=-1.0 / 64, in1=osb[:, 0:w], op0=ALU.mult, op1=ALU.add), reads=[("ps", pm), "r_osb"], writes=["r_osb"])
                            S.op("act", lambda e: e.activation(out=t2[:, 0:w], in_=osb[:, 0:w], func=AF.Square), reads=["r_osb", "r_t2"], writes=["r_t2"])
                            pv2 = self.nextps()
                            S.op("pe", lambda e: e.matmul(ps[pv2][:, 0:w], lhsT=self.blk1, rhs=t2[:, 0:w], start=True, stop=True), reads=["r_t2", "consts"], writes=[("ps", pv2)])
                            S.op("act", lambda e: e.activation(out=t2[:, 0:w], in_=ps[pv2][:, 0:w], func=AF.Ln, scale=1.0 / 64, bias=GN_EPS), reads=[("ps", pv2), "r_t2"], writes=["r_t2"])
                            S.op("act", lambda e: e.activation(out=t2[:, 0:w], in_=t2[:, 0:w], func=AF.Exp, scale=-0.5), reads=["r_t2"], writes=["r_t2"])
                            S.op("dve", lambda e: e.tensor_tensor(out=osb[:, 0:w], in0=osb[:, 0:w], in1=t2[:, 0:w], op=ALU.mult), reads=["r_osb", "r_t2"], writes=["r_osb"])
                            S.op("dve", lambda e: e.tensor_scalar(out=osb[:, 0:w], in0=osb[:, 0:w], scalar1=vec("rw_lnx_w"), scalar2=vec("rw_lnx_b"), op0=ALU.mult, op1=ALU.add), reads=["r_osb", "consts"], writes=["r_osb"])
                            S.op("pool", lambda e: e.tensor_tensor(out=osb[:, 0:w], in0=osb[:, 0:w], in1=bon[:, 0:w], op=ALU.add), reads=["r_osb", ("r_bon", par)], writes=["r_osb"])
                            S.op("dve", lambda e: e.tensor_tensor(out=og[:, 0:w], in0=osb[:, 0:w], in1=gf[:, 0:w], op=ALU.mult), reads=["r_osb", ("r_gf", par)], writes=["r_og"])
                            for dc in range(KC):
                                pi = self.nextps()
                                S.op("pe", lambda e, pi=pi, dc=dc: e.matmul(ps[pi][:, 0:w], lhsT=wout[:, dc * 128:(dc + 1) * 128], rhs=og[:, 0:w], start=True, stop=True), reads=[("r_wout", pc % 2), "r_og"], writes=[("ps", pi)])
                                S.op("dve", lambda e, pi=pi, dc=dc: e.tensor_tensor(out=xres[:, dc, c0:c0 + w], in0=xres[:, dc, c0:c0 + w], in1=ps[pi][:, 0:w], op=ALU.add), reads=[("ps", pi), ("xres", dc, c0)], writes=[("xres", dc, c0)])
                                yield 0


                    gens = []
                    gi = 0
                    for pc in range(KC):
                        for bi_, (c0, w, kind) in enumerate(blocks):
                            gens.append(blk(pc, bi_, c0, w, kind, gi % 2))
                            gi += 1
                    back = None
                    for g in gens:
                        fd, bd = False, back is None
                        while not (fd and bd):
                            if not fd:
                                fd = next(g) == "MID"
                            if not bd:
                                bd = next(back, "END") == "END"
                        back = g
                    for _ in back:
                        pass
                S.barrier()

    def mamba2(self, j):
        nc, S, TP = self.nc, self.S, self.TP
        xres, hT, ps = self.xres, self.hT, self.ps
        w_in, w_out = self.dram["mb_w_in"], self.dram["mb_w_out"]
        st_ssm, st_conv = self.dram["st_ssm"], self.dram["st_conv"]
        ssm_p, ssm_s, cv_p, cv_s = self.dram["ssm_p"], self.dram["ssm_s"], self.dram["cv_p"], self.dram["cv_s"]
        W = 512
        with ExitStack() as es:
            T = lambda n, sh, dt=F32: es.enter_context(self.T(n, sh, dt))
            wz, wx = T("m_wz", [128, KC, 512], BF16), T("m_wx", [128, KC, 512], BF16)
            wB, wC, wdt = T("m_wB", [128, KC, 128], BF16), T("m_wC", [128, KC, 128], BF16), T("m_wdt", [128, KC, 8], BF16)
            wout = T("m_wout", [128, 4, D], BF16)
            ngt = T("m_ng", [128, 512])
            pre = T("m_pre", [128, 6, 3 + W])
            acc = T("m_acc", [128, 1, W])
            xsT, BT, CT = T("m_xsT", [128, 4, W], BF16), T("m_BT", [128, W], BF16), T("m_CT", [128, W], BF16)
            cvT, cvrow = T("m_cvT", [128, 6, 48]), T("m_cvrow", [48, 768])
            zs, dtv, da = T("m_zs", [128, 512]), T("m_dt", [128, 8]), T("m_da", [128, 8])
            xtok, Btok = T("m_xtok", [128, 512], BF16), T("m_Btok", [128, 128], BF16)
            cbm, ex, wts = T("m_cbm", [128, 128]), T("m_ex", [128, 24]), T("m_wts", [128, 8])
            daM, seg, mT = T("m_daM", [128, 2, 128]), T("m_seg", [128, 2, 128]), T("m_mT", [128, 2, 128], BF16)
            y1, xd = T("m_y1", [128, 512]), T("m_xd", [128, 512])
            ssq, yn = T("m_ssq", [128, 2]), T("m_yn", [128, 512], BF16)
            yTb = T("m_yTb", [128, 4, W], BF16)
            xw = T("m_xw", [128, 512], BF16)
            ST, STbf = T("m_ST", [128, 512]), T("m_STbf", [128, 512], BF16)
            stT = T("m_stT", [128, 4, 128])
            cv0 = cvrow
            Cblk = T("m_Cblk", [128, NB, NS], BF16)
            ST0bf = T("m_ST0bf", [128, 2, 512], BF16)
            snat = T("m_snat", [128, 2, 4, 128])
            dablk, etots = T("m_dablk", [64, NB, 8]), T("m_etots", [128, NB, 8])
            fz = T("m_fz", [128, 2])
            etn = T("m_etn", [128, NB, 4])
            preflat = pre.rearrange("p f t -> p (f t)")
            snat_slots = [snat[:, 0, :, :], snat[:, 1, :, :]] + [preflat[:, k_ * 512:(k_ + 1) * 512].rearrange("p (q n) -> p q n", q=4) for k_ in range(5)]
            stT_slots = [stT, preflat[:, 2560:3072].rearrange("p (q n) -> p q n", q=4)]
            for g in range(4):
                for kk0 in range(0, KC, 2):
                    for (wt, col) in ((wz, g * 512), (wx, 2048 + g * 512)):
                        src = w_in[j, kk0 * 128:(kk0 + 2) * 128, col:col + 512].rearrange("(k p) c -> p k c", p=128)
                        self.load_w(wt[:, kk0:kk0 + 2, :], src, [128, 2, 512], ("m_w", id(wt), kk0))
                for (wt, col) in ((wB, 4096 + g * 128), (wC, 4608 + g * 128)):
                    self.load_w(wt, w_in[j, :, col:col + 128].rearrange("(k p) c -> p k c", p=128), [128, KC, 128], ("m_w", id(wt)))
                self.load_w(wdt, w_in[j, :, 5120 + 8 * g:5128 + 8 * g].rearrange("(k p) c -> p k c", p=128), [128, KC, 8], ("m_w", id(wdt)))
                for fc in range(4):
                    self.load_w(wout[:, fc, :], w_out[j, g * 512 + fc * 128:g * 512 + (fc + 1) * 128, :], [128, D], ("m_wout", fc))
                S.op("sp", lambda e: e.dma_start(out=ngt, in_=self.dram["mb_norm"][:, g * 512:(g + 1) * 512]), writes=["m_ng"], dma="m_ng")
                wk2 = lambda wt: [("m_w", id(wt), k_) for k_ in range(0, KC, 2)]
                wk1 = lambda wt: [("m_w", id(wt))]
                fcol = [g * 512 + i * 128 for i in range(4)] + [2048 + g * 128, 2560 + g * 128]
                fch24 = [c_ // 128 for c_ in fcol]
                S.op("pool", lambda e: e.memset(ST, 0.0), writes=["m_ST"])
                S.op("pool", lambda e: e.memset(STbf, 0.0), writes=["m_STbf"])
                S.op("pool", lambda e: e.memset(pre[:, :, 0:3], 0.0), writes=["m_pre"] + [("m_snat", k_) for k_ in range(2, 7)] + [("m_stT", 1)])
                S.op("pool", lambda e: e.memset(cvT, 0.0), writes=["m_cvT"])
                for (c0, w, kind) in self.tbs:
                    smp = kind == "s"
                    hc = 2 + c0
                    R = 64 if smp else 128
                    if smp:
                        pre4 = pre[:, :, 0:NB * 7].rearrange("p f (b t) -> p f b t", t=7)
                        for i_, (c_, n_) in enumerate(((fcol[0], 512), (fcol[4], 128), (fcol[5], 128))):
                            o_ = (0, 512, 640)[i_]
                            S.op("sp", lambda e, c_=c_, n_=n_, o_=o_: e.dma_start(out=cv0[:, o_:o_ + n_], in_=st_conv[:, :, c_:c_ + n_].rearrange("b w f -> (b w) f")), writes=["m_cvrow"], dma="m_cv0")
                        pt = self.nextps()
                        for f in range(6):
                            S.op("pe", lambda e, f=f: e.transpose(ps[pt][:, f * 48:(f + 1) * 48], cv0[:, f * 128:(f + 1) * 128], self.ident[0:48, 0:48]), reads=["m_cvrow", "consts"], writes=[("ps", pt)])
                        self.copy("dve", pre4[:, :, :, 0:3], ps[pt][:, 0:288].rearrange("p (f b t) -> p f b t", f=6, t=3), [("ps", pt)], ["m_pre"])
                    elif c0 > 0:
                        self.copy("dve", pre[:, :, 0:3], pre[:, :, W:W + 3], ["m_pre"], ["m_pre"])
                    for f in range(6):
                        pi = self.nextps()
                        wt, cs = (wx, f * 128) if f < 4 else ((wB, 0) if f == 4 else (wC, 0))
                        for kc in range(KC):
                            S.op("pe", lambda e, pi=pi, kc=kc, wt=wt, cs=cs: e.matmul(ps[pi][:, 0:w], lhsT=wt[:, kc, cs:cs + 128], rhs=hT[:, kc, hc:hc + w], start=(kc == 0), stop=(kc == KC - 1)),
                                 reads=(wk2(wt) if f < 4 else wk1(wt)) + [("hT", "all")], writes=[("ps", pi)])
                        if smp:
                            self.copy(self.ew(), pre4[:, f, :, 3:7], ps[pi][:, 0:NS].rearrange("p (b t) -> p b t", t=TS), [("ps", pi)], ["m_pre"])
                        else:
                            self.copy(self.ew(), pre[:, f, 3:3 + W], ps[pi][:, 0:W], [("ps", pi)], ["m_pre"])
                    if MBSTOP == 2:
                        return
                    for f in range(6):
                        f24 = fch24[f]
                        if smp:
                            a_ = acc[:, 0, 0:NS].rearrange("p (b t) -> p b t", t=TS)
                            src_k = lambda k_: pre4[:, f, :, k_:k_ + TS]
                        else:
                            a_ = acc[:, 0, :]
                            src_k = lambda k_: pre[:, f, k_:k_ + W]
                        S.op("act", lambda e, a_=a_, f24=f24, s0=src_k(0): e.activation(out=a_, in_=s0, func=AF.Identity, scale=self.mb_cw[:, f24, 0:1], bias=self.mb_cb[:, f24:f24 + 1]),
                             reads=["m_pre", "consts"], writes=[("m_acc", 0)])
                        for k_ in range(1, 4):
                            S.op("dve", lambda e, a_=a_, f24=f24, k_=k_, sk=src_k(k_): e.scalar_tensor_tensor(out=a_, in0=sk, scalar=self.mb_cw[:, f24, k_:k_ + 1], in1=a_, op0=ALU.mult, op1=ALU.add),
                                 reads=["m_pre", "consts", ("m_acc", 0)], writes=[("m_acc", 0)])
                        dst = xsT[:, f, 0:w] if f < 4 else (BT[:, 0:w] if f == 4 else CT[:, 0:w])
                        S.op("act", lambda e, dst=dst, f=f: e.activation(out=dst, in_=acc[:, 0, 0:w], func=AF.Silu), reads=[("m_acc", 0)], writes=[("m_xbc", f)])
                    if MBSTOP == 3:
                        return
                    if smp or c0 + w == TP:
                        nr = 48 if smp else 3
                        if smp:
                            self.copy("dve", cvT.rearrange("p f (b t) -> p f b t", t=3), pre4[:, :, :, 4:7], ["m_pre"], ["m_cvT"])
                        else:
                            self.copy("dve", cvT[:, :, 0:3], pre[:, :, W:W + 3], ["m_pre"], ["m_cvT"])
                        pt, ptx = self.nextps(), self.nextps()
                        for f in range(6):
                            pp = pt if f < 4 else ptx
                            nrt = max(nr, 32)
                            S.op("pe", lambda e, f=f, nrt=nrt, pp=pp: e.transpose(ps[pp][0:nrt, (f % 4) * 128:(f % 4 + 1) * 128], cvT[:, f, 0:nrt], self.ident), reads=["m_cvT", "consts"], writes=[("ps", pp)])
                        self.copy("act", cvrow[0:nr, 0:512], ps[pt][0:nr, 0:512], [("ps", pt)], ["m_cvrow"])
                        self.copy("act", cvrow[0:nr, 512:768], ps[ptx][0:nr, 0:256], [("ps", ptx)], ["m_cvrow"])
                        for i_, (c_, n_) in enumerate(((fcol[0], 512), (fcol[4], 128), (fcol[5], 128))):
                            o_ = (0, 512, 640)[i_]
                            dstd = cv_s[:, :, c_:c_ + n_].rearrange("b w f -> (b w) f") if smp else cv_p[:, c_:c_ + n_]
                            S.op("sp", lambda e, dstd=dstd, o_=o_, n_=n_, nr=nr: e.dma_start(out=dstd, in_=cvrow[0:nr, o_:o_ + n_]), reads=["m_cvrow"], dma="m_cvout")
                    if MBSTOP == 4:
                        return
                    xbk = [("m_xbc", f) for f in range(6)]
                    mU = self.maskS if smp else self.maskU
                    mL = self.sLS if smp else self.sL
                    for ct in range(1 if smp else w // 128):
                        t0 = ct * 128
                        pz, pd = self.nextps(), self.nextps()
                        for kc in range(KC):
                            S.op("pe", lambda e, kc=kc: e.matmul(ps[pz][0:R, 0:512], lhsT=hT[:, kc, hc + t0:hc + t0 + R], rhs=wz[:, kc, :], start=(kc == 0), stop=(kc == KC - 1)),
                                 reads=wk2(wz) + [("hT", "all")], writes=[("ps", pz)])
                        for kc in range(KC):
                            S.op("pe", lambda e, kc=kc: e.matmul(ps[pd][0:R, 0:8], lhsT=hT[:, kc, hc + t0:hc + t0 + R], rhs=wdt[:, kc, :], start=(kc == 0), stop=(kc == KC - 1)),
                                 reads=wk1(wdt) + [("hT", "all")], writes=[("ps", pd)])
                        S.op("act", lambda e: e.activation(out=zs[0:R, :], in_=ps[pz][0:R, 0:512], func=AF.Silu), reads=[("ps", pz)], writes=["m_zs"])
                        S.op("dve", lambda e: e.tensor_tensor(out=dtv[0:R, :], in0=ps[pd][0:R, 0:8], in1=self.mb_dtb[0:R, 8 * g:8 * g + 8], op=ALU.add), reads=[("ps", pd), "consts"], writes=["m_dt"])
                        S.op("act", lambda e: e.activation(out=dtv[0:R, :], in_=dtv[0:R, :], func=AF.Exp), reads=["m_dt"], writes=["m_dt"])
                        S.op("act", lambda e: e.activation(out=dtv[0:R, :], in_=dtv[0:R, :], func=AF.Ln, bias=1.0), reads=["m_dt"], writes=["m_dt"])
                        S.op("dve", lambda e: e.tensor_tensor(out=da[0:R, :], in0=dtv[0:R, :], in1=self.mb_negA[0:R, 8 * g:8 * g + 8], op=ALU.mult), reads=["m_dt", "mbc0"], writes=["m_da"])
                        if MBSTOP == 5:
                            return
                        pt = self.nextps()
                        ptb = ps[pt].bitcast(BF16)
                        for fc in range(4):
                            S.op("pe", lambda e, fc=fc: e.transpose(ptb[0:R, fc * 128:(fc + 1) * 128], xsT[:, fc, t0:t0 + R], self.identb), reads=xbk[0:4] + ["consts2"], writes=[("ps", pt)])
                        S.op("pe", lambda e: e.transpose(ptb[0:R, 512:640], BT[:, t0:t0 + R], self.identb), reads=[xbk[4], "consts2"], writes=[("ps", pt)])
                        self.copy("dve", xtok[0:R, :], ptb[0:R, 0:512], [("ps", pt)], ["m_xtok"])
                        self.copy("dve", Btok[0:R, :], ptb[0:R, 512:640], [("ps", pt)], ["m_Btok"])
                        pc = self.nextps()
                        S.op("pe", lambda e: e.matmul(ps[pc][0:R, 0:R], lhsT=BT[:, t0:t0 + R], rhs=CT[:, t0:t0 + R], start=True, stop=True), reads=[xbk[4], xbk[5]], writes=[("ps", pc)])
                        S.op("dve", lambda e: e.tensor_tensor(out=cbm[0:R, 0:R], in0=ps[pc][0:R, 0:R], in1=mU, op=ALU.mult), reads=[("ps", pc), "consts"], writes=["m_cbm"])
                        if MBSTOP == 6:
                            return
                        pm = self.nextps()
                        S.op("pe", lambda e: e.matmul(ps[pm][0:R, 0:8], lhsT=mU, rhs=da[0:R, :], start=True, stop=True), reads=["m_da", "consts"], writes=[("ps", pm)])
                        S.op("pe", lambda e: e.matmul(ps[pm][0:R, 8:16], lhsT=mL, rhs=da[0:R, :], start=True, stop=True), reads=["m_da", "consts"], writes=[("ps", pm)])
                        if not smp:
                            S.op("pe", lambda e: e.matmul(ps[pm][:, 16:24], lhsT=self.onesf, rhs=da, start=True, stop=True), reads=["m_da", "consts2"], writes=[("ps", pm)])
                            S.op("act", lambda e: e.activation(out=ex, in_=ps[pm][:, 0:24], func=AF.Exp), reads=[("ps", pm)], writes=["m_ex"])
                        else:
                            S.op("act", lambda e: e.activation(out=ex[0:R, 0:16], in_=ps[pm][0:R, 0:16], func=AF.Exp), reads=[("ps", pm)], writes=["m_ex"])
                            S.op("dve", lambda e: e.tensor_tensor(out=dablk, in0=da[0:NS, :].unsqueeze(1).broadcast_to([NS, NB, 8]), in1=self.maskB.unsqueeze(2).broadcast_to([NS, NB, 8]), op=ALU.mult),
                                 reads=["m_da", "consts"], writes=["m_dablk"])
                            pm2 = self.nextps()
                            S.op("pe", lambda e: e.matmul(ps[pm2][:, 0:NB * 8], lhsT=self.onesf[0:NS, :], rhs=dablk.rearrange("p b h -> p (b h)"), start=True, stop=True), reads=["m_dablk", "consts2"], writes=[("ps", pm2)])
                            S.op("act", lambda e: e.activation(out=etots.rearrange("p b h -> p (b h)"), in_=ps[pm2][:, 0:NB * 8], func=AF.Exp), reads=[("ps", pm2)], writes=["m_etots"])
                        S.op("dve", lambda e: e.tensor_tensor(out=wts[0:R, :], in0=dtv[0:R, :], in1=ex[0:R, 8:16], op=ALU.mult), reads=["m_dt", "m_ex"], writes=["m_wts"])
                        if MBSTOP == 7:
                            return
                        py = self.nextlong()
                        for hh in range(8):
                            sl = hh % 2
                            S.op("dve", lambda e, hh=hh, sl=sl: e.tensor_scalar(out=daM[0:R, sl, 0:R], in0=mL, scalar1=da[0:R, hh:hh + 1], scalar2=None, op0=ALU.mult), reads=["m_da", "consts"], writes=[("m_daM", sl)])
                            pD = self.nextps()
                            S.op("pe", lambda e, sl=sl, pD=pD: e.matmul(ps[pD][0:R, 0:R], lhsT=daM[0:R, sl, 0:R], rhs=mU, start=True, stop=True), reads=[("m_daM", sl), "consts"], writes=[("ps", pD)])
                            S.op("act", lambda e, sl=sl, pD=pD: e.activation(out=seg[0:R, sl, 0:R], in_=ps[pD][0:R, 0:R], func=AF.Exp), reads=[("ps", pD)], writes=[("m_seg", sl)])
                            S.op("dve", lambda e, sl=sl, hh=hh: e.scalar_tensor_tensor(out=mT[0:R, sl, 0:R], in0=seg[0:R, sl, 0:R], scalar=dtv[0:R, hh:hh + 1], in1=cbm[0:R, 0:R], op0=ALU.mult, op1=ALU.mult),
                                 reads=[("m_seg", sl), "m_dt", "m_cbm"], writes=[("m_mT", sl)])
                            S.op("pe", lambda e, sl=sl, hh=hh: e.matmul(ps[py][0:R, hh * 64:(hh + 1) * 64], lhsT=mT[0:R, sl, 0:R], rhs=xtok[0:R, hh * 64:(hh + 1) * 64], start=True, stop=True),
                                 reads=[("m_mT", sl), "m_xtok"], writes=[("ps", py)])
                        if MBSTOP == 8:
                            return
                        v3 = lambda t_: t_.rearrange("p (h q) -> p h q", q=64)
                        pyi = self.nextlong()
                        if not smp:
                            S.op("pe", lambda e: e.matmul(ps[pyi][:, 0:512], lhsT=CT[:, t0:t0 + 128], rhs=STbf, start=True, stop=True), reads=[xbk[5], "m_STbf"], writes=[("ps", pyi)])
                        else:
                            S.op("dve", lambda e: e.tensor_tensor(out=Cblk, in0=CT[:, 0:NS].unsqueeze(1).broadcast_to([128, NB, NS]), in1=self.maskC, op=ALU.mult), reads=[xbk[5], "consts"], writes=["m_Cblk"])
                            S.op("dve", lambda e: e.tensor_tensor(out=v3(xw[0:R, :]), in0=v3(xtok[0:R, :]), in1=wts[0:R, :].unsqueeze(2).broadcast_to([R, 8, 64]), op=ALU.mult), reads=["m_xtok", "m_wts"], writes=["m_xw"])
                            S.op("pool", lambda e: e.memset(fz, 0.0), reads=["m_pre", "m_ST", "m_STbf"], writes=[("m_snat", k_) for k_ in range(2, 7)] + [("m_stT", 1), ("m_sbf", 0), ("m_sbf", 1)])
                            et4 = etots.rearrange("p b (q t) -> p b q t", t=2)
                            S.op("dve", lambda e: e.tensor_copy(out=etn[0:64, :, :], in_=et4[0:64, :, :, 0]), reads=["m_etots"], writes=["m_etn"])
                            S.op("dve", lambda e: e.tensor_copy(out=etn[64:128, :, :], in_=et4[64:128, :, :, 1]), reads=["m_etots", "m_etn"], writes=["m_etn"])
                            sbf = ST.bitcast(BF16).rearrange("p (s c) -> p s c", s=2)
                            for b in range(NB):
                                sl = b % 7
                                s2 = b % 2
                                slot = snat_slots[sl]
                                S.op("sp", lambda e, b=b, slot=slot: e.dma_start(out=slot, in_=st_ssm[b, 8 * g:8 * g + 8].rearrange("h p n -> (h p) n").rearrange("(q r) n -> r q n", r=128)),
                                     writes=[("m_snat", sl)], dma="m_snat%d" % sl)
                                S.op("act", lambda e, slot=slot, s2=s2: e.activation(out=sbf[:, s2, :], in_=slot.rearrange("p q n -> p (q n)"), func=AF.Copy), reads=[("m_snat", sl)], writes=[("m_sbf", s2)])
                                pq_ = self.nextps()
                                pqb = ps[pq_].bitcast(BF16)
                                for q_ in range(4):
                                    S.op("pe", lambda e, q_=q_, s2=s2, pqb=pqb: e.transpose(pqb[:, q_ * 128:(q_ + 1) * 128], sbf[:, s2, q_ * 128:(q_ + 1) * 128], self.identb), reads=[("m_sbf", s2), "consts2"], writes=[("ps", pq_)])
                                self.copy("dve", ST0bf[:, s2, :], pqb[:, 0:512], [("ps", pq_)], [("m_ST0bf", s2)])
                                S.op("pe", lambda e, b=b, s2=s2: e.matmul(ps[pyi][0:NS, 0:512], lhsT=Cblk[:, b, :], rhs=ST0bf[:, s2, :], start=(b == 0), stop=(b == NB - 1)),
                                     reads=["m_Cblk", ("m_ST0bf", s2)], writes=[("ps", pyi)])
                                S.op("dve", lambda e, b=b: e.tensor_scalar(out=yn[0:NS, :], in0=xw[0:NS, :], scalar1=self.maskB[:, b:b + 1], scalar2=None, op0=ALU.mult), reads=["m_xw", "consts"], writes=["m_yn"])
                                pS = self.nextps()
                                for q_ in range(4):
                                    S.op("pe", lambda e, q_=q_, pS=pS: e.matmul(ps[pS][:, q_ * 128:(q_ + 1) * 128], lhsT=yn[0:NS, q_ * 128:(q_ + 1) * 128], rhs=Btok[0:NS, :], start=True, stop=True), reads=["m_Btok", "m_yn"], writes=[("ps", pS)])
                                S.op("dve", lambda e, b=b, slot=slot: e.tensor_tensor(out=slot, in0=slot, in1=etn[:, b, :].unsqueeze(2).broadcast_to([128, 4, 128]), op=ALU.mult), reads=[("m_snat", sl), "m_etn"], writes=[("m_snat", sl)])
                                S.op("dve", lambda e, slot=slot, pS=pS: e.tensor_tensor(out=slot, in0=slot, in1=ps[pS][:, 0:512].rearrange("p (q n) -> p q n", q=4), op=ALU.add), reads=[("m_snat", sl), ("ps", pS)], writes=[("m_snat", sl)])
                                S.op("sp", lambda e, b=b, slot=slot: e.dma_start(out=ssm_s[b, 8 * g:8 * g + 8].rearrange("h p n -> (h p) n").rearrange("(q r) n -> r q n", r=128), in_=slot), reads=[("m_snat", sl)], dma="m_ssm_s%d" % sl)
                        ecb = ex[0:R, 0:8].unsqueeze(2).broadcast_to([R, 8, 64])
                        S.op("dve", lambda e: e.tensor_tensor(out=v3(y1[0:R, :]), in0=v3(ps[pyi][0:R, 0:512]), in1=ecb, op=ALU.mult), reads=[("ps", pyi), "m_ex"], writes=["m_y1"])
                        S.op("dve", lambda e: e.tensor_tensor(out=y1[0:R, :], in0=y1[0:R, :], in1=ps[py][0:R, 0:512], op=ALU.add), reads=[("ps", py), "m_y1"], writes=["m_y1"])
                        S.op("dve", lambda e: e.tensor_tensor(out=v3(xd[0:R, :]), in0=v3(xtok[0:R, :]), in1=self.mb_D[0:R, 8 * g:8 * g + 8].unsqueeze(2).broadcast_to([R, 8, 64]), op=ALU.mult), reads=["m_xtok", "consts"], writes=["m_xd"])
                        S.op("dve", lambda e: e.tensor_tensor(out=y1[0:R, :], in0=y1[0:R, :], in1=xd[0:R, :], op=ALU.add), reads=["m_xd", "m_y1"], writes=["m_y1"])
                        S.op("dve", lambda e: e.tensor_tensor(out=y1[0:R, :], in0=y1[0:R, :], in1=zs[0:R, :], op=ALU.mult), reads=["m_zs", "m_y1"], writes=["m_y1"])
                        S.op("act", lambda e: e.activation(out=xd[0:R, :], in_=y1[0:R, :], func=AF.Square, accum_out=ssq[0:R, 0:1]), reads=["m_y1", "m_xd"], writes=["m_xd", "m_ssq"])
                        S.op("act", lambda e: e.activation(out=ssq[0:R, 1:2], in_=ssq[0:R, 0:1], func=AF.Ln, scale=1.0 / 512, bias=EPS), reads=["m_ssq"], writes=["m_ssq"])
                        S.op("act", lambda e: e.activation(out=ssq[0:R, 1:2], in_=ssq[0:R, 1:2], func=AF.Exp, scale=-0.5), reads=["m_ssq"], writes=["m_ssq"])
                        S.op("dve", lambda e: e.scalar_tensor_tensor(out=yn[0:R, :], in0=y1[0:R, :], scalar=ssq[0:R, 1:2], in1=ngt[0:R, :], op0=ALU.mult, op1=ALU.mult),
                             reads=["m_y1", "m_ssq", "m_ng"], writes=["m_yn"])
                        pt2 = self.nextps()
                        pt2b = ps[pt2].bitcast(BF16)
                        for fc in range(4):
                            S.op("pe", lambda e, fc=fc: e.transpose(pt2b[:, fc * 128:fc * 128 + R], yn[0:R, fc * 128:(fc + 1) * 128], self.identb[0:R, 0:R]), reads=["m_yn", "consts2"], writes=[("ps", pt2)])
                        self.copy("dve", yTb[:, :, t0:t0 + R], pt2b[:, 0:512].rearrange("p (f t) -> p f t", f=4)[:, :, 0:R], [("ps", pt2)], [("m_yTb", ct)])
                        if MBSTOP == 9:
                            return
                        if not smp:
                            S.op("dve", lambda e: e.tensor_tensor(out=v3(xw[0:R, :]), in0=v3(xtok[0:R, :]), in1=wts[0:R, :].unsqueeze(2).broadcast_to([R, 8, 64]), op=ALU.mult), reads=["m_xtok", "m_wts"], writes=["m_xw"])
                            pS = self.nextps()
                            S.op("pe", lambda e: e.matmul(ps[pS][:, 0:512], lhsT=Btok, rhs=xw, start=True, stop=True), reads=["m_Btok", "m_xw"], writes=[("ps", pS)])
                            S.op("dve", lambda e: e.tensor_tensor(out=v3(ST), in0=v3(ST), in1=ex[:, 16:24].unsqueeze(2).broadcast_to([128, 8, 64]), op=ALU.mult), reads=["m_ST", "m_ex"], writes=["m_ST"])
                            S.op("dve", lambda e: e.tensor_tensor(out=ST, in0=ST, in1=ps[pS][:, 0:512], op=ALU.add), reads=["m_ST", ("ps", pS)], writes=["m_ST"])
                            S.op("pool", lambda e: e.tensor_copy(out=STbf, in_=ST), reads=["m_ST"], writes=["m_STbf"])
                    for dc in range(KC):
                        pi = self.nextps()
                        for fc in range(4):
                            S.op("pe", lambda e, pi=pi, dc=dc, fc=fc: e.matmul(ps[pi][:, 0:w], lhsT=wout[:, fc, dc * 128:(dc + 1) * 128], rhs=yTb[:, fc, 0:w], start=(fc == 0), stop=(fc == 3)),
                                 reads=[("m_wout", fc)] + [("m_yTb", c_) for c_ in range(4)], writes=[("ps", pi)])
                        S.op("dve", lambda e, pi=pi, dc=dc: e.tensor_tensor(out=xres[:, dc, c0:c0 + w], in0=xres[:, dc, c0:c0 + w], in1=ps[pi][:, 0:w], op=ALU.add),
                             reads=[("ps", pi), ("xres", dc, c0)], writes=[("xres", dc, c0)])
                    if (not smp) and c0 + w == TP:
                        pq2 = self.nextps()
                        for q_ in range(4):
                            S.op("pe", lambda e, q_=q_: e.transpose(ps[pq2][:, q_ * 128:(q_ + 1) * 128], ST[:, q_ * 128:(q_ + 1) * 128], self.ident), reads=["m_ST", "consts"], writes=[("ps", pq2)])
                        self.copy("act", stT.rearrange("p q n -> p (q n)"), ps[pq2][:, 0:512], [("ps", pq2)], [("m_stT", 0)])
                        S.op("sp", lambda e: e.dma_start(out=ssm_p[8 * g:8 * g + 8].rearrange("h p n -> (h p) n").rearrange("(q r) n -> r q n", r=128), in_=stT), reads=[("m_stT", 0)], dma="m_ssm_p")


def make_in_map(inp, core, TP, names):
    b0 = core * NB
    m = {}
    m["x_p"] = np.ascontiguousarray(inp["x_prompt"][core, :TP])
    m["x_s"] = np.ascontiguousarray(inp["x_sample"][b0:b0 + NB].reshape(NS, D))
    m["ident"] = np.eye(128, dtype=np.float32)
    m["norm_mix"] = np.ascontiguousarray(_fm(inp["norm_mix"]))
    m["norm_ffn"] = np.ascontiguousarray(_fm(inp["norm_ffn"]))
    m["norm_final"] = np.ascontiguousarray(_fm(inp["norm_final"]))
    for k in ("ffn_w_gate", "ffn_w_up", "ffn_w_down"):
        m[k] = inp[k]
    extra_in_map(m, inp, core, TP)
    return {k: np.ascontiguousarray(m[k], dtype=np.float32) for k in names}


def _masks():
    r = np.arange(128)
    maskU2 = ((r[:, None] // 64 == r[None, :] // 64) & (r[None, :] % 64 >= r[:, None] % 64)).astype(np.float32)
    r = np.arange(64)
    maskS = ((r[:, None] // TS == r[None, :] // TS) & (r[None, :] >= r[:, None])).astype(np.float32)
    maskB = (r[:, None] // TS == np.arange(NB)[None, :]).astype(np.float32)
    return maskU2, maskS, maskB


def extra_in_map(m, inp, core, TP):
    b0 = core * NB
    m["maskU2"], m["maskS"], m["maskB"] = _masks()
    m["hg_lb_logits"] = _fm(inp["hg_lb_logits"])
    m["hg_norm"] = np.ascontiguousarray(inp["hg_norm"].T)
    m["hg_w_in"] = inp["hg_w_in"]
    m["hg_w_out"] = inp["hg_w_out"]
    m["st_hg"] = inp["state_hgrn"][:, b0:b0 + NB]
    r = np.arange(128)
    m["maskU"] = (r[None, :] >= r[:, None]).astype(np.float32)
    m["sL"] = (r[:, None] > r[None, :]).astype(np.float32)
    r = np.arange(64)
    m["sLS"] = ((r[:, None] // TS == r[None, :] // TS) & (r[:, None] > r[None, :])).astype(np.float32)
    m["maskC"] = np.broadcast_to((r[None, :] // TS == np.arange(NB)[:, None]).astype(np.float32)[None], (128, NB, NS))
    r = np.arange(128)
    m["sU"] = (r[:, None] < r[None, :]).astype(np.float32)
    m["blk1"] = (r[:, None] // 64 == r[None, :] // 64).astype(np.float32)
    r = np.arange(64)
    m["sUS"] = ((r[:, None] // TS == r[None, :] // TS) & (r[:, None] < r[None, :])).astype(np.float32)
    m["rw_mu"] = _fm(inp["rw_mu"][0])
    for k in ("rw_w0", "rw_a0", "rw_k_k", "rw_k_a", "rw_lnx_w", "rw_lnx_b"):
        m[k] = _fm(inp[k][0])
    m["rw_r_k"] = _fm(inp["rw_r_k"][0].reshape(D))
    m["rw_w_rkv"] = inp["rw_w_rkv"][0]
    for k in ("rw_w1", "rw_w2", "rw_a1", "rw_a2", "rw_g1", "rw_g2", "rw_w_out"):
        m[k] = inp[k][0]
    m["st_wkv"] = inp["state_wkv"][0, b0:b0 + NB]
    m["st_shift"] = inp["state_shift"][0, b0:b0 + NB]
    m["mb_conv_w"] = np.ascontiguousarray(inp["mb_conv_w"][0].reshape(4, 24, 128).transpose(2, 1, 0))
    m["mb_conv_b"] = np.ascontiguousarray(inp["mb_conv_b"][0].reshape(24, 128).T)
    for k in ("mb_dt_bias", "mb_A_log", "mb_D"):
        m[k] = np.broadcast_to(inp[k][0][None, :], (128, 32))
    m["mb_norm"] = np.broadcast_to(inp["mb_norm"][0][None, :], (128, 2048))
    m["mb_w_in"] = inp["mb_w_in"]
    m["mb_w_out"] = inp["mb_w_out"]
    m["st_ssm"] = inp["state_ssm"][0, b0:b0 + NB]
    m["st_conv"] = inp["state_conv"][0, b0:b0 + NB]


_CACHE = {}


def run_cores(inp, TP, cores, layers=(0, 1, 2, 0), with_ffn=True):
    key = (TP, tuple(layers), with_ffn)
    bld = Builder(TP, layers, with_ffn)
    nc = bld.build()
    names = [k for k, v in bld.dram.items() if k in bld.in_names]
    in_maps = [make_in_map(inp, c, TP, names) for c in cores]
    res = run_bass_kernel_spmd(nc, in_maps, core_ids=list(range(len(cores))))
    return res.results


def kernel(**inputs):
    inp = {k: np.asarray(v) for k, v in inputs.items()}
    TP = inp["x_prompt"].shape[1]
    res = run_cores(inp, TP, list(range(NCORES)))
    st = lambda k: np.stack([r[k] for r in res], axis=0)
    cat = lambda k, ax=0: np.concatenate([r[k] for r in res], axis=ax)
    f = lambda a: np.ascontiguousarray(a, dtype=np.float32)
    y_p = st("y_p")
    y_s = cat("y_s").reshape(NCORES * NB, TS, D)
    hg_p = np.stack([r["hg_p"] for r in res], axis=1)
    hg_s = cat("hg_s", 1)
    return (f(y_p), f(y_s), f(hg_p), f(hg_s), f(st("wkv_p")[None]), f(cat("wkv_s")[None]),
            f(cat("sh_p")[None]), f(cat("sh_s")[None]), f(st("ssm_p")[None]), f(cat("ssm_s")[None]),
            f(st("cv_p")[None]), f(cat("cv_s")[None]))
```

```python
import numpy as np
from contextlib import ExitStack, contextmanager
import concourse.bass as bass
import concourse.mybir as mybir
from concourse.bass_utils import run_bass_kernel_spmd

F32 = mybir.dt.float32
BF16 = mybir.dt.bfloat16
AF = mybir.ActivationFunctionType
ALU = mybir.AluOpType
AX = mybir.AxisListType

D = 1024
KC = 8
DFF = 2816
FC = 22
NCORES = 8
NB = 16
TS = 4
NS = NB * TS
EPS = 1e-6
ROT = 30000
import os
MBSTOP = 0
RWSTOP = 0


class _Rec:
    def __init__(self):
        self.call = None

    def __getattr__(self, name):
        def f(*a, **k):
            self.call = (name, a, k)
            return self
        return f


class Sched:
    ENGS = ("pe", "act", "dve", "pool", "sp")

    def __init__(self, nc):
        self.nc = nc
        self.streams = {e: [] for e in self.ENGS}
        self.count = {e: 0 for e in self.ENGS}
        self.observed = {e: {} for e in self.ENGS}
        self.last_write = {}
        self.readers = {}
        self.dmacount = {}
        self.semkeys = []
        self.lastmark = {}

    def _semkey(self, sk):
        if sk not in self.lastmark:
            self.semkeys.append(sk)
        return sk

    def op(self, eng, fn, reads=(), writes=(), dma=None):
        rec = _Rec()
        fn(rec)
        fn = rec.call
        need = {}

        def add(m):
            if m is None:
                return
            sk, v = m
            if need.get(sk, 0) < v:
                need[sk] = v

        for k in reads:
            add(self.last_write.get(k))
            if isinstance(k, tuple) and k[0] == "ps":
                for m in self.readers.get(k, ()):
                    add(m)
        for k in writes:
            add(self.last_write.get(k))
            for m in self.readers.get(k, ()):
                add(m)
        st = self.streams[eng]
        obs = self.observed[eng]
        for sk, v in need.items():
            if eng == "pe" and sk[0] == "pe":
                continue
            if obs.get(sk, 0) >= v:
                continue
            st.append(("wait", sk, v))
            obs[sk] = v
        if dma is not None:
            sk = self._semkey(("dma", dma))
            self.dmacount[dma] = self.dmacount.get(dma, 0) + 16
            marker = (sk, self.dmacount[dma])
            amt = 16
        else:
            n = self.count[eng]
            sk = self._semkey((eng, n // ROT))
            marker = (sk, n % ROT + 1)
            self.count[eng] = n + 1
            amt = 1
        self.lastmark[sk] = marker[1]
        st.append(("op", fn, sk, amt))
        for k in writes:
            self.last_write[k] = marker
            self.readers[k] = []
        for k in reads:
            if k not in writes:
                self.readers.setdefault(k, []).append(marker)
        return marker

    def barrier(self):
        for e in self.ENGS:
            st = self.streams[e]
            obs = self.observed[e]
            for sk, v in self.lastmark.items():
                if obs.get(sk, 0) >= v:
                    continue
                st.append(("wait", sk, v))
                obs[sk] = v
        self.last_write = {}
        self.readers = {}

    def emit(self):
        nc = self.nc
        sems = {}
        for sk in self.semkeys:
            sems[sk] = nc.alloc_semaphore(name="s_" + "_".join(str(x) for x in sk))
        streams = self.streams
        engmap = {"pe": "tensor", "act": "scalar", "dve": "vector", "pool": "gpsimd", "sp": "sync"}

        def run(e, engine):
            for ent in streams[e]:
                if ent[0] == "wait":
                    engine.wait_ge(sems[ent[1]], ent[2])
                else:
                    name, a, k = ent[1]
                    ins = getattr(engine, name)(*a, **k)
                    ins.then_inc(sems[ent[2]], ent[3])

        with nc.Block() as block:
            for e in self.ENGS:
                if not streams[e]:
                    continue

                def mk(e=e):
                    def f(engine):
                        run(e, engine)
                    return f

                getattr(block, engmap[e])(mk())


def _fm(v):
    v = np.asarray(v, np.float32)
    lead = v.shape[:-1]
    v = v.reshape(lead + (KC, 128))
    v = np.moveaxis(v, -1, 0)
    return np.ascontiguousarray(v)


class Builder:
    def __init__(self, TP, layers=(0, 1, 2, 0), with_ffn=True):
        self.TP = TP
        self.N = TP + NS
        self.layers = layers
        self.with_ffn = with_ffn
        self.nc = bass.Bass("TRN2", target_bir_lowering=False)
        self.S = Sched(self.nc)
        self.dram = {}
        self.in_names = []
        self.NSTG = 3
        self.stg_i = 0
        self.ps_i = 0
        self.rr = 0
        self.tbs = [(i * 512, 512, "p") for i in range(TP // 512)] + [(TP, NS, "s")]

    def din(self, name, shape):
        t = self.nc.dram_tensor(name, list(shape), F32, kind="ExternalInput").ap()
        self.dram[name] = t
        self.in_names.append(name)
        return t

    def dout(self, name, shape):
        t = self.nc.dram_tensor(name, list(shape), F32, kind="ExternalOutput").ap()
        self.dram[name] = t
        return t

    @contextmanager
    def T(self, name, shape, dt=F32):
        self.uid = getattr(self, "uid", 0) + 1
        with self.nc.sbuf_tensor("%s_%d" % (name, self.uid), list(shape), dt) as t:
            yield t.ap()

    def sb(self, name, shape, dt=F32):
        return self.nc.alloc_sbuf_tensor(name, list(shape), dt).ap()

    def nextps(self):
        i = self.ps_i % 6
        self.ps_i += 1
        return i

    def nextlong(self):
        self.pl_i = getattr(self, "pl_i", 0) + 1
        return 6 + self.pl_i % 2

    def ew(self):
        self.rr += 1
        return "act" if self.rr % 2 else "dve"

    def copy(self, eng, out, in_, reads, writes):
        if eng == "act":
            self.S.op("act", lambda e: e.activation(out=out, in_=in_, func=AF.Copy), reads=reads, writes=writes)
        else:
            self.S.op(eng, lambda e: e.tensor_copy(out=out, in_=in_), reads=reads, writes=writes)

    def load_w(self, dst, src, shape, wkey, scale=None):
        S = self.S
        slot = self.stg_i % self.NSTG
        self.stg_i += 1
        rows = shape[0]
        free = int(np.prod(shape[1:]))
        assert free <= 1024
        st = self.stage[slot][0:rows, 0:free]
        if len(shape) == 3:
            st = st.rearrange("p (a b) -> p a b", a=shape[1])
        S.op("sp", lambda e: e.dma_start(out=st, in_=src), writes=[("stg", slot)], dma="stg%d" % slot)
        if scale is None:
            S.op("pool", lambda e: e.tensor_copy(out=dst, in_=st), reads=[("stg", slot)], writes=[wkey])
        else:
            S.op("pool", lambda e: e.tensor_scalar(out=dst, in0=st, scalar1=scale, scalar2=None, op0=ALU.mult),
                 reads=[("stg", slot), "consts"], writes=[wkey])

    def build(self):
        nc, S, TP, N = self.nc, self.S, self.TP, self.N
        nl = len(self.layers)
        x_p = self.din("x_p", [TP, D])
        x_s = self.din("x_s", [NS, D])
        ident_d = self.din("ident", [128, 128])
        nmix_d = self.din("norm_mix", [128, 4, KC])
        nffn_d = self.din("norm_ffn", [128, 4, KC])
        nfin_d = self.din("norm_final", [128, KC])
        wg_d = self.din("ffn_w_gate", [4, D, DFF])
        wu_d = self.din("ffn_w_up", [4, D, DFF])
        wd_d = self.din("ffn_w_down", [4, DFF, D])
        y_p = self.dout("y_p", [TP, D])
        y_s = self.dout("y_s", [NS, D])

        self.xres = self.sb("xres", [128, KC, N])
        self.hT = self.sb("hT", [128, KC, N + 2], BF16)
        self.stage = [self.sb("stage%d" % i, [128, 1024]) for i in range(self.NSTG)]
        self.ident = self.sb("ident_sb", [128, 128])
        self.identb = self.sb("identb_sb", [128, 128], BF16)
        self.onesb = self.sb("onesb", [128, 128], BF16)
        self.nmix = self.sb("nmix", [128, 4, KC])
        self.nffn = self.sb("nffn", [128, 4, KC])
        self.nfin = self.sb("nfin", [128, KC])
        self.ps = [nc.alloc_psum_tensor("ps%d" % i, [128, 512], F32).ap() for i in range(8)]
        xres, hT = self.xres, self.hT

        S.op("sp", lambda e: e.dma_start(out=self.ident, in_=ident_d), writes=["consts"], dma="c0")
        S.op("sp", lambda e: e.dma_start(out=self.nmix, in_=nmix_d), writes=["consts"], dma="c0")
        S.op("sp", lambda e: e.dma_start(out=self.nffn, in_=nffn_d), writes=["consts"], dma="c0")
        S.op("sp", lambda e: e.dma_start(out=self.nfin, in_=nfin_d), writes=["consts"], dma="c0")
        S.op("pool", lambda e: e.tensor_copy(out=self.identb, in_=self.ident), reads=["consts"], writes=["consts2"])
        S.op("pool", lambda e: e.memset(self.onesb, 1.0), writes=["consts2"])
        self.onesf = self.sb("onesf", [128, 128])
        S.op("pool", lambda e: e.memset(self.onesf, 1.0), writes=["consts2"])
        S.op("pool", lambda e: e.memset(hT[:, :, 0:2], 0.0), writes=["hT0"])
        self.extra_consts()
        S.barrier()

        with self.T("xin0", [128, D], F32) as xin0, self.T("xin1", [128, D], F32) as xin1:
            xins = [xin0, xin1]
            ntile = TP // 128 + 1
            for j in range(ntile):
                xin = xins[j % 2]
                rows = 128 if j < TP // 128 else NS
                src = x_p[j * 128:(j + 1) * 128, :] if j < TP // 128 else x_s
                S.op("sp", lambda e, xin=xin, rows=rows, src=src: e.dma_start(out=xin[0:rows, :], in_=src),
                     writes=[("xin", j % 2)], dma="xin%d" % (j % 2))
                for half in range(2):
                    pi = self.nextps()
                    for q in range(4):
                        kc = half * 4 + q
                        S.op("pe", lambda e, pi=pi, q=q, kc=kc, xin=xin, rows=rows: e.transpose(
                            self.ps[pi][:, q * 128:q * 128 + rows], xin[0:rows, kc * 128:(kc + 1) * 128],
                            self.ident[0:rows, 0:rows]),
                            reads=[("xin", j % 2), "consts"], writes=[("ps", pi)])
                    dst = xres[:, half * 4:(half + 1) * 4, j * 128:j * 128 + rows]
                    srcp = self.ps[pi].rearrange("p (q t) -> p q t", q=4)[:, :, 0:rows]
                    self.copy(self.ew(), dst, srcp, [("ps", pi)], [("xres", j)])
        S.barrier()

        for li, kind in enumerate(self.layers):
            self.cur_li = li
            self.rmsnorm(self.nmix[:, li, :], to_h=True)
            S.barrier()
            if kind == 0:
                self.hgrn2(li // 3)
            elif kind == 1:
                self.rwkv7(li // 3)
            elif kind == 2:
                self.mamba2(li // 3)
            S.barrier()
            if self.with_ffn:
                self.rmsnorm(self.nffn[:, li, :], to_h=True)
                S.barrier()
                self.ffn(li, wg_d, wu_d, wd_d)
                S.barrier()
        self.final_norm(y_p, y_s)
        S.barrier()
        S.emit()
        return nc

    def extra_consts(self):
        nc, S = self.nc, self.S
        def cload(name, shape):
            d = self.din(name, shape)
            t = self.sb("c_" + name, shape)
            S.op("sp", lambda e: e.dma_start(out=t, in_=d), writes=["consts"], dma="c0")
            return t
        self.maskU2 = cload("maskU2", [128, 128])
        self.maskS = cload("maskS", [64, 64])
        self.maskB = cload("maskB", [64, 16])
        lg = cload("hg_lb_logits", [128, 2, KC])
        self.hgn = cload("hg_norm", [128, 2])
        self.hg_lb = self.sb("hg_lb", [128, 2, KC])
        self.hg_oml = self.sb("hg_oml", [128, 2, KC])
        self.hg_noml = self.sb("hg_noml", [128, 2, KC])
        lb, oml, noml = self.hg_lb, self.hg_oml, self.hg_noml
        S.op("dve", lambda e: e.memset(lb[:, 0, :], 0.0), writes=["hgc0"])
        S.op("dve", lambda e: e.tensor_tensor(out=lb[:, 1, :], in0=lg[:, 1, :], in1=lg[:, 0, :], op=ALU.subtract), reads=["consts"], writes=["hgc1"])
        S.op("act", lambda e: e.activation(out=lb[:, 1, :], in_=lb[:, 1, :], func=AF.Sigmoid), reads=["hgc1"], writes=["hgc1"])
        S.op("dve", lambda e: e.tensor_scalar(out=oml, in0=lb, scalar1=-1.0, scalar2=1.0, op0=ALU.mult, op1=ALU.add), reads=["hgc0", "hgc1"], writes=["hgc2"])
        S.op("dve", lambda e: e.tensor_scalar(out=noml, in0=lb, scalar1=1.0, scalar2=-1.0, op0=ALU.mult, op1=ALU.add), reads=["hgc0", "hgc1"], writes=["hgc3"])
        self.maskU = cload("maskU", [128, 128])
        self.sL = cload("sL", [128, 128])
        self.sLS = cload("sLS", [64, 64])
        self.maskC = cload("maskC", [128, NB, NS])
        self.mb_cw = cload("mb_conv_w", [128, 24, 4])
        self.mb_cb = cload("mb_conv_b", [128, 24])
        self.mb_dtb = cload("mb_dt_bias", [128, 32])
        alog = cload("mb_A_log", [128, 32])
        self.mb_D = cload("mb_D", [128, 32])
        self.din("mb_norm", [128, 2048])
        self.mb_negA = self.sb("mb_negA", [128, 32])
        S.op("act", lambda e: e.activation(out=self.mb_negA, in_=alog, func=AF.Exp), reads=["consts"], writes=["mbc0"])
        S.op("dve", lambda e: e.tensor_scalar(out=self.mb_negA, in0=self.mb_negA, scalar1=-1.0, scalar2=None, op0=ALU.mult), reads=["mbc0"], writes=["mbc0"])
        self.din("mb_w_in", [1, D, 5152])
        self.din("mb_w_out", [1, 2048, D])
        self.din("st_ssm", [NB, 32, 64, 128])
        self.din("st_conv", [NB, 3, 3072])
        self.dout("ssm_p", [32, 64, 128])
        self.dout("ssm_s", [NB, 32, 64, 128])
        self.dout("cv_p", [3, 3072])
        self.dout("cv_s", [NB, 3, 3072])
        self.sU = cload("sU", [128, 128])
        self.sUS = cload("sUS", [64, 64])
        self.blk1 = cload("blk1", [128, 128])
        self.rw_mu = cload("rw_mu", [128, 6, KC])
        self.rw_omu = self.sb("rw_omu", [128, 6, KC])
        S.op("dve", lambda e: e.tensor_scalar(out=self.rw_omu, in0=self.rw_mu, scalar1=-1.0, scalar2=1.0, op0=ALU.mult, op1=ALU.add), reads=["consts"], writes=["rwc0"])
        self.rw_vec = {}
        for nm in ("rw_w0", "rw_a0", "rw_k_k", "rw_k_a", "rw_r_k", "rw_lnx_w", "rw_lnx_b"):
            self.rw_vec[nm] = cload(nm, [128, KC])
        self.rw_omka = self.sb("rw_omka", [128, KC])
        S.op("dve", lambda e: e.tensor_scalar(out=self.rw_omka, in0=self.rw_vec["rw_k_a"], scalar1=-1.0, scalar2=1.0, op0=ALU.mult, op1=ALU.add), reads=["consts"], writes=["rwc1"])
        for nm, shp in (("rw_w_rkv", [3, D, D]), ("rw_w1", [D, 64]), ("rw_w2", [64, D]), ("rw_a1", [D, 64]), ("rw_a2", [64, D]),
                        ("rw_g1", [D, 128]), ("rw_g2", [128, D]), ("rw_w_out", [D, D]), ("st_wkv", [NB, 16, 64, 64]), ("st_shift", [NB, D])):
            self.din(nm, shp)
        self.dout("wkv_p", [16, 64, 64])
        self.dout("wkv_s", [NB, 16, 64, 64])
        self.dout("sh_p", [1, D])
        self.dout("sh_s", [NB, D])
        self.din("hg_w_in", [2, D, 4 * D])
        self.din("hg_w_out", [2, D, D])
        self.din("st_hg", [2, NB, 8, 128, 128])
        self.dout("hg_p", [2, 8, 128, 128])
        self.dout("hg_s", [2, NB, 8, 128, 128])

    def rmsnorm(self, gain, to_h=True, out_f32=None):
        nc, S = self.nc, self.S
        xres, hT = self.xres, self.hT
        with self.T("n_sq", [128, 2, 512], BF16) as sq, self.T("n_r", [128, 2, 512], F32) as rr:
            for bi, (c0, w, kind) in enumerate(self.tbs):
                pi = self.nextps()
                for kc in range(KC):
                    s = kc % 2
                    S.op("act", lambda e, s=s, kc=kc, c0=c0, w=w: e.activation(out=sq[:, s, 0:w], in_=xres[:, kc, c0:c0 + w], func=AF.Square),
                         reads=[("xres", "all")], writes=[("n_sq", s)])
                    S.op("pe", lambda e, pi=pi, s=s, kc=kc, w=w: e.matmul(self.ps[pi][:, 0:w], lhsT=self.onesb, rhs=sq[:, s, 0:w], start=(kc == 0), stop=(kc == KC - 1)),
                         reads=[("n_sq", s), "consts2"], writes=[("ps", pi)])
                r = rr[:, bi % 2, 0:w]
                S.op("act", lambda e, pi=pi, r=r, w=w: e.activation(out=r, in_=self.ps[pi][:, 0:w], func=AF.Ln, scale=1.0 / D, bias=EPS),
                     reads=[("ps", pi)], writes=[("n_r", bi % 2)])
                S.op("act", lambda e, r=r: e.activation(out=r, in_=r, func=AF.Exp, scale=-0.5),
                     reads=[("n_r", bi % 2)], writes=[("n_r", bi % 2)])
                for kc in range(KC):
                    if out_f32 is None:
                        dst = hT[:, kc, 2 + c0:2 + c0 + w]
                    else:
                        dst = out_f32(kc, c0, w)
                    S.op("dve", lambda e, dst=dst, kc=kc, c0=c0, w=w, r=r: e.scalar_tensor_tensor(
                        out=dst, in0=xres[:, kc, c0:c0 + w], scalar=gain[:, kc:kc + 1], in1=r, op0=ALU.mult, op1=ALU.mult),
                        reads=[("xres", "all"), ("n_r", bi % 2), "consts"], writes=[("hT", kc, bi)])

    def ffn(self, li, wg_d, wu_d, wd_d):
        nc, S = self.nc, self.S
        xres, hT = self.xres, self.hT
        nprompt = len(self.tbs) - 1
        half = max(1, nprompt // 2)
        sbs = [self.tbs[:half], self.tbs[half:]] if nprompt >= 2 else [self.tbs]
        maxw = max(sum(w for (_, w, _) in sb_) for sb_ in sbs)
        with self.T("f_act", [128, FC, maxw], BF16) as act, \
                self.T("f_wgu", [128, 2, KC, 2, 256], BF16) as wgu, \
                self.T("f_wd", [128, 2, FC, 128], BF16) as wd, \
                self.T("f_sg", [128, 2, 512], F32) as sg:
            for sbi, sb_ in enumerate(sbs):
                base = sb_[0][0]
                for fp in range(FC // 2):
                    slot = fp % 2
                    for gi, wsrc in enumerate((wg_d, wu_d)):
                        for kk in range(0, KC, 4):
                            src = wsrc[li, kk * 128:(kk + 4) * 128, fp * 256:(fp + 1) * 256].rearrange("(k p) c -> p k c", p=128)
                            self.load_w(wgu[:, slot, kk:kk + 4, gi, :], src, [128, 4, 256], ("f_wgu", slot, gi, kk))
                    for fi in range(2):
                        f = fp * 2 + fi
                        for (c0, w, kind) in sb_:
                            pg, pu = self.nextps(), self.nextps()
                            for gi, pi in ((0, pg), (1, pu)):
                                for kc in range(KC):
                                    S.op("pe", lambda e, pi=pi, slot=slot, kc=kc, gi=gi, fi=fi, c0=c0, w=w: e.matmul(
                                        self.ps[pi][:, 0:w], lhsT=wgu[:, slot, kc, gi, fi * 128:(fi + 1) * 128],
                                        rhs=hT[:, kc, 2 + c0:2 + c0 + w], start=(kc == 0), stop=(kc == KC - 1)),
                                        reads=[("f_wgu", slot, gi, 0), ("f_wgu", slot, gi, 4), ("hT", "all")], writes=[("ps", pi)])
                            ss = self.rr % 2
                            self.rr += 1
                            S.op("act", lambda e, pg=pg, ss=ss, w=w: e.activation(out=sg[:, ss, 0:w], in_=self.ps[pg][:, 0:w], func=AF.Silu),
                                 reads=[("ps", pg)], writes=[("f_sg", ss)])
                            S.op("dve", lambda e, pu=pu, ss=ss, f=f, c0=c0, w=w: e.tensor_tensor(
                                out=act[:, f, c0 - base:c0 - base + w], in0=sg[:, ss, 0:w], in1=self.ps[pu][:, 0:w], op=ALU.mult),
                                reads=[("ps", pu), ("f_sg", ss)], writes=[("f_act", f)])
                for dc in range(KC):
                    slot = dc % 2
                    for f0 in range(0, FC, 8):
                        nf = min(8, FC - f0)
                        src = wd_d[li, f0 * 128:(f0 + nf) * 128, dc * 128:(dc + 1) * 128].rearrange("(k p) c -> p k c", p=128)
                        self.load_w(wd[:, slot, f0:f0 + nf, :], src, [128, nf, 128], ("f_wd", slot, f0))
                    for (c0, w, kind) in sb_:
                        pi = self.nextps()
                        for f in range(FC):
                            S.op("pe", lambda e, pi=pi, slot=slot, f=f, c0=c0, w=w: e.matmul(
                                self.ps[pi][:, 0:w], lhsT=wd[:, slot, f, :],
                                rhs=act[:, f, c0 - base:c0 - base + w], start=(f == 0), stop=(f == FC - 1)),
                                reads=[("f_wd", slot, (f // 8) * 8), ("f_act", f)], writes=[("ps", pi)])
                        S.op("dve", lambda e, pi=pi, dc=dc, c0=c0, w=w: e.tensor_tensor(
                            out=xres[:, dc, c0:c0 + w], in0=xres[:, dc, c0:c0 + w], in1=self.ps[pi][:, 0:w], op=ALU.add),
                            reads=[("ps", pi), ("xres", dc, c0)], writes=[("xres", dc, c0)])

    def final_norm(self, y_p, y_s):
        nc, S, TP = self.nc, self.S, self.TP
        xres = self.xres
        gain = self.nfin
        with self.T("fn_y", [128, KC, 512], F32) as yT, self.T("fn_o", [128, 2, D], F32) as yo, \
                self.T("fn_sq", [128, 2, 512], BF16) as sq, self.T("fn_r", [128, 512], F32) as rr:
            cnt = 0
            for bi, (c0, w, kind) in enumerate(self.tbs):
                pi = self.nextps()
                for kc in range(KC):
                    s = kc % 2
                    S.op("act", lambda e, s=s, kc=kc, c0=c0, w=w: e.activation(out=sq[:, s, 0:w], in_=xres[:, kc, c0:c0 + w], func=AF.Square),
                         reads=[("xres", "all")], writes=[("fn_sq", s)])
                    S.op("pe", lambda e, s=s, kc=kc, pi=pi, w=w: e.matmul(self.ps[pi][:, 0:w], lhsT=self.onesb, rhs=sq[:, s, 0:w], start=(kc == 0), stop=(kc == KC - 1)),
                         reads=[("fn_sq", s), "consts2"], writes=[("ps", pi)])
                r = rr[:, 0:w]
                S.op("act", lambda e, pi=pi, r=r, w=w: e.activation(out=r, in_=self.ps[pi][:, 0:w], func=AF.Ln, scale=1.0 / D, bias=EPS),
                     reads=[("ps", pi)], writes=["fn_r"])
                S.op("act", lambda e, r=r: e.activation(out=r, in_=r, func=AF.Exp, scale=-0.5), reads=["fn_r"], writes=["fn_r"])
                for kc in range(KC):
                    S.op("dve", lambda e, kc=kc, c0=c0, w=w, r=r: e.scalar_tensor_tensor(
                        out=yT[:, kc, 0:w], in0=xres[:, kc, c0:c0 + w], scalar=gain[:, kc:kc + 1], in1=r, op0=ALU.mult, op1=ALU.mult),
                        reads=[("xres", "all"), "fn_r", "consts"], writes=[("fn_y", kc)])
                for j in range((w + 127) // 128):
                    rows = min(128, w - j * 128)
                    os_ = cnt % 2
                    cnt += 1
                    for half in range(2):
                        pi = self.nextps()
                        for q in range(4):
                            kc = half * 4 + q
                            S.op("pe", lambda e, pi=pi, q=q, kc=kc, j=j, rows=rows: e.transpose(
                                self.ps[pi][0:rows, q * 128:(q + 1) * 128], yT[:, kc, j * 128:j * 128 + rows], self.ident),
                                reads=[("fn_y", kc), "consts"], writes=[("ps", pi)])
                        self.copy(self.ew(), yo[0:rows, os_, half * 512:(half + 1) * 512], self.ps[pi][0:rows, :], [("ps", pi)], [("fn_o", os_, half)])
                    if kind == "p":
                        dst = y_p[c0 + j * 128:c0 + j * 128 + rows, :]
                    else:
                        dst = y_s
                    S.op("sp", lambda e, dst=dst, os_=os_, rows=rows: e.dma_start(out=dst, in_=yo[0:rows, os_, :]),
                         reads=[("fn_o", os_, 0), ("fn_o", os_, 1)], writes=[], dma="yout%d" % os_)

    def hgrn2(self, j):
        nc, S, TP = self.nc, self.S, self.TP
        xres, hT = self.xres, self.hT
        w_in, w_out = self.dram["hg_w_in"], self.dram["hg_w_out"]
        st_hg, hg_p, hg_s = self.dram["st_hg"], self.dram["hg_p"], self.dram["hg_s"]
        ps = self.ps
        W = 512
        with ExitStack() as es:
            win = es.enter_context(self.T("h_win", [128, 2, KC, 4, 128], BF16))
            wout = es.enter_context(self.T("h_wout", [128, 2, D], BF16))
            sig = es.enter_context(self.T("h_sig", [128, W]))
            lf = es.enter_context(self.T("h_lf", [128, W]))
            kk = es.enter_context(self.T("h_kk", [128, W]))
            q = es.enter_context(self.T("h_q", [128, W]))
            gate = es.enter_context(self.T("h_gate", [128, W]))
            g = es.enter_context(self.T("h_g", [128, W]))
            tmp = es.enter_context(self.T("h_tmp", [128, W]))
            tmp2 = es.enter_context(self.T("h_tmp2", [128, W]))
            ee = es.enter_context(self.T("h_e", [128, 4, W]))
            qg = es.enter_context(self.T("h_qg", [128, W], BF16))
            kg = es.enter_context(self.T("h_kg", [128, W], BF16))
            qG = es.enter_context(self.T("h_qG", [128, W], BF16))
            kdec = es.enter_context(self.T("h_kdec", [128, W], BF16))
            vtok = es.enter_context(self.T("h_vtok", [128, 4, 128], BF16))
            vT = es.enter_context(self.T("h_vT", [128, W], BF16))
            kdtok = es.enter_context(self.T("h_kdtok", [128, 4, 128], BF16))
            osb = es.enter_context(self.T("h_osb", [128, W]))
            osq = es.enter_context(self.T("h_osq", [128, W], BF16))
            rstd = es.enter_context(self.T("h_rstd", [128, W]))
            og = es.enter_context(self.T("h_og", [128, W], BF16))
            Sr = es.enter_context(self.T("h_Sr", [128, 9, 128]))
            qGf = es.enter_context(self.T("h_qGf", [128, W]))
            attm = es.enter_context(self.T("h_attm", [128, 4, 128], BF16))
            egl = es.enter_context(self.T("h_egl", [128, 16]))
            S0 = es.enter_context(self.T("h_S0", [128, NB, 128]))
            S0bf = es.enter_context(self.T("h_S0bf", [128, NB, 128], BF16))
            Vblk = es.enter_context(self.T("h_Vblk", [64, NB, 128], BF16))
            for h in range(8):
                slot = h % 2
                for p in range(4):
                    for kk0 in (0, 4):
                        src = w_in[j, kk0 * 128:(kk0 + 4) * 128, p * D + h * 128:p * D + (h + 1) * 128].rearrange("(k p) c -> p k c", p=128)
                        self.load_w(win[:, slot, kk0:kk0 + 4, p, :], src, [128, 4, 128], ("h_win", slot, p, kk0))
                self.load_w(wout[:, slot, :], w_out[j, h * 128:(h + 1) * 128, :], [128, D], ("h_wout", slot))
                wkeys = lambda p: [("h_win", slot, p, 0), ("h_win", slot, p, 4)]
                S.op("sp", lambda e: e.dma_start(out=S0, in_=st_hg[j, :, h, :, :].rearrange("b k v -> k b v")), writes=["h_S0"], dma="h_S0")
                S.op("pool", lambda e: e.tensor_copy(out=S0bf, in_=S0), reads=["h_S0"], writes=["h_S0bf"])
                S.op("pool", lambda e: e.memset(Sr[:, 0, :], 0.0), writes=[("h_Sr", 0)])
                lbh, omlh, nomlh = self.hg_lb[:, j, h:h + 1], self.hg_oml[:, j, h:h + 1], self.hg_noml[:, j, h:h + 1]
                for (c0, w, kind) in self.tbs:
                    smp = kind == "s"
                    hc = 2 + c0
                    pq, pf, pg, pv = self.nextps(), self.nextps(), self.nextps(), self.nextps()
                    for p, pi in ((0, pq), (1, pf), (3, pg)):
                        for kc in range(KC):
                            S.op("pe", lambda e, pi=pi, p=p, kc=kc, hc=hc, w=w: e.matmul(ps[pi][:, 0:w], lhsT=win[:, slot, kc, p, :], rhs=hT[:, kc, hc:hc + w],
                                 start=(kc == 0), stop=(kc == KC - 1)), reads=wkeys(p) + [("hT", "all")], writes=[("ps", pi)])
                    ntile = (w + 127) // 128
                    rows = min(128, w)
                    for kc in range(KC):
                        S.op("pe", lambda e, kc=kc, hc=hc, w=w: e.matmul(ps[pv][:, 0:w], lhsT=win[:, slot, kc, 2, :], rhs=hT[:, kc, hc:hc + w],
                             start=(kc == 0), stop=(kc == KC - 1)), reads=wkeys(2) + [("hT", "all")], writes=[("ps", pv)])
                    self.copy("act", vT[:, 0:w], ps[pv][:, 0:w], [("ps", pv)], ["h_vT"])
                    pvt = self.nextps()
                    pvtb = ps[pvt].bitcast(BF16)
                    for jt in range(ntile):
                        S.op("pe", lambda e, jt=jt, rows=rows: e.transpose(pvtb[0:rows, jt * 128:(jt + 1) * 128], vT[:, jt * 128:jt * 128 + rows], self.identb),
                             reads=["h_vT", "consts2"], writes=[("ps", pvt)])
                    self.copy("dve", vtok[0:rows, 0:ntile, :], pvtb[:, 0:512].rearrange("p (a b) -> p a b", a=4)[0:rows, 0:ntile, :], [("ps", pvt)], ["h_vtok"])
                    S.op("act", lambda e, w=w: e.activation(out=sig[:, 0:w], in_=ps[pf][:, 0:w], func=AF.Sigmoid), reads=[("ps", pf)], writes=["h_sig"])
                    S.op("act", lambda e, w=w: e.activation(out=q[:, 0:w], in_=ps[pq][:, 0:w], func=AF.Silu), reads=[("ps", pq)], writes=["h_q"])
                    S.op("act", lambda e, w=w: e.activation(out=gate[:, 0:w], in_=ps[pg][:, 0:w], func=AF.Silu), reads=[("ps", pg)], writes=["h_gate"])
                    S.op("dve", lambda e, w=w: e.tensor_scalar(out=lf[:, 0:w], in0=sig[:, 0:w], scalar1=omlh, scalar2=lbh, op0=ALU.mult, op1=ALU.add), reads=["h_sig"], writes=["h_lf"])
                    S.op("act", lambda e, w=w: e.activation(out=lf[:, 0:w], in_=lf[:, 0:w], func=AF.Ln), reads=["h_lf"], writes=["h_lf"])
                    S.op("dve", lambda e, w=w: e.tensor_scalar(out=kk[:, 0:w], in0=sig[:, 0:w], scalar1=nomlh, scalar2=omlh, op0=ALU.mult, op1=ALU.add), reads=["h_sig"], writes=["h_kk"])
                    if not smp:
                        nch = w // 64
                        for c in range(nch):
                            S.op("dve", lambda e, c=c: e.tensor_tensor_scan(out=g[:, c * 64:(c + 1) * 64], data0=self.onesf[:, 0:64], data1=lf[:, c * 64:(c + 1) * 64],
                                 initial=0.0, op0=ALU.mult, op1=ALU.add), reads=["h_lf", "consts2"], writes=["h_g"])
                        g3 = g.rearrange("p (c t) -> p c t", t=64)
                        bc = lambda col: g3[:, :, col:col + 1].broadcast_to([128, nch, 64])
                        v3 = lambda t_: t_.rearrange("p (c t) -> p c t", t=64)
                        S.op("dve", lambda e: e.tensor_tensor(out=v3(tmp), in0=g3, in1=bc(31), op=ALU.subtract), reads=["h_g"], writes=["h_tmp"])
                        S.op("dve", lambda e: e.tensor_tensor(out=v3(tmp2), in0=bc(63), in1=g3, op=ALU.subtract), reads=["h_g"], writes=["h_tmp2"])
                        S.op("act", lambda e: e.activation(out=ee[:, 0, :], in_=tmp, func=AF.Exp), reads=["h_tmp"], writes=[("h_e", 0)])
                        S.op("act", lambda e: e.activation(out=ee[:, 1, :], in_=tmp, func=AF.Exp, scale=-1.0), reads=["h_tmp"], writes=[("h_e", 1)])
                        S.op("act", lambda e: e.activation(out=ee[:, 2, :], in_=g, func=AF.Exp), reads=["h_g"], writes=[("h_e", 2)])
                        S.op("act", lambda e: e.activation(out=ee[:, 3, :], in_=tmp2, func=AF.Exp), reads=["h_tmp2"], writes=[("h_e", 3)])
                        S.op("act", lambda e: e.activation(out=egl[:, 0:nch], in_=g3[:, :, 63], func=AF.Exp), reads=["h_g"], writes=["h_egl"])
                        S.op("pool", lambda e: e.tensor_tensor(out=qg, in0=q, in1=ee[:, 0, :], op=ALU.mult), reads=["h_q", ("h_e", 0)], writes=["h_qg"])
                        S.op("pool", lambda e: e.tensor_tensor(out=kg, in0=kk, in1=ee[:, 1, :], op=ALU.mult), reads=["h_kk", ("h_e", 1)], writes=["h_kg"])
                        S.op("dve", lambda e: e.tensor_tensor(out=qGf, in0=q, in1=ee[:, 2, :], op=ALU.mult), reads=["h_q", ("h_e", 2)], writes=["h_qGf"])
                        S.op("pool", lambda e: e.tensor_tensor(out=kdec, in0=kk, in1=ee[:, 3, :], op=ALU.mult), reads=["h_kk", ("h_e", 3)], writes=["h_kdec"])
                    else:
                        g3 = g[:, 0:NS].rearrange("p (b t) -> p b t", t=TS)
                        lf3 = lf[:, 0:NS].rearrange("p (b t) -> p b t", t=TS)
                        S.op("dve", lambda e: e.tensor_copy(out=g3[:, :, 0:1], in_=lf3[:, :, 0:1]), reads=["h_lf"], writes=["h_g"])
                        for t_ in range(1, TS):
                            S.op("dve", lambda e, t_=t_: e.tensor_tensor(out=g3[:, :, t_:t_ + 1], in0=g3[:, :, t_ - 1:t_], in1=lf3[:, :, t_:t_ + 1], op=ALU.add), reads=["h_lf", "h_g"], writes=["h_g"])
                        t23 = tmp2[:, 0:NS].rearrange("p (b t) -> p b t", t=TS)
                        S.op("dve", lambda e: e.tensor_tensor(out=t23, in0=g3[:, :, 3:4].broadcast_to([128, NB, TS]), in1=g3, op=ALU.subtract), reads=["h_g"], writes=["h_tmp2"])
                        S.op("act", lambda e: e.activation(out=ee[:, 1, 0:NS], in_=g[:, 0:NS], func=AF.Exp, scale=-1.0), reads=["h_g"], writes=[("h_e", 1)])
                        S.op("act", lambda e: e.activation(out=ee[:, 2, 0:NS], in_=g[:, 0:NS], func=AF.Exp), reads=["h_g"], writes=[("h_e", 2)])
                        S.op("act", lambda e: e.activation(out=ee[:, 3, 0:NS], in_=tmp2[:, 0:NS], func=AF.Exp), reads=["h_tmp2"], writes=[("h_e", 3)])
                        S.op("act", lambda e: e.activation(out=egl[:, 0:NB], in_=g3[:, :, 3], func=AF.Exp), reads=["h_g"], writes=["h_egl"])
                        S.op("pool", lambda e: e.tensor_tensor(out=kg[:, 0:NS], in0=kk[:, 0:NS], in1=ee[:, 1, 0:NS], op=ALU.mult), reads=["h_kk", ("h_e", 1)], writes=["h_kg"])
                        S.op("dve", lambda e: e.tensor_tensor(out=qG[:, 0:NS], in0=q[:, 0:NS], in1=ee[:, 2, 0:NS], op=ALU.mult), reads=["h_q", ("h_e", 2)], writes=["h_qG"])
                        S.op("pool", lambda e: e.tensor_tensor(out=kdec[:, 0:NS], in0=kk[:, 0:NS], in1=ee[:, 3, 0:NS], op=ALU.mult), reads=["h_kk", ("h_e", 3)], writes=["h_kdec"])
                    pt = self.nextps()
                    ptb = ps[pt].bitcast(BF16)
                    for jt in range(ntile):
                        S.op("pe", lambda e, jt=jt, rows=rows: e.transpose(ptb[0:rows, jt * 128:(jt + 1) * 128], kdec[:, jt * 128:jt * 128 + rows], self.identb),
                             reads=["h_kdec", "consts2"], writes=[("ps", pt)])
                    self.copy("dve", kdtok[0:rows, 0:ntile, :], ptb[:, 0:512].rearrange("p (a b) -> p a b", a=4)[0:rows, 0:ntile, :], [("ps", pt)], ["h_kdtok"])
                    po = self.nextlong()
                    if not smp:
                        nchk = w // 64
                        if c0 > 0:
                            S.op("dve", lambda e: e.tensor_copy(out=Sr[:, 0, :], in_=Sr[:, 8, :]), reads=[("h_Sr", c_) for c_ in range(9)], writes=[("h_Sr", 0)])
                        pa = self.nextps()
                        for jt in range(ntile):
                            S.op("pe", lambda e, jt=jt: e.matmul(ps[pa][:, jt * 128:(jt + 1) * 128], lhsT=kg[:, jt * 128:(jt + 1) * 128], rhs=qg[:, jt * 128:(jt + 1) * 128], start=True, stop=True),
                                 reads=["h_kg", "h_qg"], writes=[("ps", pa)])
                        S.op("dve", lambda e: e.tensor_tensor(out=attm[:, 0:ntile, :], in0=ps[pa][:, 0:ntile * 128].rearrange("p (a b) -> p a b", a=ntile),
                             in1=self.maskU2.unsqueeze(1).broadcast_to([128, ntile, 128]), op=ALU.mult), reads=[("ps", pa), "consts"], writes=["h_attm"])
                        pus = [self.nextps(), self.nextps()]
                        for c in range(nchk):
                            jt, cc = c // 2, c % 2
                            S.op("pe", lambda e, c=c, cc=cc, jt=jt: e.matmul(ps[pus[c % 2]][:, (c // 2) * 128:(c // 2 + 1) * 128], lhsT=kdtok[cc * 64:(cc + 1) * 64, jt, :], rhs=vtok[cc * 64:(cc + 1) * 64, jt, :], start=True, stop=True),
                                 reads=["h_kdtok", "h_vtok"], writes=[("ps", pus[c % 2])])
                        for c in range(nchk):
                            S.op("dve", lambda e, c=c: e.scalar_tensor_tensor(out=Sr[:, c + 1, :], in0=Sr[:, c, :], scalar=egl[:, c:c + 1], in1=ps[pus[c % 2]][:, (c // 2) * 128:(c // 2 + 1) * 128], op0=ALU.mult, op1=ALU.add),
                                 reads=[("ps", pus[c % 2]), ("h_Sr", c), "h_egl"], writes=[("h_Sr", c + 1)])
                        for jt in range(ntile):
                            S.op("pe", lambda e, jt=jt: e.matmul(ps[po][:, jt * 128:(jt + 1) * 128], lhsT=vtok[:, jt, :], rhs=attm[:, jt, :], start=True, stop=False),
                                 reads=["h_vtok", "h_attm"], writes=[("ps", po)])
                            for cc in range(2):
                                c = jt * 2 + cc
                                S.op("pe", lambda e, c=c, cc=cc: e.matmul(ps[po][:, c * 64:(c + 1) * 64], lhsT=Sr[:, c, :], rhs=qGf[:, c * 64:(c + 1) * 64], start=False, stop=(cc == 1)),
                                     reads=[("h_Sr", c), "h_qGf"], writes=[("ps", po)])
                        if c0 + w == TP:
                            S.op("sp", lambda e: e.dma_start(out=hg_p[j, h], in_=Sr[:, 8, :]), reads=[("h_Sr", 8)], dma="hg_p")
                    else:
                        pa = self.nextps()
                        S.op("pe", lambda e, pa=pa: e.matmul(ps[pa][0:NS, 0:NS], lhsT=kg[:, 0:NS], rhs=qG[:, 0:NS], start=True, stop=True), reads=["h_kg", "h_qG"], writes=[("ps", pa)])
                        S.op("dve", lambda e, pa=pa: e.tensor_tensor(out=attm[0:NS, 0, 0:NS], in0=ps[pa][0:NS, 0:NS], in1=self.maskS, op=ALU.mult), reads=[("ps", pa), "consts"], writes=[("h_attm", 0)])
                        S.op("pe", lambda e: e.matmul(ps[po][:, 0:NS], lhsT=vtok[0:NS, 0, :], rhs=attm[0:NS, 0, 0:NS], start=True, stop=False), reads=["h_vtok", ("h_attm", 0)], writes=[("ps", po)])
                        for b in range(NB):
                            S.op("pe", lambda e, b=b: e.matmul(ps[po][:, b * TS:(b + 1) * TS], lhsT=S0bf[:, b, :], rhs=qG[:, b * TS:(b + 1) * TS], start=False, stop=(b == NB - 1)),
                                 reads=["h_S0bf", "h_qG"], writes=[("ps", po)])
                        S.op("dve", lambda e: e.tensor_tensor(out=Vblk, in0=vtok[0:NS, 0, :].unsqueeze(1).broadcast_to([NS, NB, 128]),
                             in1=self.maskB.unsqueeze(2).broadcast_to([NS, NB, 128]), op=ALU.mult), reads=["h_vtok", "consts"], writes=["h_Vblk"])
                        S.op("dve", lambda e: e.tensor_tensor(out=S0, in0=S0, in1=egl[:, 0:NB].unsqueeze(2).broadcast_to([128, NB, 128]), op=ALU.mult), reads=["h_S0", "h_egl", "h_S0bf"], writes=["h_S0"])
                        for bq in range(4):
                            pu = self.nextps()
                            S.op("pe", lambda e, pu=pu, bq=bq: e.matmul(ps[pu][:, 0:512], lhsT=kdtok[0:NS, 0, :], rhs=Vblk[:, bq * 4:(bq + 1) * 4, :], start=True, stop=True),
                                 reads=["h_kdtok", "h_Vblk"], writes=[("ps", pu)])
                            S.op("dve", lambda e, pu=pu, bq=bq: e.tensor_tensor(out=S0[:, bq * 4:(bq + 1) * 4, :], in0=S0[:, bq * 4:(bq + 1) * 4, :],
                                 in1=ps[pu].rearrange("p (a b) -> p a b", a=4), op=ALU.add), reads=[("ps", pu), "h_S0"], writes=["h_S0"])
                        S.op("sp", lambda e: e.dma_start(out=hg_s[j, :, h, :, :].rearrange("b k v -> k b v"), in_=S0), reads=["h_S0"], dma="hg_s")
                    S.op("act", lambda e, w=w: e.activation(out=osb[:, 0:w], in_=ps[po][:, 0:w], func=AF.Copy), reads=[("ps", po)], writes=["h_osb"])
                    S.op("act", lambda e, w=w: e.activation(out=osq[:, 0:w], in_=ps[po][:, 0:w], func=AF.Square), reads=[("ps", po)], writes=["h_osq"])
                    pn = self.nextps()
                    S.op("pe", lambda e, pn=pn, w=w: e.matmul(ps[pn][:, 0:w], lhsT=self.onesb, rhs=osq[:, 0:w], start=True, stop=True), reads=["h_osq", "consts2"], writes=[("ps", pn)])
                    S.op("act", lambda e, pn=pn, w=w: e.activation(out=rstd[:, 0:w], in_=ps[pn][:, 0:w], func=AF.Ln, scale=1.0 / 128, bias=EPS), reads=[("ps", pn)], writes=["h_rstd"])
                    S.op("act", lambda e, w=w: e.activation(out=rstd[:, 0:w], in_=rstd[:, 0:w], func=AF.Exp, scale=-0.5), reads=["h_rstd"], writes=["h_rstd"])
                    S.op("dve", lambda e, w=w: e.tensor_tensor(out=osb[:, 0:w], in0=osb[:, 0:w], in1=rstd[:, 0:w], op=ALU.mult), reads=["h_osb", "h_rstd"], writes=["h_osb"])
                    S.op("dve", lambda e, w=w: e.scalar_tensor_tensor(out=og[:, 0:w], in0=osb[:, 0:w], scalar=self.hgn[:, j:j + 1], in1=gate[:, 0:w], op0=ALU.mult, op1=ALU.mult),
                         reads=["h_osb", "h_gate", "consts"], writes=["h_og"])
                    for dc in range(KC):
                        pi = self.nextps()
                        S.op("pe", lambda e, pi=pi, dc=dc, w=w: e.matmul(ps[pi][:, 0:w], lhsT=wout[:, slot, dc * 128:(dc + 1) * 128], rhs=og[:, 0:w], start=True, stop=True),
                             reads=[("h_wout", slot), "h_og"], writes=[("ps", pi)])
                        S.op("dve", lambda e, pi=pi, dc=dc, c0=c0, w=w: e.tensor_tensor(out=xres[:, dc, c0:c0 + w], in0=xres[:, dc, c0:c0 + w], in1=ps[pi][:, 0:w], op=ALU.add),
                             reads=[("ps", pi), ("xres", dc, c0)], writes=[("xres", dc, c0)])

    def rwkv7(self, j):
        nc, S, TP = self.nc, self.S, self.TP
        xres, hT, ps = self.xres, self.hT, self.ps
        dr = self.dram
        V = self.rw_vec
        GN_EPS = 64e-5
        li = self.cur_li
        with ExitStack() as es0:
            T0 = lambda n, sh, dt=F32: es0.enter_context(self.T(n, sh, dt))
            nblk = len(self.tbs)
            edge = T0("r_edge", [128, KC, 2 * nblk + 2], BF16)
            shiftT = T0("r_shiftT", [128, KC, NB], BF16)
            l1T = T0("r_l1T", [128, 3, self.N], BF16)
            prevS = T0("r_prevS", [128, KC, NS], BF16)
            prevB = T0("r_prevB", [128, KC, 512], BF16)

            def fill_prev(c0, w, kind):
                if kind == "p":
                    S.op("dve", lambda e: e.tensor_copy(out=prevB[:, :, 0:w], in_=hT[:, :, 1 + c0:1 + c0 + w]), reads=[("hT", "all"), "hT0"], writes=["r_prevB"])
            with ExitStack() as es:
                T = lambda n, sh, dt=F32: es.enter_context(self.T(n, sh, dt))
                shin, xsh, sq, rr = T("r_shin", [32, D]), T("r_xsh", [128, KC, 32]), T("r_sq", [128, KC, 32]), T("r_rr", [128, 32])
                shrow = T("r_shrow", [32, D])
                S.op("pool", lambda e: e.memset(shin, 0.0), writes=["r_shin"])
                S.op("pool", lambda e: e.memset(xsh, 0.0), writes=["r_xsh"])
                S.op("pool", lambda e: e.memset(edge, 0.0), writes=["r_edge"])
                S.op("sp", lambda e: e.dma_start(out=shin[0:NB, :], in_=dr["st_shift"]), reads=[], writes=["r_shin"], dma="r_shin")
                for half in range(2):
                    pi = self.nextps()
                    for q in range(4):
                        kc = half * 4 + q
                        S.op("pe", lambda e, q=q, kc=kc, pi=pi: e.transpose(ps[pi][:, q * 32:(q + 1) * 32], shin[:, kc * 128:(kc + 1) * 128], self.ident[0:32, 0:32]), reads=["r_shin", "consts"], writes=[("ps", pi)])
                    self.copy("dve", shiftT[:, half * 4:(half + 1) * 4, :], ps[pi][:, 0:128].rearrange("p (q t) -> p q t", q=4)[:, :, 0:NB], [("ps", pi)], ["r_shiftT"])
                S.op("dve", lambda e: e.tensor_copy(out=xsh[:, :, 0:1], in_=xres[:, :, TP - 1:TP]), reads=[("xres", "all"), "r_xsh"], writes=["r_xsh"])
                S.op("dve", lambda e: e.tensor_copy(out=xsh[:, :, 1:1 + NB], in_=xres[:, :, TP:TP + NS].rearrange("p k (b t) -> p k b t", t=TS)[:, :, :, 3]), reads=[("xres", "all"), "r_xsh"], writes=["r_xsh"])
                S.op("act", lambda e: e.activation(out=sq, in_=xsh, func=AF.Square), reads=["r_xsh"], writes=["r_sq"])
                pi = self.nextps()
                for kc in range(KC):
                    S.op("pe", lambda e, kc=kc: e.matmul(ps[pi][:, 0:32], lhsT=self.onesf, rhs=sq[:, kc, :], start=(kc == 0), stop=(kc == KC - 1)), reads=["r_sq", "consts2"], writes=[("ps", pi)])
                S.op("act", lambda e: e.activation(out=rr, in_=ps[pi][:, 0:32], func=AF.Ln, scale=1.0 / D, bias=EPS), reads=[("ps", pi)], writes=["r_rr"])
                S.op("act", lambda e: e.activation(out=rr, in_=rr, func=AF.Exp, scale=-0.5), reads=["r_rr"], writes=["r_rr"])
                for kc in range(KC):
                    S.op("dve", lambda e, kc=kc: e.scalar_tensor_tensor(out=xsh[:, kc, :], in0=xsh[:, kc, :], scalar=self.nmix[:, li, kc:kc + 1], in1=rr, op0=ALU.mult, op1=ALU.mult), reads=["r_xsh", "r_rr", "consts"], writes=["r_xsh"])
                for half in range(2):
                    pi = self.nextps()
                    for q in range(4):
                        kc = half * 4 + q
                        S.op("pe", lambda e, q=q, kc=kc, pi=pi: e.transpose(ps[pi][0:32, q * 128:(q + 1) * 128], xsh[:, kc, :], self.ident), reads=["r_xsh", "consts"], writes=[("ps", pi)])
                    self.copy("act", shrow[:, half * 512:(half + 1) * 512], ps[pi][0:32, :], [("ps", pi)], [("r_shrow", half)])
                S.op("sp", lambda e: e.dma_start(out=dr["sh_p"], in_=shrow[0:1, :]), reads=[("r_shrow", 0), ("r_shrow", 1)], dma="r_sh")
                S.op("sp", lambda e: e.dma_start(out=dr["sh_s"], in_=shrow[1:1 + NB, :]), reads=[("r_shrow", 0), ("r_shrow", 1)], dma="r_sh")
                for bi, (c0, w, kind) in enumerate(self.tbs):
                    if kind == "p" and c0 > 0:
                        S.op("dve", lambda e, bi=bi, c0=c0: e.tensor_copy(out=edge[:, :, 2 * bi:2 * bi + 1], in_=hT[:, :, 1 + c0:2 + c0]), reads=[("hT", "all"), "r_edge"], writes=["r_edge"])
                hs3 = hT[:, :, 2 + TP:2 + TP + NS].rearrange("p k (b t) -> p k b t", t=TS)
                pv3 = prevS.rearrange("p k (b t) -> p k b t", t=TS)
                for kc in range(KC):
                    S.op("dve", lambda e, kc=kc: e.tensor_copy(out=pv3[:, kc, :, 1:TS], in_=hs3[:, kc, :, 0:TS - 1]), reads=[("hT", "all")], writes=["r_prevS"])
                    S.op("dve", lambda e, kc=kc: e.tensor_copy(out=pv3[:, kc, :, 0], in_=shiftT[:, kc, :]), reads=["r_shiftT", "r_prevS"], writes=["r_prevS"])
            S.barrier()

            def shifted_proj(pi, w_a, w_b, c0, w, kind, rd, M=128):
                hc = 2 + c0
                out = ps[pi][0:M, 0:w]
                for kc in range(KC):
                    S.op("pe", lambda e, kc=kc: e.matmul(out, lhsT=w_a(kc), rhs=hT[:, kc, hc:hc + w], start=(kc == 0), stop=False), reads=rd + [("hT", "all")], writes=[("ps", pi)])
                if kind == "p":
                    for kc in range(KC):
                        S.op("pe", lambda e, kc=kc: e.matmul(out, lhsT=w_b(kc), rhs=prevB[:, kc, 0:w], start=False, stop=(kc == KC - 1)), reads=rd + ["r_prevB"], writes=[("ps", pi)])
                else:
                    for kc in range(KC):
                        S.op("pe", lambda e, kc=kc: e.matmul(out, lhsT=w_b(kc), rhs=prevS[:, kc, :], start=False, stop=(kc == KC - 1)), reads=rd + ["r_prevS"], writes=[("ps", pi)])

            nq = TP // 256
            self.r_e2 = T0("r_e2", [128, KC, 2 * nq], BF16)
            e2 = self.r_e2
            S.op("pool", lambda e: e.memset(e2, 0.0), writes=["r_e2"])
            for q in range(nq):
                c0 = q * 256
                if c0 > 0:
                    S.op("dve", lambda e, q=q, c0=c0: e.tensor_copy(out=e2[:, :, 2 * q:2 * q + 1], in_=hT[:, :, 1 + c0:2 + c0]), reads=[("hT", "all"), "r_e2"], writes=["r_e2"])
            S.barrier()
            rblocks = [(q * 256, 256, "p") for q in range(nq)] + [(TP, NS, "s")]

            def load_scaled(dst_a, dst_b, src, shape, n, kk0, nk, key):
                slot = self.stg_i % self.NSTG
                self.stg_i += 1
                cols = shape[2]
                st = self.stage[slot][:, 0:nk * cols].rearrange("p (a b) -> p a b", a=nk)
                S.op("sp", lambda e: e.dma_start(out=st, in_=src), writes=[("stg", slot)], dma="stg%d" % slot)
                for q in range(nk):
                    kc = kk0 + q
                    S.op("dve", lambda e, q=q, kc=kc: e.tensor_scalar(out=dst_a(kc), in0=st[:, q, :], scalar1=self.rw_omu[:, n, kc:kc + 1], scalar2=1.0, op0=ALU.mult, op1=ALU.mult), reads=[("stg", slot), "rwc0"], writes=[key])
                    S.op("dve", lambda e, q=q, kc=kc: e.tensor_scalar(out=dst_b(kc), in0=st[:, q, :], scalar1=self.rw_mu[:, n, kc:kc + 1], scalar2=1.0, op0=ALU.mult, op1=ALU.mult), reads=[("stg", slot), "consts"], writes=[key])

            with ExitStack() as es:
                T = lambda n, sh, dt=F32: es.enter_context(self.T(n, sh, dt))
                wl = T("r_wl", [128, 3, 2, KC, 128], BF16)
                for li_, (nm, n, cols) in enumerate((("rw_w1", 3, 64), ("rw_a1", 4, 64), ("rw_g1", 5, 128))):
                    src = dr[nm].rearrange("(k p) c -> p k c", p=128)
                    load_scaled(lambda kc, li_=li_, cols=cols: wl[:, li_, 0, kc, 0:cols], lambda kc, li_=li_, cols=cols: wl[:, li_, 1, kc, 0:cols], src, [128, KC, cols], n, 0, KC, ("r_wl", li_))
                for (c0, w, kind) in rblocks:
                    fill_prev(c0, w, kind)
                    for li_, (M, fn) in enumerate(((64, AF.Tanh), (64, AF.Copy), (128, AF.Sigmoid))):
                        pi = self.nextps()
                        shifted_proj(pi, lambda kc, li_=li_, M=M: wl[:, li_, 0, kc, 0:M], lambda kc, li_=li_, M=M: wl[:, li_, 1, kc, 0:M], c0, w, kind, [("r_wl", li_)], M=M)
                        S.op("act", lambda e, li_=li_, M=M, fn=fn, pi=pi: e.activation(out=l1T[0:M, li_, c0:c0 + w], in_=ps[pi][0:M, 0:w], func=fn), reads=[("ps", pi)], writes=[("r_l1T", li_)])
            S.barrier()

            for sample_pass in (False, True):
                blocks = [b_ for b_ in rblocks if (b_[2] == "s") == sample_pass]
                Wd = NS if sample_pass else 256
                R = NS if sample_pass else 128
                nch = 1 if sample_pass else 2
                nlev = 1 if sample_pass else 6
                mStrict = self.sUS if sample_pass else self.sU
                mIncl = self.maskS if sample_pass else self.maskU
                mLow = self.sLS if sample_pass else self.sL
                with ExitStack() as es:
                    T = lambda n, sh, dt=F32: es.enter_context(self.T(n, sh, dt))
                    wrkv = T("r_wrkv", [128, 3, 2, KC, 128], BF16)
                    w2nd = T("r_w2nd", [128, 3, 128], BF16)
                    wout2 = [T("r_wout", [128, D], BF16) for _ in range(2)]
                    rfA, kfA, vfA, lwf, af = [T("r_f%d" % i_, [128, Wd]) for i_ in range(5)]
                    rkvA = [[rfA, kfA, vfA, None], [T("r_rf2", [128, Wd]), T("r_kf2", [128, Wd]), T("r_vf2", [128, Wd]), T("r_vb2", [128, Wd], BF16)]]
                    gf2 = [T("r_gf", [128, Wd]) for _ in range(2)]
                    bon2 = [T("r_bon", [128, Wd]) for _ in range(2)]
                    G, t1, e0, e1 = T("r_G", [128, Wd]), T("r_t1", [128, Wd]), T("r_e0", [128, Wd]), T("r_e1", [128, Wd])
                    kap, k2, beta, osb = T("r_kap", [128, Wd]), T("r_k2", [128, Wd]), T("r_beta", [128, Wd]), T("r_osb", [128, Wd])
                    KR2 = [T("r_KR", [128, nch, 2, 128], BF16) for _ in range(2)]
                    BK2 = [T("r_BK", [128, nch, 2, 128], BF16) for _ in range(2)]
                    btT, ktT, vbA, og = T("r_btT", [128, Wd], BF16), T("r_ktT", [128, Wd], BF16), T("r_vb", [128, Wd], BF16), T("r_og", [128, Wd], BF16)
                    rkvA[0][3] = vbA
                    bttok2 = [T("r_bttok", [128, nch, 128], BF16) for _ in range(2)]
                    kttok2 = [T("r_kttok", [128, nch, 128], BF16) for _ in range(2)]
                    vtok2 = [T("r_vtok", [128, nch, 128], BF16) for _ in range(2)]
                    Nk, Ak = T("r_Nk", [128, 2, 2 * nch, 128], BF16), T("r_Ak", [128, 2, 2 * nch, 128], BF16)
                    Wm = T("r_Wm", [128, 2 * nch, 128], BF16)
                    MbT, BTm, MkT = T("r_MbT", [128, 2 * nch, 128], BF16), T("r_BTm", [128, 2 * nch, 128], BF16), T("r_MkT", [128, 2 * nch, 128], BF16)
                    EC2 = [T("r_EC", [128, NB]) for _ in range(2)]
                    t2 = T("r_t2", [128, Wd])
                    Xsb, Ssb = T("r_Xsb", [128, 2, 64], BF16), T("r_Ssb", [128, 2, 64], BF16)
                    Sst, Sbf = T("r_Sst", [128, 64]), T("r_Sbf", [128, 64], BF16)
                    sT = T("r_sT", [64, 128])
                    if sample_pass:
                        s0nat = T("r_s0nat", [64, NB, 128])
                        S0, S0bf = T("r_S0", [128, NB, 64]), T("r_S0bf", [128, NB, 64], BF16)
                        kblk = T("r_kblk", [128, NB, NS], BF16)
                        rblk = T("r_rblk", [128, NB, NS], BF16)
                        Sblk, Vblk = T("r_Sblk", [128, 2, NB, 64], BF16), T("r_Vblk", [128, 2, NB, 64], BF16)
                        for t_, k_ in ((Ssb, ("r_Ssb", 0)), (MbT, "r_zMbT"), (MkT, "r_zMkT"), (BTm, "r_zBTm"), (vtok2[0], ("r_vtok", 0)), (vtok2[1], ("r_vtok", 1)), (bttok2[0], ("r_bttok", 0)), (bttok2[1], ("r_bttok", 1)), (kttok2[0], ("r_kttok", 0)), (kttok2[1], ("r_kttok", 1)), (Sblk, ("r_Sblk", 0)), (Vblk, ("r_Vblk", 0))):
                            S.op("pool", lambda e, t_=t_: e.memset(t_, 0.0), writes=[k_])
                        S.barrier()
                    def blk(pc, bi_, c0, w, kind, par):
                        cs = slice(pc * 128, (pc + 1) * 128)
                        KR, BK, bt_tok, kt_tok, v_tok, EC, gf, bon = KR2[par], BK2[par], bttok2[par], kttok2[par], vtok2[par], EC2[par], gf2[par], bon2[par]
                        wout = wout2[pc % 2]
                        if bi_ == 0:
                            for n in range(3):
                                for kk0 in (0, 4):
                                    src = dr["rw_w_rkv"][n, kk0 * 128:(kk0 + 4) * 128, cs].rearrange("(k p) c -> p k c", p=128)
                                    load_scaled(lambda kc, n=n: wrkv[:, n, 0, kc, :], lambda kc, n=n: wrkv[:, n, 1, kc, :], src, [128, 4, 128], n, kk0, 4, ("r_wrkv", n, kk0))
                            self.load_w(w2nd[0:64, 0, :], dr["rw_w2"][:, cs], [64, 128], ("r_w2nd", 0))
                            self.load_w(w2nd[0:64, 1, :], dr["rw_a2"][:, cs], [64, 128], ("r_w2nd", 1))
                            self.load_w(w2nd[:, 2, :], dr["rw_g2"][:, cs], [128, 128], ("r_w2nd", 2))
                            self.load_w(wout, dr["rw_w_out"][cs, :], [128, D], ("r_wout", pc % 2))
                        wk = lambda n: [("r_wrkv", n, 0), ("r_wrkv", n, 4)]
                        vec = lambda nm: V[nm][:, pc:pc + 1]
                        for _once in (0,):
                            half = bi_ % 2 if kind == "p" else 0
                            rf, kf, vf, vb = rkvA[half]
                            if kind == "s" or half == 0:
                                wp = w if kind == "s" else 2 * w
                                fill_prev(c0, wp, kind)
                                pr, pk, pv = self.nextps(), self.nextps(), self.nextps()
                                for n, pi in ((0, pr), (1, pk), (2, pv)):
                                    shifted_proj(pi, lambda kc, n=n: wrkv[:, n, 0, kc, :], lambda kc, n=n: wrkv[:, n, 1, kc, :], c0, wp, kind, wk(n))
                                for hh_ in range(wp // w):
                                    rf_, kf_, vf_, vb_ = rkvA[hh_]
                                    cs_ = slice(hh_ * w, (hh_ + 1) * w)
                                    self.copy("act", rf_[:, 0:w], ps[pr][:, cs_], [("ps", pr)], [("r_rf", hh_)])
                                    self.copy("act", kf_[:, 0:w], ps[pk][:, cs_], [("ps", pk)], [("r_kf", hh_)])
                                    self.copy("act", vf_[:, 0:w], ps[pv][:, cs_], [("ps", pv)], [("r_vf", hh_)])
                                    S.op("dve", lambda e, vb_=vb_, cs_=cs_: e.tensor_copy(out=vb_[:, 0:w], in_=ps[pv][:, cs_]), reads=[("ps", pv)], writes=[("r_vb", hh_)])
                            yield 0
                            pw, pa, pg = self.nextps(), self.nextps(), self.nextps()
                            S.op("pe", lambda e: e.matmul(ps[pw][:, 0:w], lhsT=w2nd[0:64, 0, :], rhs=l1T[0:64, 0, c0:c0 + w], start=True, stop=True), reads=[("r_w2nd", 0), ("r_l1T", 0)], writes=[("ps", pw)])
                            S.op("pe", lambda e: e.matmul(ps[pa][:, 0:w], lhsT=w2nd[0:64, 1, :], rhs=l1T[0:64, 1, c0:c0 + w], start=True, stop=True), reads=[("r_w2nd", 1), ("r_l1T", 1)], writes=[("ps", pa)])
                            S.op("pe", lambda e: e.matmul(ps[pg][:, 0:w], lhsT=w2nd[:, 2, :], rhs=l1T[:, 2, c0:c0 + w], start=True, stop=True), reads=[("r_w2nd", 2), ("r_l1T", 2)], writes=[("ps", pg)])
                            S.op("act", lambda e: e.activation(out=lwf[:, 0:w], in_=ps[pw][:, 0:w], func=AF.Sigmoid, bias=vec("rw_w0")), reads=[("ps", pw), "consts"], writes=["r_lwf"])
                            S.op("dve", lambda e: e.tensor_scalar(out=lwf[:, 0:w], in0=lwf[:, 0:w], scalar1=-0.6065306597126334, scalar2=1.0, op0=ALU.mult, op1=ALU.mult), reads=["r_lwf"], writes=["r_lwf"])
                            S.op("act", lambda e: e.activation(out=af[:, 0:w], in_=ps[pa][:, 0:w], func=AF.Sigmoid, bias=vec("rw_a0")), reads=[("ps", pa), "consts"], writes=["r_af"])
                            self.copy("act", gf[:, 0:w], ps[pg][:, 0:w], [("ps", pg)], [("r_gf", par)])
                            yield 0
                            if not sample_pass:
                                for c in range(nch):
                                    S.op("dve", lambda e, c=c: e.tensor_tensor_scan(out=G[:, c * 128:(c + 1) * 128], data0=self.onesf[:, 0:128], data1=lwf[:, c * 128:(c + 1) * 128], initial=0.0, op0=ALU.mult, op1=ALU.add), reads=["r_lwf", "consts2"], writes=["r_G"])
                                v3 = lambda t_: t_[:, 0:w].rearrange("p (c t) -> p c t", t=128)
                                glast = v3(G)[:, :, 127:128].broadcast_to([128, nch, 128])
                                ngrp = nch
                                S.op("act", lambda e: e.activation(out=EC[:, 0:nch], in_=v3(G)[:, :, 127], func=AF.Exp), reads=["r_G"], writes=[("r_EC", par)])
                            else:
                                G3 = G[:, 0:NS].rearrange("p (b t) -> p b t", t=TS)
                                l3 = lwf[:, 0:NS].rearrange("p (b t) -> p b t", t=TS)
                                S.op("dve", lambda e: e.tensor_copy(out=G3[:, :, 0:1], in_=l3[:, :, 0:1]), reads=["r_lwf"], writes=["r_G"])
                                for t_ in range(1, TS):
                                    S.op("dve", lambda e, t_=t_: e.tensor_tensor(out=G3[:, :, t_:t_ + 1], in0=G3[:, :, t_ - 1:t_], in1=l3[:, :, t_:t_ + 1], op=ALU.add), reads=["r_lwf", "r_G"], writes=["r_G"])
                                v3 = lambda t_: t_[:, 0:NS].rearrange("p (b t) -> p b t", t=TS)
                                glast = G3[:, :, TS - 1:TS].broadcast_to([128, NB, TS])
                                S.op("act", lambda e: e.activation(out=EC[:, 0:NB], in_=G3[:, :, TS - 1], func=AF.Exp), reads=["r_G"], writes=[("r_EC", par)])
                            yield 0
                            S.op("dve", lambda e: e.tensor_scalar(out=kap[:, 0:w], in0=kf[:, 0:w], scalar1=vec("rw_k_k"), scalar2=1.0, op0=ALU.mult, op1=ALU.mult), reads=[("r_kf", half), "consts"], writes=["r_kap"])
                            S.op("act", lambda e: e.activation(out=t1[:, 0:w], in_=kap[:, 0:w], func=AF.Square), reads=["r_kap"], writes=["r_t1"])
                            pn = self.nextps()
                            S.op("pe", lambda e: e.matmul(ps[pn][:, 0:w], lhsT=self.blk1, rhs=t1[:, 0:w], start=True, stop=True), reads=["r_t1", "consts"], writes=[("ps", pn)])
                            S.op("dve", lambda e: e.tensor_scalar(out=t1[:, 0:w], in0=ps[pn][:, 0:w], scalar1=1e-24, scalar2=None, op0=ALU.max), reads=[("ps", pn), "r_t1"], writes=["r_t1"])
                            S.op("act", lambda e: e.activation(out=t1[:, 0:w], in_=t1[:, 0:w], func=AF.Ln), reads=["r_t1"], writes=["r_t1"])
                            S.op("act", lambda e: e.activation(out=t1[:, 0:w], in_=t1[:, 0:w], func=AF.Exp, scale=-0.5), reads=["r_t1"], writes=["r_t1"])
                            S.op("dve", lambda e: e.tensor_tensor(out=kap[:, 0:w], in0=kap[:, 0:w], in1=t1[:, 0:w], op=ALU.mult), reads=["r_kap", "r_t1"], writes=["r_kap"])
                            yield 0
                            S.op("dve", lambda e: e.tensor_scalar(out=k2[:, 0:w], in0=af[:, 0:w], scalar1=vec("rw_k_a"), scalar2=self.rw_omka[:, pc:pc + 1], op0=ALU.mult, op1=ALU.add), reads=["r_af", "consts", "rwc1"], writes=["r_k2"])
                            S.op("pool", lambda e: e.tensor_tensor(out=k2[:, 0:w], in0=k2[:, 0:w], in1=kf[:, 0:w], op=ALU.mult), reads=["r_k2", ("r_kf", half)], writes=["r_k2"])
                            S.op("pool", lambda e: e.tensor_tensor(out=beta[:, 0:w], in0=af[:, 0:w], in1=kap[:, 0:w], op=ALU.mult), reads=["r_af", "r_kap"], writes=["r_beta"])
                            yield 0
                            S.op("dve", lambda e: e.scalar_tensor_tensor(out=bon[:, 0:w], in0=rf[:, 0:w], scalar=vec("rw_r_k"), in1=k2[:, 0:w], op0=ALU.mult, op1=ALU.mult), reads=[("r_rf", half), "r_k2", "consts"], writes=[("r_bon", par)])
                            pb = self.nextps()
                            S.op("pe", lambda e: e.matmul(ps[pb][:, 0:w], lhsT=self.blk1, rhs=bon[:, 0:w], start=True, stop=True), reads=[("r_bon", par), "consts"], writes=[("ps", pb)])
                            S.op("dve", lambda e: e.tensor_tensor(out=bon[:, 0:w], in0=ps[pb][:, 0:w], in1=vf[:, 0:w], op=ALU.mult), reads=[("ps", pb), ("r_vf", half), ("r_bon", par)], writes=[("r_bon", par)])
                            yield 0
                            if not sample_pass:
                                kr = lambda i_: KR[:, :, i_, :]
                                bk = lambda i_: BK[:, :, i_, :]
                            else:
                                kr = lambda i_: KR[:, 0, i_, 0:NS].rearrange("p (b t) -> p b t", t=TS)
                                bk = lambda i_: BK[:, 0, i_, 0:NS].rearrange("p (b t) -> p b t", t=TS)
                            S.op("act", lambda e: e.activation(out=e0[:, 0:w], in_=G[:, 0:w], func=AF.Exp), reads=["r_G"], writes=["r_e0"])
                            S.op("pool", lambda e: e.tensor_tensor(out=kr(1), in0=v3(rf), in1=v3(e0), op=ALU.mult), reads=[("r_rf", half), "r_e0"], writes=[("r_KR1", par)])
                            S.op("dve", lambda e: e.tensor_tensor(out=t1[:, 0:w], in0=G[:, 0:w], in1=lwf[:, 0:w], op=ALU.subtract), reads=["r_G", "r_lwf", "r_t1"], writes=["r_t1"])
                            S.op("act", lambda e: e.activation(out=e1[:, 0:w], in_=t1[:, 0:w], func=AF.Exp), reads=["r_t1"], writes=["r_e1"])
                            S.op("pool", lambda e: e.tensor_tensor(out=kr(0), in0=v3(kap), in1=v3(e1), op=ALU.mult), reads=["r_kap", "r_e1"], writes=[("r_KR0", par)])
                            S.op("act", lambda e: e.activation(out=e0[:, 0:w], in_=G[:, 0:w], func=AF.Exp, scale=-1.0), reads=["r_G", ("r_KR1", par)], writes=["r_e0"])
                            S.op("pool", lambda e: e.tensor_tensor(out=bk(0), in0=v3(beta), in1=v3(e0), op=ALU.mult), reads=["r_beta", "r_e0"], writes=[("r_BK0", par)])
                            S.op("dve", lambda e: e.tensor_tensor(out=bk(1), in0=v3(k2), in1=v3(e0), op=ALU.mult), reads=["r_k2", "r_e0"], writes=[("r_BK1", par)])
                            S.op("dve", lambda e: e.tensor_tensor(out=v3(t1), in0=glast, in1=v3(G), op=ALU.subtract), reads=["r_G", "r_t1", "r_e1"], writes=["r_t1"])
                            S.op("act", lambda e: e.activation(out=e1[:, 0:w], in_=t1[:, 0:w], func=AF.Exp), reads=["r_t1", ("r_KR0", par)], writes=["r_e1"])
                            S.op("pool", lambda e: e.tensor_tensor(out=btT[:, 0:w], in0=beta[:, 0:w], in1=e1[:, 0:w], op=ALU.mult), reads=["r_beta", "r_e1"], writes=["r_btT"])
                            S.op("dve", lambda e: e.tensor_tensor(out=ktT[:, 0:w], in0=k2[:, 0:w], in1=e1[:, 0:w], op=ALU.mult), reads=["r_k2", "r_e1"], writes=["r_ktT"])
                            yield 0
                            pt = self.nextps()
                            ptb = ps[pt].bitcast(BF16)
                            for i_, (src_, nm_) in enumerate(((btT, "r_btT"), (ktT, "r_ktT"), (vb, ("r_vb", half)))):
                                for c in range(nch):
                                    S.op("pe", lambda e, i_=i_, c=c, src_=src_: e.transpose(ptb[0:R, (i_ * nch + c) * 128:(i_ * nch + c + 1) * 128], src_[:, c * 128:c * 128 + R], self.identb), reads=[nm_, "consts2"], writes=[("ps", pt)])
                            for i_, (dst_, nm_) in enumerate(((bt_tok, ("r_bttok", par)), (kt_tok, ("r_kttok", par)), (v_tok, ("r_vtok", par)))):
                                self.copy("dve", dst_[0:R, :, :], ptb[0:R, i_ * nch * 128:(i_ + 1) * nch * 128].rearrange("p (c k) -> p c k", c=nch), [("ps", pt)], [nm_])
                            yield "MID"
                            if bi_ == 0:
                                S.op("pool", lambda e: e.memset(Sst, 0.0), writes=[("r_Sst", 0), ("r_Sst", 1)])
                                S.op("pool", lambda e: e.memset(Sbf, 0.0), writes=[("r_Sbf", 0), ("r_Sbf", 1)])
                            for hd in range(2):
                                P0 = hd * 64
                                for c in range(nch):
                                    m_ = hd * nch + c
                                    kr_c = KR[P0:P0 + 64, c, :, 0:R]
                                    p1, p2, p3 = self.nextps(), self.nextps(), self.nextps()
                                    o1 = ps[p1][0:R, 0:2 * R].rearrange("p (i t) -> p i t", i=2)
                                    o2 = ps[p2][0:R, 0:2 * R].rearrange("p (i t) -> p i t", i=2)
                                    S.op("pe", lambda e, c=c, P0=P0, kr_c=kr_c, o1=o1: e.matmul(o1, lhsT=BK[P0:P0 + 64, c, 0, 0:R], rhs=kr_c, start=True, stop=True), reads=[("r_BK0", par), ("r_KR0", par), ("r_KR1", par)], writes=[("ps", p1)])
                                    S.op("pe", lambda e, c=c, P0=P0, kr_c=kr_c, o2=o2: e.matmul(o2, lhsT=BK[P0:P0 + 64, c, 1, 0:R], rhs=kr_c, start=True, stop=True), reads=[("r_BK1", par), ("r_KR0", par), ("r_KR1", par)], writes=[("ps", p2)])
                                    S.op("pe", lambda e, c=c, P0=P0, p3=p3: e.matmul(ps[p3][0:R, 0:R], lhsT=KR[P0:P0 + 64, c, 0, 0:R], rhs=BK[P0:P0 + 64, c, 0, 0:R], start=True, stop=True), reads=[("r_BK0", par), ("r_KR0", par)], writes=[("ps", p3)])
                                    S.op("dve", lambda e, m_=m_, p1=p1: e.tensor_tensor(out=Nk[0:R, 0, m_, 0:R], in0=ps[p1][0:R, 0:R], in1=mStrict, op=ALU.mult), reads=[("ps", p1), "consts"], writes=[("r_Nk", 0, m_)])
                                    S.op("dve", lambda e, m_=m_, p1=p1: e.tensor_tensor(out=MbT[0:R, m_, 0:R], in0=ps[p1][0:R, R:2 * R], in1=mIncl, op=ALU.mult), reads=[("ps", p1), "consts"], writes=[("r_MbT", m_)])
                                    S.op("dve", lambda e, m_=m_, p2=p2: e.tensor_tensor(out=BTm[0:R, m_, 0:R], in0=ps[p2][0:R, 0:R], in1=mStrict, op=ALU.mult), reads=[("ps", p2), "consts"], writes=[("r_BTm", m_)])
                                    S.op("dve", lambda e, m_=m_, p2=p2: e.tensor_tensor(out=MkT[0:R, m_, 0:R], in0=ps[p2][0:R, R:2 * R], in1=mIncl, op=ALU.mult), reads=[("ps", p2), "consts"], writes=[("r_MkT", m_)])
                                    S.op("dve", lambda e, m_=m_, p3=p3: e.tensor_tensor(out=Ak[0:R, 0, m_, 0:R], in0=ps[p3][0:R, 0:R], in1=mLow, op=ALU.mult), reads=[("ps", p3), "consts"], writes=[("r_Ak", 0, m_)])
                                    S.op("pool", lambda e, m_=m_: e.tensor_tensor(out=Wm[0:R, m_, 0:R], in0=self.ident[0:R, 0:R], in1=Nk[0:R, 0, m_, 0:R], op=ALU.subtract), reads=[("r_Nk", 0, m_), "consts"], writes=[("r_Wm", m_)])
                                    yield 0
                            yield 0
                            nm_ = 2 * nch
                            for lev in range(1, nlev + 1):
                                cur, nxt = (lev - 1) % 2, lev % 2
                                for m_ in range(nm_):
                                    p1 = self.nextps()
                                    S.op("pe", lambda e, m_=m_, p1=p1, cur=cur: e.matmul(ps[p1][0:R, 0:R], lhsT=Nk[0:R, cur, m_, 0:R], rhs=Ak[0:R, cur, m_, 0:R], start=True, stop=True), reads=[("r_Nk", cur, m_), ("r_Ak", cur, m_)], writes=[("ps", p1)])
                                    if lev < nlev:
                                        S.op("pe", lambda e, m_=m_, p1=p1, cur=cur: e.matmul(ps[p1][0:R, 128:128 + R], lhsT=Ak[0:R, cur, m_, 0:R], rhs=Nk[0:R, cur, m_, 0:R], start=True, stop=True), reads=[("r_Nk", cur, m_), ("r_Ak", cur, m_)], writes=[("ps", p1)])
                                    self.copy("act", Ak[0:R, nxt, m_, 0:R], ps[p1][0:R, 0:R], [("ps", p1)], [("r_Ak", nxt, m_)])
                                    if lev < nlev:
                                        self.copy("dve", Nk[0:R, nxt, m_, 0:R], ps[p1][0:R, 128:128 + R], [("ps", p1)], [("r_Nk", nxt, m_)])
                                    yield 0
                                for m_ in range(nm_):
                                    p2 = self.nextps()
                                    S.op("pe", lambda e, m_=m_, p2=p2, nxt=nxt: e.matmul(ps[p2][0:R, 0:R], lhsT=Ak[0:R, nxt, m_, 0:R], rhs=Wm[0:R, m_, 0:R], start=True, stop=True), reads=[("r_Ak", nxt, m_), ("r_Wm", m_)], writes=[("ps", p2)])
                                    S.op("dve", lambda e, m_=m_, p2=p2: e.tensor_tensor(out=Wm[0:R, m_, 0:R], in0=Wm[0:R, m_, 0:R], in1=ps[p2][0:R, 0:R], op=ALU.add), reads=[("ps", p2), ("r_Wm", m_)], writes=[("r_Wm", m_)])
                                    yield 0
                            yield 0
                            pO = self.nextlong()
                            if sample_pass:
                                for hd in range(2):
                                    S.op("sp", lambda e, hd=hd: e.dma_start(out=s0nat[:, :, hd * 64:(hd + 1) * 64], in_=dr["st_wkv"][:, 2 * pc + hd].rearrange("b v k -> v b k")), writes=[("r_s0nat", hd)], dma="r_s0nat")
                                for q4 in range(4):
                                    p1 = self.nextps()
                                    for bb in range(4):
                                        b = q4 * 4 + bb
                                        S.op("pe", lambda e, b=b, bb=bb, p1=p1: e.transpose(ps[p1][:, bb * 64:(bb + 1) * 64], s0nat[:, b, :], self.ident[0:64, 0:64]), reads=[("r_s0nat", 0), ("r_s0nat", 1), "consts"], writes=[("ps", p1)])
                                    self.copy("act", S0[:, q4 * 4:(q4 + 1) * 4, :], ps[p1][:, 0:256].rearrange("p (b v) -> p b v", b=4), [("ps", p1)], [("r_S0", q4)])
                                    S.op("dve", lambda e, q4=q4, p1=p1: e.tensor_copy(out=S0bf[:, q4 * 4:(q4 + 1) * 4, :], in_=ps[p1][:, 0:256].rearrange("p (b v) -> p b v", b=4)), reads=[("ps", p1)], writes=[("r_S0bf", q4)])
                                s0k = [("r_S0", q4) for q4 in range(4)]
                                s0bk = [("r_S0bf", q4) for q4 in range(4)]
                                S.op("dve", lambda e: e.tensor_tensor(out=kblk, in0=KR[:, 0, 0, 0:NS].unsqueeze(1).broadcast_to([128, NB, NS]), in1=self.maskC, op=ALU.mult), reads=[("r_KR0", par), "consts"], writes=["r_kblk"])
                                S.op("dve", lambda e: e.tensor_tensor(out=rblk, in0=KR[:, 0, 1, 0:NS].unsqueeze(1).broadcast_to([128, NB, NS]), in1=self.maskC, op=ALU.mult), reads=[("r_KR1", par), "consts"], writes=["r_rblk"])
                            for c in range(nch):
                                for hd in range(2):
                                    P0 = hd * 64
                                    m_ = hd * nch + c
                                    hs = slice(P0, P0 + 64)
                                    yield 0
                                    pX = self.nextps()
                                    if not sample_pass:
                                        S.op("pe", lambda e, c=c, hs=hs, pX=pX: e.matmul(ps[pX][0:R, 0:64], lhsT=KR[hs, c, 0, 0:R], rhs=Sbf[hs, :], start=True, stop=False), reads=[("r_KR0", par), ("r_Sbf", hd)], writes=[("ps", pX)])
                                    else:
                                        for b in range(NB):
                                            S.op("pe", lambda e, b=b, hs=hs, pX=pX: e.matmul(ps[pX][0:R, 0:64], lhsT=kblk[hs, b, :], rhs=S0bf[hs, b, :], start=(b == 0), stop=False), reads=["r_kblk"] + s0bk, writes=[("ps", pX)])
                                    S.op("pe", lambda e, c=c, hs=hs, pX=pX, m_=m_: e.matmul(ps[pX][0:R, 0:64], lhsT=BTm[:, m_, 0:R], rhs=v_tok[:, c, hs], start=False, stop=True), reads=[("r_BTm", m_), ("r_vtok", par)], writes=[("ps", pX)])
                                    S.op("act", lambda e, hd=hd, pX=pX: e.activation(out=Xsb[0:R, hd, :], in_=ps[pX][0:R, 0:64], func=AF.Copy, scale=-1.0), reads=[("ps", pX)], writes=[("r_Xsb", hd)])
                                    pS = self.nextps()
                                    S.op("pe", lambda e, hd=hd, pS=pS, m_=m_: e.matmul(ps[pS][0:R, 0:64], lhsT=Wm[0:R, m_, 0:R], rhs=Xsb[0:R, hd, :], start=True, stop=True), reads=[("r_Wm", m_), ("r_Xsb", hd)], writes=[("ps", pS)])
                                    S.op("dve", lambda e, hd=hd, pS=pS: e.tensor_copy(out=Ssb[0:R, hd, :], in_=ps[pS][0:R, 0:64]), reads=[("ps", pS)], writes=[("r_Ssb", hd)])
                                    oo = ps[pO][hs, c * 128:c * 128 + R]
                                    if not sample_pass:
                                        S.op("pe", lambda e, c=c, hs=hs, oo=oo: e.matmul(oo, lhsT=Sbf[hs, :], rhs=KR[hs, c, 1, 0:R], start=True, stop=False), reads=[("r_KR1", par), ("r_Sbf", hd)], writes=[("ps", pO)])
                                        S.op("pe", lambda e, hd=hd, m_=m_, oo=oo: e.matmul(oo, lhsT=Ssb[0:R, hd, :], rhs=MbT[0:R, m_, 0:R], start=False, stop=False), reads=[("r_Ssb", hd), ("r_MbT", m_)], writes=[("ps", pO)])
                                        S.op("pe", lambda e, c=c, hs=hs, m_=m_, oo=oo: e.matmul(oo, lhsT=v_tok[0:R, c, hs], rhs=MkT[0:R, m_, 0:R], start=False, stop=True), reads=[("r_vtok", par), ("r_MkT", m_)], writes=[("ps", pO)])
                                        pU = self.nextps()
                                        S.op("pe", lambda e, c=c, hs=hs, hd=hd, pU=pU: e.matmul(ps[pU][hs, 0:64], lhsT=bt_tok[0:R, c, hs], rhs=Ssb[0:R, hd, :], start=True, stop=False), reads=[("r_bttok", par), ("r_Ssb", hd)], writes=[("ps", pU)])
                                        S.op("pe", lambda e, c=c, hs=hs, pU=pU: e.matmul(ps[pU][hs, 0:64], lhsT=kt_tok[0:R, c, hs], rhs=v_tok[0:R, c, hs], start=False, stop=True), reads=[("r_kttok", par), ("r_vtok", par)], writes=[("ps", pU)])
                                        S.op("dve", lambda e, c=c, hs=hs, pU=pU: e.scalar_tensor_tensor(out=Sst[hs, :], in0=Sst[hs, :], scalar=EC[hs, c:c + 1], in1=ps[pU][hs, 0:64], op0=ALU.mult, op1=ALU.add), reads=[("ps", pU), ("r_Sst", hd), ("r_EC", par)], writes=[("r_Sst", hd)])
                                        S.op("pool", lambda e, hs=hs: e.tensor_copy(out=Sbf[hs, :], in_=Sst[hs, :]), reads=[("r_Sst", hd)], writes=[("r_Sbf", hd)])
                                    else:
                                        S.op("pe", lambda e, hd=hd, m_=m_, oo=oo: e.matmul(oo, lhsT=Ssb[:, hd, :], rhs=MbT[:, m_, 0:R], start=True, stop=False), reads=[("r_Ssb", hd), ("r_MbT", m_)], writes=[("ps", pO)])
                                        S.op("pe", lambda e, hs=hs, m_=m_, oo=oo: e.matmul(oo, lhsT=v_tok[:, 0, hs], rhs=MkT[:, m_, 0:R], start=False, stop=False), reads=[("r_vtok", par), ("r_MkT", m_)], writes=[("ps", pO)])
                                        for b in range(NB):
                                            S.op("pe", lambda e, b=b, hs=hs, oo=oo: e.matmul(oo, lhsT=S0bf[hs, b, :], rhs=rblk[hs, b, :], start=False, stop=(b == NB - 1)), reads=["r_rblk"] + s0bk, writes=[("ps", pO)])
                                        S.op("dve", lambda e, hd=hd: e.tensor_tensor(out=Sblk[0:NS, hd, :, :], in0=Ssb[0:NS, hd, :].unsqueeze(1).broadcast_to([NS, NB, 64]), in1=self.maskB.unsqueeze(2).broadcast_to([NS, NB, 64]), op=ALU.mult), reads=[("r_Ssb", hd), "consts"], writes=[("r_Sblk", hd)])
                                        S.op("dve", lambda e, hd=hd, hs=hs: e.tensor_tensor(out=Vblk[0:NS, hd, :, :], in0=v_tok[0:NS, 0, hs].unsqueeze(1).broadcast_to([NS, NB, 64]), in1=self.maskB.unsqueeze(2).broadcast_to([NS, NB, 64]), op=ALU.mult), reads=[("r_vtok", par), "consts"], writes=[("r_Vblk", hd)])
                                        for half in range(2):
                                            pU = self.nextps()
                                            S.op("pe", lambda e, hs=hs, hd=hd, half=half, pU=pU: e.matmul(ps[pU][hs, 0:512], lhsT=bt_tok[:, 0, hs], rhs=Sblk[:, hd, half * 8:(half + 1) * 8, :], start=True, stop=False), reads=[("r_bttok", par), ("r_Sblk", hd)], writes=[("ps", pU)])
                                            S.op("pe", lambda e, hs=hs, hd=hd, half=half, pU=pU: e.matmul(ps[pU][hs, 0:512], lhsT=kt_tok[:, 0, hs], rhs=Vblk[:, hd, half * 8:(half + 1) * 8, :], start=False, stop=True), reads=[("r_kttok", par), ("r_Vblk", hd)], writes=[("ps", pU)])
                                            bs = slice(half * 8, (half + 1) * 8)
                                            S.op("dve", lambda e, hs=hs, bs=bs: e.tensor_tensor(out=S0[hs, bs, :], in0=S0[hs, bs, :], in1=EC[hs, bs].unsqueeze(2).broadcast_to([64, 8, 64]), op=ALU.mult), reads=s0k + [("r_EC", par)], writes=s0k)
                                            S.op("dve", lambda e, hs=hs, bs=bs, pU=pU: e.tensor_tensor(out=S0[hs, bs, :], in0=S0[hs, bs, :], in1=ps[pU][hs, 0:512].rearrange("p (b v) -> p b v", b=8), op=ALU.add), reads=s0k + [("ps", pU)], writes=s0k)
                            if sample_pass:
                                for q4 in range(4):
                                    p1 = self.nextps()
                                    for bb in range(4):
                                        b = q4 * 4 + bb
                                        S.op("pe", lambda e, b=b, bb=bb, p1=p1: e.transpose(ps[p1][0:64, bb * 128:(bb + 1) * 128], S0[:, b, :], self.ident), reads=s0k + ["consts"], writes=[("ps", p1)])
                                    self.copy("act", s0nat[:, q4 * 4:(q4 + 1) * 4, :], ps[p1][0:64, :].rearrange("p (b k) -> p b k", b=4), [("ps", p1)], [("r_s0nat", 0), ("r_s0nat", 1)])
                                for hd in range(2):
                                    S.op("sp", lambda e, hd=hd: e.dma_start(out=dr["wkv_s"][:, 2 * pc + hd].rearrange("b v k -> v b k"), in_=s0nat[:, :, hd * 64:(hd + 1) * 64]), reads=[("r_s0nat", 0), ("r_s0nat", 1)], dma="r_wkv_s")
                            elif c0 + w == TP:
                                p1 = self.nextps()
                                S.op("pe", lambda e, p1=p1: e.transpose(ps[p1][0:64, 0:128], Sst, self.ident), reads=[("r_Sst", 0), ("r_Sst", 1), "consts"], writes=[("ps", p1)])
                                self.copy("act", sT, ps[p1][0:64, 0:128], [("ps", p1)], ["r_sT"])
                                S.op("sp", lambda e: e.dma_start(out=dr["wkv_p"][2 * pc:2 * pc + 2].rearrange("h v k -> v h k"), in_=sT.rearrange("v (h k) -> v h k", h=2)), reads=["r_sT"], dma="r_wkv_p")
                            yield 0
                            self.copy("act", osb[:, 0:w], ps[pO][:, 0:w], [("ps", pO)], ["r_osb"])
                            pm = self.nextps()
                            S.op("pe", lambda e: e.matmul(ps[pm][:, 0:w], lhsT=self.blk1, rhs=osb[:, 0:w], start=True, stop=True), reads=["r_osb", "consts"], writes=[("ps", pm)])
                            S.op("dve", lambda e: e.scalar_tensor_tensor(out=osb[:, 0:w], in0=ps[pm][:, 0:w], scalar=-1.0 / 64, in1=osb[:, 0:w], op0=ALU.mult, op1=ALU.add), reads=[("ps", pm), "r_osb"], writes=["r_osb"])
                            S.op("act", lambda e: e.activation(out=t2[:, 0:w], in_=osb[:, 0:w], func=AF.Square), reads=["r_osb", "r_t2"], writes=["r_t2"])
                            pv2 = self.nextps()
                            S.op("pe", lambda e: e.matmul(ps[pv2][:, 0:w], lhsT=self.blk1, rhs=t2[:, 0:w], start=True, stop=True), reads=["r_t2", "consts"], writes=[("ps", pv2)])
                            S.op("act", lambda e: e.activation(out=t2[:, 0:w], in_=ps[pv2][:, 0:w], func=AF.Ln, scale=1.0 / 64, bias=GN_EPS), reads=[("ps", pv2), "r_t2"], writes=["r_t2"])
                            S.op("act", lambda e: e.activation(out=t2[:, 0:w], in_=t2[:, 0:w], func=AF.Exp, scale=-0.5), reads=["r_t2"], writes=["r_t2"])
                            S.op("dve", lambda e: e.tensor_tensor(out=osb[:, 0:w], in0=osb[:, 0:w], in1=t2[:, 0:w], op=ALU.mult), reads=["r_osb", "r_t2"], writes=["r_osb"])
                            S.op("dve", lambda e: e.tensor_scalar(out=osb[:, 0:w], in0=osb[:, 0:w], scalar1=vec("rw_lnx_w"), scalar2=vec("rw_lnx_b"), op0=ALU.mult, op1=ALU.add), reads=["r_osb", "consts"], writes=["r_osb"])
                            S.op("pool", lambda e: e.tensor_tensor(out=osb[:, 0:w], in0=osb[:, 0:w], in1=bon[:, 0:w], op=ALU.add), reads=["r_osb", ("r_bon", par)], writes=["r_osb"])
                            S.op("dve", lambda e: e.tensor_tensor(out=og[:, 0:w], in0=osb[:, 0:w], in1=gf[:, 0:w], op=ALU.mult), reads=["r_osb", ("r_gf", par)], writes=["r_og"])
                            for dc in range(KC):
                                pi = self.nextps()
                                S.op("pe", lambda e, pi=pi, dc=dc: e.matmul(ps[pi][:, 0:w], lhsT=wout[:, dc * 128:(dc + 1) * 128], rhs=og[:, 0:w], start=True, stop=True), reads=[("r_wout", pc % 2), "r_og"], writes=[("ps", pi)])
                                S.op("dve", lambda e, pi=pi, dc=dc: e.tensor_tensor(out=xres[:, dc, c0:c0 + w], in0=xres[:, dc, c0:c0 + w], in1=ps[pi][:, 0:w], op=ALU.add), reads=[("ps", pi), ("xres", dc, c0)], writes=[("xres", dc, c0)])
                                yield 0


                    gens = []
                    gi = 0
                    for pc in range(KC):
                        for bi_, (c0, w, kind) in enumerate(blocks):
                            gens.append(blk(pc, bi_, c0, w, kind, gi % 2))
                            gi += 1
                    back = None
                    for g in gens:
                        fd, bd = False, back is None
                        while not (fd and bd):
                            if not fd:
                                fd = next(g) == "MID"
                            if not bd:
                                bd = next(back, "END") == "END"
                        back = g
                    for _ in back:
                        pass
                S.barrier()

    def mamba2(self, j):
        nc, S, TP = self.nc, self.S, self.TP
        xres, hT, ps = self.xres, self.hT, self.ps
        w_in, w_out = self.dram["mb_w_in"], self.dram["mb_w_out"]
        st_ssm, st_conv = self.dram["st_ssm"], self.dram["st_conv"]
        ssm_p, ssm_s, cv_p, cv_s = self.dram["ssm_p"], self.dram["ssm_s"], self.dram["cv_p"], self.dram["cv_s"]
        W = 512
        with ExitStack() as es:
            T = lambda n, sh, dt=F32: es.enter_context(self.T(n, sh, dt))
            wz, wx = T("m_wz", [128, KC, 512], BF16), T("m_wx", [128, KC, 512], BF16)
            wB, wC, wdt = T("m_wB", [128, KC, 128], BF16), T("m_wC", [128, KC, 128], BF16), T("m_wdt", [128, KC, 8], BF16)
            wout = T("m_wout", [128, 4, D], BF16)
            ngt = T("m_ng", [128, 512])
            pre = T("m_pre", [128, 6, 3 + W])
            acc = T("m_acc", [128, 1, W])
            xsT, BT, CT = T("m_xsT", [128, 4, W], BF16), T("m_BT", [128, W], BF16), T("m_CT", [128, W], BF16)
            cvT, cvrow = T("m_cvT", [128, 6, 48]), T("m_cvrow", [48, 768])
            zs, dtv, da = T("m_zs", [128, 512]), T("m_dt", [128, 8]), T("m_da", [128, 8])
            xtok, Btok = T("m_xtok", [128, 512], BF16), T("m_Btok", [128, 128], BF16)
            cbm, ex, wts = T("m_cbm", [128, 128]), T("m_ex", [128, 24]), T("m_wts", [128, 8])
            daM, seg, mT = T("m_daM", [128, 4, 128]), T("m_seg", [128, 4, 128]), T("m_mT", [128, 4, 128], BF16)
            y1, xd = T("m_y1", [128, 512]), T("m_xd", [128, 512])
            ssq, yn = T("m_ssq", [128, 2]), T("m_yn", [128, 512], BF16)
            yTb = T("m_yTb", [128, 4, W], BF16)
            xw = T("m_xw", [128, 512], BF16)
            ST, STbf = T("m_ST", [128, 512]), T("m_STbf", [128, 512], BF16)
            stT = T("m_stT", [128, 4, 128])
            cv0 = cvrow
            Cblk = T("m_Cblk", [128, NB, NS], BF16)
            ST0bf = T("m_ST0bf", [128, 2, 512], BF16)
            dablk, etots = T("m_dablk", [64, NB, 8]), T("m_etots", [128, NB, 8])
            fz = T("m_fz", [128, 2])
            etn = T("m_etn", [128, NB, 4])
            preflat = pre.rearrange("p f t -> p (f t)")
            snat_slots = [preflat[:, k_ * 512:(k_ + 1) * 512].rearrange("p (q n) -> p q n", q=4) for k_ in range(5)]
            stT_slots = [stT, preflat[:, 2560:3072].rearrange("p (q n) -> p q n", q=4)]
            for g in range(4):
                for kk0 in range(0, KC, 2):
                    for (wt, col) in ((wz, g * 512), (wx, 2048 + g * 512)):
                        src = w_in[j, kk0 * 128:(kk0 + 2) * 128, col:col + 512].rearrange("(k p) c -> p k c", p=128)
                        self.load_w(wt[:, kk0:kk0 + 2, :], src, [128, 2, 512], ("m_w", id(wt), kk0))
                for (wt, col) in ((wB, 4096 + g * 128), (wC, 4608 + g * 128)):
                    self.load_w(wt, w_in[j, :, col:col + 128].rearrange("(k p) c -> p k c", p=128), [128, KC, 128], ("m_w", id(wt)))
                self.load_w(wdt, w_in[j, :, 5120 + 8 * g:5128 + 8 * g].rearrange("(k p) c -> p k c", p=128), [128, KC, 8], ("m_w", id(wdt)))
                for fc in range(4):
                    self.load_w(wout[:, fc, :], w_out[j, g * 512 + fc * 128:g * 512 + (fc + 1) * 128, :], [128, D], ("m_wout", fc))
                S.op("sp", lambda e: e.dma_start(out=ngt, in_=self.dram["mb_norm"][:, g * 512:(g + 1) * 512]), writes=["m_ng"], dma="m_ng")
                wk2 = lambda wt: [("m_w", id(wt), k_) for k_ in range(0, KC, 2)]
                wk1 = lambda wt: [("m_w", id(wt))]
                fcol = [g * 512 + i * 128 for i in range(4)] + [2048 + g * 128, 2560 + g * 128]
                fch24 = [c_ // 128 for c_ in fcol]
                S.op("pool", lambda e: e.memset(ST, 0.0), writes=["m_ST"])
                S.op("pool", lambda e: e.memset(STbf, 0.0), writes=["m_STbf"])
                S.op("pool", lambda e: e.memset(pre[:, :, 0:3], 0.0), writes=["m_pre"] + [("m_snat", k_) for k_ in range(5)] + [("m_stT", 1)])
                S.op("pool", lambda e: e.memset(cvT, 0.0), writes=["m_cvT"])
                for (c0, w, kind) in self.tbs:
                    smp = kind == "s"
                    hc = 2 + c0
                    R = 64 if smp else 128
                    if smp:
                        pre4 = pre[:, :, 0:NB * 7].rearrange("p f (b t) -> p f b t", t=7)
                        for i_, (c_, n_) in enumerate(((fcol[0], 512), (fcol[4], 128), (fcol[5], 128))):
                            o_ = (0, 512, 640)[i_]
                            S.op("sp", lambda e, c_=c_, n_=n_, o_=o_: e.dma_start(out=cv0[:, o_:o_ + n_], in_=st_conv[:, :, c_:c_ + n_].rearrange("b w f -> (b w) f")), writes=["m_cvrow"], dma="m_cv0")
                        pt = self.nextps()
                        for f in range(6):
                            S.op("pe", lambda e, f=f: e.transpose(ps[pt][:, f * 48:(f + 1) * 48], cv0[:, f * 128:(f + 1) * 128], self.ident[0:48, 0:48]), reads=["m_cvrow", "consts"], writes=[("ps", pt)])
                        self.copy("dve", pre4[:, :, :, 0:3], ps[pt][:, 0:288].rearrange("p (f b t) -> p f b t", f=6, t=3), [("ps", pt)], ["m_pre"])
                    elif c0 > 0:
                        self.copy("dve", pre[:, :, 0:3], pre[:, :, W:W + 3], ["m_pre"], ["m_pre"])
                    for f in range(6):
                        pi = self.nextps()
                        wt, cs = (wx, f * 128) if f < 4 else ((wB, 0) if f == 4 else (wC, 0))
                        for kc in range(KC):
                            S.op("pe", lambda e, pi=pi, kc=kc, wt=wt, cs=cs: e.matmul(ps[pi][:, 0:w], lhsT=wt[:, kc, cs:cs + 128], rhs=hT[:, kc, hc:hc + w], start=(kc == 0), stop=(kc == KC - 1)),
                                 reads=(wk2(wt) if f < 4 else wk1(wt)) + [("hT", "all")], writes=[("ps", pi)])
                        if smp:
                            self.copy(self.ew(), pre4[:, f, :, 3:7], ps[pi][:, 0:NS].rearrange("p (b t) -> p b t", t=TS), [("ps", pi)], ["m_pre"])
                        else:
                            self.copy(self.ew(), pre[:, f, 3:3 + W], ps[pi][:, 0:W], [("ps", pi)], ["m_pre"])
                    if MBSTOP == 2:
                        return
                    for f in range(6):
                        f24 = fch24[f]
                        if smp:
                            a_ = acc[:, 0, 0:NS].rearrange("p (b t) -> p b t", t=TS)
                            src_k = lambda k_: pre4[:, f, :, k_:k_ + TS]
                        else:
                            a_ = acc[:, 0, :]
                            src_k = lambda k_: pre[:, f, k_:k_ + W]
                        S.op("act", lambda e, a_=a_, f24=f24, s0=src_k(0): e.activation(out=a_, in_=s0, func=AF.Identity, scale=self.mb_cw[:, f24, 0:1], bias=self.mb_cb[:, f24:f24 + 1]),
                             reads=["m_pre", "consts"], writes=[("m_acc", 0)])
                        for k_ in range(1, 4):
                            S.op("dve", lambda e, a_=a_, f24=f24, k_=k_, sk=src_k(k_): e.scalar_tensor_tensor(out=a_, in0=sk, scalar=self.mb_cw[:, f24, k_:k_ + 1], in1=a_, op0=ALU.mult, op1=ALU.add),
                                 reads=["m_pre", "consts", ("m_acc", 0)], writes=[("m_acc", 0)])
                        dst = xsT[:, f, 0:w] if f < 4 else (BT[:, 0:w] if f == 4 else CT[:, 0:w])
                        S.op("act", lambda e, dst=dst, f=f: e.activation(out=dst, in_=acc[:, 0, 0:w], func=AF.Silu), reads=[("m_acc", 0)], writes=[("m_xbc", f)])
                    if MBSTOP == 3:
                        return
                    if smp or c0 + w == TP:
                        nr = 48 if smp else 3
                        if smp:
                            self.copy("dve", cvT.rearrange("p f (b t) -> p f b t", t=3), pre4[:, :, :, 4:7], ["m_pre"], ["m_cvT"])
                        else:
                            self.copy("dve", cvT[:, :, 0:3], pre[:, :, W:W + 3], ["m_pre"], ["m_cvT"])
                        pt, ptx = self.nextps(), self.nextps()
                        for f in range(6):
                            pp = pt if f < 4 else ptx
                            nrt = max(nr, 32)
                            S.op("pe", lambda e, f=f, nrt=nrt, pp=pp: e.transpose(ps[pp][0:nrt, (f % 4) * 128:(f % 4 + 1) * 128], cvT[:, f, 0:nrt], self.ident), reads=["m_cvT", "consts"], writes=[("ps", pp)])
                        self.copy("act", cvrow[0:nr, 0:512], ps[pt][0:nr, 0:512], [("ps", pt)], ["m_cvrow"])
                        self.copy("act", cvrow[0:nr, 512:768], ps[ptx][0:nr, 0:256], [("ps", ptx)], ["m_cvrow"])
                        for i_, (c_, n_) in enumerate(((fcol[0], 512), (fcol[4], 128), (fcol[5], 128))):
                            o_ = (0, 512, 640)[i_]
                            dstd = cv_s[:, :, c_:c_ + n_].rearrange("b w f -> (b w) f") if smp else cv_p[:, c_:c_ + n_]
                            S.op("sp", lambda e, dstd=dstd, o_=o_, n_=n_, nr=nr: e.dma_start(out=dstd, in_=cvrow[0:nr, o_:o_ + n_]), reads=["m_cvrow"], dma="m_cvout")
                    if MBSTOP == 4:
                        return
                    xbk = [("m_xbc", f) for f in range(6)]
                    mU = self.maskS if smp else self.maskU
                    mL = self.sLS if smp else self.sL
                    for ct in range(1 if smp else w // 128):
                        t0 = ct * 128
                        pz, pd = self.nextps(), self.nextps()
                        for kc in range(KC):
                            S.op("pe", lambda e, kc=kc: e.matmul(ps[pz][0:R, 0:512], lhsT=hT[:, kc, hc + t0:hc + t0 + R], rhs=wz[:, kc, :], start=(kc == 0), stop=(kc == KC - 1)),
                                 reads=wk2(wz) + [("hT", "all")], writes=[("ps", pz)])
                        for kc in range(KC):
                            S.op("pe", lambda e, kc=kc: e.matmul(ps[pd][0:R, 0:8], lhsT=hT[:, kc, hc + t0:hc + t0 + R], rhs=wdt[:, kc, :], start=(kc == 0), stop=(kc == KC - 1)),
                                 reads=wk1(wdt) + [("hT", "all")], writes=[("ps", pd)])
                        S.op("act", lambda e: e.activation(out=zs[0:R, :], in_=ps[pz][0:R, 0:512], func=AF.Silu), reads=[("ps", pz)], writes=["m_zs"])
                        S.op("dve", lambda e: e.tensor_tensor(out=dtv[0:R, :], in0=ps[pd][0:R, 0:8], in1=self.mb_dtb[0:R, 8 * g:8 * g + 8], op=ALU.add), reads=[("ps", pd), "consts"], writes=["m_dt"])
                        S.op("act", lambda e: e.activation(out=dtv[0:R, :], in_=dtv[0:R, :], func=AF.Exp), reads=["m_dt"], writes=["m_dt"])
                        S.op("act", lambda e: e.activation(out=dtv[0:R, :], in_=dtv[0:R, :], func=AF.Ln, bias=1.0), reads=["m_dt"], writes=["m_dt"])
                        S.op("dve", lambda e: e.tensor_tensor(out=da[0:R, :], in0=dtv[0:R, :], in1=self.mb_negA[0:R, 8 * g:8 * g + 8], op=ALU.mult), reads=["m_dt", "mbc0"], writes=["m_da"])
                        if MBSTOP == 5:
                            return
                        pt = self.nextps()
                        ptb = ps[pt].bitcast(BF16)
                        for fc in range(4):
                            S.op("pe", lambda e, fc=fc: e.transpose(ptb[0:R, fc * 128:(fc + 1) * 128], xsT[:, fc, t0:t0 + R], self.identb), reads=xbk[0:4] + ["consts2"], writes=[("ps", pt)])
                        S.op("pe", lambda e: e.transpose(ptb[0:R, 512:640], BT[:, t0:t0 + R], self.identb), reads=[xbk[4], "consts2"], writes=[("ps", pt)])
                        self.copy("dve", xtok[0:R, :], ptb[0:R, 0:512], [("ps", pt)], ["m_xtok"])
                        self.copy("dve", Btok[0:R, :], ptb[0:R, 512:640], [("ps", pt)], ["m_Btok"])
                        pc = self.nextps()
                        S.op("pe", lambda e: e.matmul(ps[pc][0:R, 0:R], lhsT=BT[:, t0:t0 + R], rhs=CT[:, t0:t0 + R], start=True, stop=True), reads=[xbk[4], xbk[5]], writes=[("ps", pc)])
                        S.op("dve", lambda e: e.tensor_tensor(out=cbm[0:R, 0:R], in0=ps[pc][0:R, 0:R], in1=mU, op=ALU.mult), reads=[("ps", pc), "consts"], writes=["m_cbm"])
                        if MBSTOP == 6:
                            return
                        pm = self.nextps()
                        S.op("pe", lambda e: e.matmul(ps[pm][0:R, 0:8], lhsT=mU, rhs=da[0:R, :], start=True, stop=True), reads=["m_da", "consts"], writes=[("ps", pm)])
                        S.op("pe", lambda e: e.matmul(ps[pm][0:R, 8:16], lhsT=mL, rhs=da[0:R, :], start=True, stop=True), reads=["m_da", "consts"], writes=[("ps", pm)])
                        if not smp:
                            S.op("pe", lambda e: e.matmul(ps[pm][:, 16:24], lhsT=self.onesf, rhs=da, start=True, stop=True), reads=["m_da", "consts2"], writes=[("ps", pm)])
                            S.op("act", lambda e: e.activation(out=ex, in_=ps[pm][:, 0:24], func=AF.Exp), reads=[("ps", pm)], writes=["m_ex"])
                        else:
                            S.op("act", lambda e: e.activation(out=ex[0:R, 0:16], in_=ps[pm][0:R, 0:16], func=AF.Exp), reads=[("ps", pm)], writes=["m_ex"])
                            S.op("dve", lambda e: e.tensor_tensor(out=dablk, in0=da[0:NS, :].unsqueeze(1).broadcast_to([NS, NB, 8]), in1=self.maskB.unsqueeze(2).broadcast_to([NS, NB, 8]), op=ALU.mult),
                                 reads=["m_da", "consts"], writes=["m_dablk"])
                            pm2 = self.nextps()
                            S.op("pe", lambda e: e.matmul(ps[pm2][:, 0:NB * 8], lhsT=self.onesf[0:NS, :], rhs=dablk.rearrange("p b h -> p (b h)"), start=True, stop=True), reads=["m_dablk", "consts2"], writes=[("ps", pm2)])
                            S.op("act", lambda e: e.activation(out=etots.rearrange("p b h -> p (b h)"), in_=ps[pm2][:, 0:NB * 8], func=AF.Exp), reads=[("ps", pm2)], writes=["m_etots"])
                        S.op("dve", lambda e: e.tensor_tensor(out=wts[0:R, :], in0=dtv[0:R, :], in1=ex[0:R, 8:16], op=ALU.mult), reads=["m_dt", "m_ex"], writes=["m_wts"])
                        if MBSTOP == 7:
                            return
                        py = self.nextlong()
                        for hh in range(8):
                            sl = hh % 4
                            S.op("dve", lambda e, hh=hh, sl=sl: e.tensor_scalar(out=daM[0:R, sl, 0:R], in0=mL, scalar1=da[0:R, hh:hh + 1], scalar2=None, op0=ALU.mult), reads=["m_da", "consts"], writes=[("m_daM", sl)])
                            pD = self.nextps()
                            S.op("pe", lambda e, sl=sl, pD=pD: e.matmul(ps[pD][0:R, 0:R], lhsT=daM[0:R, sl, 0:R], rhs=mU, start=True, stop=True), reads=[("m_daM", sl), "consts"], writes=[("ps", pD)])
                            S.op("act", lambda e, sl=sl, pD=pD: e.activation(out=seg[0:R, sl, 0:R], in_=ps[pD][0:R, 0:R], func=AF.Exp), reads=[("ps", pD)], writes=[("m_seg", sl)])
                            S.op("dve", lambda e, sl=sl, hh=hh: e.scalar_tensor_tensor(out=mT[0:R, sl, 0:R], in0=seg[0:R, sl, 0:R], scalar=dtv[0:R, hh:hh + 1], in1=cbm[0:R, 0:R], op0=ALU.mult, op1=ALU.mult),
                                 reads=[("m_seg", sl), "m_dt", "m_cbm"], writes=[("m_mT", sl)])
                            S.op("pe", lambda e, sl=sl, hh=hh: e.matmul(ps[py][0:R, hh * 64:(hh + 1) * 64], lhsT=mT[0:R, sl, 0:R], rhs=xtok[0:R, hh * 64:(hh + 1) * 64], start=True, stop=True),
                                 reads=[("m_mT", sl), "m_xtok"], writes=[("ps", py)])
                        if MBSTOP == 8:
                            return
                        v3 = lambda t_: t_.rearrange("p (h q) -> p h q", q=64)
                        pyi = self.nextlong()
                        if not smp:
                            S.op("pe", lambda e: e.matmul(ps[pyi][:, 0:512], lhsT=CT[:, t0:t0 + 128], rhs=STbf, start=True, stop=True), reads=[xbk[5], "m_STbf"], writes=[("ps", pyi)])
                        else:
                            S.op("dve", lambda e: e.tensor_tensor(out=Cblk, in0=CT[:, 0:NS].unsqueeze(1).broadcast_to([128, NB, NS]), in1=self.maskC, op=ALU.mult), reads=[xbk[5], "consts"], writes=["m_Cblk"])
                            S.op("dve", lambda e: e.tensor_tensor(out=v3(xw[0:R, :]), in0=v3(xtok[0:R, :]), in1=wts[0:R, :].unsqueeze(2).broadcast_to([R, 8, 64]), op=ALU.mult), reads=["m_xtok", "m_wts"], writes=["m_xw"])
                            S.op("pool", lambda e: e.memset(fz, 0.0), reads=["m_pre", "m_ST", "m_STbf"], writes=[("m_snat", k_) for k_ in range(5)] + [("m_stT", 1), ("m_sbf", 0), ("m_sbf", 1)])
                            et4 = etots.rearrange("p b (q t) -> p b q t", t=2)
                            S.op("dve", lambda e: e.tensor_copy(out=etn[0:64, :, :], in_=et4[0:64, :, :, 0]), reads=["m_etots"], writes=["m_etn"])
                            S.op("dve", lambda e: e.tensor_copy(out=etn[64:128, :, :], in_=et4[64:128, :, :, 1]), reads=["m_etots", "m_etn"], writes=["m_etn"])
                            sbf = ST.bitcast(BF16).rearrange("p (s c) -> p s c", s=2)
                            for b in range(NB):
                                sl = b % 5
                                s2 = b % 2
                                slot = snat_slots[sl]
                                S.op("sp", lambda e, b=b, slot=slot: e.dma_start(out=slot, in_=st_ssm[b, 8 * g:8 * g + 8].rearrange("h p n -> (h p) n").rearrange("(q r) n -> r q n", r=128)),
                                     writes=[("m_snat", sl)], dma="m_snat%d" % sl)
                                S.op("act", lambda e, slot=slot, s2=s2: e.activation(out=sbf[:, s2, :], in_=slot.rearrange("p q n -> p (q n)"), func=AF.Copy), reads=[("m_snat", sl)], writes=[("m_sbf", s2)])
                                pq_ = self.nextps()
                                pqb = ps[pq_].bitcast(BF16)
                                for q_ in range(4):
                                    S.op("pe", lambda e, q_=q_, s2=s2, pqb=pqb: e.transpose(pqb[:, q_ * 128:(q_ + 1) * 128], sbf[:, s2, q_ * 128:(q_ + 1) * 128], self.identb), reads=[("m_sbf", s2), "consts2"], writes=[("ps", pq_)])
                                self.copy("dve", ST0bf[:, s2, :], pqb[:, 0:512], [("ps", pq_)], [("m_ST0bf", s2)])
                                S.op("pe", lambda e, b=b, s2=s2: e.matmul(ps[pyi][0:NS, 0:512], lhsT=Cblk[:, b, :], rhs=ST0bf[:, s2, :], start=(b == 0), stop=(b == NB - 1)),
                                     reads=["m_Cblk", ("m_ST0bf", s2)], writes=[("ps", pyi)])
                                S.op("dve", lambda e, b=b: e.tensor_scalar(out=yn[0:NS, :], in0=xw[0:NS, :], scalar1=self.maskB[:, b:b + 1], scalar2=None, op0=ALU.mult), reads=["m_xw", "consts"], writes=["m_yn"])
                                pS = self.nextps()
                                for q_ in range(4):
                                    S.op("pe", lambda e, q_=q_, pS=pS: e.matmul(ps[pS][:, q_ * 128:(q_ + 1) * 128], lhsT=yn[0:NS, q_ * 128:(q_ + 1) * 128], rhs=Btok[0:NS, :], start=True, stop=True), reads=["m_Btok", "m_yn"], writes=[("ps", pS)])
                                S.op("dve", lambda e, b=b, slot=slot: e.tensor_tensor(out=slot, in0=slot, in1=etn[:, b, :].unsqueeze(2).broadcast_to([128, 4, 128]), op=ALU.mult), reads=[("m_snat", sl), "m_etn"], writes=[("m_snat", sl)])
                                S.op("dve", lambda e, slot=slot, pS=pS: e.tensor_tensor(out=slot, in0=slot, in1=ps[pS][:, 0:512].rearrange("p (q n) -> p q n", q=4), op=ALU.add), reads=[("m_snat", sl), ("ps", pS)], writes=[("m_snat", sl)])
                                S.op("sp", lambda e, b=b, slot=slot: e.dma_start(out=ssm_s[b, 8 * g:8 * g + 8].rearrange("h p n -> (h p) n").rearrange("(q r) n -> r q n", r=128), in_=slot), reads=[("m_snat", sl)], dma="m_ssm_s%d" % sl)
                        ecb = ex[0:R, 0:8].unsqueeze(2).broadcast_to([R, 8, 64])
                        S.op("dve", lambda e: e.tensor_tensor(out=v3(y1[0:R, :]), in0=v3(ps[pyi][0:R, 0:512]), in1=ecb, op=ALU.mult), reads=[("ps", pyi), "m_ex"], writes=["m_y1"])
                        S.op("dve", lambda e: e.tensor_tensor(out=y1[0:R, :], in0=y1[0:R, :], in1=ps[py][0:R, 0:512], op=ALU.add), reads=[("ps", py), "m_y1"], writes=["m_y1"])
                        S.op("dve", lambda e: e.tensor_tensor(out=v3(xd[0:R, :]), in0=v3(xtok[0:R, :]), in1=self.mb_D[0:R, 8 * g:8 * g + 8].unsqueeze(2).broadcast_to([R, 8, 64]), op=ALU.mult), reads=["m_xtok", "consts"], writes=["m_xd"])
                        S.op("dve", lambda e: e.tensor_tensor(out=y1[0:R, :], in0=y1[0:R, :], in1=xd[0:R, :], op=ALU.add), reads=["m_xd", "m_y1"], writes=["m_y1"])
                        S.op("dve", lambda e: e.tensor_tensor(out=y1[0:R, :], in0=y1[0:R, :], in1=zs[0:R, :], op=ALU.mult), reads=["m_zs", "m_y1"], writes=["m_y1"])
                        S.op("act", lambda e: e.activation(out=xd[0:R, :], in_=y1[0:R, :], func=AF.Square, accum_out=ssq[0:R, 0:1]), reads=["m_y1", "m_xd"], writes=["m_xd", "m_ssq"])
                        S.op("act", lambda e: e.activation(out=ssq[0:R, 1:2], in_=ssq[0:R, 0:1], func=AF.Ln, scale=1.0 / 512, bias=EPS), reads=["m_ssq"], writes=["m_ssq"])
                        S.op("act", lambda e: e.activation(out=ssq[0:R, 1:2], in_=ssq[0:R, 1:2], func=AF.Exp, scale=-0.5), reads=["m_ssq"], writes=["m_ssq"])
                        S.op("dve", lambda e: e.scalar_tensor_tensor(out=yn[0:R, :], in0=y1[0:R, :], scalar=ssq[0:R, 1:2], in1=ngt[0:R, :], op0=ALU.mult, op1=ALU.mult),
                             reads=["m_y1", "m_ssq", "m_ng"], writes=["m_yn"])
                        pt2 = self.nextps()
                        pt2b = ps[pt2].bitcast(BF16)
                        for fc in range(4):
                            S.op("pe", lambda e, fc=fc: e.transpose(pt2b[:, fc * 128:fc * 128 + R], yn[0:R, fc * 128:(fc + 1) * 128], self.identb[0:R, 0:R]), reads=["m_yn", "consts2"], writes=[("ps", pt2)])
                        self.copy("dve", yTb[:, :, t0:t0 + R], pt2b[:, 0:512].rearrange("p (f t) -> p f t", f=4)[:, :, 0:R], [("ps", pt2)], [("m_yTb", ct)])
                        if MBSTOP == 9:
                            return
                        if not smp:
                            S.op("dve", lambda e: e.tensor_tensor(out=v3(xw[0:R, :]), in0=v3(xtok[0:R, :]), in1=wts[0:R, :].unsqueeze(2).broadcast_to([R, 8, 64]), op=ALU.mult), reads=["m_xtok", "m_wts"], writes=["m_xw"])
                            pS = self.nextps()
                            S.op("pe", lambda e: e.matmul(ps[pS][:, 0:512], lhsT=Btok, rhs=xw, start=True, stop=True), reads=["m_Btok", "m_xw"], writes=[("ps", pS)])
                            S.op("dve", lambda e: e.tensor_tensor(out=v3(ST), in0=v3(ST), in1=ex[:, 16:24].unsqueeze(2).broadcast_to([128, 8, 64]), op=ALU.mult), reads=["m_ST", "m_ex"], writes=["m_ST"])
                            S.op("dve", lambda e: e.tensor_tensor(out=ST, in0=ST, in1=ps[pS][:, 0:512], op=ALU.add), reads=["m_ST", ("ps", pS)], writes=["m_ST"])
                            S.op("pool", lambda e: e.tensor_copy(out=STbf, in_=ST), reads=["m_ST"], writes=["m_STbf"])
                    for dc in range(KC):
                        pi = self.nextps()
                        for fc in range(4):
                            S.op("pe", lambda e, pi=pi, dc=dc, fc=fc: e.matmul(ps[pi][:, 0:w], lhsT=wout[:, fc, dc * 128:(dc + 1) * 128], rhs=yTb[:, fc, 0:w], start=(fc == 0), stop=(fc == 3)),
                                 reads=[("m_wout", fc)] + [("m_yTb", c_) for c_ in range(4)], writes=[("ps", pi)])
                        S.op("dve", lambda e, pi=pi, dc=dc: e.tensor_tensor(out=xres[:, dc, c0:c0 + w], in0=xres[:, dc, c0:c0 + w], in1=ps[pi][:, 0:w], op=ALU.add),
                             reads=[("ps", pi), ("xres", dc, c0)], writes=[("xres", dc, c0)])
                    if (not smp) and c0 + w == TP:
                        pq2 = self.nextps()
                        for q_ in range(4):
                            S.op("pe", lambda e, q_=q_: e.transpose(ps[pq2][:, q_ * 128:(q_ + 1) * 128], ST[:, q_ * 128:(q_ + 1) * 128], self.ident), reads=["m_ST", "consts"], writes=[("ps", pq2)])
                        self.copy("act", stT.rearrange("p q n -> p (q n)"), ps[pq2][:, 0:512], [("ps", pq2)], [("m_stT", 0)])
                        S.op("sp", lambda e: e.dma_start(out=ssm_p[8 * g:8 * g + 8].rearrange("h p n -> (h p) n").rearrange("(q r) n -> r q n", r=128), in_=stT), reads=[("m_stT", 0)], dma="m_ssm_p")


def make_in_map(inp, core, TP, names):
    b0 = core * NB
    m = {}
    m["x_p"] = np.ascontiguousarray(inp["x_prompt"][core, :TP])
    m["x_s"] = np.ascontiguousarray(inp["x_sample"][b0:b0 + NB].reshape(NS, D))
    m["ident"] = np.eye(128, dtype=np.float32)
    m["norm_mix"] = np.ascontiguousarray(_fm(inp["norm_mix"]))
    m["norm_ffn"] = np.ascontiguousarray(_fm(inp["norm_ffn"]))
    m["norm_final"] = np.ascontiguousarray(_fm(inp["norm_final"]))
    for k in ("ffn_w_gate", "ffn_w_up", "ffn_w_down"):
        m[k] = inp[k]
    extra_in_map(m, inp, core, TP)
    return {k: np.ascontiguousarray(m[k], dtype=np.float32) for k in names}


def _masks():
    r = np.arange(128)
    maskU2 = ((r[:, None] // 64 == r[None, :] // 64) & (r[None, :] % 64 >= r[:, None] % 64)).astype(np.float32)
    r = np.arange(64)
    maskS = ((r[:, None] // TS == r[None, :] // TS) & (r[None, :] >= r[:, None])).astype(np.float32)
    maskB = (r[:, None] // TS == np.arange(NB)[None, :]).astype(np.float32)
    return maskU2, maskS, maskB


def extra_in_map(m, inp, core, TP):
    b0 = core * NB
    m["maskU2"], m["maskS"], m["maskB"] = _masks()
    m["hg_lb_logits"] = _fm(inp["hg_lb_logits"])
    m["hg_norm"] = np.ascontiguousarray(inp["hg_norm"].T)
    m["hg_w_in"] = inp["hg_w_in"]
    m["hg_w_out"] = inp["hg_w_out"]
    m["st_hg"] = inp["state_hgrn"][:, b0:b0 + NB]
    r = np.arange(128)
    m["maskU"] = (r[None, :] >= r[:, None]).astype(np.float32)
    m["sL"] = (r[:, None] > r[None, :]).astype(np.float32)
    r = np.arange(64)
    m["sLS"] = ((r[:, None] // TS == r[None, :] // TS) & (r[:, None] > r[None, :])).astype(np.float32)
    m["maskC"] = np.broadcast_to((r[None, :] // TS == np.arange(NB)[:, None]).astype(np.float32)[None], (128, NB, NS))
    r = np.arange(128)
    m["sU"] = (r[:, None] < r[None, :]).astype(np.float32)
    m["blk1"] = (r[:, None] // 64 == r[None, :] // 64).astype(np.float32)
    r = np.arange(64)
    m["sUS"] = ((r[:, None] // TS == r[None, :] // TS) & (r[:, None] < r[None, :])).astype(np.float32)
    m["rw_mu"] = _fm(inp["rw_mu"][0])
    for k in ("rw_w0", "rw_a0", "rw_k_k", "rw_k_a", "rw_lnx_w", "rw_lnx_b"):
        m[k] = _fm(inp[k][0])
    m["rw_r_k"] = _fm(inp["rw_r_k"][0].reshape(D))
    m["rw_w_rkv"] = inp["rw_w_rkv"][0]
    for k in ("rw_w1", "rw_w2", "rw_a1", "rw_a2", "rw_g1", "rw_g2", "rw_w_out"):
        m[k] = inp[k][0]
    m["st_wkv"] = inp["state_wkv"][0, b0:b0 + NB]
    m["st_shift"] = inp["state_shift"][0, b0:b0 + NB]
    m["mb_conv_w"] = np.ascontiguousarray(inp["mb_conv_w"][0].reshape(4, 24, 128).transpose(2, 1, 0))
    m["mb_conv_b"] = np.ascontiguousarray(inp["mb_conv_b"][0].reshape(24, 128).T)
    for k in ("mb_dt_bias", "mb_A_log", "mb_D"):
        m[k] = np.broadcast_to(inp[k][0][None, :], (128, 32))
    m["mb_norm"] = np.broadcast_to(inp["mb_norm"][0][None, :], (128, 2048))
    m["mb_w_in"] = inp["mb_w_in"]
    m["mb_w_out"] = inp["mb_w_out"]
    m["st_ssm"] = inp["state_ssm"][0, b0:b0 + NB]
    m["st_conv"] = inp["state_conv"][0, b0:b0 + NB]


_CACHE = {}


def run_cores(inp, TP, cores, layers=(0, 1, 2, 0), with_ffn=True):
    key = (TP, tuple(layers), with_ffn)
    bld = Builder(TP, layers, with_ffn)
    nc = bld.build()
    names = [k for k, v in bld.dram.items() if k in bld.in_names]
    in_maps = [make_in_map(inp, c, TP, names) for c in cores]
    res = run_bass_kernel_spmd(nc, in_maps, core_ids=list(range(len(cores))))
    return res.results


def kernel(**inputs):
    inp = {k: np.asarray(v) for k, v in inputs.items()}
    TP = inp["x_prompt"].shape[1]
    res = run_cores(inp, TP, list(range(NCORES)))
    st = lambda k: np.stack([r[k] for r in res], axis=0)
    cat = lambda k, ax=0: np.concatenate([r[k] for r in res], axis=ax)
    f = lambda a: np.ascontiguousarray(a, dtype=np.float32)
    y_p = st("y_p")
    y_s = cat("y_s").reshape(NCORES * NB, TS, D)
    hg_p = np.stack([r["hg_p"] for r in res], axis=1)
    hg_s = cat("hg_s", 1)
    return (f(y_p), f(y_s), f(hg_p), f(hg_s), f(st("wkv_p")[None]), f(cat("wkv_s")[None]),
            f(cat("sh_p")[None]), f(cat("sh_s")[None]), f(st("ssm_p")[None]), f(cat("ssm_s")[None]),
            f(st("cv_p")[None]), f(cat("cv_s")[None]))
```

```python
import numpy as np
from contextlib import ExitStack, contextmanager
import concourse.bass as bass
import concourse.mybir as mybir
from concourse.bass_utils import run_bass_kernel_spmd

F32 = mybir.dt.float32
BF16 = mybir.dt.bfloat16
AF = mybir.ActivationFunctionType
ALU = mybir.AluOpType
AX = mybir.AxisListType

D = 1024
KC = 8
DFF = 2816
FC = 22
NCORES = 8
NB = 16
TS = 4
NS = NB * TS
EPS = 1e-6
ROT = 30000
import os
MBSTOP = 0
RWSTOP = 0


class _Rec:
    def __init__(self):
        self.call = None

    def __getattr__(self, name):
        def f(*a, **k):
            self.call = (name, a, k)
            return self
        return f


class Sched:
    ENGS = ("pe", "act", "dve", "pool", "sp")

    def __init__(self, nc):
        self.nc = nc
        self.streams = {e: [] for e in self.ENGS}
        self.count = {e: 0 for e in self.ENGS}
        self.observed = {e: {} for e in self.ENGS}
        self.last_write = {}
        self.readers = {}
        self.dmacount = {}
        self.semkeys = []
        self.lastmark = {}

    def _semkey(self, sk):
        if sk not in self.lastmark:
            self.semkeys.append(sk)
        return sk

    def op(self, eng, fn, reads=(), writes=(), dma=None):
        rec = _Rec()
        fn(rec)
        fn = rec.call
        need = {}

        def add(m):
            if m is None:
                return
            sk, v = m
            if need.get(sk, 0) < v:
                need[sk] = v

        for k in reads:
            add(self.last_write.get(k))
            if isinstance(k, tuple) and k[0] == "ps":
                for m in self.readers.get(k, ()):
                    add(m)
        for k in writes:
            add(self.last_write.get(k))
            for m in self.readers.get(k, ()):
                add(m)
        st = self.streams[eng]
        obs = self.observed[eng]
        for sk, v in need.items():
            if eng == "pe" and sk[0] == "pe":
                continue
            if obs.get(sk, 0) >= v:
                continue
            st.append(("wait", sk, v))
            obs[sk] = v
        if dma is not None:
            sk = self._semkey(("dma", dma))
            self.dmacount[dma] = self.dmacount.get(dma, 0) + 16
            marker = (sk, self.dmacount[dma])
            amt = 16
        else:
            n = self.count[eng]
            sk = self._semkey((eng, n // ROT))
            marker = (sk, n % ROT + 1)
            self.count[eng] = n + 1
            amt = 1
        self.lastmark[sk] = marker[1]
        st.append(("op", fn, sk, amt))
        for k in writes:
            self.last_write[k] = marker
            self.readers[k] = []
        for k in reads:
            if k not in writes:
                self.readers.setdefault(k, []).append(marker)
        return marker

    def barrier(self):
        for e in self.ENGS:
            st = self.streams[e]
            obs = self.observed[e]
            for sk, v in self.lastmark.items():
                if obs.get(sk, 0) >= v:
                    continue
                st.append(("wait", sk, v))
                obs[sk] = v
        self.last_write = {}
        self.readers = {}

    def emit(self):
        nc = self.nc
        sems = {}
        for sk in self.semkeys:
            sems[sk] = nc.alloc_semaphore(name="s_" + "_".join(str(x) for x in sk))
        streams = self.streams
        engmap = {"pe": "tensor", "act": "scalar", "dve": "vector", "pool": "gpsimd", "sp": "sync"}

        def run(e, engine):
            for ent in streams[e]:
                if ent[0] == "wait":
                    engine.wait_ge(sems[ent[1]], ent[2])
                else:
                    name, a, k = ent[1]
                    ins = getattr(engine, name)(*a, **k)
                    ins.then_inc(sems[ent[2]], ent[3])

        with nc.Block() as block:
            for e in self.ENGS:
                if not streams[e]:
                    continue

                def mk(e=e):
                    def f(engine):
                        run(e, engine)
                    return f

                getattr(block, engmap[e])(mk())


def _fm(v):
    v = np.asarray(v, np.float32)
    lead = v.shape[:-1]
    v = v.reshape(lead + (KC, 128))
    v = np.moveaxis(v, -1, 0)
    return np.ascontiguousarray(v)


class Builder:
    def __init__(self, TP, layers=(0, 1, 2, 0), with_ffn=True):
        self.TP = TP
        self.N = TP + NS
        self.layers = layers
        self.with_ffn = with_ffn
        self.nc = bass.Bass("TRN2", target_bir_lowering=False)
        self.S = Sched(self.nc)
        self.dram = {}
        self.in_names = []
        self.NSTG = 3
        self.stg_i = 0
        self.ps_i = 0
        self.rr = 0
        self.tbs = [(i * 512, 512, "p") for i in range(TP // 512)] + [(TP, NS, "s")]

    def din(self, name, shape):
        t = self.nc.dram_tensor(name, list(shape), F32, kind="ExternalInput").ap()
        self.dram[name] = t
        self.in_names.append(name)
        return t

    def dout(self, name, shape):
        t = self.nc.dram_tensor(name, list(shape), F32, kind="ExternalOutput").ap()
        self.dram[name] = t
        return t

    @contextmanager
    def T(self, name, shape, dt=F32):
        self.uid = getattr(self, "uid", 0) + 1
        with self.nc.sbuf_tensor("%s_%d" % (name, self.uid), list(shape), dt) as t:
            yield t.ap()

    def sb(self, name, shape, dt=F32):
        return self.nc.alloc_sbuf_tensor(name, list(shape), dt).ap()

    def nextps(self):
        i = self.ps_i % 6
        self.ps_i += 1
        return i

    def nextlong(self):
        self.pl_i = getattr(self, "pl_i", 0) + 1
        return 6 + self.pl_i % 2

    def ew(self):
        self.rr += 1
        return "act" if self.rr % 2 else "dve"

    def copy(self, eng, out, in_, reads, writes):
        if eng == "act":
            self.S.op("act", lambda e: e.activation(out=out, in_=in_, func=AF.Copy), reads=reads, writes=writes)
        else:
            self.S.op(eng, lambda e: e.tensor_copy(out=out, in_=in_), reads=reads, writes=writes)

    def load_w(self, dst, src, shape, wkey, scale=None):
        S = self.S
        slot = self.stg_i % self.NSTG
        self.stg_i += 1
        rows = shape[0]
        free = int(np.prod(shape[1:]))
        assert free <= 1024
        st = self.stage[slot][0:rows, 0:free]
        if len(shape) == 3:
            st = st.rearrange("p (a b) -> p a b", a=shape[1])
        S.op("sp", lambda e: e.dma_start(out=st, in_=src), writes=[("stg", slot)], dma="stg%d" % slot)
        if scale is None:
            S.op("pool", lambda e: e.tensor_copy(out=dst, in_=st), reads=[("stg", slot)], writes=[wkey])
        else:
            S.op("pool", lambda e: e.tensor_scalar(out=dst, in0=st, scalar1=scale, scalar2=None, op0=ALU.mult),
                 reads=[("stg", slot), "consts"], writes=[wkey])

    def build(self):
        nc, S, TP, N = self.nc, self.S, self.TP, self.N
        nl = len(self.layers)
        x_p = self.din("x_p", [TP, D])
        x_s = self.din("x_s", [NS, D])
        ident_d = self.din("ident", [128, 128])
        nmix_d = self.din("norm_mix", [128, 4, KC])
        nffn_d = self.din("norm_ffn", [128, 4, KC])
        nfin_d = self.din("norm_final", [128, KC])
        wg_d = self.din("ffn_w_gate", [4, D, DFF])
        wu_d = self.din("ffn_w_up", [4, D, DFF])
        wd_d = self.din("ffn_w_down", [4, DFF, D])
        y_p = self.dout("y_p", [TP, D])
        y_s = self.dout("y_s", [NS, D])

        self.xres = self.sb("xres", [128, KC, N])
        self.hT = self.sb("hT", [128, KC, N + 2], BF16)
        self.stage = [self.sb("stage%d" % i, [128, 1024]) for i in range(self.NSTG)]
        self.ident = self.sb("ident_sb", [128, 128])
        self.identb = self.sb("identb_sb", [128, 128], BF16)
        self.onesb = self.sb("onesb", [128, 128], BF16)
        self.nmix = self.sb("nmix", [128, 4, KC])
        self.nffn = self.sb("nffn", [128, 4, KC])
        self.nfin = self.sb("nfin", [128, KC])
        self.ps = [nc.alloc_psum_tensor("ps%d" % i, [128, 512], F32).ap() for i in range(8)]
        xres, hT = self.xres, self.hT

        S.op("sp", lambda e: e.dma_start(out=self.ident, in_=ident_d), writes=["consts"], dma="c0")
        S.op("sp", lambda e: e.dma_start(out=self.nmix, in_=nmix_d), writes=["consts"], dma="c0")
        S.op("sp", lambda e: e.dma_start(out=self.nffn, in_=nffn_d), writes=["consts"], dma="c0")
        S.op("sp", lambda e: e.dma_start(out=self.nfin, in_=nfin_d), writes=["consts"], dma="c0")
        S.op("pool", lambda e: e.tensor_copy(out=self.identb, in_=self.ident), reads=["consts"], writes=["consts2"])
        S.op("pool", lambda e: e.memset(self.onesb, 1.0), writes=["consts2"])
        self.onesf = self.sb("onesf", [128, 128])
        S.op("pool", lambda e: e.memset(self.onesf, 1.0), writes=["consts2"])
        S.op("pool", lambda e: e.memset(hT[:, :, 0:2], 0.0), writes=["hT0"])
        self.extra_consts()
        S.barrier()

        with self.T("xin0", [128, D], F32) as xin0, self.T("xin1", [128, D], F32) as xin1:
            xins = [xin0, xin1]
            ntile = TP // 128 + 1
            for j in range(ntile):
                xin = xins[j % 2]
                rows = 128 if j < TP // 128 else NS
                src = x_p[j * 128:(j + 1) * 128, :] if j < TP // 128 else x_s
                S.op("sp", lambda e, xin=xin, rows=rows, src=src: e.dma_start(out=xin[0:rows, :], in_=src),
                     writes=[("xin", j % 2)], dma="xin%d" % (j % 2))
                for half in range(2):
                    pi = self.nextps()
                    for q in range(4):
                        kc = half * 4 + q
                        S.op("pe", lambda e, pi=pi, q=q, kc=kc, xin=xin, rows=rows: e.transpose(
                            self.ps[pi][:, q * 128:q * 128 + rows], xin[0:rows, kc * 128:(kc + 1) * 128],
                            self.ident[0:rows, 0:rows]),
                            reads=[("xin", j % 2), "consts"], writes=[("ps", pi)])
                    dst = xres[:, half * 4:(half + 1) * 4, j * 128:j * 128 + rows]
                    srcp = self.ps[pi].rearrange("p (q t) -> p q t", q=4)[:, :, 0:rows]
                    self.copy(self.ew(), dst, srcp, [("ps", pi)], [("xres", j)])
        S.barrier()

        for li, kind in enumerate(self.layers):
            self.cur_li = li
            self.rmsnorm(self.nmix[:, li, :], to_h=True)
            S.barrier()
            if kind == 0:
                self.hgrn2(li // 3)
            elif kind == 1:
                self.rwkv7(li // 3)
            elif kind == 2:
                self.mamba2(li // 3)
            S.barrier()
            if self.with_ffn:
                self.rmsnorm(self.nffn[:, li, :], to_h=True)
                S.barrier()
                self.ffn(li, wg_d, wu_d, wd_d)
                S.barrier()
        self.final_norm(y_p, y_s)
        S.barrier()
        S.emit()
        return nc

    def extra_consts(self):
        nc, S = self.nc, self.S
        def cload(name, shape):
            d = self.din(name, shape)
            t = self.sb("c_" + name, shape)
            S.op("sp", lambda e: e.dma_start(out=t, in_=d), writes=["consts"], dma="c0")
            return t
        self.maskU2 = cload("maskU2", [128, 128])
        self.maskS = cload("maskS", [64, 64])
        self.maskB = cload("maskB", [64, 16])
        lg = cload("hg_lb_logits", [128, 2, KC])
        self.hgn = cload("hg_norm", [128, 2])
        self.hg_lb = self.sb("hg_lb", [128, 2, KC])
        self.hg_oml = self.sb("hg_oml", [128, 2, KC])
        self.hg_noml = self.sb("hg_noml", [128, 2, KC])
        lb, oml, noml = self.hg_lb, self.hg_oml, self.hg_noml
        S.op("dve", lambda e: e.memset(lb[:, 0, :], 0.0), writes=["hgc0"])
        S.op("dve", lambda e: e.tensor_tensor(out=lb[:, 1, :], in0=lg[:, 1, :], in1=lg[:, 0, :], op=ALU.subtract), reads=["consts"], writes=["hgc1"])
        S.op("act", lambda e: e.activation(out=lb[:, 1, :], in_=lb[:, 1, :], func=AF.Sigmoid), reads=["hgc1"], writes=["hgc1"])
        S.op("dve", lambda e: e.tensor_scalar(out=oml, in0=lb, scalar1=-1.0, scalar2=1.0, op0=ALU.mult, op1=ALU.add), reads=["hgc0", "hgc1"], writes=["hgc2"])
        S.op("dve", lambda e: e.tensor_scalar(out=noml, in0=lb, scalar1=1.0, scalar2=-1.0, op0=ALU.mult, op1=ALU.add), reads=["hgc0", "hgc1"], writes=["hgc3"])
        self.maskU = cload("maskU", [128, 128])
        self.sL = cload("sL", [128, 128])
        self.sLS = cload("sLS", [64, 64])
        self.maskC = cload("maskC", [128, NB, NS])
        self.mb_cw = cload("mb_conv_w", [128, 24, 4])
        self.mb_cb = cload("mb_conv_b", [128, 24])
        self.mb_dtb = cload("mb_dt_bias", [128, 32])
        alog = cload("mb_A_log", [128, 32])
        self.mb_D = cload("mb_D", [128, 32])
        self.din("mb_norm", [128, 2048])
        self.mb_negA = self.sb("mb_negA", [128, 32])
        S.op("act", lambda e: e.activation(out=self.mb_negA, in_=alog, func=AF.Exp), reads=["consts"], writes=["mbc0"])
        S.op("dve", lambda e: e.tensor_scalar(out=self.mb_negA, in0=self.mb_negA, scalar1=-1.0, scalar2=None, op0=ALU.mult), reads=["mbc0"], writes=["mbc0"])
        self.din("mb_w_in", [1, D, 5152])
        self.din("mb_w_out", [1, 2048, D])
        self.din("st_ssm", [NB, 32, 64, 128])
        self.din("st_conv", [NB, 3, 3072])
        self.dout("ssm_p", [32, 64, 128])
        self.dout("ssm_s", [NB, 32, 64, 128])
        self.dout("cv_p", [3, 3072])
        self.dout("cv_s", [NB, 3, 3072])
        self.sU = cload("sU", [128, 128])
        self.sUS = cload("sUS", [64, 64])
        self.blk1 = cload("blk1", [128, 128])
        self.rw_mu = cload("rw_mu", [128, 6, KC])
        self.rw_omu = self.sb("rw_omu", [128, 6, KC])
        S.op("dve", lambda e: e.tensor_scalar(out=self.rw_omu, in0=self.rw_mu, scalar1=-1.0, scalar2=1.0, op0=ALU.mult, op1=ALU.add), reads=["consts"], writes=["rwc0"])
        self.rw_vec = {}
        for nm in ("rw_w0", "rw_a0", "rw_k_k", "rw_k_a", "rw_r_k", "rw_lnx_w", "rw_lnx_b"):
            self.rw_vec[nm] = cload(nm, [128, KC])
        self.rw_omka = self.sb("rw_omka", [128, KC])
        S.op("dve", lambda e: e.tensor_scalar(out=self.rw_omka, in0=self.rw_vec["rw_k_a"], scalar1=-1.0, scalar2=1.0, op0=ALU.mult, op1=ALU.add), reads=["consts"], writes=["rwc1"])
        for nm, shp in (("rw_w_rkv", [3, D, D]), ("rw_w1", [D, 64]), ("rw_w2", [64, D]), ("rw_a1", [D, 64]), ("rw_a2", [64, D]),
                        ("rw_g1", [D, 128]), ("rw_g2", [128, D]), ("rw_w_out", [D, D]), ("st_wkv", [NB, 16, 64, 64]), ("st_shift", [NB, D])):
            self.din(nm, shp)
        self.dout("wkv_p", [16, 64, 64])
        self.dout("wkv_s", [NB, 16, 64, 64])
        self.dout("sh_p", [1, D])
        self.dout("sh_s", [NB, D])
        self.din("hg_w_in", [2, D, 4 * D])
        self.din("hg_w_out", [2, D, D])
        self.din("st_hg", [2, NB, 8, 128, 128])
        self.dout("hg_p", [2, 8, 128, 128])
        self.dout("hg_s", [2, NB, 8, 128, 128])

    def rmsnorm(self, gain, to_h=True, out_f32=None):
        nc, S = self.nc, self.S
        xres, hT = self.xres, self.hT
        with self.T("n_sq", [128, 2, 512], BF16) as sq, self.T("n_r", [128, 2, 512], F32) as rr:
            for bi, (c0, w, kind) in enumerate(self.tbs):
                pi = self.nextps()
                for kc in range(KC):
                    s = kc % 2
                    S.op("act", lambda e, s=s, kc=kc, c0=c0, w=w: e.activation(out=sq[:, s, 0:w], in_=xres[:, kc, c0:c0 + w], func=AF.Square),
                         reads=[("xres", "all")], writes=[("n_sq", s)])
                    S.op("pe", lambda e, pi=pi, s=s, kc=kc, w=w: e.matmul(self.ps[pi][:, 0:w], lhsT=self.onesb, rhs=sq[:, s, 0:w], start=(kc == 0), stop=(kc == KC - 1)),
                         reads=[("n_sq", s), "consts2"], writes=[("ps", pi)])
                r = rr[:, bi % 2, 0:w]
                S.op("act", lambda e, pi=pi, r=r, w=w: e.activation(out=r, in_=self.ps[pi][:, 0:w], func=AF.Ln, scale=1.0 / D, bias=EPS),
                     reads=[("ps", pi)], writes=[("n_r", bi % 2)])
                S.op("act", lambda e, r=r: e.activation(out=r, in_=r, func=AF.Exp, scale=-0.5),
                     reads=[("n_r", bi % 2)], writes=[("n_r", bi % 2)])
                for kc in range(KC):
                    if out_f32 is None:
                        dst = hT[:, kc, 2 + c0:2 + c0 + w]
                    else:
                        dst = out_f32(kc, c0, w)
                    S.op("dve", lambda e, dst=dst, kc=kc, c0=c0, w=w, r=r: e.scalar_tensor_tensor(
                        out=dst, in0=xres[:, kc, c0:c0 + w], scalar=gain[:, kc:kc + 1], in1=r, op0=ALU.mult, op1=ALU.mult),
                        reads=[("xres", "all"), ("n_r", bi % 2), "consts"], writes=[("hT", kc, bi)])

    def ffn(self, li, wg_d, wu_d, wd_d):
        nc, S = self.nc, self.S
        xres, hT = self.xres, self.hT
        nprompt = len(self.tbs) - 1
        half = max(1, nprompt // 2)
        sbs = [self.tbs[:half], self.tbs[half:]] if nprompt >= 2 else [self.tbs]
        maxw = max(sum(w for (_, w, _) in sb_) for sb_ in sbs)
        with self.T("f_act", [128, FC, maxw], BF16) as act, \
                self.T("f_wgu", [128, 2, KC, 2, 256], BF16) as wgu, \
                self.T("f_wd", [128, 2, FC, 128], BF16) as wd, \
                self.T("f_sg", [128, 2, 512], F32) as sg:
            for sbi, sb_ in enumerate(sbs):
                base = sb_[0][0]
                for fp in range(FC // 2):
                    slot = fp % 2
                    for gi, wsrc in enumerate((wg_d, wu_d)):
                        for kk in range(0, KC, 4):
                            src = wsrc[li, kk * 128:(kk + 4) * 128, fp * 256:(fp + 1) * 256].rearrange("(k p) c -> p k c", p=128)
                            self.load_w(wgu[:, slot, kk:kk + 4, gi, :], src, [128, 4, 256], ("f_wgu", slot, gi, kk))
                    for fi in range(2):
                        f = fp * 2 + fi
                        for (c0, w, kind) in sb_:
                            pg, pu = self.nextps(), self.nextps()
                            for gi, pi in ((0, pg), (1, pu)):
                                for kc in range(KC):
                                    S.op("pe", lambda e, pi=pi, slot=slot, kc=kc, gi=gi, fi=fi, c0=c0, w=w: e.matmul(
                                        self.ps[pi][:, 0:w], lhsT=wgu[:, slot, kc, gi, fi * 128:(fi + 1) * 128],
                                        rhs=hT[:, kc, 2 + c0:2 + c0 + w], start=(kc == 0), stop=(kc == KC - 1)),
                                        reads=[("f_wgu", slot, gi, 0), ("f_wgu", slot, gi, 4), ("hT", "all")], writes=[("ps", pi)])
                            ss = self.rr % 2
                            self.rr += 1
                            S.op("act", lambda e, pg=pg, ss=ss, w=w: e.activation(out=sg[:, ss, 0:w], in_=self.ps[pg][:, 0:w], func=AF.Silu),
                                 reads=[("ps", pg)], writes=[("f_sg", ss)])
                            S.op("dve", lambda e, pu=pu, ss=ss, f=f, c0=c0, w=w: e.tensor_tensor(
                                out=act[:, f, c0 - base:c0 - base + w], in0=sg[:, ss, 0:w], in1=self.ps[pu][:, 0:w], op=ALU.mult),
                                reads=[("ps", pu), ("f_sg", ss)], writes=[("f_act", f)])
                for dc in range(KC):
                    slot = dc % 2
                    for f0 in range(0, FC, 8):
                        nf = min(8, FC - f0)
                        src = wd_d[li, f0 * 128:(f0 + nf) * 128, dc * 128:(dc + 1) * 128].rearrange("(k p) c -> p k c", p=128)
                        self.load_w(wd[:, slot, f0:f0 + nf, :], src, [128, nf, 128], ("f_wd", slot, f0))
                    for (c0, w, kind) in sb_:
                        pi = self.nextps()
                        for f in range(FC):
                            S.op("pe", lambda e, pi=pi, slot=slot, f=f, c0=c0, w=w: e.matmul(
                                self.ps[pi][:, 0:w], lhsT=wd[:, slot, f, :],
                                rhs=act[:, f, c0 - base:c0 - base + w], start=(f == 0), stop=(f == FC - 1)),
                                reads=[("f_wd", slot, (f // 8) * 8), ("f_act", f)], writes=[("ps", pi)])
                        S.op("dve", lambda e, pi=pi, dc=dc, c0=c0, w=w: e.tensor_tensor(
                            out=xres[:, dc, c0:c0 + w], in0=xres[:, dc, c0:c0 + w], in1=self.ps[pi][:, 0:w], op=ALU.add),
                            reads=[("ps", pi), ("xres", dc, c0)], writes=[("xres", dc, c0)])

    def final_norm(self, y_p, y_s):
        nc, S, TP = self.nc, self.S, self.TP
        xres = self.xres
        gain = self.nfin
        with self.T("fn_y", [128, KC, 512], F32) as yT, self.T("fn_o", [128, 2, D], F32) as yo, \
                self.T("fn_sq", [128, 2, 512], BF16) as sq, self.T("fn_r", [128, 512], F32) as rr:
            cnt = 0
            for bi, (c0, w, kind) in enumerate(self.tbs):
                pi = self.nextps()
                for kc in range(KC):
                    s = kc % 2
                    S.op("act", lambda e, s=s, kc=kc, c0=c0, w=w: e.activation(out=sq[:, s, 0:w], in_=xres[:, kc, c0:c0 + w], func=AF.Square),
                         reads=[("xres", "all")], writes=[("fn_sq", s)])
                    S.op("pe", lambda e, s=s, kc=kc, pi=pi, w=w: e.matmul(self.ps[pi][:, 0:w], lhsT=self.onesb, rhs=sq[:, s, 0:w], start=(kc == 0), stop=(kc == KC - 1)),
                         reads=[("fn_sq", s), "consts2"], writes=[("ps", pi)])
                r = rr[:, 0:w]
                S.op("act", lambda e, pi=pi, r=r, w=w: e.activation(out=r, in_=self.ps[pi][:, 0:w], func=AF.Ln, scale=1.0 / D, bias=EPS),
                     reads=[("ps", pi)], writes=["fn_r"])
                S.op("act", lambda e, r=r: e.activation(out=r, in_=r, func=AF.Exp, scale=-0.5), reads=["fn_r"], writes=["fn_r"])
                for kc in range(KC):
                    S.op("dve", lambda e, kc=kc, c0=c0, w=w, r=r: e.scalar_tensor_tensor(
                        out=yT[:, kc, 0:w], in0=xres[:, kc, c0:c0 + w], scalar=gain[:, kc:kc + 1], in1=r, op0=ALU.mult, op1=ALU.mult),
                        reads=[("xres", "all"), "fn_r", "consts"], writes=[("fn_y", kc)])
                for j in range((w + 127) // 128):
                    rows = min(128, w - j * 128)
                    os_ = cnt % 2
                    cnt += 1
                    for half in range(2):
                        pi = self.nextps()
                        for q in range(4):
                            kc = half * 4 + q
                            S.op("pe", lambda e, pi=pi, q=q, kc=kc, j=j, rows=rows: e.transpose(
                                self.ps[pi][0:rows, q * 128:(q + 1) * 128], yT[:, kc, j * 128:j * 128 + rows], self.ident),
                                reads=[("fn_y", kc), "consts"], writes=[("ps", pi)])
                        self.copy(self.ew(), yo[0:rows, os_, half * 512:(half + 1) * 512], self.ps[pi][0:rows, :], [("ps", pi)], [("fn_o", os_, half)])
                    if kind == "p":
                        dst = y_p[c0 + j * 128:c0 + j * 128 + rows, :]
                    else:
                        dst = y_s
                    S.op("sp", lambda e, dst=dst, os_=os_, rows=rows: e.dma_start(out=dst, in_=yo[0:rows, os_, :]),
                         reads=[("fn_o", os_, 0), ("fn_o", os_, 1)], writes=[], dma="yout%d" % os_)

    def hgrn2(self, j):
        nc, S, TP = self.nc, self.S, self.TP
        xres, hT = self.xres, self.hT
        w_in, w_out = self.dram["hg_w_in"], self.dram["hg_w_out"]
        st_hg, hg_p, hg_s = self.dram["st_hg"], self.dram["hg_p"], self.dram["hg_s"]
        ps = self.ps
        W = 512
        with ExitStack() as es:
            win = es.enter_context(self.T("h_win", [128, 2, KC, 4, 128], BF16))
            wout = es.enter_context(self.T("h_wout", [128, 2, D], BF16))
            sig = es.enter_context(self.T("h_sig", [128, W]))
            lf = es.enter_context(self.T("h_lf", [128, W]))
            kk = es.enter_context(self.T("h_kk", [128, W]))
            q = es.enter_context(self.T("h_q", [128, W]))
            gate = es.enter_context(self.T("h_gate", [128, W]))
            g = es.enter_context(self.T("h_g", [128, W]))
            tmp = es.enter_context(self.T("h_tmp", [128, W]))
            tmp2 = es.enter_context(self.T("h_tmp2", [128, W]))
            ee = es.enter_context(self.T("h_e", [128, 4, W]))
            qg = es.enter_context(self.T("h_qg", [128, W], BF16))
            kg = es.enter_context(self.T("h_kg", [128, W], BF16))
            qG = es.enter_context(self.T("h_qG", [128, W], BF16))
            kdec = es.enter_context(self.T("h_kdec", [128, W], BF16))
            vtok = es.enter_context(self.T("h_vtok", [128, 4, 128], BF16))
            vT = es.enter_context(self.T("h_vT", [128, W], BF16))
            kdtok = es.enter_context(self.T("h_kdtok", [128, 4, 128], BF16))
            osb = es.enter_context(self.T("h_osb", [128, W]))
            osq = es.enter_context(self.T("h_osq", [128, W], BF16))
            rstd = es.enter_context(self.T("h_rstd", [128, W]))
            og = es.enter_context(self.T("h_og", [128, W], BF16))
            Sr = es.enter_context(self.T("h_Sr", [128, 9, 128]))
            qGf = es.enter_context(self.T("h_qGf", [128, W]))
            attm = es.enter_context(self.T("h_attm", [128, 4, 128], BF16))
            egl = es.enter_context(self.T("h_egl", [128, 16]))
            S0 = es.enter_context(self.T("h_S0", [128, NB, 128]))
            S0bf = es.enter_context(self.T("h_S0bf", [128, NB, 128], BF16))
            Vblk = es.enter_context(self.T("h_Vblk", [64, NB, 128], BF16))
            for h in range(8):
                slot = h % 2
                for p in range(4):
                    for kk0 in (0, 4):
                        src = w_in[j, kk0 * 128:(kk0 + 4) * 128, p * D + h * 128:p * D + (h + 1) * 128].rearrange("(k p) c -> p k c", p=128)
                        self.load_w(win[:, slot, kk0:kk0 + 4, p, :], src, [128, 4, 128], ("h_win", slot, p, kk0))
                self.load_w(wout[:, slot, :], w_out[j, h * 128:(h + 1) * 128, :], [128, D], ("h_wout", slot))
                wkeys = lambda p: [("h_win", slot, p, 0), ("h_win", slot, p, 4)]
                S.op("sp", lambda e: e.dma_start(out=S0, in_=st_hg[j, :, h, :, :].rearrange("b k v -> k b v")), writes=["h_S0"], dma="h_S0")
                S.op("pool", lambda e: e.tensor_copy(out=S0bf, in_=S0), reads=["h_S0"], writes=["h_S0bf"])
                S.op("pool", lambda e: e.memset(Sr[:, 0, :], 0.0), writes=[("h_Sr", 0)])
                lbh, omlh, nomlh = self.hg_lb[:, j, h:h + 1], self.hg_oml[:, j, h:h + 1], self.hg_noml[:, j, h:h + 1]
                for (c0, w, kind) in self.tbs:
                    smp = kind == "s"
                    hc = 2 + c0
                    pq, pf, pg, pv = self.nextps(), self.nextps(), self.nextps(), self.nextps()
                    for p, pi in ((0, pq), (1, pf), (3, pg)):
                        for kc in range(KC):
                            S.op("pe", lambda e, pi=pi, p=p, kc=kc, hc=hc, w=w: e.matmul(ps[pi][:, 0:w], lhsT=win[:, slot, kc, p, :], rhs=hT[:, kc, hc:hc + w],
                                 start=(kc == 0), stop=(kc == KC - 1)), reads=wkeys(p) + [("hT", "all")], writes=[("ps", pi)])
                    ntile = (w + 127) // 128
                    rows = min(128, w)
                    for kc in range(KC):
                        S.op("pe", lambda e, kc=kc, hc=hc, w=w: e.matmul(ps[pv][:, 0:w], lhsT=win[:, slot, kc, 2, :], rhs=hT[:, kc, hc:hc + w],
                             start=(kc == 0), stop=(kc == KC - 1)), reads=wkeys(2) + [("hT", "all")], writes=[("ps", pv)])
                    self.copy("act", vT[:, 0:w], ps[pv][:, 0:w], [("ps", pv)], ["h_vT"])
                    pvt = self.nextps()
                    pvtb = ps[pvt].bitcast(BF16)
                    for jt in range(ntile):
                        S.op("pe", lambda e, jt=jt, rows=rows: e.transpose(pvtb[0:rows, jt * 128:(jt + 1) * 128], vT[:, jt * 128:jt * 128 + rows], self.identb),
                             reads=["h_vT", "consts2"], writes=[("ps", pvt)])
                    self.copy("dve", vtok[0:rows, 0:ntile, :], pvtb[:, 0:512].rearrange("p (a b) -> p a b", a=4)[0:rows, 0:ntile, :], [("ps", pvt)], ["h_vtok"])
                    S.op("act", lambda e, w=w: e.activation(out=sig[:, 0:w], in_=ps[pf][:, 0:w], func=AF.Sigmoid), reads=[("ps", pf)], writes=["h_sig"])
                    S.op("act", lambda e, w=w: e.activation(out=q[:, 0:w], in_=ps[pq][:, 0:w], func=AF.Silu), reads=[("ps", pq)], writes=["h_q"])
                    S.op("act", lambda e, w=w: e.activation(out=gate[:, 0:w], in_=ps[pg][:, 0:w], func=AF.Silu), reads=[("ps", pg)], writes=["h_gate"])
                    S.op("dve", lambda e, w=w: e.tensor_scalar(out=lf[:, 0:w], in0=sig[:, 0:w], scalar1=omlh, scalar2=lbh, op0=ALU.mult, op1=ALU.add), reads=["h_sig"], writes=["h_lf"])
                    S.op("act", lambda e, w=w: e.activation(out=lf[:, 0:w], in_=lf[:, 0:w], func=AF.Ln), reads=["h_lf"], writes=["h_lf"])
                    S.op("dve", lambda e, w=w: e.tensor_scalar(out=kk[:, 0:w], in0=sig[:, 0:w], scalar1=nomlh, scalar2=omlh, op0=ALU.mult, op1=ALU.add), reads=["h_sig"], writes=["h_kk"])
                    if not smp:
                        nch = w // 64
                        for c in range(nch):
                            S.op("dve", lambda e, c=c: e.tensor_tensor_scan(out=g[:, c * 64:(c + 1) * 64], data0=self.onesf[:, 0:64], data1=lf[:, c * 64:(c + 1) * 64],
                                 initial=0.0, op0=ALU.mult, op1=ALU.add), reads=["h_lf", "consts2"], writes=["h_g"])
                        g3 = g.rearrange("p (c t) -> p c t", t=64)
                        bc = lambda col: g3[:, :, col:col + 1].broadcast_to([128, nch, 64])
                        v3 = lambda t_: t_.rearrange("p (c t) -> p c t", t=64)
                        S.op("dve", lambda e: e.tensor_tensor(out=v3(tmp), in0=g3, in1=bc(31), op=ALU.subtract), reads=["h_g"], writes=["h_tmp"])
                        S.op("dve", lambda e: e.tensor_tensor(out=v3(tmp2), in0=bc(63), in1=g3, op=ALU.subtract), reads=["h_g"], writes=["h_tmp2"])
                        S.op("act", lambda e: e.activation(out=ee[:, 0, :], in_=tmp, func=AF.Exp), reads=["h_tmp"], writes=[("h_e", 0)])
                        S.op("act", lambda e: e.activation(out=ee[:, 1, :], in_=tmp, func=AF.Exp, scale=-1.0), reads=["h_tmp"], writes=[("h_e", 1)])
                        S.op("act", lambda e: e.activation(out=ee[:, 2, :], in_=g, func=AF.Exp), reads=["h_g"], writes=[("h_e", 2)])
                        S.op("act", lambda e: e.activation(out=ee[:, 3, :], in_=tmp2, func=AF.Exp), reads=["h_tmp2"], writes=[("h_e", 3)])
                        S.op("act", lambda e: e.activation(out=egl[:, 0:nch], in_=g3[:, :, 63], func=AF.Exp), reads=["h_g"], writes=["h_egl"])
                        S.op("pool", lambda e: e.tensor_tensor(out=qg, in0=q, in1=ee[:, 0, :], op=ALU.mult), reads=["h_q", ("h_e", 0)], writes=["h_qg"])
                        S.op("pool", lambda e: e.tensor_tensor(out=kg, in0=kk, in1=ee[:, 1, :], op=ALU.mult), reads=["h_kk", ("h_e", 1)], writes=["h_kg"])
                        S.op("dve", lambda e: e.tensor_tensor(out=qGf, in0=q, in1=ee[:, 2, :], op=ALU.mult), reads=["h_q", ("h_e", 2)], writes=["h_qGf"])
                        S.op("pool", lambda e: e.tensor_tensor(out=kdec, in0=kk, in1=ee[:, 3, :], op=ALU.mult), reads=["h_kk", ("h_e", 3)], writes=["h_kdec"])
                    else:
                        g3 = g[:, 0:NS].rearrange("p (b t) -> p b t", t=TS)
                        lf3 = lf[:, 0:NS].rearrange("p (b t) -> p b t", t=TS)
                        S.op("dve", lambda e: e.tensor_copy(out=g3[:, :, 0:1], in_=lf3[:, :, 0:1]), reads=["h_lf"], writes=["h_g"])
                        for t_ in range(1, TS):
                            S.op("dve", lambda e, t_=t_: e.tensor_tensor(out=g3[:, :, t_:t_ + 1], in0=g3[:, :, t_ - 1:t_], in1=lf3[:, :, t_:t_ + 1], op=ALU.add), reads=["h_lf", "h_g"], writes=["h_g"])
                        t23 = tmp2[:, 0:NS].rearrange("p (b t) -> p b t", t=TS)
                        S.op("dve", lambda e: e.tensor_tensor(out=t23, in0=g3[:, :, 3:4].broadcast_to([128, NB, TS]), in1=g3, op=ALU.subtract), reads=["h_g"], writes=["h_tmp2"])
                        S.op("act", lambda e: e.activation(out=ee[:, 1, 0:NS], in_=g[:, 0:NS], func=AF.Exp, scale=-1.0), reads=["h_g"], writes=[("h_e", 1)])
                        S.op("act", lambda e: e.activation(out=ee[:, 2, 0:NS], in_=g[:, 0:NS], func=AF.Exp), reads=["h_g"], writes=[("h_e", 2)])
                        S.op("act", lambda e: e.activation(out=ee[:, 3, 0:NS], in_=tmp2[:, 0:NS], func=AF.Exp), reads=["h_tmp2"], writes=[("h_e", 3)])
                        S.op("act", lambda e: e.activation(out=egl[:, 0:NB], in_=g3[:, :, 3], func=AF.Exp), reads=["h_g"], writes=["h_egl"])
                        S.op("pool", lambda e: e.tensor_tensor(out=kg[:, 0:NS], in0=kk[:, 0:NS], in1=ee[:, 1, 0:NS], op=ALU.mult), reads=["h_kk", ("h_e", 1)], writes=["h_kg"])
                        S.op("dve", lambda e: e.tensor_tensor(out=qG[:, 0:NS], in0=q[:, 0:NS], in1=ee[:, 2, 0:NS], op=ALU.mult), reads=["h_q", ("h_e", 2)], writes=["h_qG"])
                        S.op("pool", lambda e: e.tensor_tensor(out=kdec[:, 0:NS], in0=kk[:, 0:NS], in1=ee[:, 3, 0:NS], op=ALU.mult), reads=["h_kk", ("h_e", 3)], writes=["h_kdec"])
                    pt = self.nextps()
                    ptb = ps[pt].bitcast(BF16)
                    for jt in range(ntile):
                        S.op("pe", lambda e, jt=jt, rows=rows: e.transpose(ptb[0:rows, jt * 128:(jt + 1) * 128], kdec[:, jt * 128:jt * 128 + rows], self.identb),
                             reads=["h_kdec", "consts2"], writes=[("ps", pt)])
                    self.copy("dve", kdtok[0:rows, 0:ntile, :], ptb[:, 0:512].rearrange("p (a b) -> p a b", a=4)[0:rows, 0:ntile, :], [("ps", pt)], ["h_kdtok"])
                    po = self.nextlong()
                    if not smp:
                        nchk = w // 64
                        if c0 > 0:
                            S.op("dve", lambda e: e.tensor_copy(out=Sr[:, 0, :], in_=Sr[:, 8, :]), reads=[("h_Sr", c_) for c_ in range(9)], writes=[("h_Sr", 0)])
                        pa = self.nextps()
                        for jt in range(ntile):
                            S.op("pe", lambda e, jt=jt: e.matmul(ps[pa][:, jt * 128:(jt + 1) * 128], lhsT=kg[:, jt * 128:(jt + 1) * 128], rhs=qg[:, jt * 128:(jt + 1) * 128], start=True, stop=True),
                                 reads=["h_kg", "h_qg"], writes=[("ps", pa)])
                        S.op("dve", lambda e: e.tensor_tensor(out=attm[:, 0:ntile, :], in0=ps[pa][:, 0:ntile * 128].rearrange("p (a b) -> p a b", a=ntile),
                             in1=self.maskU2.unsqueeze(1).broadcast_to([128, ntile, 128]), op=ALU.mult), reads=[("ps", pa), "consts"], writes=["h_attm"])
                        pus = [self.nextps(), self.nextps()]
                        for c in range(nchk):
                            jt, cc = c // 2, c % 2
                            S.op("pe", lambda e, c=c, cc=cc, jt=jt: e.matmul(ps[pus[c % 2]][:, (c // 2) * 128:(c // 2 + 1) * 128], lhsT=kdtok[cc * 64:(cc + 1) * 64, jt, :], rhs=vtok[cc * 64:(cc + 1) * 64, jt, :], start=True, stop=True),
                                 reads=["h_kdtok", "h_vtok"], writes=[("ps", pus[c % 2])])
                        for c in range(nchk):
                            S.op("dve", lambda e, c=c: e.scalar_tensor_tensor(out=Sr[:, c + 1, :], in0=Sr[:, c, :], scalar=egl[:, c:c + 1], in1=ps[pus[c % 2]][:, (c // 2) * 128:(c // 2 + 1) * 128], op0=ALU.mult, op1=ALU.add),
                                 reads=[("ps", pus[c % 2]), ("h_Sr", c), "h_egl"], writes=[("h_Sr", c + 1)])
                        for jt in range(ntile):
                            S.op("pe", lambda e, jt=jt: e.matmul(ps[po][:, jt * 128:(jt + 1) * 128], lhsT=vtok[:, jt, :], rhs=attm[:, jt, :], start=True, stop=False),
                                 reads=["h_vtok", "h_attm"], writes=[("ps", po)])
                            for cc in range(2):
                                c = jt * 2 + cc
                                S.op("pe", lambda e, c=c, cc=cc: e.matmul(ps[po][:, c * 64:(c + 1) * 64], lhsT=Sr[:, c, :], rhs=qGf[:, c * 64:(c + 1) * 64], start=False, stop=(cc == 1)),
                                     reads=[("h_Sr", c), "h_qGf"], writes=[("ps", po)])
                        if c0 + w == TP:
                            S.op("sp", lambda e: e.dma_start(out=hg_p[j, h], in_=Sr[:, 8, :]), reads=[("h_Sr", 8)], dma="hg_p")
                    else:
                        pa = self.nextps()
                        S.op("pe", lambda e, pa=pa: e.matmul(ps[pa][0:NS, 0:NS], lhsT=kg[:, 0:NS], rhs=qG[:, 0:NS], start=True, stop=True), reads=["h_kg", "h_qG"], writes=[("ps", pa)])
                        S.op("dve", lambda e, pa=pa: e.tensor_tensor(out=attm[0:NS, 0, 0:NS], in0=ps[pa][0:NS, 0:NS], in1=self.maskS, op=ALU.mult), reads=[("ps", pa), "consts"], writes=[("h_attm", 0)])
                        S.op("pe", lambda e: e.matmul(ps[po][:, 0:NS], lhsT=vtok[0:NS, 0, :], rhs=attm[0:NS, 0, 0:NS], start=True, stop=False), reads=["h_vtok", ("h_attm", 0)], writes=[("ps", po)])
                        for b in range(NB):
                            S.op("pe", lambda e, b=b: e.matmul(ps[po][:, b * TS:(b + 1) * TS], lhsT=S0bf[:, b, :], rhs=qG[:, b * TS:(b + 1) * TS], start=False, stop=(b == NB - 1)),
                                 reads=["h_S0bf", "h_qG"], writes=[("ps", po)])
                        S.op("dve", lambda e: e.tensor_tensor(out=Vblk, in0=vtok[0:NS, 0, :].unsqueeze(1).broadcast_to([NS, NB, 128]),
                             in1=self.maskB.unsqueeze(2).broadcast_to([NS, NB, 128]), op=ALU.mult), reads=["h_vtok", "consts"], writes=["h_Vblk"])
                        S.op("dve", lambda e: e.tensor_tensor(out=S0, in0=S0, in1=egl[:, 0:NB].unsqueeze(2).broadcast_to([128, NB, 128]), op=ALU.mult), reads=["h_S0", "h_egl", "h_S0bf"], writes=["h_S0"])
                        for bq in range(4):
                            pu = self.nextps()
                            S.op("pe", lambda e, pu=pu, bq=bq: e.matmul(ps[pu][:, 0:512], lhsT=kdtok[0:NS, 0, :], rhs=Vblk[:, bq * 4:(bq + 1) * 4, :], start=True, stop=True),
                                 reads=["h_kdtok", "h_Vblk"], writes=[("ps", pu)])
                            S.op("dve", lambda e, pu=pu, bq=bq: e.tensor_tensor(out=S0[:, bq * 4:(bq + 1) * 4, :], in0=S0[:, bq * 4:(bq + 1) * 4, :],
                                 in1=ps[pu].rearrange("p (a b) -> p a b", a=4), op=ALU.add), reads=[("ps", pu), "h_S0"], writes=["h_S0"])
                        S.op("sp", lambda e: e.dma_start(out=hg_s[j, :, h, :, :].rearrange("b k v -> k b v"), in_=S0), reads=["h_S0"], dma="hg_s")
                    S.op("act", lambda e, w=w: e.activation(out=osb[:, 0:w], in_=ps[po][:, 0:w], func=AF.Copy), reads=[("ps", po)], writes=["h_osb"])
                    S.op("act", lambda e, w=w: e.activation(out=osq[:, 0:w], in_=ps[po][:, 0:w], func=AF.Square), reads=[("ps", po)], writes=["h_osq"])
                    pn = self.nextps()
                    S.op("pe", lambda e, pn=pn, w=w: e.matmul(ps[pn][:, 0:w], lhsT=self.onesb, rhs=osq[:, 0:w], start=True, stop=True), reads=["h_osq", "consts2"], writes=[("ps", pn)])
                    S.op("act", lambda e, pn=pn, w=w: e.activation(out=rstd[:, 0:w], in_=ps[pn][:, 0:w], func=AF.Ln, scale=1.0 / 128, bias=EPS), reads=[("ps", pn)], writes=["h_rstd"])
                    S.op("act", lambda e, w=w: e.activation(out=rstd[:, 0:w], in_=rstd[:, 0:w], func=AF.Exp, scale=-0.5), reads=["h_rstd"], writes=["h_rstd"])
                    S.op("dve", lambda e, w=w: e.tensor_tensor(out=osb[:, 0:w], in0=osb[:, 0:w], in1=rstd[:, 0:w], op=ALU.mult), reads=["h_osb", "h_rstd"], writes=["h_osb"])
                    S.op("dve", lambda e, w=w: e.scalar_tensor_tensor(out=og[:, 0:w], in0=osb[:, 0:w], scalar=self.hgn[:, j:j + 1], in1=gate[:, 0:w], op0=ALU.mult, op1=ALU.mult),
                         reads=["h_osb", "h_gate", "consts"], writes=["h_og"])
                    for dc in range(KC):
                        pi = self.nextps()
                        S.op("pe", lambda e, pi=pi, dc=dc, w=w: e.matmul(ps[pi][:, 0:w], lhsT=wout[:, slot, dc * 128:(dc + 1) * 128], rhs=og[:, 0:w], start=True, stop=True),
                             reads=[("h_wout", slot), "h_og"], writes=[("ps", pi)])
                        S.op("dve", lambda e, pi=pi, dc=dc, c0=c0, w=w: e.tensor_tensor(out=xres[:, dc, c0:c0 + w], in0=xres[:, dc, c0:c0 + w], in1=ps[pi][:, 0:w], op=ALU.add),
                             reads=[("ps", pi), ("xres", dc, c0)], writes=[("xres", dc, c0)])

    def rwkv7(self, j):
        nc, S, TP = self.nc, self.S, self.TP
        xres, hT, ps = self.xres, self.hT, self.ps
        dr = self.dram
        V = self.rw_vec
        GN_EPS = 64e-5
        li = self.cur_li
        with ExitStack() as es0:
            T0 = lambda n, sh, dt=F32: es0.enter_context(self.T(n, sh, dt))
            nblk = len(self.tbs)
            edge = T0("r_edge", [128, KC, 2 * nblk + 2], BF16)
            shiftT = T0("r_shiftT", [128, KC, NB], BF16)
            l1T = T0("r_l1T", [128, 3, self.N], BF16)
            prevS = T0("r_prevS", [128, KC, NS], BF16)
            prevB = T0("r_prevB", [128, KC, 512], BF16)

            def fill_prev(c0, w, kind):
                if kind == "p":
                    S.op("dve", lambda e: e.tensor_copy(out=prevB[:, :, 0:w], in_=hT[:, :, 1 + c0:1 + c0 + w]), reads=[("hT", "all"), "hT0"], writes=["r_prevB"])
            with ExitStack() as es:
                T = lambda n, sh, dt=F32: es.enter_context(self.T(n, sh, dt))
                shin, xsh, sq, rr = T("r_shin", [32, D]), T("r_xsh", [128, KC, 32]), T("r_sq", [128, KC, 32]), T("r_rr", [128, 32])
                shrow = T("r_shrow", [32, D])
                S.op("pool", lambda e: e.memset(shin, 0.0), writes=["r_shin"])
                S.op("pool", lambda e: e.memset(xsh, 0.0), writes=["r_xsh"])
                S.op("pool", lambda e: e.memset(edge, 0.0), writes=["r_edge"])
                S.op("sp", lambda e: e.dma_start(out=shin[0:NB, :], in_=dr["st_shift"]), reads=[], writes=["r_shin"], dma="r_shin")
                for half in range(2):
                    pi = self.nextps()
                    for q in range(4):
                        kc = half * 4 + q
                        S.op("pe", lambda e, q=q, kc=kc, pi=pi: e.transpose(ps[pi][:, q * 32:(q + 1) * 32], shin[:, kc * 128:(kc + 1) * 128], self.ident[0:32, 0:32]), reads=["r_shin", "consts"], writes=[("ps", pi)])
                    self.copy("dve", shiftT[:, half * 4:(half + 1) * 4, :], ps[pi][:, 0:128].rearrange("p (q t) -> p q t", q=4)[:, :, 0:NB], [("ps", pi)], ["r_shiftT"])
                S.op("dve", lambda e: e.tensor_copy(out=xsh[:, :, 0:1], in_=xres[:, :, TP - 1:TP]), reads=[("xres", "all"), "r_xsh"], writes=["r_xsh"])
                S.op("dve", lambda e: e.tensor_copy(out=xsh[:, :, 1:1 + NB], in_=xres[:, :, TP:TP + NS].rearrange("p k (b t) -> p k b t", t=TS)[:, :, :, 3]), reads=[("xres", "all"), "r_xsh"], writes=["r_xsh"])
                S.op("act", lambda e: e.activation(out=sq, in_=xsh, func=AF.Square), reads=["r_xsh"], writes=["r_sq"])
                pi = self.nextps()
                for kc in range(KC):
                    S.op("pe", lambda e, kc=kc: e.matmul(ps[pi][:, 0:32], lhsT=self.onesf, rhs=sq[:, kc, :], start=(kc == 0), stop=(kc == KC - 1)), reads=["r_sq", "consts2"], writes=[("ps", pi)])
                S.op("act", lambda e: e.activation(out=rr, in_=ps[pi][:, 0:32], func=AF.Ln, scale=1.0 / D, bias=EPS), reads=[("ps", pi)], writes=["r_rr"])
                S.op("act", lambda e: e.activation(out=rr, in_=rr, func=AF.Exp, scale=-0.5), reads=["r_rr"], writes=["r_rr"])
                for kc in range(KC):
                    S.op("dve", lambda e, kc=kc: e.scalar_tensor_tensor(out=xsh[:, kc, :], in0=xsh[:, kc, :], scalar=self.nmix[:, li, kc:kc + 1], in1=rr, op0=ALU.mult, op1=ALU.mult), reads=["r_xsh", "r_rr", "consts"], writes=["r_xsh"])
                for half in range(2):
                    pi = self.nextps()
                    for q in range(4):
                        kc = half * 4 + q
                        S.op("pe", lambda e, q=q, kc=kc, pi=pi: e.transpose(ps[pi][0:32, q * 128:(q + 1) * 128], xsh[:, kc, :], self.ident), reads=["r_xsh", "consts"], writes=[("ps", pi)])
                    self.copy("act", shrow[:, half * 512:(half + 1) * 512], ps[pi][0:32, :], [("ps", pi)], [("r_shrow", half)])
                S.op("sp", lambda e: e.dma_start(out=dr["sh_p"], in_=shrow[0:1, :]), reads=[("r_shrow", 0), ("r_shrow", 1)], dma="r_sh")
                S.op("sp", lambda e: e.dma_start(out=dr["sh_s"], in_=shrow[1:1 + NB, :]), reads=[("r_shrow", 0), ("r_shrow", 1)], dma="r_sh")
                for bi, (c0, w, kind) in enumerate(self.tbs):
                    if kind == "p" and c0 > 0:
                        S.op("dve", lambda e, bi=bi, c0=c0: e.tensor_copy(out=edge[:, :, 2 * bi:2 * bi + 1], in_=hT[:, :, 1 + c0:2 + c0]), reads=[("hT", "all"), "r_edge"], writes=["r_edge"])
                hs3 = hT[:, :, 2 + TP:2 + TP + NS].rearrange("p k (b t) -> p k b t", t=TS)
                pv3 = prevS.rearrange("p k (b t) -> p k b t", t=TS)
                for kc in range(KC):
                    S.op("dve", lambda e, kc=kc: e.tensor_copy(out=pv3[:, kc, :, 1:TS], in_=hs3[:, kc, :, 0:TS - 1]), reads=[("hT", "all")], writes=["r_prevS"])
                    S.op("dve", lambda e, kc=kc: e.tensor_copy(out=pv3[:, kc, :, 0], in_=shiftT[:, kc, :]), reads=["r_shiftT", "r_prevS"], writes=["r_prevS"])
            S.barrier()

            def shifted_proj(pi, w_a, w_b, c0, w, kind, rd, M=128):
                hc = 2 + c0
                out = ps[pi][0:M, 0:w]
                for kc in range(KC):
                    S.op("pe", lambda e, kc=kc: e.matmul(out, lhsT=w_a(kc), rhs=hT[:, kc, hc:hc + w], start=(kc == 0), stop=False), reads=rd + [("hT", "all")], writes=[("ps", pi)])
                if kind == "p":
                    for kc in range(KC):
                        S.op("pe", lambda e, kc=kc: e.matmul(out, lhsT=w_b(kc), rhs=prevB[:, kc, 0:w], start=False, stop=(kc == KC - 1)), reads=rd + ["r_prevB"], writes=[("ps", pi)])
                else:
                    for kc in range(KC):
                        S.op("pe", lambda e, kc=kc: e.matmul(out, lhsT=w_b(kc), rhs=prevS[:, kc, :], start=False, stop=(kc == KC - 1)), reads=rd + ["r_prevS"], writes=[("ps", pi)])

            nq = TP // 256
            self.r_e2 = T0("r_e2", [128, KC, 2 * nq], BF16)
            e2 = self.r_e2
            S.op("pool", lambda e: e.memset(e2, 0.0), writes=["r_e2"])
            for q in range(nq):
                c0 = q * 256
                if c0 > 0:
                    S.op("dve", lambda e, q=q, c0=c0: e.tensor_copy(out=e2[:, :, 2 * q:2 * q + 1], in_=hT[:, :, 1 + c0:2 + c0]), reads=[("hT", "all"), "r_e2"], writes=["r_e2"])
            S.barrier()
            rblocks = [(q * 256, 256, "p") for q in range(nq)] + [(TP, NS, "s")]

            def load_scaled(dst_a, dst_b, src, shape, n, kk0, nk, key):
                slot = self.stg_i % self.NSTG
                self.stg_i += 1
                cols = shape[2]
                st = self.stage[slot][:, 0:nk * cols].rearrange("p (a b) -> p a b", a=nk)
                S.op("sp", lambda e: e.dma_start(out=st, in_=src), writes=[("stg", slot)], dma="stg%d" % slot)
                for q in range(nk):
                    kc = kk0 + q
                    S.op("dve", lambda e, q=q, kc=kc: e.tensor_scalar(out=dst_a(kc), in0=st[:, q, :], scalar1=self.rw_omu[:, n, kc:kc + 1], scalar2=1.0, op0=ALU.mult, op1=ALU.mult), reads=[("stg", slot), "rwc0"], writes=[key])
                    S.op("dve", lambda e, q=q, kc=kc: e.tensor_scalar(out=dst_b(kc), in0=st[:, q, :], scalar1=self.rw_mu[:, n, kc:kc + 1], scalar2=1.0, op0=ALU.mult, op1=ALU.mult), reads=[("stg", slot), "consts"], writes=[key])

            with ExitStack() as es:
                T = lambda n, sh, dt=F32: es.enter_context(self.T(n, sh, dt))
                wl = T("r_wl", [128, 3, 2, KC, 128], BF16)
                for li_, (nm, n, cols) in enumerate((("rw_w1", 3, 64), ("rw_a1", 4, 64), ("rw_g1", 5, 128))):
                    src = dr[nm].rearrange("(k p) c -> p k c", p=128)
                    load_scaled(lambda kc, li_=li_, cols=cols: wl[:, li_, 0, kc, 0:cols], lambda kc, li_=li_, cols=cols: wl[:, li_, 1, kc, 0:cols], src, [128, KC, cols], n, 0, KC, ("r_wl", li_))
                for (c0, w, kind) in rblocks:
                    fill_prev(c0, w, kind)
                    for li_, (M, fn) in enumerate(((64, AF.Tanh), (64, AF.Copy), (128, AF.Sigmoid))):
                        pi = self.nextps()
                        shifted_proj(pi, lambda kc, li_=li_, M=M: wl[:, li_, 0, kc, 0:M], lambda kc, li_=li_, M=M: wl[:, li_, 1, kc, 0:M], c0, w, kind, [("r_wl", li_)], M=M)
                        S.op("act", lambda e, li_=li_, M=M, fn=fn, pi=pi: e.activation(out=l1T[0:M, li_, c0:c0 + w], in_=ps[pi][0:M, 0:w], func=fn), reads=[("ps", pi)], writes=[("r_l1T", li_)])
            S.barrier()

            for sample_pass in (False, True):
                blocks = [b_ for b_ in rblocks if (b_[2] == "s") == sample_pass]
                Wd = NS if sample_pass else 256
                R = NS if sample_pass else 128
                nch = 1 if sample_pass else 2
                nlev = 1 if sample_pass else 6
                mStrict = self.sUS if sample_pass else self.sU
                mIncl = self.maskS if sample_pass else self.maskU
                mLow = self.sLS if sample_pass else self.sL
                with ExitStack() as es:
                    T = lambda n, sh, dt=F32: es.enter_context(self.T(n, sh, dt))
                    wrkv = T("r_wrkv", [128, 3, 2, KC, 128], BF16)
                    w2nd = T("r_w2nd", [128, 3, 128], BF16)
                    wout2 = [T("r_wout", [128, D], BF16) for _ in range(2)]
                    rfA, kfA, vfA, lwf, af = [T("r_f%d" % i_, [128, Wd]) for i_ in range(5)]
                    rkvA = [[rfA, kfA, vfA, None], [T("r_rf2", [128, Wd]), T("r_kf2", [128, Wd]), T("r_vf2", [128, Wd]), T("r_vb2", [128, Wd], BF16)]]
                    gf2 = [T("r_gf", [128, Wd]) for _ in range(2)]
                    bon2 = [T("r_bon", [128, Wd]) for _ in range(2)]
                    G, t1, e0, e1 = T("r_G", [128, Wd]), T("r_t1", [128, Wd]), T("r_e0", [128, Wd]), T("r_e1", [128, Wd])
                    kap, k2, beta, osb = T("r_kap", [128, Wd]), T("r_k2", [128, Wd]), T("r_beta", [128, Wd]), T("r_osb", [128, Wd])
                    KR2 = [T("r_KR", [128, nch, 2, 128], BF16) for _ in range(2)]
                    BK2 = [T("r_BK", [128, nch, 2, 128], BF16) for _ in range(2)]
                    btT, ktT, vbA, og = T("r_btT", [128, Wd], BF16), T("r_ktT", [128, Wd], BF16), T("r_vb", [128, Wd], BF16), T("r_og", [128, Wd], BF16)
                    rkvA[0][3] = vbA
                    bttok2 = [T("r_bttok", [128, nch, 128], BF16) for _ in range(2)]
                    kttok2 = [T("r_kttok", [128, nch, 128], BF16) for _ in range(2)]
                    vtok2 = [T("r_vtok", [128, nch, 128], BF16) for _ in range(2)]
                    Nk, Ak = T("r_Nk", [128, 2, 2 * nch, 128], BF16), T("r_Ak", [128, 2, 2 * nch, 128], BF16)
                    Wm = T("r_Wm", [128, 2 * nch, 128], BF16)
                    MbT, BTm, MkT = T("r_MbT", [128, 2 * nch, 128], BF16), T("r_BTm", [128, 2 * nch, 128], BF16), T("r_MkT", [128, 2 * nch, 128], BF16)
                    EC2 = [T("r_EC", [128, NB]) for _ in range(2)]
                    t2 = T("r_t2", [128, Wd])
                    Xsb, Ssb = T("r_Xsb", [128, 2, 64], BF16), T("r_Ssb", [128, 2, 64], BF16)
                    Sst, Sbf = T("r_Sst", [128, 64]), T("r_Sbf", [128, 64], BF16)
                    sT = T("r_sT", [64, 128])
                    if sample_pass:
                        s0nat = T("r_s0nat", [64, NB, 128])
                        S0, S0bf = T("r_S0", [128, NB, 64]), T("r_S0bf", [128, NB, 64], BF16)
                        kblk = T("r_kblk", [128, NB, NS], BF16)
                        rblk = T("r_rblk", [128, NB, NS], BF16)
                        Sblk, Vblk = T("r_Sblk", [128, 2, NB, 64], BF16), T("r_Vblk", [128, 2, NB, 64], BF16)
                        for t_, k_ in ((Ssb, ("r_Ssb", 0)), (MbT, "r_zMbT"), (MkT, "r_zMkT"), (BTm, "r_zBTm"), (vtok2[0], ("r_vtok", 0)), (vtok2[1], ("r_vtok", 1)), (bttok2[0], ("r_bttok", 0)), (bttok2[1], ("r_bttok", 1)), (kttok2[0], ("r_kttok", 0)), (kttok2[1], ("r_kttok", 1)), (Sblk, ("r_Sblk", 0)), (Vblk, ("r_Vblk", 0))):
                            S.op("pool", lambda e, t_=t_: e.memset(t_, 0.0), writes=[k_])
                        S.barrier()
                    def blk(pc, bi_, c0, w, kind, par):
                        cs = slice(pc * 128, (pc + 1) * 128)
                        KR, BK, bt_tok, kt_tok, v_tok, EC, gf, bon = KR2[par], BK2[par], bttok2[par], kttok2[par], vtok2[par], EC2[par], gf2[par], bon2[par]
                        wout = wout2[pc % 2]
                        if bi_ == 0:
                            for n in range(3):
                                for kk0 in (0, 4):
                                    src = dr["rw_w_rkv"][n, kk0 * 128:(kk0 + 4) * 128, cs].rearrange("(k p) c -> p k c", p=128)
                                    load_scaled(lambda kc, n=n: wrkv[:, n, 0, kc, :], lambda kc, n=n: wrkv[:, n, 1, kc, :], src, [128, 4, 128], n, kk0, 4, ("r_wrkv", n, kk0))
                            self.load_w(w2nd[0:64, 0, :], dr["rw_w2"][:, cs], [64, 128], ("r_w2nd", 0))
                            self.load_w(w2nd[0:64, 1, :], dr["rw_a2"][:, cs], [64, 128], ("r_w2nd", 1))
                            self.load_w(w2nd[:, 2, :], dr["rw_g2"][:, cs], [128, 128], ("r_w2nd", 2))
                            self.load_w(wout, dr["rw_w_out"][cs, :], [128, D], ("r_wout", pc % 2))
                        wk = lambda n: [("r_wrkv", n, 0), ("r_wrkv", n, 4)]
                        vec = lambda nm: V[nm][:, pc:pc + 1]
                        for _once in (0,):
                            half = bi_ % 2 if kind == "p" else 0
                            rf, kf, vf, vb = rkvA[half]
                            if kind == "s" or half == 0:
                                wp = w if kind == "s" else 2 * w
                                fill_prev(c0, wp, kind)
                                pr, pk, pv = self.nextps(), self.nextps(), self.nextps()
                                for n, pi in ((0, pr), (1, pk), (2, pv)):
                                    shifted_proj(pi, lambda kc, n=n: wrkv[:, n, 0, kc, :], lambda kc, n=n: wrkv[:, n, 1, kc, :], c0, wp, kind, wk(n))
                                for hh_ in range(wp // w):
                                    rf_, kf_, vf_, vb_ = rkvA[hh_]
                                    cs_ = slice(hh_ * w, (hh_ + 1) * w)
                                    self.copy("act", rf_[:, 0:w], ps[pr][:, cs_], [("ps", pr)], [("r_rf", hh_)])
                                    self.copy("act", kf_[:, 0:w], ps[pk][:, cs_], [("ps", pk)], [("r_kf", hh_)])
                                    self.copy("act", vf_[:, 0:w], ps[pv][:, cs_], [("ps", pv)], [("r_vf", hh_)])
                                    S.op("dve", lambda e, vb_=vb_, cs_=cs_: e.tensor_copy(out=vb_[:, 0:w], in_=ps[pv][:, cs_]), reads=[("ps", pv)], writes=[("r_vb", hh_)])
                            yield 0
                            pw, pa, pg = self.nextps(), self.nextps(), self.nextps()
                            S.op("pe", lambda e: e.matmul(ps[pw][:, 0:w], lhsT=w2nd[0:64, 0, :], rhs=l1T[0:64, 0, c0:c0 + w], start=True, stop=True), reads=[("r_w2nd", 0), ("r_l1T", 0)], writes=[("ps", pw)])
                            S.op("pe", lambda e: e.matmul(ps[pa][:, 0:w], lhsT=w2nd[0:64, 1, :], rhs=l1T[0:64, 1, c0:c0 + w], start=True, stop=True), reads=[("r_w2nd", 1), ("r_l1T", 1)], writes=[("ps", pa)])
                            S.op("pe", lambda e: e.matmul(ps[pg][:, 0:w], lhsT=w2nd[:, 2, :], rhs=l1T[:, 2, c0:c0 + w], start=True, stop=True), reads=[("r_w2nd", 2), ("r_l1T", 2)], writes=[("ps", pg)])
                            S.op("act", lambda e: e.activation(out=lwf[:, 0:w], in_=ps[pw][:, 0:w], func=AF.Sigmoid, bias=vec("rw_w0")), reads=[("ps", pw), "consts"], writes=["r_lwf"])
                            S.op("dve", lambda e: e.tensor_scalar(out=lwf[:, 0:w], in0=lwf[:, 0:w], scalar1=-0.6065306597126334, scalar2=1.0, op0=ALU.mult, op1=ALU.mult), reads=["r_lwf"], writes=["r_lwf"])
                            S.op("act", lambda e: e.activation(out=af[:, 0:w], in_=ps[pa][:, 0:w], func=AF.Sigmoid, bias=vec("rw_a0")), reads=[("ps", pa), "consts"], writes=["r_af"])
                            self.copy("act", gf[:, 0:w], ps[pg][:, 0:w], [("ps", pg)], [("r_gf", par)])
                            yield 0
                            if not sample_pass:
                                for c in range(nch):
                                    S.op("dve", lambda e, c=c: e.tensor_tensor_scan(out=G[:, c * 128:(c + 1) * 128], data0=self.onesf[:, 0:128], data1=lwf[:, c * 128:(c + 1) * 128], initial=0.0, op0=ALU.mult, op1=ALU.add), reads=["r_lwf", "consts2"], writes=["r_G"])
                                v3 = lambda t_: t_[:, 0:w].rearrange("p (c t) -> p c t", t=128)
                                glast = v3(G)[:, :, 127:128].broadcast_to([128, nch, 128])
                                ngrp = nch
                                S.op("act", lambda e: e.activation(out=EC[:, 0:nch], in_=v3(G)[:, :, 127], func=AF.Exp), reads=["r_G"], writes=[("r_EC", par)])
                            else:
                                G3 = G[:, 0:NS].rearrange("p (b t) -> p b t", t=TS)
                                l3 = lwf[:, 0:NS].rearrange("p (b t) -> p b t", t=TS)
                                S.op("dve", lambda e: e.tensor_copy(out=G3[:, :, 0:1], in_=l3[:, :, 0:1]), reads=["r_lwf"], writes=["r_G"])
                                for t_ in range(1, TS):
                                    S.op("dve", lambda e, t_=t_: e.tensor_tensor(out=G3[:, :, t_:t_ + 1], in0=G3[:, :, t_ - 1:t_], in1=l3[:, :, t_:t_ + 1], op=ALU.add), reads=["r_lwf", "r_G"], writes=["r_G"])
                                v3 = lambda t_: t_[:, 0:NS].rearrange("p (b t) -> p b t", t=TS)
                                glast = G3[:, :, TS - 1:TS].broadcast_to([128, NB, TS])
                                S.op("act", lambda e: e.activation(out=EC[:, 0:NB], in_=G3[:, :, TS - 1], func=AF.Exp), reads=["r_G"], writes=[("r_EC", par)])
                            yield 0
                            S.op("dve", lambda e: e.tensor_scalar(out=kap[:, 0:w], in0=kf[:, 0:w], scalar1=vec("rw_k_k"), scalar2=1.0, op0=ALU.mult, op1=ALU.mult), reads=[("r_kf", half), "consts"], writes=["r_kap"])
                            S.op("act", lambda e: e.activation(out=t1[:, 0:w], in_=kap[:, 0:w], func=AF.Square), reads=["r_kap"], writes=["r_t1"])
                            pn = self.nextps()
                            S.op("pe", lambda e: e.matmul(ps[pn][:, 0:w], lhsT=self.blk1, rhs=t1[:, 0:w], start=True, stop=True), reads=["r_t1", "consts"], writes=[("ps", pn)])
                            S.op("dve", lambda e: e.tensor_scalar(out=t1[:, 0:w], in0=ps[pn][:, 0:w], scalar1=1e-24, scalar2=None, op0=ALU.max), reads=[("ps", pn), "r_t1"], writes=["r_t1"])
                            S.op("act", lambda e: e.activation(out=t1[:, 0:w], in_=t1[:, 0:w], func=AF.Ln), reads=["r_t1"], writes=["r_t1"])
                            S.op("act", lambda e: e.activation(out=t1[:, 0:w], in_=t1[:, 0:w], func=AF.Exp, scale=-0.5), reads=["r_t1"], writes=["r_t1"])
                            S.op("dve", lambda e: e.tensor_tensor(out=kap[:, 0:w], in0=kap[:, 0:w], in1=t1[:, 0:w], op=ALU.mult), reads=["r_kap", "r_t1"], writes=["r_kap"])
                            yield 0
                            S.op("dve", lambda e: e.tensor_scalar(out=k2[:, 0:w], in0=af[:, 0:w], scalar1=vec("rw_k_a"), scalar2=self.rw_omka[:, pc:pc + 1], op0=ALU.mult, op1=ALU.add), reads=["r_af", "consts", "rwc1"], writes=["r_k2"])
                            S.op("pool", lambda e: e.tensor_tensor(out=k2[:, 0:w], in0=k2[:, 0:w], in1=kf[:, 0:w], op=ALU.mult), reads=["r_k2", ("r_kf", half)], writes=["r_k2"])
                            S.op("pool", lambda e: e.tensor_tensor(out=beta[:, 0:w], in0=af[:, 0:w], in1=kap[:, 0:w], op=ALU.mult), reads=["r_af", "r_kap"], writes=["r_beta"])
                            yield 0
                            S.op("dve", lambda e: e.scalar_tensor_tensor(out=bon[:, 0:w], in0=rf[:, 0:w], scalar=vec("rw_r_k"), in1=k2[:, 0:w], op0=ALU.mult, op1=ALU.mult), reads=[("r_rf", half), "r_k2", "consts"], writes=[("r_bon", par)])
                            pb = self.nextps()
                            S.op("pe", lambda e: e.matmul(ps[pb][:, 0:w], lhsT=self.blk1, rhs=bon[:, 0:w], start=True, stop=True), reads=[("r_bon", par), "consts"], writes=[("ps", pb)])
                            S.op("dve", lambda e: e.tensor_tensor(out=bon[:, 0:w], in0=ps[pb][:, 0:w], in1=vf[:, 0:w], op=ALU.mult), reads=[("ps", pb), ("r_vf", half), ("r_bon", par)], writes=[("r_bon", par)])
                            yield 0
                            if not sample_pass:
                                kr = lambda i_: KR[:, :, i_, :]
                                bk = lambda i_: BK[:, :, i_, :]
                            else:
                                kr = lambda i_: KR[:, 0, i_, 0:NS].rearrange("p (b t) -> p b t", t=TS)
                                bk = lambda i_: BK[:, 0, i_, 0:NS].rearrange("p (b t) -> p b t", t=TS)
                            S.op("act", lambda e: e.activation(out=e0[:, 0:w], in_=G[:, 0:w], func=AF.Exp), reads=["r_G"], writes=["r_e0"])
                            S.op("pool", lambda e: e.tensor_tensor(out=kr(1), in0=v3(rf), in1=v3(e0), op=ALU.mult), reads=[("r_rf", half), "r_e0"], writes=[("r_KR1", par)])
                            S.op("dve", lambda e: e.tensor_tensor(out=t1[:, 0:w], in0=G[:, 0:w], in1=lwf[:, 0:w], op=ALU.subtract), reads=["r_G", "r_lwf", "r_t1"], writes=["r_t1"])
                            S.op("act", lambda e: e.activation(out=e1[:, 0:w], in_=t1[:, 0:w], func=AF.Exp), reads=["r_t1"], writes=["r_e1"])
                            S.op("pool", lambda e: e.tensor_tensor(out=kr(0), in0=v3(kap), in1=v3(e1), op=ALU.mult), reads=["r_kap", "r_e1"], writes=[("r_KR0", par)])
                            S.op("act", lambda e: e.activation(out=e0[:, 0:w], in_=G[:, 0:w], func=AF.Exp, scale=-1.0), reads=["r_G", ("r_KR1", par)], writes=["r_e0"])
                            S.op("pool", lambda e: e.tensor_tensor(out=bk(0), in0=v3(beta), in1=v3(e0), op=ALU.mult), reads=["r_beta", "r_e0"], writes=[("r_BK0", par)])
                            S.op("dve", lambda e: e.tensor_tensor(out=bk(1), in0=v3(k2), in1=v3(e0), op=ALU.mult), reads=["r_k2", "r_e0"], writes=[("r_BK1", par)])
                            S.op("dve", lambda e: e.tensor_tensor(out=v3(t1), in0=glast, in1=v3(G), op=ALU.subtract), reads=["r_G", "r_t1", "r_e1"], writes=["r_t1"])
                            S.op("act", lambda e: e.activation(out=e1[:, 0:w], in_=t1[:, 0:w], func=AF.Exp), reads=["r_t1", ("r_KR0", par)], writes=["r_e1"])
                            S.op("pool", lambda e: e.tensor_tensor(out=btT[:, 0:w], in0=beta[:, 0:w], in1=e1[:, 0:w], op=ALU.mult), reads=["r_beta", "r_e1"], writes=["r_btT"])
                            S.op("dve", lambda e: e.tensor_tensor(out=ktT[:, 0:w], in0=k2[:, 0:w], in1=e1[:, 0:w], op=ALU.mult), reads=["r_k2", "r_e1"], writes=["r_ktT"])
                            yield 0
                            pt = self.nextps()
                            ptb = ps[pt].bitcast(BF16)
                            for i_, (src_, nm_) in enumerate(((btT, "r_btT"), (ktT, "r_ktT"), (vb, ("r_vb", half)))):
                                for c in range(nch):
                                    S.op("pe", lambda e, i_=i_, c=c, src_=src_: e.transpose(ptb[0:R, (i_ * nch + c) * 128:(i_ * nch + c + 1) * 128], src_[:, c * 128:c * 128 + R], self.identb), reads=[nm_, "consts2"], writes=[("ps", pt)])
                            for i_, (dst_, nm_) in enumerate(((bt_tok, ("r_bttok", par)), (kt_tok, ("r_kttok", par)), (v_tok, ("r_vtok", par)))):
                                self.copy("dve", dst_[0:R, :, :], ptb[0:R, i_ * nch * 128:(i_ + 1) * nch * 128].rearrange("p (c k) -> p c k", c=nch), [("ps", pt)], [nm_])
                            yield "MID"
                            if bi_ == 0:
                                S.op("pool", lambda e: e.memset(Sst, 0.0), writes=[("r_Sst", 0), ("r_Sst", 1)])
                                S.op("pool", lambda e: e.memset(Sbf, 0.0), writes=[("r_Sbf", 0), ("r_Sbf", 1)])
                            for hd in range(2):
                                P0 = hd * 64
                                for c in range(nch):
                                    m_ = hd * nch + c
                                    kr_c = KR[P0:P0 + 64, c, :, 0:R]
                                    p1, p2, p3 = self.nextps(), self.nextps(), self.nextps()
                                    o1 = ps[p1][0:R, 0:2 * R].rearrange("p (i t) -> p i t", i=2)
                                    o2 = ps[p2][0:R, 0:2 * R].rearrange("p (i t) -> p i t", i=2)
                                    S.op("pe", lambda e, c=c, P0=P0, kr_c=kr_c, o1=o1: e.matmul(o1, lhsT=BK[P0:P0 + 64, c, 0, 0:R], rhs=kr_c, start=True, stop=True), reads=[("r_BK0", par), ("r_KR0", par), ("r_KR1", par)], writes=[("ps", p1)])
                                    S.op("pe", lambda e, c=c, P0=P0, kr_c=kr_c, o2=o2: e.matmul(o2, lhsT=BK[P0:P0 + 64, c, 1, 0:R], rhs=kr_c, start=True, stop=True), reads=[("r_BK1", par), ("r_KR0", par), ("r_KR1", par)], writes=[("ps", p2)])
                                    S.op("pe", lambda e, c=c, P0=P0, p3=p3: e.matmul(ps[p3][0:R, 0:R], lhsT=KR[P0:P0 + 64, c, 0, 0:R], rhs=BK[P0:P0 + 64, c, 0, 0:R], start=True, stop=True), reads=[("r_BK0", par), ("r_KR0", par)], writes=[("ps", p3)])
                                    S.op("dve", lambda e, m_=m_, p1=p1: e.tensor_tensor(out=Nk[0:R, 0, m_, 0:R], in0=ps[p1][0:R, 0:R], in1=mStrict, op=ALU.mult), reads=[("ps", p1), "consts"], writes=[("r_Nk", 0, m_)])
                                    S.op("dve", lambda e, m_=m_, p1=p1: e.tensor_tensor(out=MbT[0:R, m_, 0:R], in0=ps[p1][0:R, R:2 * R], in1=mIncl, op=ALU.mult), reads=[("ps", p1), "consts"], writes=[("r_MbT", m_)])
                                    S.op("dve", lambda e, m_=m_, p2=p2: e.tensor_tensor(out=BTm[0:R, m_, 0:R], in0=ps[p2][0:R, 0:R], in1=mStrict, op=ALU.mult), reads=[("ps", p2), "consts"], writes=[("r_BTm", m_)])
                                    S.op("dve", lambda e, m_=m_, p2=p2: e.tensor_tensor(out=MkT[0:R, m_, 0:R], in0=ps[p2][0:R, R:2 * R], in1=mIncl, op=ALU.mult), reads=[("ps", p2), "consts"], writes=[("r_MkT", m_)])
                                    S.op("dve", lambda e, m_=m_, p3=p3: e.tensor_tensor(out=Ak[0:R, 0, m_, 0:R], in0=ps[p3][0:R, 0:R], in1=mLow, op=ALU.mult), reads=[("ps", p3), "consts"], writes=[("r_Ak", 0, m_)])
                                    S.op("pool", lambda e, m_=m_: e.tensor_tensor(out=Wm[0:R, m_, 0:R], in0=self.ident[0:R, 0:R], in1=Nk[0:R, 0, m_, 0:R], op=ALU.subtract), reads=[("r_Nk", 0, m_), "consts"], writes=[("r_Wm", m_)])
                                    yield 0
                            yield 0
                            nm_ = 2 * nch
                            for lev in range(1, nlev + 1):
                                cur, nxt = (lev - 1) % 2, lev % 2
                                for m_ in range(nm_):
                                    p1 = self.nextps()
                                    S.op("pe", lambda e, m_=m_, p1=p1, cur=cur: e.matmul(ps[p1][0:R, 0:R], lhsT=Nk[0:R, cur, m_, 0:R], rhs=Ak[0:R, cur, m_, 0:R], start=True, stop=True), reads=[("r_Nk", cur, m_), ("r_Ak", cur, m_)], writes=[("ps", p1)])
                                    if lev < nlev:
                                        S.op("pe", lambda e, m_=m_, p1=p1, cur=cur: e.matmul(ps[p1][0:R, 128:128 + R], lhsT=Ak[0:R, cur, m_, 0:R], rhs=Nk[0:R, cur, m_, 0:R], start=True, stop=True), reads=[("r_Nk", cur, m_), ("r_Ak", cur, m_)], writes=[("ps", p1)])
                                    self.copy("act", Ak[0:R, nxt, m_, 0:R], ps[p1][0:R, 0:R], [("ps", p1)], [("r_Ak", nxt, m_)])
                                    if lev < nlev:
                                        self.copy("dve", Nk[0:R, nxt, m_, 0:R], ps[p1][0:R, 128:128 + R], [("ps", p1)], [("r_Nk", nxt, m_)])
                                    yield 0
                                for m_ in range(nm_):
                                    p2 = self.nextps()
                                    S.op("pe", lambda e, m_=m_, p2=p2, nxt=nxt: e.matmul(ps[p2][0:R, 0:R], lhsT=Ak[0:R, nxt, m_, 0:R], rhs=Wm[0:R, m_, 0:R], start=True, stop=True), reads=[("r_Ak", nxt, m_), ("r_Wm", m_)], writes=[("ps", p2)])
                                    S.op("dve", lambda e, m_=m_, p2=p2: e.tensor_tensor(out=Wm[0:R, m_, 0:R], in0=Wm[0:R, m_, 0:R], in1=ps[p2][0:R, 0:R], op=ALU.add), reads=[("ps", p2), ("r_Wm", m_)], writes=[("r_Wm", m_)])
                                    yield 0
                            yield 0
                            pO = self.nextlong()
                            if sample_pass:
                                for hd in range(2):
                                    S.op("sp", lambda e, hd=hd: e.dma_start(out=s0nat[:, :, hd * 64:(hd + 1) * 64], in_=dr["st_wkv"][:, 2 * pc + hd].rearrange("b v k -> v b k")), writes=[("r_s0nat", hd)], dma="r_s0nat")
                                for q4 in range(4):
                                    p1 = self.nextps()
                                    for bb in range(4):
                                        b = q4 * 4 + bb
                                        S.op("pe", lambda e, b=b, bb=bb, p1=p1: e.transpose(ps[p1][:, bb * 64:(bb + 1) * 64], s0nat[:, b, :], self.ident[0:64, 0:64]), reads=[("r_s0nat", 0), ("r_s0nat", 1), "consts"], writes=[("ps", p1)])
                                    self.copy("act", S0[:, q4 * 4:(q4 + 1) * 4, :], ps[p1][:, 0:256].rearrange("p (b v) -> p b v", b=4), [("ps", p1)], [("r_S0", q4)])
                                    S.op("dve", lambda e, q4=q4, p1=p1: e.tensor_copy(out=S0bf[:, q4 * 4:(q4 + 1) * 4, :], in_=ps[p1][:, 0:256].rearrange("p (b v) -> p b v", b=4)), reads=[("ps", p1)], writes=[("r_S0bf", q4)])
                                s0k = [("r_S0", q4) for q4 in range(4)]
                                s0bk = [("r_S0bf", q4) for q4 in range(4)]
                                S.op("dve", lambda e: e.tensor_tensor(out=kblk, in0=KR[:, 0, 0, 0:NS].unsqueeze(1).broadcast_to([128, NB, NS]), in1=self.maskC, op=ALU.mult), reads=[("r_KR0", par), "consts"], writes=["r_kblk"])
                                S.op("dve", lambda e: e.tensor_tensor(out=rblk, in0=KR[:, 0, 1, 0:NS].unsqueeze(1).broadcast_to([128, NB, NS]), in1=self.maskC, op=ALU.mult), reads=[("r_KR1", par), "consts"], writes=["r_rblk"])
                            for c in range(nch):
                                for hd in range(2):
                                    P0 = hd * 64
                                    m_ = hd * nch + c
                                    hs = slice(P0, P0 + 64)
                                    yield 0
                                    pX = self.nextps()
                                    if not sample_pass:
                                        S.op("pe", lambda e, c=c, hs=hs, pX=pX: e.matmul(ps[pX][0:R, 0:64], lhsT=KR[hs, c, 0, 0:R], rhs=Sbf[hs, :], start=True, stop=False), reads=[("r_KR0", par), ("r_Sbf", hd)], writes=[("ps", pX)])
                                    else:
                                        for b in range(NB):
                                            S.op("pe", lambda e, b=b, hs=hs, pX=pX: e.matmul(ps[pX][0:R, 0:64], lhsT=kblk[hs, b, :], rhs=S0bf[hs, b, :], start=(b == 0), stop=False), reads=["r_kblk"] + s0bk, writes=[("ps", pX)])
                                    S.op("pe", lambda e, c=c, hs=hs, pX=pX, m_=m_: e.matmul(ps[pX][0:R, 0:64], lhsT=BTm[:, m_, 0:R], rhs=v_tok[:, c, hs], start=False, stop=True), reads=[("r_BTm", m_), ("r_vtok", par)], writes=[("ps", pX)])
                                    S.op("act", lambda e, hd=hd, pX=pX: e.activation(out=Xsb[0:R, hd, :], in_=ps[pX][0:R, 0:64], func=AF.Copy, scale=-1.0), reads=[("ps", pX)], writes=[("r_Xsb", hd)])
                                    pS = self.nextps()
                                    S.op("pe", lambda e, hd=hd, pS=pS, m_=m_: e.matmul(ps[pS][0:R, 0:64], lhsT=Wm[0:R, m_, 0:R], rhs=Xsb[0:R, hd, :], start=True, stop=True), reads=[("r_Wm", m_), ("r_Xsb", hd)], writes=[("ps", pS)])
                                    S.op("dve", lambda e, hd=hd, pS=pS: e.tensor_copy(out=Ssb[0:R, hd, :], in_=ps[pS][0:R, 0:64]), reads=[("ps", pS)], writes=[("r_Ssb", hd)])
                                    oo = ps[pO][hs, c * 128:c * 128 + R]
                                    if not sample_pass:
                                        S.op("pe", lambda e, c=c, hs=hs, oo=oo: e.matmul(oo, lhsT=Sbf[hs, :], rhs=KR[hs, c, 1, 0:R], start=True, stop=False), reads=[("r_KR1", par), ("r_Sbf", hd)], writes=[("ps", pO)])
                                        S.op("pe", lambda e, hd=hd, m_=m_, oo=oo: e.matmul(oo, lhsT=Ssb[0:R, hd, :], rhs=MbT[0:R, m_, 0:R], start=False, stop=False), reads=[("r_Ssb", hd), ("r_MbT", m_)], writes=[("ps", pO)])
                                        S.op("pe", lambda e, c=c, hs=hs, m_=m_, oo=oo: e.matmul(oo, lhsT=v_tok[0:R, c, hs], rhs=MkT[0:R, m_, 0:R], start=False, stop=True), reads=[("r_vtok", par), ("r_MkT", m_)], writes=[("ps", pO)])
                                        pU = self.nextps()
                                        S.op("pe", lambda e, c=c, hs=hs, hd=hd, pU=pU: e.matmul(ps[pU][hs, 0:64], lhsT=bt_tok[0:R, c, hs], rhs=Ssb[0:R, hd, :], start=True, stop=False), reads=[("r_bttok", par), ("r_Ssb", hd)], writes=[("ps", pU)])
                                        S.op("pe", lambda e, c=c, hs=hs, pU=pU: e.matmul(ps[pU][hs, 0:64], lhsT=kt_tok[0:R, c, hs], rhs=v_tok[0:R, c, hs], start=False, stop=True), reads=[("r_kttok", par), ("r_vtok", par)], writes=[("ps", pU)])
                                        S.op("dve", lambda e, c=c, hs=hs, pU=pU: e.scalar_tensor_tensor(out=Sst[hs, :], in0=Sst[hs, :], scalar=EC[hs, c:c + 1], in1=ps[pU][hs, 0:64], op0=ALU.mult, op1=ALU.add), reads=[("ps", pU), ("r_Sst", hd), ("r_EC", par)], writes=[("r_Sst", hd)])
                                        S.op("pool", lambda e, hs=hs: e.tensor_copy(out=Sbf[hs, :], in_=Sst[hs, :]), reads=[("r_Sst", hd)], writes=[("r_Sbf", hd)])
                                    else:
                                        S.op("pe", lambda e, hd=hd, m_=m_, oo=oo: e.matmul(oo, lhsT=Ssb[:, hd, :], rhs=MbT[:, m_, 0:R], start=True, stop=False), reads=[("r_Ssb", hd), ("r_MbT", m_)], writes=[("ps", pO)])
                                        S.op("pe", lambda e, hs=hs, m_=m_, oo=oo: e.matmul(oo, lhsT=v_tok[:, 0, hs], rhs=MkT[:, m_, 0:R], start=False, stop=False), reads=[("r_vtok", par), ("r_MkT", m_)], writes=[("ps", pO)])
                                        for b in range(NB):
                                            S.op("pe", lambda e, b=b, hs=hs, oo=oo: e.matmul(oo, lhsT=S0bf[hs, b, :], rhs=rblk[hs, b, :], start=False, stop=(b == NB - 1)), reads=["r_rblk"] + s0bk, writes=[("ps", pO)])
                                        S.op("dve", lambda e, hd=hd: e.tensor_tensor(out=Sblk[0:NS, hd, :, :], in0=Ssb[0:NS, hd, :].unsqueeze(1).broadcast_to([NS, NB, 64]), in1=self.maskB.unsqueeze(2).broadcast_to([NS, NB, 64]), op=ALU.mult), reads=[("r_Ssb", hd), "consts"], writes=[("r_Sblk", hd)])
                                        S.op("dve", lambda e, hd=hd, hs=hs: e.tensor_tensor(out=Vblk[0:NS, hd, :, :], in0=v_tok[0:NS, 0, hs].unsqueeze(1).broadcast_to([NS, NB, 64]), in1=self.maskB.unsqueeze(2).broadcast_to([NS, NB, 64]), op=ALU.mult), reads=[("r_vtok", par), "consts"], writes=[("r_Vblk", hd)])
                                        for half in range(2):
                                            pU = self.nextps()
                                            S.op("pe", lambda e, hs=hs, hd=hd, half=half, pU=pU: e.matmul(ps[pU][hs, 0:512], lhsT=bt_tok[:, 0, hs], rhs=Sblk[:, hd, half * 8:(half + 1) * 8, :], start=True, stop=False), reads=[("r_bttok", par), ("r_Sblk", hd)], writes=[("ps", pU)])
                                            S.op("pe", lambda e, hs=hs, hd=hd, half=half, pU=pU: e.matmul(ps[pU][hs, 0:512], lhsT=kt_tok[:, 0, hs], rhs=Vblk[:, hd, half * 8:(half + 1) * 8, :], start=False, stop=True), reads=[("r_kttok", par), ("r_Vblk", hd)], writes=[("ps", pU)])
                                            bs = slice(half * 8, (half + 1) * 8)
                                            S.op("dve", lambda e, hs=hs, bs=bs: e.tensor_tensor(out=S0[hs, bs, :], in0=S0[hs, bs, :], in1=EC[hs, bs].unsqueeze(2).broadcast_to([64, 8, 64]), op=ALU.mult), reads=s0k + [("r_EC", par)], writes=s0k)
                                            S.op("dve", lambda e, hs=hs, bs=bs, pU=pU: e.tensor_tensor(out=S0[hs, bs, :], in0=S0[hs, bs, :], in1=ps[pU][hs, 0:512].rearrange("p (b v) -> p b v", b=8), op=ALU.add), reads=s0k + [("ps", pU)], writes=s0k)
                            if sample_pass:
                                for q4 in range(4):
                                    p1 = self.nextps()
                                    for bb in range(4):
                                        b = q4 * 4 + bb
                                        S.op("pe", lambda e, b=b, bb=bb, p1=p1: e.transpose(ps[p1][0:64, bb * 128:(bb + 1) * 128], S0[:, b, :], self.ident), reads=s0k + ["consts"], writes=[("ps", p1)])
                                    self.copy("act", s0nat[:, q4 * 4:(q4 + 1) * 4, :], ps[p1][0:64, :].rearrange("p (b k) -> p b k", b=4), [("ps", p1)], [("r_s0nat", 0), ("r_s0nat", 1)])
                                for hd in range(2):
                                    S.op("sp", lambda e, hd=hd: e.dma_start(out=dr["wkv_s"][:, 2 * pc + hd].rearrange("b v k -> v b k"), in_=s0nat[:, :, hd * 64:(hd + 1) * 64]), reads=[("r_s0nat", 0), ("r_s0nat", 1)], dma="r_wkv_s")
                            elif c0 + w == TP:
                                p1 = self.nextps()
                                S.op("pe", lambda e, p1=p1: e.transpose(ps[p1][0:64, 0:128], Sst, self.ident), reads=[("r_Sst", 0), ("r_Sst", 1), "consts"], writes=[("ps", p1)])
                                self.copy("act", sT, ps[p1][0:64, 0:128], [("ps", p1)], ["r_sT"])
                                S.op("sp", lambda e: e.dma_start(out=dr["wkv_p"][2 * pc:2 * pc + 2].rearrange("h v k -> v h k"), in_=sT.rearrange("v (h k) -> v h k", h=2)), reads=["r_sT"], dma="r_wkv_p")
                            yield 0
                            self.copy("act", osb[:, 0:w], ps[pO][:, 0:w], [("ps", pO)], ["r_osb"])
                            pm = self.nextps()
                            S.op("pe", lambda e: e.matmul(ps[pm][:, 0:w], lhsT=self.blk1, rhs=osb[:, 0:w], start=True, stop=True), reads=["r_osb", "consts"], writes=[("ps", pm)])
                            S.op("dve", lambda e: e.scalar_tensor_tensor(out=osb[:, 0:w], in0=ps[pm][:, 0:w], scalar=-1.0 / 64, in1=osb[:, 0:w], op0=ALU.mult, op1=ALU.add), reads=[("ps", pm), "r_osb"], writes=["r_osb"])
                            S.op("act", lambda e: e.activation(out=t2[:, 0:w], in_=osb[:, 0:w], func=AF.Square), reads=["r_osb", "r_t2"], writes=["r_t2"])
                            pv2 = self.nextps()
                            S.op("pe", lambda e: e.matmul(ps[pv2][:, 0:w], lhsT=self.blk1, rhs=t2[:, 0:w], start=True, stop=True), reads=["r_t2", "consts"], writes=[("ps", pv2)])
                            S.op("act", lambda e: e.activation(out=t2[:, 0:w], in_=ps[pv2][:, 0:w], func=AF.Ln, scale=1.0 / 64, bias=GN_EPS), reads=[("ps", pv2), "r_t2"], writes=["r_t2"])
                            S.op("act", lambda e: e.activation(out=t2[:, 0:w], in_=t2[:, 0:w], func=AF.Exp, scale=-0.5), reads=["r_t2"], writes=["r_t2"])
                            S.op("dve", lambda e: e.tensor_tensor(out=osb[:, 0:w], in0=osb[:, 0:w], in1=t2[:, 0:w], op=ALU.mult), reads=["r_osb", "r_t2"], writes=["r_osb"])
                            S.op("dve", lambda e: e.tensor_scalar(out=osb[:, 0:w], in0=osb[:, 0:w], scalar1=vec("rw_lnx_w"), scalar2=vec("rw_lnx_b"), op0=ALU.mult, op1=ALU.add), reads=["r_osb", "consts"], writes=["r_osb"])
                            S.op("pool", lambda e: e.tensor_tensor(out=osb[:, 0:w], in0=osb[:, 0:w], in1=bon[:, 0:w], op=ALU.add), reads=["r_osb", ("r_bon", par)], writes=["r_osb"])
                            S.op("dve", lambda e: e.tensor_tensor(out=og[:, 0:w], in0=osb[:, 0:w], in1=gf[:, 0:w], op=ALU.mult), reads=["r_osb", ("r_gf", par)], writes=["r_og"])
                            for dc in range(KC):
                                pi = self.nextps()
                                S.op("pe", lambda e, pi=pi, dc=dc: e.matmul(ps[pi][:, 0:w], lhsT=wout[:, dc * 128:(dc + 1) * 128], rhs=og[:, 0:w], start=True, stop=True), reads=[("r_wout", pc % 2), "r_og"], writes=[("ps", pi)])
                                S.op("dve", lambda e, pi=pi, dc=dc: e.tensor_tensor(out=xres[:, dc, c0:c0 + w], in0=xres[:, dc, c0:c0 + w], in1=ps[pi][:, 0:w], op=ALU.add), reads=[("ps", pi), ("xres", dc, c0)], writes=[("xres", dc, c0)])
                                yield 0


                    gens = []
                    gi = 0
                    for pc in range(KC):
                        for bi_, (c0, w, kind) in enumerate(blocks):
                            gens.append(blk(pc, bi_, c0, w, kind, gi % 2))
                            gi += 1
                    back = None
                    for g in gens:
                        fd, bd = False, back is None
                        while not (fd and bd):
                            if not fd:
                                fd = next(g) == "MID"
                            if not bd:
                                bd = next(back, "END") == "END"
                        back = g
                    for _ in back:
                        pass
                S.barrier()

    def mamba2(self, j):
        nc, S, TP = self.nc, self.S, self.TP
        xres, hT, ps = self.xres, self.hT, self.ps
        w_in, w_out = self.dram["mb_w_in"], self.dram["mb_w_out"]
        st_ssm, st_conv = self.dram["st_ssm"], self.dram["st_conv"]
        ssm_p, ssm_s, cv_p, cv_s = self.dram["ssm_p"], self.dram["ssm_s"], self.dram["cv_p"], self.dram["cv_s"]
        W = 512
        with ExitStack() as es:
            T = lambda n, sh, dt=F32: es.enter_context(self.T(n, sh, dt))
            wz, wx = T("m_wz", [128, KC, 512], BF16), T("m_wx", [128, KC, 512], BF16)
            wB, wC, wdt = T("m_wB", [128, KC, 128], BF16), T("m_wC", [128, KC, 128], BF16), T("m_wdt", [128, KC, 8], BF16)
            wout = T("m_wout", [128, 4, D], BF16)
            ngt = T("m_ng", [128, 512])
            pre = T("m_pre", [128, 6, 3 + W])
            acc = T("m_acc", [128, 2, W])
            xsT, BT, CT = T("m_xsT", [128, 4, W], BF16), T("m_BT", [128, W], BF16), T("m_CT", [128, W], BF16)
            cvT, cvrow = T("m_cvT", [128, 6, 48]), T("m_cvrow", [48, 768])
            zs, dtv, da = T("m_zs", [128, 512]), T("m_dt", [128, 8]), T("m_da", [128, 8])
            xtok, Btok = T("m_xtok", [128, 512], BF16), T("m_Btok", [128, 128], BF16)
            cbm, ex, wts = T("m_cbm", [128, 128]), T("m_ex", [128, 24]), T("m_wts", [128, 8])
            daM, seg, mT = T("m_daM", [128, 4, 128]), T("m_seg", [128, 4, 128]), T("m_mT", [128, 4, 128], BF16)
            y1, xd = T("m_y1", [128, 512]), T("m_xd", [128, 512])
            ssq, yn = T("m_ssq", [128, 2]), T("m_yn", [128, 512], BF16)
            yTb = T("m_yTb", [128, 4, W], BF16)
            xw = T("m_xw", [128, 512], BF16)
            ST, STbf = T("m_ST", [128, 512]), T("m_STbf", [128, 512], BF16)
            stT = T("m_stT", [128, 4, 128])
            cv0 = cvrow
            Cblk = T("m_Cblk", [128, NB, NS], BF16)
            ST0bf = T("m_ST0bf", [128, 2, 512], BF16)
            dablk, etots = T("m_dablk", [64, NB, 8]), T("m_etots", [128, NB, 8])
            fz = T("m_fz", [128, 2])
            etn = T("m_etn", [128, NB, 4])
            preflat = pre.rearrange("p f t -> p (f t)")
            snat_slots = [preflat[:, k_ * 512:(k_ + 1) * 512].rearrange("p (q n) -> p q n", q=4) for k_ in range(5)]
            stT_slots = [stT, preflat[:, 2560:3072].rearrange("p (q n) -> p q n", q=4)]
            for g in range(4):
                for kk0 in range(0, KC, 2):
                    for (wt, col) in ((wz, g * 512), (wx, 2048 + g * 512)):
                        src = w_in[j, kk0 * 128:(kk0 + 2) * 128, col:col + 512].rearrange("(k p) c -> p k c", p=128)
                        self.load_w(wt[:, kk0:kk0 + 2, :], src, [128, 2, 512], ("m_w", id(wt), kk0))
                for (wt, col) in ((wB, 4096 + g * 128), (wC, 4608 + g * 128)):
                    self.load_w(wt, w_in[j, :, col:col + 128].rearrange("(k p) c -> p k c", p=128), [128, KC, 128], ("m_w", id(wt)))
                self.load_w(wdt, w_in[j, :, 5120 + 8 * g:5128 + 8 * g].rearrange("(k p) c -> p k c", p=128), [128, KC, 8], ("m_w", id(wdt)))
                for fc in range(4):
                    self.load_w(wout[:, fc, :], w_out[j, g * 512 + fc * 128:g * 512 + (fc + 1) * 128, :], [128, D], ("m_wout", fc))
                S.op("sp", lambda e: e.dma_start(out=ngt, in_=self.dram["mb_norm"][:, g * 512:(g + 1) * 512]), writes=["m_ng"], dma="m_ng")
                wk2 = lambda wt: [("m_w", id(wt), k_) for k_ in range(0, KC, 2)]
                wk1 = lambda wt: [("m_w", id(wt))]
                fcol = [g * 512 + i * 128 for i in range(4)] + [2048 + g * 128, 2560 + g * 128]
                fch24 = [c_ // 128 for c_ in fcol]
                S.op("pool", lambda e: e.memset(ST, 0.0), writes=["m_ST"])
                S.op("pool", lambda e: e.memset(STbf, 0.0), writes=["m_STbf"])
                S.op("pool", lambda e: e.memset(pre[:, :, 0:3], 0.0), writes=["m_pre"] + [("m_snat", k_) for k_ in range(5)] + [("m_stT", 1)])
                S.op("pool", lambda e: e.memset(cvT, 0.0), writes=["m_cvT"])
                for (c0, w, kind) in self.tbs:
                    smp = kind == "s"
                    hc = 2 + c0
                    R = 64 if smp else 128
                    if smp:
                        pre4 = pre[:, :, 0:NB * 7].rearrange("p f (b t) -> p f b t", t=7)
                        for i_, (c_, n_) in enumerate(((fcol[0], 512), (fcol[4], 128), (fcol[5], 128))):
                            o_ = (0, 512, 640)[i_]
                            S.op("sp", lambda e, c_=c_, n_=n_, o_=o_: e.dma_start(out=cv0[:, o_:o_ + n_], in_=st_conv[:, :, c_:c_ + n_].rearrange("b w f -> (b w) f")), writes=["m_cvrow"], dma="m_cv0")
                        pt = self.nextps()
                        for f in range(6):
                            S.op("pe", lambda e, f=f: e.transpose(ps[pt][:, f * 48:(f + 1) * 48], cv0[:, f * 128:(f + 1) * 128], self.ident[0:48, 0:48]), reads=["m_cvrow", "consts"], writes=[("ps", pt)])
                        self.copy("dve", pre4[:, :, :, 0:3], ps[pt][:, 0:288].rearrange("p (f b t) -> p f b t", f=6, t=3), [("ps", pt)], ["m_pre"])
                    elif c0 > 0:
                        self.copy("dve", pre[:, :, 0:3], pre[:, :, W:W + 3], ["m_pre"], ["m_pre"])
                    for f in range(6):
                        pi = self.nextps()
                        wt, cs = (wx, f * 128) if f < 4 else ((wB, 0) if f == 4 else (wC, 0))
                        for kc in range(KC):
                            S.op("pe", lambda e, pi=pi, kc=kc, wt=wt, cs=cs: e.matmul(ps[pi][:, 0:w], lhsT=wt[:, kc, cs:cs + 128], rhs=hT[:, kc, hc:hc + w], start=(kc == 0), stop=(kc == KC - 1)),
                                 reads=(wk2(wt) if f < 4 else wk1(wt)) + [("hT", "all")], writes=[("ps", pi)])
                        if smp:
                            self.copy(self.ew(), pre4[:, f, :, 3:7], ps[pi][:, 0:NS].rearrange("p (b t) -> p b t", t=TS), [("ps", pi)], ["m_pre"])
                        else:
                            self.copy(self.ew(), pre[:, f, 3:3 + W], ps[pi][:, 0:W], [("ps", pi)], ["m_pre"])
                    if MBSTOP == 2:
                        return
                    for f in range(6):
                        f24 = fch24[f]
                        if smp:
                            a_ = acc[:, f % 2, 0:NS].rearrange("p (b t) -> p b t", t=TS)
                            src_k = lambda k_: pre4[:, f, :, k_:k_ + TS]
                        else:
                            a_ = acc[:, f % 2, :]
                            src_k = lambda k_: pre[:, f, k_:k_ + W]
                        S.op("act", lambda e, a_=a_, f24=f24, s0=src_k(0): e.activation(out=a_, in_=s0, func=AF.Identity, scale=self.mb_cw[:, f24, 0:1], bias=self.mb_cb[:, f24:f24 + 1]),
                             reads=["m_pre", "consts"], writes=[("m_acc", f % 2)])
                        for k_ in range(1, 4):
                            S.op("dve", lambda e, a_=a_, f24=f24, k_=k_, sk=src_k(k_): e.scalar_tensor_tensor(out=a_, in0=sk, scalar=self.mb_cw[:, f24, k_:k_ + 1], in1=a_, op0=ALU.mult, op1=ALU.add),
                                 reads=["m_pre", "consts", ("m_acc", f % 2)], writes=[("m_acc", f % 2)])
                        dst = xsT[:, f, 0:w] if f < 4 else (BT[:, 0:w] if f == 4 else CT[:, 0:w])
                        S.op("act", lambda e, dst=dst, f=f: e.activation(out=dst, in_=acc[:, f % 2, 0:w], func=AF.Silu), reads=[("m_acc", f % 2)], writes=[("m_xbc", f)])
                    if MBSTOP == 3:
                        return
                    if smp or c0 + w == TP:
                        nr = 48 if smp else 3
                        if smp:
                            self.copy("dve", cvT.rearrange("p f (b t) -> p f b t", t=3), pre4[:, :, :, 4:7], ["m_pre"], ["m_cvT"])
                        else:
                            self.copy("dve", cvT[:, :, 0:3], pre[:, :, W:W + 3], ["m_pre"], ["m_cvT"])
                        pt, ptx = self.nextps(), self.nextps()
                        for f in range(6):
                            pp = pt if f < 4 else ptx
                            nrt = max(nr, 32)
                            S.op("pe", lambda e, f=f, nrt=nrt, pp=pp: e.transpose(ps[pp][0:nrt, (f % 4) * 128:(f % 4 + 1) * 128], cvT[:, f, 0:nrt], self.ident), reads=["m_cvT", "consts"], writes=[("ps", pp)])
                        self.copy("act", cvrow[0:nr, 0:512], ps[pt][0:nr, 0:512], [("ps", pt)], ["m_cvrow"])
                        self.copy("act", cvrow[0:nr, 512:768], ps[ptx][0:nr, 0:256], [("ps", ptx)], ["m_cvrow"])
                        for i_, (c_, n_) in enumerate(((fcol[0], 512), (fcol[4], 128), (fcol[5], 128))):
                            o_ = (0, 512, 640)[i_]
                            dstd = cv_s[:, :, c_:c_ + n_].rearrange("b w f -> (b w) f") if smp else cv_p[:, c_:c_ + n_]
                            S.op("sp", lambda e, dstd=dstd, o_=o_, n_=n_, nr=nr: e.dma_start(out=dstd, in_=cvrow[0:nr, o_:o_ + n_]), reads=["m_cvrow"], dma="m_cvout")
                    if MBSTOP == 4:
                        return
                    xbk = [("m_xbc", f) for f in range(6)]
                    mU = self.maskS if smp else self.maskU
                    mL = self.sLS if smp else self.sL
                    for ct in range(1 if smp else w // 128):
                        t0 = ct * 128
                        pz, pd = self.nextps(), self.nextps()
                        for kc in range(KC):
                            S.op("pe", lambda e, kc=kc: e.matmul(ps[pz][0:R, 0:512], lhsT=hT[:, kc, hc + t0:hc + t0 + R], rhs=wz[:, kc, :], start=(kc == 0), stop=(kc == KC - 1)),
                                 reads=wk2(wz) + [("hT", "all")], writes=[("ps", pz)])
                        for kc in range(KC):
                            S.op("pe", lambda e, kc=kc: e.matmul(ps[pd][0:R, 0:8], lhsT=hT[:, kc, hc + t0:hc + t0 + R], rhs=wdt[:, kc, :], start=(kc == 0), stop=(kc == KC - 1)),
                                 reads=wk1(wdt) + [("hT", "all")], writes=[("ps", pd)])
                        S.op("act", lambda e: e.activation(out=zs[0:R, :], in_=ps[pz][0:R, 0:512], func=AF.Silu), reads=[("ps", pz)], writes=["m_zs"])
                        S.op("dve", lambda e: e.tensor_tensor(out=dtv[0:R, :], in0=ps[pd][0:R, 0:8], in1=self.mb_dtb[0:R, 8 * g:8 * g + 8], op=ALU.add), reads=[("ps", pd), "consts"], writes=["m_dt"])
                        S.op("act", lambda e: e.activation(out=dtv[0:R, :], in_=dtv[0:R, :], func=AF.Exp), reads=["m_dt"], writes=["m_dt"])
                        S.op("act", lambda e: e.activation(out=dtv[0:R, :], in_=dtv[0:R, :], func=AF.Ln, bias=1.0), reads=["m_dt"], writes=["m_dt"])
                        S.op("dve", lambda e: e.tensor_tensor(out=da[0:R, :], in0=dtv[0:R, :], in1=self.mb_negA[0:R, 8 * g:8 * g + 8], op=ALU.mult), reads=["m_dt", "mbc0"], writes=["m_da"])
                        if MBSTOP == 5:
                            return
                        pt = self.nextps()
                        ptb = ps[pt].bitcast(BF16)
                        for fc in range(4):
                            S.op("pe", lambda e, fc=fc: e.transpose(ptb[0:R, fc * 128:(fc + 1) * 128], xsT[:, fc, t0:t0 + R], self.identb), reads=xbk[0:4] + ["consts2"], writes=[("ps", pt)])
                        S.op("pe", lambda e: e.transpose(ptb[0:R, 512:640], BT[:, t0:t0 + R], self.identb), reads=[xbk[4], "consts2"], writes=[("ps", pt)])
                        self.copy("dve", xtok[0:R, :], ptb[0:R, 0:512], [("ps", pt)], ["m_xtok"])
                        self.copy("dve", Btok[0:R, :], ptb[0:R, 512:640], [("ps", pt)], ["m_Btok"])
                        pc = self.nextps()
                        S.op("pe", lambda e: e.matmul(ps[pc][0:R, 0:R], lhsT=BT[:, t0:t0 + R], rhs=CT[:, t0:t0 + R], start=True, stop=True), reads=[xbk[4], xbk[5]], writes=[("ps", pc)])
                        S.op("dve", lambda e: e.tensor_tensor(out=cbm[0:R, 0:R], in0=ps[pc][0:R, 0:R], in1=mU, op=ALU.mult), reads=[("ps", pc), "consts"], writes=["m_cbm"])
                        if MBSTOP == 6:
                            return
                        pm = self.nextps()
                        S.op("pe", lambda e: e.matmul(ps[pm][0:R, 0:8], lhsT=mU, rhs=da[0:R, :], start=True, stop=True), reads=["m_da", "consts"], writes=[("ps", pm)])
                        S.op("pe", lambda e: e.matmul(ps[pm][0:R, 8:16], lhsT=mL, rhs=da[0:R, :], start=True, stop=True), reads=["m_da", "consts"], writes=[("ps", pm)])
                        if not smp:
                            S.op("pe", lambda e: e.matmul(ps[pm][:, 16:24], lhsT=self.onesf, rhs=da, start=True, stop=True), reads=["m_da", "consts2"], writes=[("ps", pm)])
                            S.op("act", lambda e: e.activation(out=ex, in_=ps[pm][:, 0:24], func=AF.Exp), reads=[("ps", pm)], writes=["m_ex"])
                        else:
                            S.op("act", lambda e: e.activation(out=ex[0:R, 0:16], in_=ps[pm][0:R, 0:16], func=AF.Exp), reads=[("ps", pm)], writes=["m_ex"])
                            S.op("dve", lambda e: e.tensor_tensor(out=dablk, in0=da[0:NS, :].unsqueeze(1).broadcast_to([NS, NB, 8]), in1=self.maskB.unsqueeze(2).broadcast_to([NS, NB, 8]), op=ALU.mult),
                                 reads=["m_da", "consts"], writes=["m_dablk"])
                            pm2 = self.nextps()
                            S.op("pe", lambda e: e.matmul(ps[pm2][:, 0:NB * 8], lhsT=self.onesf[0:NS, :], rhs=dablk.rearrange("p b h -> p (b h)"), start=True, stop=True), reads=["m_dablk", "consts2"], writes=[("ps", pm2)])
                            S.op("act", lambda e: e.activation(out=etots.rearrange("p b h -> p (b h)"), in_=ps[pm2][:, 0:NB * 8], func=AF.Exp), reads=[("ps", pm2)], writes=["m_etots"])
                        S.op("dve", lambda e: e.tensor_tensor(out=wts[0:R, :], in0=dtv[0:R, :], in1=ex[0:R, 8:16], op=ALU.mult), reads=["m_dt", "m_ex"], writes=["m_wts"])
                        if MBSTOP == 7:
                            return
                        py = self.nextlong()
                        for hh in range(8):
                            sl = hh % 4
                            S.op("dve", lambda e, hh=hh, sl=sl: e.tensor_scalar(out=daM[0:R, sl, 0:R], in0=mL, scalar1=da[0:R, hh:hh + 1], scalar2=None, op0=ALU.mult), reads=["m_da", "consts"], writes=[("m_daM", sl)])
                            pD = self.nextps()
                            S.op("pe", lambda e, sl=sl, pD=pD: e.matmul(ps[pD][0:R, 0:R], lhsT=daM[0:R, sl, 0:R], rhs=mU, start=True, stop=True), reads=[("m_daM", sl), "consts"], writes=[("ps", pD)])
                            S.op("act", lambda e, sl=sl, pD=pD: e.activation(out=seg[0:R, sl, 0:R], in_=ps[pD][0:R, 0:R], func=AF.Exp), reads=[("ps", pD)], writes=[("m_seg", sl)])
                            S.op("dve", lambda e, sl=sl, hh=hh: e.scalar_tensor_tensor(out=mT[0:R, sl, 0:R], in0=seg[0:R, sl, 0:R], scalar=dtv[0:R, hh:hh + 1], in1=cbm[0:R, 0:R], op0=ALU.mult, op1=ALU.mult),
                                 reads=[("m_seg", sl), "m_dt", "m_cbm"], writes=[("m_mT", sl)])
                            S.op("pe", lambda e, sl=sl, hh=hh: e.matmul(ps[py][0:R, hh * 64:(hh + 1) * 64], lhsT=mT[0:R, sl, 0:R], rhs=xtok[0:R, hh * 64:(hh + 1) * 64], start=True, stop=True),
                                 reads=[("m_mT", sl), "m_xtok"], writes=[("ps", py)])
                        if MBSTOP == 8:
                            return
                        v3 = lambda t_: t_.rearrange("p (h q) -> p h q", q=64)
                        pyi = self.nextlong()
                        if not smp:
                            S.op("pe", lambda e: e.matmul(ps[pyi][:, 0:512], lhsT=CT[:, t0:t0 + 128], rhs=STbf, start=True, stop=True), reads=[xbk[5], "m_STbf"], writes=[("ps", pyi)])
                        else:
                            S.op("dve", lambda e: e.tensor_tensor(out=Cblk, in0=CT[:, 0:NS].unsqueeze(1).broadcast_to([128, NB, NS]), in1=self.maskC, op=ALU.mult), reads=[xbk[5], "consts"], writes=["m_Cblk"])
                            S.op("dve", lambda e: e.tensor_tensor(out=v3(xw[0:R, :]), in0=v3(xtok[0:R, :]), in1=wts[0:R, :].unsqueeze(2).broadcast_to([R, 8, 64]), op=ALU.mult), reads=["m_xtok", "m_wts"], writes=["m_xw"])
                            S.op("pool", lambda e: e.memset(fz, 0.0), reads=["m_pre", "m_ST", "m_STbf"], writes=[("m_snat", k_) for k_ in range(5)] + [("m_stT", 1), ("m_sbf", 0), ("m_sbf", 1)])
                            et4 = etots.rearrange("p b (q t) -> p b q t", t=2)
                            S.op("dve", lambda e: e.tensor_copy(out=etn[0:64, :, :], in_=et4[0:64, :, :, 0]), reads=["m_etots"], writes=["m_etn"])
                            S.op("dve", lambda e: e.tensor_copy(out=etn[64:128, :, :], in_=et4[64:128, :, :, 1]), reads=["m_etots", "m_etn"], writes=["m_etn"])
                            sbf = ST.bitcast(BF16).rearrange("p (s c) -> p s c", s=2)
                            for b in range(NB):
                                sl = b % 5
                                s2 = b % 2
                                slot = snat_slots[sl]
                                S.op("sp", lambda e, b=b, slot=slot: e.dma_start(out=slot, in_=st_ssm[b, 8 * g:8 * g + 8].rearrange("h p n -> (h p) n").rearrange("(q r) n -> r q n", r=128)),
                                     writes=[("m_snat", sl)], dma="m_snat%d" % sl)
                                S.op("act", lambda e, slot=slot, s2=s2: e.activation(out=sbf[:, s2, :], in_=slot.rearrange("p q n -> p (q n)"), func=AF.Copy), reads=[("m_snat", sl)], writes=[("m_sbf", s2)])
                                pq_ = self.nextps()
                                pqb = ps[pq_].bitcast(BF16)
                                for q_ in range(4):
                                    S.op("pe", lambda e, q_=q_, s2=s2, pqb=pqb: e.transpose(pqb[:, q_ * 128:(q_ + 1) * 128], sbf[:, s2, q_ * 128:(q_ + 1) * 128], self.identb), reads=[("m_sbf", s2), "consts2"], writes=[("ps", pq_)])
                                self.copy("dve", ST0bf[:, s2, :], pqb[:, 0:512], [("ps", pq_)], [("m_ST0bf", s2)])
                                S.op("pe", lambda e, b=b, s2=s2: e.matmul(ps[pyi][0:NS, 0:512], lhsT=Cblk[:, b, :], rhs=ST0bf[:, s2, :], start=(b == 0), stop=(b == NB - 1)),
                                     reads=["m_Cblk", ("m_ST0bf", s2)], writes=[("ps", pyi)])
                                S.op("dve", lambda e, b=b: e.tensor_scalar(out=yn[0:NS, :], in0=xw[0:NS, :], scalar1=self.maskB[:, b:b + 1], scalar2=None, op0=ALU.mult), reads=["m_xw", "consts"], writes=["m_yn"])
                                pS = self.nextps()
                                for q_ in range(4):
                                    S.op("pe", lambda e, q_=q_, pS=pS: e.matmul(ps[pS][:, q_ * 128:(q_ + 1) * 128], lhsT=yn[0:NS, q_ * 128:(q_ + 1) * 128], rhs=Btok[0:NS, :], start=True, stop=True), reads=["m_Btok", "m_yn"], writes=[("ps", pS)])
                                S.op("dve", lambda e, b=b, slot=slot: e.tensor_tensor(out=slot, in0=slot, in1=etn[:, b, :].unsqueeze(2).broadcast_to([128, 4, 128]), op=ALU.mult), reads=[("m_snat", sl), "m_etn"], writes=[("m_snat", sl)])
                                S.op("dve", lambda e, slot=slot, pS=pS: e.tensor_tensor(out=slot, in0=slot, in1=ps[pS][:, 0:512].rearrange("p (q n) -> p q n", q=4), op=ALU.add), reads=[("m_snat", sl), ("ps", pS)], writes=[("m_snat", sl)])
                                S.op("sp", lambda e, b=b, slot=slot: e.dma_start(out=ssm_s[b, 8 * g:8 * g + 8].rearrange("h p n -> (h p) n").rearrange("(q r) n -> r q n", r=128), in_=slot), reads=[("m_snat", sl)], dma="m_ssm_s%d" % sl)
                        ecb = ex[0:R, 0:8].unsqueeze(2).broadcast_to([R, 8, 64])
                        S.op("dve", lambda e: e.tensor_tensor(out=v3(y1[0:R, :]), in0=v3(ps[pyi][0:R, 0:512]), in1=ecb, op=ALU.mult), reads=[("ps", pyi), "m_ex"], writes=["m_y1"])
                        S.op("dve", lambda e: e.tensor_tensor(out=y1[0:R, :], in0=y1[0:R, :], in1=ps[py][0:R, 0:512], op=ALU.add), reads=[("ps", py), "m_y1"], writes=["m_y1"])
                        S.op("dve", lambda e: e.tensor_tensor(out=v3(xd[0:R, :]), in0=v3(xtok[0:R, :]), in1=self.mb_D[0:R, 8 * g:8 * g + 8].unsqueeze(2).broadcast_to([R, 8, 64]), op=ALU.mult), reads=["m_xtok", "consts"], writes=["m_xd"])
                        S.op("dve", lambda e: e.tensor_tensor(out=y1[0:R, :], in0=y1[0:R, :], in1=xd[0:R, :], op=ALU.add), reads=["m_xd", "m_y1"], writes=["m_y1"])
                        S.op("dve", lambda e: e.tensor_tensor(out=y1[0:R, :], in0=y1[0:R, :], in1=zs[0:R, :], op=ALU.mult), reads=["m_zs", "m_y1"], writes=["m_y1"])
                        S.op("act", lambda e: e.activation(out=xd[0:R, :], in_=y1[0:R, :], func=AF.Square, accum_out=ssq[0:R, 0:1]), reads=["m_y1", "m_xd"], writes=["m_xd", "m_ssq"])
                        S.op("act", lambda e: e.activation(out=ssq[0:R, 1:2], in_=ssq[0:R, 0:1], func=AF.Ln, scale=1.0 / 512, bias=EPS), reads=["m_ssq"], writes=["m_ssq"])
                        S.op("act", lambda e: e.activation(out=ssq[0:R, 1:2], in_=ssq[0:R, 1:2], func=AF.Exp, scale=-0.5), reads=["m_ssq"], writes=["m_ssq"])
                        S.op("dve", lambda e: e.scalar_tensor_tensor(out=yn[0:R, :], in0=y1[0:R, :], scalar=ssq[0:R, 1:2], in1=ngt[0:R, :], op0=ALU.mult, op1=ALU.mult),
                             reads=["m_y1", "m_ssq", "m_ng"], writes=["m_yn"])
                        pt2 = self.nextps()
                        pt2b = ps[pt2].bitcast(BF16)
                        for fc in range(4):
                            S.op("pe", lambda e, fc=fc: e.transpose(pt2b[:, fc * 128:fc * 128 + R], yn[0:R, fc * 128:(fc + 1) * 128], self.identb[0:R, 0:R]), reads=["m_yn", "consts2"], writes=[("ps", pt2)])
                        self.copy("dve", yTb[:, :, t0:t0 + R], pt2b[:, 0:512].rearrange("p (f t) -> p f t", f=4)[:, :, 0:R], [("ps", pt2)], [("m_yTb", ct)])
                        if MBSTOP == 9:
                            return
                        if not smp:
                            S.op("dve", lambda e: e.tensor_tensor(out=v3(xw[0:R, :]), in0=v3(xtok[0:R, :]), in1=wts[0:R, :].unsqueeze(2).broadcast_to([R, 8, 64]), op=ALU.mult), reads=["m_xtok", "m_wts"], writes=["m_xw"])
                            pS = self.nextps()
                            S.op("pe", lambda e: e.matmul(ps[pS][:, 0:512], lhsT=Btok, rhs=xw, start=True, stop=True), reads=["m_Btok", "m_xw"], writes=[("ps", pS)])
                            S.op("dve", lambda e: e.tensor_tensor(out=v3(ST), in0=v3(ST), in1=ex[:, 16:24].unsqueeze(2).broadcast_to([128, 8, 64]), op=ALU.mult), reads=["m_ST", "m_ex"], writes=["m_ST"])
                            S.op("dve", lambda e: e.tensor_tensor(out=ST, in0=ST, in1=ps[pS][:, 0:512], op=ALU.add), reads=["m_ST", ("ps", pS)], writes=["m_ST"])
                            S.op("pool", lambda e: e.tensor_copy(out=STbf, in_=ST), reads=["m_ST"], writes=["m_STbf"])
                    for dc in range(KC):
                        pi = self.nextps()
                        for fc in range(4):
                            S.op("pe", lambda e, pi=pi, dc=dc, fc=fc: e.matmul(ps[pi][:, 0:w], lhsT=wout[:, fc, dc * 128:(dc + 1) * 128], rhs=yTb[:, fc, 0:w], start=(fc == 0), stop=(fc == 3)),
                                 reads=[("m_wout", fc)] + [("m_yTb", c_) for c_ in range(4)], writes=[("ps", pi)])
                        S.op("dve", lambda e, pi=pi, dc=dc: e.tensor_tensor(out=xres[:, dc, c0:c0 + w], in0=xres[:, dc, c0:c0 + w], in1=ps[pi][:, 0:w], op=ALU.add),
                             reads=[("ps", pi), ("xres", dc, c0)], writes=[("xres", dc, c0)])
                    if (not smp) and c0 + w == TP:
                        pq2 = self.nextps()
                        for q_ in range(4):
                            S.op("pe", lambda e, q_=q_: e.transpose(ps[pq2][:, q_ * 128:(q_ + 1) * 128], ST[:, q_ * 128:(q_ + 1) * 128], self.ident), reads=["m_ST", "consts"], writes=[("ps", pq2)])
                        self.copy("act", stT.rearrange("p q n -> p (q n)"), ps[pq2][:, 0:512], [("ps", pq2)], [("m_stT", 0)])
                        S.op("sp", lambda e: e.dma_start(out=ssm_p[8 * g:8 * g + 8].rearrange("h p n -> (h p) n").rearrange("(q r) n -> r q n", r=128), in_=stT), reads=[("m_stT", 0)], dma="m_ssm_p")


def make_in_map(inp, core, TP, names):
    b0 = core * NB
    m = {}
    m["x_p"] = np.ascontiguousarray(inp["x_prompt"][core, :TP])
    m["x_s"] = np.ascontiguousarray(inp["x_sample"][b0:b0 + NB].reshape(NS, D))
    m["ident"] = np.eye(128, dtype=np.float32)
    m["norm_mix"] = np.ascontiguousarray(_fm(inp["norm_mix"]))
    m["norm_ffn"] = np.ascontiguousarray(_fm(inp["norm_ffn"]))
    m["norm_final"] = np.ascontiguousarray(_fm(inp["norm_final"]))
    for k in ("ffn_w_gate", "ffn_w_up", "ffn_w_down"):
        m[k] = inp[k]
    extra_in_map(m, inp, core, TP)
    return {k: np.ascontiguousarray(m[k], dtype=np.float32) for k in names}


def _masks():
    r = np.arange(128)
    maskU2 = ((r[:, None] // 64 == r[None, :] // 64) & (r[None, :] % 64 >= r[:, None] % 64)).astype(np.float32)
    r = np.arange(64)
    maskS = ((r[:, None] // TS == r[None, :] // TS) & (r[None, :] >= r[:, None])).astype(np.float32)
    maskB = (r[:, None] // TS == np.arange(NB)[None, :]).astype(np.float32)
    return maskU2, maskS, maskB


def extra_in_map(m, inp, core, TP):
    b0 = core * NB
    m["maskU2"], m["maskS"], m["maskB"] = _masks()
    m["hg_lb_logits"] = _fm(inp["hg_lb_logits"])
    m["hg_norm"] = np.ascontiguousarray(inp["hg_norm"].T)
    m["hg_w_in"] = inp["hg_w_in"]
    m["hg_w_out"] = inp["hg_w_out"]
    m["st_hg"] = inp["state_hgrn"][:, b0:b0 + NB]
    r = np.arange(128)
    m["maskU"] = (r[None, :] >= r[:, None]).astype(np.float32)
    m["sL"] = (r[:, None] > r[None, :]).astype(np.float32)
    r = np.arange(64)
    m["sLS"] = ((r[:, None] // TS == r[None, :] // TS) & (r[:, None] > r[None, :])).astype(np.float32)
    m["maskC"] = np.broadcast_to((r[None, :] // TS == np.arange(NB)[:, None]).astype(np.float32)[None], (128, NB, NS))
    r = np.arange(128)
    m["sU"] = (r[:, None] < r[None, :]).astype(np.float32)
    m["blk1"] = (r[:, None] // 64 == r[None, :] // 64).astype(np.float32)
    r = np.arange(64)
    m["sUS"] = ((r[:, None] // TS == r[None, :] // TS) & (r[:, None] < r[None, :])).astype(np.float32)
    m["rw_mu"] = _fm(inp["rw_mu"][0])
    for k in ("rw_w0", "rw_a0", "rw_k_k", "rw_k_a", "rw_lnx_w", "rw_lnx_b"):
        m[k] = _fm(inp[k][0])
    m["rw_r_k"] = _fm(inp["rw_r_k"][0].reshape(D))
    m["rw_w_rkv"] = inp["rw_w_rkv"][0]
    for k in ("rw_w1", "rw_w2", "rw_a1", "rw_a2", "rw_g1", "rw_g2", "rw_w_out"):
        m[k] = inp[k][0]
    m["st_wkv"] = inp["state_wkv"][0, b0:b0 + NB]
    m["st_shift"] = inp["state_shift"][0, b0:b0 + NB]
    m["mb_conv_w"] = np.ascontiguousarray(inp["mb_conv_w"][0].reshape(4, 24, 128).transpose(2, 1, 0))
    m["mb_conv_b"] = np.ascontiguousarray(inp["mb_conv_b"][0].reshape(24, 128).T)
    for k in ("mb_dt_bias", "mb_A_log", "mb_D"):
        m[k] = np.broadcast_to(inp[k][0][None, :], (128, 32))
    m["mb_norm"] = np.broadcast_to(inp["mb_norm"][0][None, :], (128, 2048))
    m["mb_w_in"] = inp["mb_w_in"]
    m["mb_w_out"] = inp["mb_w_out"]
    m["st_ssm"] = inp["state_ssm"][0, b0:b0 + NB]
    m["st_conv"] = inp["state_conv"][0, b0:b0 + NB]


_CACHE = {}


def run_cores(inp, TP, cores, layers=(0, 1, 2, 0), with_ffn=True):
    key = (TP, tuple(layers), with_ffn)
    bld = Builder(TP, layers, with_ffn)
    nc = bld.build()
    names = [k for k, v in bld.dram.items() if k in bld.in_names]
    in_maps = [make_in_map(inp, c, TP, names) for c in cores]
    res = run_bass_kernel_spmd(nc, in_maps, core_ids=list(range(len(cores))))
    return res.results


def kernel(**inputs):
    inp = {k: np.asarray(v) for k, v in inputs.items()}
    TP = inp["x_prompt"].shape[1]
    res = run_cores(inp, TP, list(range(NCORES)))
    st = lambda k: np.stack([r[k] for r in res], axis=0)
    cat = lambda k, ax=0: np.concatenate([r[k] for r in res], axis=ax)
    f = lambda a: np.ascontiguousarray(a, dtype=np.float32)
    y_p = st("y_p")
    y_s = cat("y_s").reshape(NCORES * NB, TS, D)
    hg_p = np.stack([r["hg_p"] for r in res], axis=1)
    hg_s = cat("hg_s", 1)
    return (f(y_p), f(y_s), f(hg_p), f(hg_s), f(st("wkv_p")[None]), f(cat("wkv_s")[None]),
            f(cat("sh_p")[None]), f(cat("sh_s")[None]), f(st("ssm_p")[None]), f(cat("ssm_s")[None]),
            f(st("cv_p")[None]), f(cat("cv_s")[None]))
```
